# Optimizing a Trainium2 kernel written in Bass

```python
import math, functools
import jax, jax.numpy as jnp
from jax import lax
import numpy as np


D_MODEL = 2048
BATCH = 2
SEQ = 4096
DEPTH = 1
DEC_BATCH = 128
DEC_SEQ = 8
PAST_LEN = 8192
PAGE_SIZE = 128

N_HEADS = 32
KV_HEADS = 4
HEAD_DIM = 64
GROUP = N_HEADS // KV_HEADS
WINDOW = 128
BLOCK = 128
N_BUCKETS = 32
MAX_DISTANCE = 128
D_CONV = D_MODEL
CONV_WIDTH = 31
D_FF = 4 * D_MODEL
FFN_CONV_WIDTH = 3
EPS = 1e-6
D_Q = N_HEADS * HEAD_DIM
D_KV = KV_HEADS * HEAD_DIM
D_IN = 2 * D_CONV + D_Q + 2 * D_KV + 2 * D_MODEL

kernel_name = "hybrid_conformer_swa_sink_convffn_step"


def rms_norm(x, g):
    xf = x.astype(jnp.float32)
    y = xf * lax.rsqrt(jnp.mean(xf * xf, axis=-1, keepdims=True) + EPS)
    return (y * g.astype(jnp.float32)).astype(x.dtype)


def layer_norm(x, g, b):
    xf = x.astype(jnp.float32)
    mu = jnp.mean(xf, axis=-1, keepdims=True)
    var = jnp.mean(jnp.square(xf - mu), axis=-1, keepdims=True)
    y = (xf - mu) * lax.rsqrt(var + EPS)
    return (y * g.astype(jnp.float32) + b.astype(jnp.float32)).astype(x.dtype)


def causal_depthwise_conv(x_ext, k, b):
    c = x_ext.shape[-1]
    y = lax.conv_general_dilated(x_ext, k[:, None, :].astype(x_ext.dtype), window_strides=(1,),
                                 padding='VALID', dimension_numbers=('NWC', 'WIO', 'NWC'),
                                 feature_group_count=c)
    return y + b


def rel_bucket(dist):
    max_exact = N_BUCKETS // 2
    d = jnp.maximum(dist, 1).astype(jnp.float32)
    large = max_exact + (jnp.log(d / max_exact) / math.log(MAX_DISTANCE / max_exact)
                         * (N_BUCKETS - max_exact)).astype(jnp.int32)
    large = jnp.minimum(large, N_BUCKETS - 1)
    return jnp.where(dist < max_exact, dist, large)


def window_softmax_attention(q, k, v, dist, valid, sinks, rel_bias):
    n, nq = q.shape[:2]
    nk = k.shape[1]
    qg = q.reshape(n, nq, KV_HEADS, GROUP, HEAD_DIM)
    s = jnp.einsum('nqhgd,nshd->nhgqs', qg, k).astype(jnp.float32) * (HEAD_DIM ** -0.5)
    bias = rel_bias[rel_bucket(jnp.maximum(dist, 0))].astype(jnp.float32)
    bias = jnp.transpose(bias, (2, 0, 1)).reshape(KV_HEADS, GROUP, nq, nk)
    s = jnp.where(valid[:, None, None], s + bias, -jnp.inf)
    sink = sinks.astype(jnp.float32).reshape(KV_HEADS, GROUP, 1, 1)
    m = jnp.maximum(jnp.max(s, axis=-1, keepdims=True), sink)
    p = jnp.exp(s - m)
    p = p / (jnp.sum(p, axis=-1, keepdims=True) + jnp.exp(sink - m))
    o = jnp.einsum('nhgqs,nshd->nqhgd', p.astype(v.dtype), v)
    return o.reshape(n, nq, D_Q)


def prompt_attention(q, k, v, keep, sinks, rel_bias):
    b, t = q.shape[:2]
    nb = t // BLOCK
    qb = q.reshape(b * nb, BLOCK, N_HEADS, HEAD_DIM)
    kb = k.reshape(b, nb, BLOCK, KV_HEADS, HEAD_DIM)
    vb = v.reshape(b, nb, BLOCK, KV_HEADS, HEAD_DIM)
    pad = ((0, 0), (1, 0), (0, 0), (0, 0), (0, 0))
    k_band = jnp.concatenate([jnp.pad(kb, pad)[:, :-1], kb], axis=2).reshape(b * nb, 2 * BLOCK, KV_HEADS, HEAD_DIM)
    v_band = jnp.concatenate([jnp.pad(vb, pad)[:, :-1], vb], axis=2).reshape(b * nb, 2 * BLOCK, KV_HEADS, HEAD_DIM)
    qi = jnp.arange(BLOCK)[:, None]
    kr = jnp.arange(2 * BLOCK)[None, :]
    dist = qi + BLOCK - kr
    k_pos = jnp.arange(nb)[:, None, None] * BLOCK - BLOCK + kr
    valid = (dist >= 0) & (dist < WINDOW) & (k_pos >= 0)
    valid = jnp.tile(valid, (b, 1, 1))
    o = window_softmax_attention(qb, k_band, v_band, dist, valid, sinks, rel_bias)
    return o.reshape(b, t, D_Q), k[:, t - keep:], v[:, t - keep:]


def sample_attention(q, k, v, cache_k, cache_v, sinks, rel_bias):
    t = q.shape[1]
    keep = cache_k.shape[1]
    k_all = jnp.concatenate([cache_k, k], axis=1)
    v_all = jnp.concatenate([cache_v, v], axis=1)
    dist = jnp.arange(t)[:, None] + keep - jnp.arange(keep + t)[None, :]
    valid = ((dist >= 0) & (dist < WINDOW))[None]
    o = window_softmax_attention(q, k_all, v_all, dist, valid, sinks, rel_bias)
    return o, k_all[:, -keep:], v_all[:, -keep:]


def trunk_layer(x, conv_prev, ffn_prev, attention, params):
    (g_pre, w_in, b_in, conv_k, conv_b, ln_g, ln_b, w_conv_proj, w_attn_proj, w_out, g_post,
     g_ffn_pre, w_up, ffn_k, ffn_b, w_down, g_ffn_post) = params
    n, t, _ = x.shape
    xn = rms_norm(x, g_pre)
    u = xn @ w_in + b_in
    idx = np.cumsum([D_CONV, D_CONV, D_Q, D_KV, D_KV, D_MODEL]).tolist()
    glu_a, glu_b, q, k, v, gate_c, gate_a = jnp.split(u, idx, axis=-1)
    glu = glu_a * jax.nn.sigmoid(glu_b)
    glu_ext = jnp.concatenate([conv_prev, glu], axis=1)
    c = causal_depthwise_conv(glu_ext, conv_k, conv_b)
    conv_out = jax.nn.silu(layer_norm(c, ln_g, ln_b)) @ w_conv_proj
    new_conv = glu_ext[:, -(CONV_WIDTH - 1):]
    attn, new_k, new_v = attention(q.reshape(n, t, N_HEADS, HEAD_DIM),
                                   k.reshape(n, t, KV_HEADS, HEAD_DIM),
                                   v.reshape(n, t, KV_HEADS, HEAD_DIM))
    attn_out = attn @ w_attn_proj
    mixed = jax.nn.sigmoid(gate_c) * conv_out + jax.nn.sigmoid(gate_a) * attn_out
    h = x + rms_norm(mixed @ w_out, g_post)
    up = rms_norm(h, g_ffn_pre) @ w_up
    up_ext = jnp.concatenate([ffn_prev, up], axis=1)
    gate, val = jnp.split(causal_depthwise_conv(up_ext, ffn_k, ffn_b), 2, axis=-1)
    y = h + rms_norm((jax.nn.gelu(gate, approximate=True) * val) @ w_down, g_ffn_post)
    return y, new_k, new_v, new_conv, up_ext[:, -(FFN_CONV_WIDTH - 1):]


def setup_inputs(seed: int = 0) -> dict:
    key = jax.random.key(seed)
    ks = jax.random.split(key, 32)
    f32 = jnp.float32
    keep = min(WINDOW, PAST_LEN)

    def nrm(k, shape, scale):
        return jax.random.normal(k, shape, f32) * scale

    def gain(k, shape):
        return 1.0 + 0.01 * jax.random.normal(k, shape, f32)

    return {
        "x_prompt": nrm(ks[0], (BATCH, SEQ, D_MODEL), 1.0),
        "x_sample": nrm(ks[1], (DEC_BATCH, DEC_SEQ, D_MODEL), 1.0),
        "cache_k": nrm(ks[2], (DEPTH, DEC_BATCH, keep, KV_HEADS, HEAD_DIM), 1.0),
        "cache_v": nrm(ks[3], (DEPTH, DEC_BATCH, keep, KV_HEADS, HEAD_DIM), 1.0),
        "state_conv": nrm(ks[4], (DEPTH, DEC_BATCH, CONV_WIDTH - 1, D_CONV), 0.5),
        "state_ffn_conv": nrm(ks[5], (DEPTH, DEC_BATCH, FFN_CONV_WIDTH - 1, 2 * D_FF), 1.0),
        "norm_mix_pre": gain(ks[6], (DEPTH, D_MODEL)),
        "w_in": nrm(ks[7], (DEPTH, D_MODEL, D_IN), D_MODEL ** -0.5),
        "b_in": nrm(ks[8], (DEPTH, D_IN), 0.01),
        "conv_dw_k": nrm(ks[9], (DEPTH, CONV_WIDTH, D_CONV), CONV_WIDTH ** -0.5),
        "conv_dw_b": nrm(ks[10], (DEPTH, D_CONV), 0.01),
        "conv_ln_g": gain(ks[11], (DEPTH, D_CONV)),
        "conv_ln_b": nrm(ks[12], (DEPTH, D_CONV), 0.01),
        "w_conv_proj": nrm(ks[13], (DEPTH, D_CONV, D_MODEL), D_CONV ** -0.5),
        "attn_sinks": nrm(ks[14], (DEPTH, N_HEADS), 0.5),
        "rel_bias": nrm(ks[15], (N_BUCKETS, N_HEADS), 0.1),
        "w_attn_proj": nrm(ks[16], (DEPTH, D_Q, D_MODEL), D_Q ** -0.5),
        "w_out": nrm(ks[17], (DEPTH, D_MODEL, D_MODEL), D_MODEL ** -0.5),
        "norm_mix_post": gain(ks[18], (DEPTH, D_MODEL)),
        "norm_ffn_pre": gain(ks[19], (DEPTH, D_MODEL)),
        "w_up": nrm(ks[20], (DEPTH, D_MODEL, 2 * D_FF), D_MODEL ** -0.5),
        "ffn_dw_k": nrm(ks[21], (DEPTH, FFN_CONV_WIDTH, 2 * D_FF), FFN_CONV_WIDTH ** -0.5),
        "ffn_dw_b": nrm(ks[22], (DEPTH, 2 * D_FF), 0.01),
        "w_down": nrm(ks[23], (DEPTH, D_FF, D_MODEL), D_FF ** -0.5),
        "norm_ffn_post": gain(ks[24], (DEPTH, D_MODEL)),
    }


def reference(x_prompt, x_sample, cache_k, cache_v, state_conv, state_ffn_conv,
              norm_mix_pre, w_in, b_in, conv_dw_k, conv_dw_b, conv_ln_g, conv_ln_b, w_conv_proj,
              attn_sinks, rel_bias, w_attn_proj, w_out, norm_mix_post,
              norm_ffn_pre, w_up, ffn_dw_k, ffn_dw_b, w_down, norm_ffn_post):
    keep = cache_k.shape[2]
    b = x_prompt.shape[0]
    y_p, y_s = x_prompt, x_sample
    kp, vp, cp, fp, ksl, vsl, csl, fsl = ([] for _ in range(8))
    for l in range(DEPTH):
        params = (norm_mix_pre[l], w_in[l], b_in[l], conv_dw_k[l], conv_dw_b[l], conv_ln_g[l], conv_ln_b[l],
                  w_conv_proj[l], w_attn_proj[l], w_out[l], norm_mix_post[l],
                  norm_ffn_pre[l], w_up[l], ffn_dw_k[l], ffn_dw_b[l], w_down[l], norm_ffn_post[l])
        attn_p = functools.partial(prompt_attention, keep=keep, sinks=attn_sinks[l], rel_bias=rel_bias)
        conv0 = jnp.zeros((b, CONV_WIDTH - 1, D_CONV), x_prompt.dtype)
        ffn0 = jnp.zeros((b, FFN_CONV_WIDTH - 1, 2 * D_FF), x_prompt.dtype)
        y_p, k1, v1, c1, f1 = trunk_layer(y_p, conv0, ffn0, attn_p, params)
        attn_s = functools.partial(sample_attention, cache_k=cache_k[l], cache_v=cache_v[l],
                                   sinks=attn_sinks[l], rel_bias=rel_bias)
        y_s, k2, v2, c2, f2 = trunk_layer(y_s, state_conv[l], state_ffn_conv[l], attn_s, params)
        kp.append(k1); vp.append(v1); cp.append(c1); fp.append(f1)
        ksl.append(k2); vsl.append(v2); csl.append(c2); fsl.append(f2)
    k_prompt = jnp.stack(kp)
    v_prompt = jnp.stack(vp)
    conv_prompt = jnp.stack(cp)
    ffn_conv_prompt = jnp.stack(fp)
    k_sample = jnp.stack(ksl)
    v_sample = jnp.stack(vsl)
    conv_sample = jnp.stack(csl)
    ffn_conv_sample = jnp.stack(fsl)
    return (y_p, y_s, k_prompt, v_prompt, conv_prompt, ffn_conv_prompt, k_sample, v_sample, conv_sample, ffn_conv_sample)
```

```python
import contextlib
import math
import numpy as np
import concourse.bass as bass
import concourse.mybir as mybir
from concourse.bass_utils import run_bass_kernel_spmd

F32 = mybir.dt.float32
BF16 = mybir.dt.bfloat16
AF = mybir.ActivationFunctionType
ALU = mybir.AluOpType

D = 2048
NH = 32
DIN = 10752
DFF2 = 16384
EPS = 1e-6
NCORES = 8
NROWS = 1282
NEG = -30000.0


class Op:
    __slots__ = ("eng", "fn", "deps", "signal", "sigval", "dma", "idx")

    def __init__(self, eng, fn, dma=None):
        self.eng = eng; self.fn = fn; self.deps = []; self.signal = False
        self.sigval = 0; self.dma = dma; self.idx = 0


class Sched:
    ENGS = ("pe", "act", "dve", "pool", "sp")

    def __init__(self):
        self.q = {e: [] for e in self.ENGS}
        self.lastw = {}
        self.readers = {}
        self.nops = 0

    def add(self, eng, fn, reads=(), writes=(), dma=None):
        op = Op(eng, fn, dma)
        op.idx = self.nops; self.nops += 1
        deps = {}
        for r in reads:
            w = self.lastw.get(r)
            if w is not None: deps[id(w)] = w
        for w in writes:
            lw = self.lastw.get(w)
            if lw is not None: deps[id(lw)] = lw
            for rd in self.readers.get(w, ()):
                deps[id(rd)] = rd
        for d in deps.values():
            if d is op: continue
            if d.dma is None and op.dma is None and d.eng == eng and eng == "pe":
                continue
            op.deps.append(d)
            d.signal = True
        for w in writes:
            self.lastw[w] = op
            self.readers[w] = []
        for r in reads:
            if r in writes: continue
            self.readers.setdefault(r, []).append(op)
        self.q[eng].append(op)
        return op

    def finalize(self):
        for e in self.ENGS:
            cnt = 0
            for op in self.q[e]:
                if op.dma is None and op.signal:
                    cnt += 1; op.sigval = cnt
        gc = {}
        allops = sorted([op for e in self.ENGS for op in self.q[e] if op.dma is not None], key=lambda o: o.idx)
        for op in allops:
            gc[op.dma] = gc.get(op.dma, 0) + 16
            op.sigval = gc[op.dma]
        self.dma_final = gc

    def emit(self, eng, e, sems):
        waited = {}
        for op in self.q[eng]:
            need = {}
            for d in op.deps:
                key = ("dma", d.dma) if d.dma is not None else ("eng", d.eng)
                if need.get(key, 0) < d.sigval: need[key] = d.sigval
            for key, val in need.items():
                if waited.get(key, 0) < val:
                    e.wait_ge(sems[key], val)
                    waited[key] = val
            ins = op.fn(e)
            if op.dma is not None:
                ins.then_inc(sems[("dma", op.dma)], 16)
            elif op.signal:
                ins.then_inc(sems[("eng", eng)], 1)


def head_of(half, cp):
    if cp < 8:
        return cp if half == 0 else 8 + cp
    return 16 + (cp - 8) if half == 0 else 24 + (cp - 8)


def win_perm():
    idx = []
    for c in range(16):
        idx += list(range(c * 128, c * 128 + 128))
        idx += list(range(2048 + c * 128, 2048 + c * 128 + 128))
    for cp in range(16):
        for half in range(2):
            h = head_of(half, cp)
            idx += list(range(4096 + h * 64, 4096 + h * 64 + 64))
    idx += list(range(6144, 6144 + 512))
    idx += list(range(6656, 6656 + 4096))
    return np.array(idx, dtype=np.int64)


def attn_row_perm():
    idx = []
    for cp in range(16):
        for half in range(2):
            h = head_of(half, cp)
            idx += list(range(h * 64, h * 64 + 64))
    return np.array(idx, dtype=np.int64)


def wup_perm():
    idx = []
    for j in range(64):
        idx += list(range(j * 128, j * 128 + 128))
        idx += list(range(8192 + j * 128, 8192 + j * 128 + 128))
    return np.array(idx, dtype=np.int64)


def rel_bucket_np(dist):
    max_exact = 16
    d = np.maximum(dist, 1).astype(np.float32)
    large = max_exact + (np.log(d / max_exact) / math.log(128 / max_exact) * (32 - max_exact)).astype(np.int32)
    large = np.minimum(large, 31)
    return np.where(dist < max_exact, dist, large)


def fmT(v, nch):
    return np.ascontiguousarray(v.reshape(nch, 128).T)


TILES = [("p", [(128, 2), (130, 128)])] + [("p", [(258 + 128 * j, 128)]) for j in range(7)] + [("s", [(1154, 128)])]
LASTP = 7
WMAX = 130
EXT = 128


def build():
    nc = bass.Bass("TRN2", target_bir_lowering=False)

    def din(name, shape, dt=F32):
        return nc.dram_tensor(name, list(shape), dt, kind="ExternalInput").ap()

    def dout(name, shape, dt=F32):
        return nc.dram_tensor(name, list(shape), dt, kind="ExternalOutput").ap()

    xin = din("xin", [NROWS, D])
    kmask_d = din("kmask", [128, 2])
    ck_d = din("ck", [16, 128, 256]); cv_d = din("cv", [16, 128, 256])
    sc_d = din("sc", [16, 30, D]); sf_d = din("sf", [16, 2, DFF2])
    w_in_d = din("w_in", [D, DIN]); b_in_T_d = din("b_in_T", [128, 84]); b_kv_d = din("b_kv", [512])
    conv_k_d = din("conv_k_T", [128, 16, 31]); conv_b_d = din("conv_b_T", [128, 16])
    ln_g_d = din("ln_g_T", [128, 16]); ln_b_d = din("ln_b_T", [128, 16])
    w_cp_d = din("w_cp", [D, D]); w_ap_d = din("w_ap", [D, D]); w_o_d = din("w_o", [D, D])
    g_pre_d = din("g_pre", [D]); g_post_d = din("g_post", [D]); g_fpre_d = din("g_fpre", [D]); g_fpost_d = din("g_fpost", [D])
    w_up_d = din("w_up", [D, DFF2]); ffn_k_d = din("ffn_k_T", [128, 128, 3]); ffn_b_d = din("ffn_b_T", [128, 128])
    w_dn_d = din("w_dn", [8192, D])
    sink_d = din("sinkT", [128, 16]); rbp_d = din("rbp", [32, 32])
    ohb_d = din("ohb", [32, 128]); jm_d = din("jm", [128, 384]); mneg_d = din("mneg", [128, 2, 129])

    y_d = dout("y", [1152, D])
    kvp_d = dout("kvp", [128, 512])
    convp_d = dout("convp", [30, D])
    ffnp_d = dout("ffnp", [2, DFF2])
    ks_d = dout("ks", [16, 128, 256]); vs_d = dout("vs", [16, 128, 256])
    convs_d = dout("convs", [16, 30, D])
    ffns_d = dout("ffns", [32, DFF2])

    S = Sched()
    out_dma_groups = []

    with contextlib.ExitStack() as es:
        def sb(name, shape, dt=F32):
            return es.enter_context(nc.sbuf_tensor(name, list(shape), dt))

        def ps(name, shape, dt=F32):
            return es.enter_context(nc.psum_tensor(name, list(shape), dt))

        A = S.add
        NBK = 6
        banks = [ps("bk%d" % i, [128, 512], F32) for i in range(NBK)]
        tbanks = [ps("tb%d" % i, [128, 1024], BF16) for i in range(2)]
        st = {"bk": 0, "tb": 0, "ws": 0, "dq": 0}

        def nb():
            i = st["bk"]; st["bk"] = (i + 1) % NBK
            return i

        def ntb():
            i = st["tb"]; st["tb"] = (i + 1) % 2
            return i

        NSLOT = 3
        ring = [sb("ring%d" % i, [128, 8, 512], BF16) for i in range(NSLOT)]

        def wload(src):
            s = st["ws"]; st["ws"] = (s + 1) % NSLOT
            A("pool", lambda e, s=s, src=src: e.dma_start(out=ring[s][:], in_=src), writes=[("w", s)], dma="w%d" % s)
            return s

        w_in_v = w_in_d.rearrange("(k p) c -> p k c", p=128)
        w_cp_v = w_cp_d.rearrange("(k p) c -> p k c", p=128)
        w_ap_v = w_ap_d.rearrange("(k p) c -> p k c", p=128)
        w_o_v = w_o_d.rearrange("(k p) c -> p k c", p=128)
        w_up_v = w_up_d.rearrange("(k p) c -> p k c", p=128)
        w_dn_v = w_dn_d.rearrange("(k p) c -> p k c", p=128)

        def spq(fn, reads=(), writes=(), grp=None):
            if grp is None:
                grp = "q%d" % st["dq"]; st["dq"] = (st["dq"] + 1) % 6
            return A("sp", fn, reads=reads, writes=writes, dma=grp)

        ident_f = sb("ident_f", [128, 128]); ident_b = sb("ident_b", [128, 128], BF16)
        ones_f = sb("ones_f", [128, 128])
        epsT = sb("epsT", [128, 1])
        kmask = sb("kmaskS", [128, 2])
        b_in_T = sb("b_in_TS", [128, 84]); bq8 = sb("bq8", [128, 16])
        bkvB = sb("bkvB", [128, 512])
        conv_k = sb("conv_kS", [128, 16, 31]); conv_b = sb("conv_bS", [128, 16])
        ln_g = sb("ln_gS", [128, 16]); ln_b = sb("ln_bS", [128, 16])
        ffn_k = sb("ffn_kS", [128, 128, 3]); ffn_b = sb("ffn_bS", [128, 128])
        esink = sb("esink", [128, 16])
        GB0 = sb("GB0", [128, D])
        TBL = sb("TBL", [128, 2, 32, 129], BF16)
        OP1 = sb("OP1", [128, 2, 128], BF16); OPF = sb("OPF", [128, 2, 128], BF16); OPB = sb("OPB", [128, 2, 128], BF16)

        A("dve", lambda e: e.memset(ones_f[:], 1.0), writes=["ones_f"])
        A("dve", lambda e: e.memset(epsT[:], EPS), writes=["epsT"])
        A("pool", lambda e: e.memset(ident_f[:], 0.0), writes=["ident_f"])
        A("pool", lambda e: e.affine_select(out=ident_f[:], in_=ident_f[:], pattern=[[-1, 128]], compare_op=ALU.not_equal,
                                            fill=1.0, base=0, channel_multiplier=1), reads=["ident_f"], writes=["ident_f"])
        A("dve", lambda e: e.tensor_copy(out=ident_b[:], in_=ident_f[:]), reads=["ident_f"], writes=["ident_b"])
        for (dst, src, nm) in ((kmask, kmask_d, "kmask"), (b_in_T, b_in_T_d, "b_in_T"), (conv_k, conv_k_d, "conv_k"),
                               (conv_b, conv_b_d, "conv_b"), (ln_g, ln_g_d, "ln_g"), (ln_b, ln_b_d, "ln_b"),
                               (ffn_k, ffn_k_d, "ffn_k"), (ffn_b, ffn_b_d, "ffn_b"), (esink, sink_d, "esink")):
            spq(lambda e, dst=dst, src=src: e.dma_start(out=dst[:], in_=src), writes=[nm])
        spq(lambda e: e.dma_start(out=bkvB[:], in_=b_kv_d.partition_broadcast(128)), writes=["bkvB"])
        A("act", lambda e: e.activation(out=esink[:], in_=esink[:], func=AF.Exp), reads=["esink"], writes=["esink"])
        A("dve", lambda e: e.tensor_scalar(out=bq8[:], in0=b_in_T[:, 32:48], scalar1=0.125, scalar2=None, op0=ALU.mult),
          reads=["b_in_T"], writes=["bq8"])
        for (T, nm, col) in ((OP1, "OP1", None), (OPF, "OPF", 0), (OPB, "OPB", 1)):
            A("dve", lambda e, T=T: e.memset(T[:], 0.0), writes=[nm])
            for half in range(2):
                if col is None:
                    A("dve", lambda e, T=T, half=half: e.memset(T[:, half, half * 64:half * 64 + 64], 1.0), reads=[], writes=[nm])
                else:
                    A("dve", lambda e, T=T, half=half, col=col: e.tensor_scalar(
                        out=T[:, half, half * 64:half * 64 + 64], in0=ones_f[:, 0:64], scalar1=kmask[:, col:col + 1], scalar2=None,
                        op0=ALU.mult), reads=["ones_f", "kmask"], writes=[nm])

        rbp = sb("rbpS", [32, 32]); ohb = sb("ohbS", [32, 128]); jm = sb("jmS", [128, 384]); mneg = sb("mnegS", [128, 2, 129])
        vec = sb("vecS", [128, 32])
        for (dst, src, nm) in ((rbp, rbp_d, "rbp"), (ohb, ohb_d, "ohb"), (jm, jm_d, "jm"), (mneg, mneg_d, "mneg")):
            spq(lambda e, dst=dst, src=src: e.dma_start(out=dst[:], in_=src), writes=[nm])
        b0 = nb()
        A("pe", lambda e, b0=b0: e.matmul(banks[b0][:, 0:32], lhsT=ohb[:, :], rhs=rbp[:, :], start=True, stop=True),
          reads=["ohb", "rbp"], writes=[("bk", b0)])
        A("dve", lambda e, b0=b0: e.tensor_copy(out=vec[:], in_=banks[b0][:, 0:32]), reads=[("bk", b0)], writes=["vec"])
        for kc in range(2):
            for r0 in range(0, 129, 16):
                nr = min(16, 129 - r0)
                bk = nb()
                for rr in range(nr):
                    r = r0 + rr
                    start_col = (128 - r) if kc == 0 else (256 - r)
                    A("pe", lambda e, bk=bk, rr=rr, sc_=start_col: e.matmul(
                        banks[bk][:, rr * 32:(rr + 1) * 32], lhsT=jm[:, sc_:sc_ + 128], rhs=vec[:, :], start=True, stop=True),
                      reads=["jm", "vec"], writes=[("bk", bk)])
                A("dve", lambda e, bk=bk, kc=kc, r0=r0, nr=nr: e.tensor_tensor(
                    out=TBL[:, kc, :, r0:r0 + nr].rearrange("p h r -> p r h"),
                    in0=banks[bk][:, 0:nr * 32].rearrange("p (r h) -> p r h", h=32),
                    in1=mneg[:, kc, r0:r0 + nr].unsqueeze(2).to_broadcast([128, nr, 32]), op=ALU.add),
                  reads=[("bk", bk), "mneg"], writes=["TBL"])

        KT = sb("KT", [128, 2, NROWS], BF16)
        xh = [sb("xh%d" % i, [128, D]) for i in range(2)]
        XS = sb("XS", [128, D], BF16)
        ssq = sb("ssq", [128, 8]); rstd_t = sb("rstd_t", [128, 8])
        xnT = sb("xnT", [128, 16, EXT + WMAX], BF16)
        hnT = sb("hnT", [128, 16, WMAX], BF16)
        arena = sb("arena", [128, 64, WMAX], BF16)
        actT = arena
        cT = arena[:, 0:32, :].rearrange("p a b -> p (a b)").bitcast(F32).rearrange("p (k w) -> p k w", w=WMAX)
        qT = arena[:, 32:48, :]
        oT = arena[:, 48:64, :]
        sT = sb("sT", [128, 16, WMAX], BF16)
        mixT = sb("mixT", [128, 16, WMAX], BF16)
        MS = [sb("MS%d" % i, [128, D]) for i in range(2)]
        Gb = [sb("Gb%d" % i, [128, 32 + WMAX]) for i in range(2)]
        ghist = sb("ghist", [128, 16, 32])
        uhist = sb("uhist", [128, 128, 2])
        tmpA = [sb("tmpA%d" % i, [128, 512]) for i in range(3)]
        VP = [sb("VP%d" % i, [128, 4, 128], BF16) for i in range(6)]
        Vf = sb("Vf", [128, 512])
        Eb = [sb("Eb%d" % i, [128, 2, 512], BF16) for i in range(2)]
        rden = [sb("rden%d" % i, [128, 512]) for i in range(1)]
        lnm = sb("lnm", [128, WMAX]); lnr = sb("lnr", [128, WMAX]); lnt = sb("lnt", [128, WMAX]); lnz = [sb("lnz%d" % i, [128, WMAX]) for i in range(2)]
        CO = sb("CO", [128, 4, WMAX]); T1 = sb("T1", [128, 4, WMAX]); sgt = [sb("sgt%d" % i, [128, WMAX]) for i in range(2)]
        Ub = [sb("Ub%d" % i, [128, 2 + WMAX]) for i in range(4)]
        cvb = [sb("cvb%d" % i, [128, WMAX]) for i in range(4)]
        geb = [sb("geb%d" % i, [128, WMAX]) for i in range(2)]
        ckb = sb("ckb", [128, 2, 256], BF16); KcT = sb("KcT", [128, 2, 2, 128], BF16)
        cvbuf = sb("cvbuf", [128, 2, 256], BF16); VPc = [sb("VPc%d" % i, [128, 4, 128], BF16) for i in range(2)]
        VsP = [sb("VsP%d" % i, [8, 4, 128], BF16) for i in range(2)]
        GS = sb("GS", [128, 16, 38]); scb = [sb("scb%d" % i, [120, 4, 128]) for i in range(1)]
        sfb = [sb("sfb%d" % i, [32, 512]) for i in range(1)]
        UST = sb("UST", [128, 4, 16, 10])
        UO = sb("UO", [128, 4, 34]); uost = [sb("uost%d" % i, [34, 512]) for i in range(1)]
        GO = sb("GO", [128, 16, 30])
        cnt = {"g": 0, "e": 0, "u": 0, "o": 0}

        def rmsnorm_stats(src_ap, n, col, rd):
            A("dve", lambda e: e.memset(ssq[0:n, col:col + 1], 0.0), writes=[("ssq", col)])
            A("act", lambda e: e.activation(out=XS[0:n, :], in_=src_ap, func=AF.Square, accum_out=ssq[0:n, col:col + 1]),
              reads=rd + [("ssq", col)], writes=["XS", ("ssq", col)])
            A("act", lambda e: e.activation(out=rstd_t[0:n, col:col + 1], in_=ssq[0:n, col:col + 1], func=AF.Sqrt,
                                            scale=1.0 / D, bias=epsT[0:n, 0:1]), reads=[("ssq", col), "epsT"], writes=[("rstd", col)])
            A("dve", lambda e: e.reciprocal(out=rstd_t[0:n, col:col + 1], in_=rstd_t[0:n, col:col + 1]),
              reads=[("rstd", col)], writes=[("rstd", col)])

        def transpose_to(dstT, dst_res, col0, n, src_res):
            for half in range(2):
                tb = ntb()
                for k in range(8):
                    kk = half * 8 + k
                    A("pe", lambda e, tb=tb, k=k, kk=kk: e.transpose(out=tbanks[tb][:, k * 128:k * 128 + n],
                                                                  in_=XS[0:n, kk * 128:(kk + 1) * 128], identity=ident_b[0:n, 0:n]),
                      reads=[src_res, "ident_b"], writes=[("tb", tb)])
                A("act", lambda e, tb=tb, half=half: e.copy(
                    out=dstT[:, half * 8:half * 8 + 8, col0:col0 + n],
                    in_=tbanks[tb][:, :].rearrange("p (k c) -> p k c", c=128)[:, :, 0:n]),
                  reads=[("tb", tb)], writes=[dst_res])

        def gload(GBt, nm, src):
            spq(lambda e: e.dma_start(out=GBt[:], in_=src.partition_broadcast(128)), writes=[nm])

        def stage0_block(row0, n, dst_col, xbuf, xres):
            spq(lambda e: e.dma_start(out=xbuf[0:n, :], in_=xin[row0:row0 + n, :]), writes=[xres])
            rmsnorm_stats(xbuf[0:n, :], n, 0, [xres])
            A("dve", lambda e: e.scalar_tensor_tensor(out=XS[0:n, :], in0=xbuf[0:n, :], scalar=rstd_t[0:n, 0:1], in1=GB0[0:n, :],
                                                      op0=ALU.mult, op1=ALU.mult), reads=[xres, ("rstd", 0), "GB0"], writes=["XS"])
            transpose_to(xnT, "xnT", dst_col, n, "XS")

        def fm_group(wv, c0, chunks, rhs_fn, N, rhs_res, evac):
            s0 = wload(wv[:, 0:8, c0:c0 + 512]); s1 = wload(wv[:, 8:16, c0:c0 + 512])
            bks = {i: nb() for i in chunks}
            for h, s in enumerate((s0, s1)):
                for i in chunks:
                    for k in range(8):
                        kk = 8 * h + k
                        A("pe", lambda e, s=s, i=i, k=k, kk=kk: e.matmul(
                            banks[bks[i]][:, 0:N], lhsT=ring[s][:, k, i * 128:(i + 1) * 128], rhs=rhs_fn(kk),
                            start=(kk == 0), stop=(kk == 15)), reads=[("w", s)] + rhs_res, writes=[("bk", bks[i])])
            for i in chunks:
                evac(i, bks[i])
            return s0, s1

        def make_vpad(win_cols, vp_idx, mask, with_k_out=False, s01=None):
            s0, s1 = s01
            bk = nb()
            c_lo = 0 if with_k_out else 256
            for h, s in enumerate((s0, s1)):
                for k in range(8):
                    kk = 8 * h + k
                    A("pe", lambda e, s=s, k=k, kk=kk: e.matmul(
                        banks[bk][:, c_lo:512], lhsT=xnT[:, kk, win_cols:win_cols + 128], rhs=ring[s][:, k, c_lo:512],
                        start=(kk == 0), stop=(kk == 15)), reads=[("w", s), "xnT"], writes=[("bk", bk)])
            A("dve", lambda e: e.tensor_tensor(out=Vf[:, c_lo:512], in0=banks[bk][:, c_lo:512], in1=bkvB[:, c_lo:512], op=ALU.add),
              reads=[("bk", bk), "bkvB"], writes=["Vf"])
            vpad_from(Vf[:, 256:512], "Vf", VP[vp_idx], ("VP", vp_idx), mask)

        def vpad_from(src, src_res, dstT, dst_res, mask, npart=128):
            A("dve", lambda e: e.memset(dstT[0:npart], 0.0), writes=[dst_res])
            for par in range(2):
                sv = src.rearrange("p (g t d) -> p g t d", g=2, t=2)[:, :, par, :]
                dv = dstT[0:npart].rearrange("p (g t) c -> p g t c", t=2)[:, :, par, par * 64:par * 64 + 64]
                if mask is None:
                    A("dve", lambda e, sv=sv, dv=dv: e.tensor_copy(out=dv, in_=sv), reads=[src_res], writes=[dst_res])
                else:
                    A("dve", lambda e, sv=sv, dv=dv: e.tensor_scalar(out=dv, in0=sv, scalar1=kmask[0:npart, mask:mask + 1], scalar2=None,
                                                                 op0=ALU.mult), reads=[src_res, "kmask"], writes=[dst_res])

        def attn_block(qc0, n, r0, k0_fn, k0_res, k1_fn, k1_res, nk1, vp0, vp0_res, op0, vp1, vp1_res, op1):
            CB = 4 if n > 64 else 8
            def do_gc(gp, cb):
                if True:
                    cp0 = gp * 8 + cb * CB
                    ebs = []
                    for half in range(2):
                        eb = Eb[cnt["e"] % 2]; ebr = ("Eb", cnt["e"] % 2); cnt["e"] += 1
                        ebs.append((eb, ebr))
                        hs = slice(half * 64, half * 64 + 64)
                        for kc in range(2):
                            nk = 128 if kc == 0 else nk1
                            kfn, kres = (k0_fn, k0_res) if kc == 0 else (k1_fn, k1_res)
                            bk = nb()
                            A("pe", lambda e, bk=bk, kfn=kfn, hs=hs, nk=nk: e.matmul(
                                banks[bk][0:nk, 0:CB * n].rearrange("p (c q) -> p c q", q=n), lhsT=kfn(gp, hs),
                                rhs=qT[hs, cp0:cp0 + CB, qc0:qc0 + n], start=True, stop=False),
                              reads=[kres, "qT"], writes=[("bk", bk)])
                            A("pe", lambda e, bk=bk, kc=kc, nk=nk, half=half: e.matmul(
                                banks[bk][0:nk, 0:CB * n].rearrange("p (c q) -> p c q", q=n), lhsT=ident_b[0:nk, 0:nk],
                                rhs=TBL[0:nk, kc, half * 16 + cp0:half * 16 + cp0 + CB, r0:r0 + n], start=False, stop=True),
                              reads=["ident_b", "TBL"], writes=[("bk", bk)])
                            A("act", lambda e, bk=bk, kc=kc, nk=nk, eb=eb: e.activation(
                                out=eb[0:nk, kc, 0:CB * n], in_=banks[bk][0:nk, 0:CB * n], func=AF.Exp),
                              reads=[("bk", bk)], writes=[ebr])
                    bo = nb(); bd = nb()
                    first = True
                    for half in range(2):
                        eb, ebr = ebs[half]
                        for kc in range(2):
                            nk = 128 if kc == 0 else nk1
                            vp, vpr, opt = (vp0, vp0_res, op0) if kc == 0 else (vp1, vp1_res, op1)
                            last = (half == 1 and kc == 1)
                            A("pe", lambda e, vp=vp, nk=nk, eb=eb, kc=kc, half=half, first=first, last=last: e.matmul(
                                banks[bo][:, 0:CB * n], lhsT=vp[0:nk, 2 * gp + half, :], rhs=eb[0:nk, kc, 0:CB * n],
                                start=first, stop=last), reads=[vpr, ebr], writes=[("bk", bo)])
                            A("pe", lambda e, opt=opt, nk=nk, eb=eb, kc=kc, half=half, first=first, last=last: e.matmul(
                                banks[bd][:, 0:CB * n], lhsT=opt[0][0:nk, half, :], rhs=eb[0:nk, kc, 0:CB * n],
                                start=first, stop=last), reads=[opt[1], ebr], writes=[("bk", bd)])
                            first = False
                    rd = rden[0]; rdr = ("rden", 0)
                    for i in range(CB):
                        A("dve", lambda e, i=i: e.tensor_scalar(out=rd[:, i * n:(i + 1) * n], in0=banks[bd][:, i * n:(i + 1) * n],
                                                               scalar1=esink[:, cp0 + i:cp0 + i + 1], scalar2=None, op0=ALU.add),
                          reads=[("bk", bd), "esink"], writes=[rdr])
                    A("dve", lambda e: e.reciprocal(out=rd[:, 0:CB * n], in_=rd[:, 0:CB * n]), reads=[rdr], writes=[rdr])
                    A("dve", lambda e: e.tensor_tensor(
                        out=oT[:, cp0:cp0 + CB, qc0:qc0 + n], in0=banks[bo][:, 0:CB * n].rearrange("p (c q) -> p c q", q=n),
                        in1=rd[:, 0:CB * n].rearrange("p (c q) -> p c q", q=n), op=ALU.mult),
                      reads=[("bk", bo), rdr], writes=["oT"])
            for gp_ in range(2):
                for cb_ in range(8 // CB):
                    do_gc(gp_, cb_)

        def out_dma(fn, reads, grp):
            if grp not in out_dma_groups:
                out_dma_groups.append(grp)
            A("sp", fn, reads=reads, dma=grp)

        vp_of_row = {}
        def process_tile(ti, kind, blocks):
            W = sum(n for _, n in blocks)
            row_lo = blocks[0][0]
            mo = EXT if ti == 0 else 0
            is_s = (kind == "s")
            gload(GB0, "GB0", g_pre_d)
            if ti == 0:
                stage0_block(0, 128, 0, xh[1], ("xh", 1))
            col = mo
            xbufs = {}
            for bi, (r0_, n) in enumerate(blocks):
                slot = 1 if n == 2 else 0
                xb, xr = xh[slot], ("xh", slot)
                xbufs[bi] = (xb, xr)
                stage0_block(r0_, n, col, xb, xr)
                col += n
            ge = 32 if ti == 0 else 0
            NG = ge + W
            Wc = W
            def sample_post(ch, Gt, Gr):
                sj = 0
                spq(lambda e: e.dma_start(out=scb[sj][:, :, :], in_=sc_d.rearrange("(a s) r c -> (s r) a c", a=4)[:, :, ch * 128:(ch + 1) * 128]),
                    writes=[("scb", sj)])
                for a4 in range(4):
                    bt = nb()
                    A("pe", lambda e, a4=a4, bt=bt: e.transpose(out=banks[bt][:, 0:120], in_=scb[sj][0:120, a4, :], identity=ident_f[0:120, 0:120]),
                      reads=[("scb", sj), "ident_f"], writes=[("bk", bt)])
                    A("act", lambda e, a4=a4, bt=bt: e.copy(out=GS[:, a4 * 4:a4 * 4 + 4, 0:30],
                                                           in_=banks[bt][:, 0:120].rearrange("p (s r) -> p s r", r=30)),
                      reads=[("bk", bt)], writes=["GS"])
                A("dve", lambda e: e.tensor_copy(out=GS[:, :, 30:38], in_=Gt[:, 32:32 + 128].rearrange("p (s t) -> p s t", t=8)),
                  reads=[Gr], writes=["GS"])
                cv3 = cT[:, ch, 0:128].rearrange("p (s t) -> p s t", t=8)
                A("dve", lambda e: e.tensor_scalar(out=cv3, in0=GS[:, :, 0:8], scalar1=conv_k[:, ch, 0:1], scalar2=conv_b[:, ch:ch + 1],
                                                   op0=ALU.mult, op1=ALU.add), reads=["GS", "conv_k", "conv_b"], writes=[("cT", ch)])
                for j in range(1, 31):
                    A("dve", lambda e, j=j: e.scalar_tensor_tensor(out=cv3, in0=GS[:, :, j:j + 8], scalar=conv_k[:, ch, j:j + 1], in1=cv3,
                                                                  op0=ALU.mult, op1=ALU.add), reads=["GS", "conv_k", ("cT", ch)], writes=[("cT", ch)])
                bt = nb()
                A("pe", lambda e, bt=bt: e.transpose(out=banks[bt][:, 0:128], in_=Gt[:, 32:160], identity=ident_f[:, :]),
                  reads=[Gr, "ident_f"], writes=[("bk", bt)])
                A("act", lambda e, bt=bt: e.copy(out=MS[0][:, ch * 128:(ch + 1) * 128], in_=banks[bt][:, 0:128]),
                  reads=[("bk", bt)], writes=[("MS", 0)])

            for g in range(8):
                s_pair = {}
                posts = []

                def evac_ab(i, bk, g=g):
                    s_pair[i] = bk
                    if i % 2 == 0:
                        return
                    ch = g * 2 + i // 2
                    ba, bb = s_pair[i - 1], bk
                    gi = cnt["g"] % 2; cnt["g"] += 1
                    Gt, Gr = Gb[gi], ("Gb", gi)
                    sg = tmpA[gi]; sgr = ("tmpA", gi)
                    A("act", lambda e: e.activation(out=sg[:, 0:NG], in_=banks[bb][:, 0:NG], func=AF.Sigmoid,
                                                    bias=b_in_T[:, 2 * ch + 1:2 * ch + 2]), reads=[("bk", bb), "b_in_T"], writes=[sgr])
                    g0 = 32 - ge
                    A("dve", lambda e: e.scalar_tensor_tensor(out=Gt[:, g0:g0 + NG], in0=banks[ba][:, 0:NG],
                                                              scalar=b_in_T[:, 2 * ch:2 * ch + 1], in1=sg[:, 0:NG], op0=ALU.add, op1=ALU.mult),
                      reads=[("bk", ba), "b_in_T", sgr], writes=[Gr])
                    if is_s:
                        posts.append((ch, Gt, Gr))
                        return
                    if ti == 0:
                        A("dve", lambda e: e.tensor_scalar(out=Gt[:, 0:34], in0=Gt[:, 0:34], scalar1=kmask[:, 0:1], scalar2=None, op0=ALU.mult),
                          reads=[Gr, "kmask"], writes=[Gr])
                    else:
                        A("dve", lambda e: e.tensor_copy(out=Gt[:, 0:32], in_=ghist[:, ch, :]), reads=[("ghist", ch)], writes=[Gr])
                    A("dve", lambda e: e.tensor_copy(out=ghist[:, ch, :], in_=Gt[:, W:W + 32]), reads=[Gr], writes=[("ghist", ch)])
                    if ti == LASTP:
                        A("act", lambda e: e.copy(out=GO[:, ch, :], in_=Gt[:, 32 + W - 30:32 + W]), reads=[Gr], writes=["GO"])
                    cv = cT[:, ch, 0:Wc]
                    A("dve", lambda e: e.tensor_scalar(out=cv, in0=Gt[:, 2:2 + Wc], scalar1=conv_k[:, ch, 0:1], scalar2=conv_b[:, ch:ch + 1],
                                                       op0=ALU.mult, op1=ALU.add), reads=[Gr, "conv_k", "conv_b"], writes=[("cT", ch)])
                    for j in range(1, 31):
                        A("dve", lambda e, j=j: e.scalar_tensor_tensor(out=cv, in0=Gt[:, 2 + j:2 + j + Wc], scalar=conv_k[:, ch, j:j + 1], in1=cv,
                                                                      op0=ALU.mult, op1=ALU.add), reads=[Gr, "conv_k", ("cT", ch)], writes=[("cT", ch)])

                xc0 = mo - ge
                fm_group(w_in_v, g * 512, [0, 1, 2, 3], lambda kk, xc0=xc0, NG=NG: xnT[:, kk, xc0:xc0 + NG], NG, ["xnT"], evac_ab)
                for (ch_, Gt_, Gr_) in posts:
                    sample_post(ch_, Gt_, Gr_)
            if is_s:
                for s in range(16):
                    out_dma(lambda e, s=s: e.dma_start(out=convs_d[s, 22:30, :], in_=MS[0][s * 8:s * 8 + 8, :]), [("MS", 0)], "o_convs")
                out_dma(lambda e: e.dma_start(out=convs_d[:, 0:22, :], in_=sc_d[:, 8:30, :]), [], "o_convs")
            if ti == LASTP:
                for ch in range(16):
                    bt = nb()
                    A("pe", lambda e, bt=bt, ch=ch: e.transpose(out=banks[bt][0:30, 0:128], in_=GO[:, ch, :], identity=ident_f[:, :]),
                      reads=["GO", "ident_f"], writes=[("bk", bt)])
                    A("act", lambda e, bt=bt, ch=ch: e.copy(out=MS[0][0:30, ch * 128:(ch + 1) * 128], in_=banks[bt][0:30, 0:128]),
                      reads=[("bk", bt)], writes=[("MS", 0)])
                out_dma(lambda e: e.dma_start(out=convp_d[:, :], in_=MS[0][0:30, :]), [("MS", 0)], "o_convp")
            for g in range(4):
                def evac_q(i, bk, g=g):
                    cp = g * 4 + i
                    A("dve", lambda e: e.tensor_scalar(out=qT[:, cp, 0:W], in0=banks[bk][:, 0:W], scalar1=0.125, scalar2=bq8[:, cp:cp + 1],
                                                       op0=ALU.mult, op1=ALU.add), reads=[("bk", bk), "bq8"], writes=["qT"])
                fm_group(w_in_v, (8 + g) * 512, [0, 1, 2, 3], lambda kk: xnT[:, kk, mo:mo + W], W, ["xnT"], evac_q)
            def evac_k(i, bk):
                A("act", lambda e: e.activation(out=KT[:, i, row_lo:row_lo + W], in_=banks[bk][:, 0:W], func=AF.Identity,
                                                bias=b_in_T[:, 48 + i:49 + i]), reads=[("bk", bk), "b_in_T"], writes=["KT"])
            s01 = fm_group(w_in_v, 12 * 512, [0, 1], lambda kk: xnT[:, kk, mo:mo + W], W, ["xnT"], evac_k)
            if ti == 0:
                for i in range(2):
                    bk = nb()
                    for h, s in enumerate(s01):
                        for k in range(8):
                            kk = 8 * h + k
                            A("pe", lambda e, s=s, k=k, kk=kk, i=i, bk=bk: e.matmul(
                                banks[bk][:, 0:128], lhsT=ring[s][:, k, i * 128:(i + 1) * 128], rhs=xnT[:, kk, 0:128],
                                start=(kk == 0), stop=(kk == 15)), reads=[("w", s), "xnT"], writes=[("bk", bk)])
                    A("act", lambda e, i=i, bk=bk: e.activation(out=KT[:, i, 0:128], in_=banks[bk][:, 0:128], func=AF.Identity,
                                                              bias=b_in_T[:, 48 + i:49 + i]), reads=[("bk", bk), "b_in_T"], writes=["KT"])
            if not is_s:
                wins = []
                if ti == 0:
                    wins += [(0, 0), (128, 1), (2, 0)]
                for (r0_, n) in blocks:
                    if n == 128:
                        wins.append((r0_, None))
                for (wr, mask) in wins:
                    vi = len(vp_of_row) % 6
                    vp_of_row[wr] = vi
                    last_win = (wr == 1026)
                    make_vpad(wr - row_lo + mo, vi, mask, with_k_out=last_win, s01=s01)
                    if last_win:
                        out_dma(lambda e: e.dma_start(out=kvp_d[:, :], in_=Vf[:, :]), ["Vf"], "o_kvp")
            else:
                make_vpad(0, 5, None, with_k_out=True, s01=s01)
                for s in range(16):
                    out_dma(lambda e, s=s: e.dma_start(out=ks_d[s, 120:128, :], in_=Vf[s * 8:s * 8 + 8, 0:256]), ["Vf"], "o_ks")
                    out_dma(lambda e, s=s: e.dma_start(out=vs_d[s, 120:128, :], in_=Vf[s * 8:s * 8 + 8, 256:512]), ["Vf"], "o_vs")
                out_dma(lambda e: e.dma_start(out=ks_d[:, 0:120, :], in_=ck_d[:, 8:128, :]), [], "o_ks")
                out_dma(lambda e: e.dma_start(out=vs_d[:, 0:120, :], in_=cv_d[:, 8:128, :]), [], "o_vs")
            if not is_s:
                col = 0
                for (r0_, n) in blocks:
                    if n == 2:
                        w0, w1 = 0, 128
                        o0 = (OPF, "OPF"); o1 = (OPB, "OPB"); rr0 = 1
                    else:
                        w0, w1 = r0_ - 128, r0_
                        o0 = (OPF, "OPF") if w0 == 2 else (OP1, "OP1"); o1 = (OP1, "OP1"); rr0 = 1
                    v0, v1 = vp_of_row[w0], vp_of_row[w1]
                    attn_block(col, n, rr0,
                               lambda gp, hs, w0=w0: KT[hs, gp, w0:w0 + 128], "KT",
                               lambda gp, hs, w1=w1: KT[hs, gp, w1:w1 + 128], "KT", 128,
                               VP[v0], ("VP", v0), o0, VP[v1], ("VP", v1), o1)
                    col += n
            else:
                for s in range(16):
                    j2 = s % 2
                    if j2 == 0:
                        A("pool", lambda e, s=s: e.dma_start(out=ckb[:, :, :], in_=ck_d[s:s + 2].rearrange("s k d -> k s d")), writes=["ckb"], dma="ckb")
                        A("pool", lambda e, s=s: e.dma_start(out=cvbuf[:, :, :], in_=cv_d[s:s + 2].rearrange("s k d -> k s d")), writes=["cvbuf"], dma="cvb")
                        for s2 in range(2):
                            tb = ntb()
                            for gp in range(2):
                                A("pe", lambda e, tb=tb, gp=gp, s2=s2: e.transpose(out=tbanks[tb][:, gp * 128:(gp + 1) * 128],
                                                                              in_=ckb[:, s2, gp * 128:(gp + 1) * 128], identity=ident_b[:, :]),
                                  reads=["ckb", "ident_b"], writes=[("tb", tb)])
                            A("act", lambda e, tb=tb, s2=s2: e.copy(out=KcT[:, s2, :, :], in_=tbanks[tb][:, 0:256].rearrange("p (g k) -> p g k", g=2)),
                              reads=[("tb", tb)], writes=[("KcT", s2)])
                            vpad_from(cvbuf[:, s2, :], "cvbuf", VPc[s2], ("VPc", s2), None)
                    bk = nb()
                    for h, sl in enumerate(s01):
                        for k in range(8):
                            kk = 8 * h + k
                            A("pe", lambda e, sl=sl, k=k, kk=kk, bk=bk, s=s: e.matmul(
                                banks[bk][0:8, 0:256], lhsT=xnT[:, kk, s * 8:s * 8 + 8], rhs=ring[sl][:, k, 256:512],
                                start=(kk == 0), stop=(kk == 15)), reads=[("w", sl), "xnT"], writes=[("bk", bk)])
                    A("dve", lambda e, bk=bk: e.tensor_tensor(out=tmpA[2][0:8, 0:256], in0=banks[bk][0:8, 0:256], in1=bkvB[0:8, 256:512], op=ALU.add),
                      reads=[("bk", bk), "bkvB"], writes=[("tmpA", 2)])
                    vpad_from(tmpA[2][0:8, 0:256], ("tmpA", 2), VsP[j2], ("VsP", j2), None, npart=8)
                    rs = 1154 + s * 8
                    attn_block(s * 8, 8, 1,
                               lambda gp, hs, j2=j2: KcT[hs, j2, gp, :], ("KcT", j2),
                               lambda gp, hs, rs=rs: KT[hs, gp, rs:rs + 8], "KT", 8,
                               VPc[j2], ("VPc", j2), (OP1, "OP1"), VsP[j2], ("VsP", j2), (OP1, "OP1"))
            bsum = nb(); bsq = nb()
            for k in range(16):
                A("pe", lambda e, k=k: e.matmul(banks[bsum][:, 0:W], lhsT=ones_f[:, :], rhs=cT[:, k, 0:W], start=(k == 0), stop=(k == 15)),
                  reads=["ones_f", ("cT", k)], writes=[("bk", bsum)])
                zi = k % 2
                A("act", lambda e, k=k, zi=zi: e.activation(out=lnz[zi][:, 0:W], in_=cT[:, k, 0:W], func=AF.Square), reads=[("cT", k)], writes=[("lnz", zi)])
                A("pe", lambda e, k=k, zi=zi: e.matmul(banks[bsq][:, 0:W], lhsT=ones_f[:, :], rhs=lnz[zi][:, 0:W], start=(k == 0), stop=(k == 15)),
                  reads=["ones_f", ("lnz", zi)], writes=[("bk", bsq)])
            A("dve", lambda e: e.tensor_scalar(out=lnm[:, 0:W], in0=banks[bsum][:, 0:W], scalar1=1.0 / D, scalar2=None, op0=ALU.mult),
              reads=[("bk", bsum)], writes=["lnm"])
            A("dve", lambda e: e.tensor_tensor(out=lnt[:, 0:W], in0=lnm[:, 0:W], in1=lnm[:, 0:W], op=ALU.mult), reads=["lnm"], writes=["lnt"])
            A("dve", lambda e: e.scalar_tensor_tensor(out=lnr[:, 0:W], in0=banks[bsq][:, 0:W], scalar=1.0 / D, in1=lnt[:, 0:W],
                                                      op0=ALU.mult, op1=ALU.subtract), reads=[("bk", bsq), "lnt"], writes=["lnr"])
            A("act", lambda e: e.activation(out=lnr[:, 0:W], in_=lnr[:, 0:W], func=AF.Sqrt, bias=epsT[:, 0:1]), reads=["lnr", "epsT"], writes=["lnr"])
            A("dve", lambda e: e.reciprocal(out=lnr[:, 0:W], in_=lnr[:, 0:W]), reads=["lnr"], writes=["lnr"])
            for k in range(16):
                zi = k % 2
                A("dve", lambda e, k=k, zi=zi: e.tensor_tensor(out=lnz[zi][:, 0:W], in0=cT[:, k, 0:W], in1=lnm[:, 0:W], op=ALU.subtract),
                  reads=[("cT", k), "lnm"], writes=[("lnz", zi)])
                A("dve", lambda e, k=k, zi=zi: e.tensor_tensor(out=lnz[zi][:, 0:W], in0=lnz[zi][:, 0:W], in1=lnr[:, 0:W], op=ALU.mult),
                  reads=[("lnz", zi), "lnr"], writes=[("lnz", zi)])
                A("act", lambda e, k=k, zi=zi: e.activation(out=sT[:, k, 0:W], in_=lnz[zi][:, 0:W], func=AF.Silu, scale=ln_g[:, k:k + 1], bias=ln_b[:, k:k + 1]),
                  reads=[("lnz", zi), "ln_g", "ln_b"], writes=["sT"])
            for cg in range(4):
                def ev_co(i, bk):
                    A("act", lambda e: e.copy(out=CO[:, i, 0:W], in_=banks[bk][:, 0:W]), reads=[("bk", bk)], writes=["CO"])
                fm_group(w_cp_v, cg * 512, [0, 1, 2, 3], lambda kk: sT[:, kk, 0:W], W, ["sT"], ev_co)

                def ev_gc(i, bk, cg=cg):
                    ch = 52 + cg * 4 + i
                    zi = i % 2
                    A("act", lambda e: e.activation(out=sgt[zi][:, 0:W], in_=banks[bk][:, 0:W], func=AF.Sigmoid, bias=b_in_T[:, ch:ch + 1]),
                      reads=[("bk", bk), "b_in_T"], writes=[("sgt", zi)])
                    A("dve", lambda e: e.tensor_tensor(out=T1[:, i, 0:W], in0=sgt[zi][:, 0:W], in1=CO[:, i, 0:W], op=ALU.mult),
                      reads=[("sgt", zi), "CO"], writes=["T1"])
                fm_group(w_in_v, (13 + cg) * 512, [0, 1, 2, 3], lambda kk: xnT[:, kk, mo:mo + W], W, ["xnT"], ev_gc)
                fm_group(w_ap_v, cg * 512, [0, 1, 2, 3], lambda kk: oT[:, kk, 0:W], W, ["oT"], ev_co)

                def ev_ga(i, bk, cg=cg):
                    ch = 68 + cg * 4 + i
                    zi = i % 2
                    A("act", lambda e: e.activation(out=sgt[zi][:, 0:W], in_=banks[bk][:, 0:W], func=AF.Sigmoid, bias=b_in_T[:, ch:ch + 1]),
                      reads=[("bk", bk), "b_in_T"], writes=[("sgt", zi)])
                    A("dve", lambda e: e.tensor_tensor(out=sgt[zi][:, 0:W], in0=sgt[zi][:, 0:W], in1=CO[:, i, 0:W], op=ALU.mult),
                      reads=[("sgt", zi), "CO"], writes=[("sgt", zi)])
                    A("dve", lambda e: e.tensor_tensor(out=mixT[:, cg * 4 + i, 0:W], in0=sgt[zi][:, 0:W], in1=T1[:, i, 0:W], op=ALU.add),
                      reads=[("sgt", zi), "T1"], writes=["mixT"])
                fm_group(w_in_v, (17 + cg) * 512, [0, 1, 2, 3], lambda kk: xnT[:, kk, mo:mo + W], W, ["xnT"], ev_ga)
            gload(GB0, "GB0", g_post_d)
            for cg in range(4):
                s0 = wload(w_o_v[:, 0:8, cg * 512:(cg + 1) * 512]); s1 = wload(w_o_v[:, 8:16, cg * 512:(cg + 1) * 512])
                col = 0
                for bi, (r0_, n) in enumerate(blocks):
                    bk = nb()
                    for h, s in enumerate((s0, s1)):
                        for k in range(8):
                            kk = 8 * h + k
                            A("pe", lambda e, s=s, k=k, kk=kk, bk=bk, col=col, n=n: e.matmul(
                                banks[bk][0:n, :], lhsT=mixT[:, kk, col:col + n], rhs=ring[s][:, k, :], start=(kk == 0), stop=(kk == 15)),
                              reads=[("w", s), "mixT"], writes=[("bk", bk)])
                    ms_i = 1 if n == 2 else 0
                    A("act", lambda e, bk=bk, cg=cg, ms_i=ms_i, n=n: e.copy(out=MS[ms_i][0:n, cg * 512:(cg + 1) * 512], in_=banks[bk][0:n, :]),
                      reads=[("bk", bk)], writes=[("MS", ms_i)])
                    col += n
            col = 0
            for bi, (r0_, n) in enumerate(blocks):
                ms_i = 1 if n == 2 else 0
                xb, xr = xbufs[bi]
                rmsnorm_stats(MS[ms_i][0:n, :], n, 1, [("MS", ms_i)])
                A("dve", lambda e, ms_i=ms_i, n=n: e.scalar_tensor_tensor(out=MS[ms_i][0:n, :], in0=MS[ms_i][0:n, :], scalar=rstd_t[0:n, 1:2], in1=GB0[0:n, :],
                                                                        op0=ALU.mult, op1=ALU.mult), reads=[("MS", ms_i), ("rstd", 1), "GB0"], writes=[("MS", ms_i)])
                A("dve", lambda e, ms_i=ms_i, n=n, xb=xb: e.tensor_tensor(out=xb[0:n, :], in0=xb[0:n, :], in1=MS[ms_i][0:n, :], op=ALU.add),
                  reads=[xr, ("MS", ms_i)], writes=[xr])
                col += n
            gload(GB0, "GB0", g_fpre_d)
            col = 0
            for bi, (r0_, n) in enumerate(blocks):
                xb, xr = xbufs[bi]
                rmsnorm_stats(xb[0:n, :], n, 2, [xr])
                A("dve", lambda e, n=n, xb=xb: e.scalar_tensor_tensor(out=XS[0:n, :], in0=xb[0:n, :], scalar=rstd_t[0:n, 2:3], in1=GB0[0:n, :],
                                                                    op0=ALU.mult, op1=ALU.mult), reads=[xr, ("rstd", 2), "GB0"], writes=["XS"])
                transpose_to(hnT, "hnT", col, n, "XS")
                col += n
            for g in range(32):
                u_pair = {}

                def ev_up(i, bk, g=g):
                    u_pair[i] = bk
                    if i % 2 == 0:
                        return
                    j = g * 2 + i // 2
                    res = []
                    for t2, bkx in enumerate((u_pair[i - 1], bk)):
                        chn = 2 * j + t2
                        ui = cnt["u"] % 4; cnt["u"] += 1
                        Ut, Ur = Ub[ui], ("Ub", ui)
                        A("act", lambda e, Ut=Ut, bkx=bkx: e.copy(out=Ut[:, 2:2 + W], in_=banks[bkx][:, 0:W]), reads=[("bk", bkx)], writes=[Ur])
                        cvt, cvr = cvb[ui], ("cvb", ui)
                        if is_s:
                            i4 = i // 2 * 2 + t2
                            U3 = UST[:, i4, :, :]
                            A("dve", lambda e, Ut=Ut, U3=U3: e.tensor_copy(out=U3[:, :, 2:10], in_=Ut[:, 2:130].rearrange("p (s t) -> p s t", t=8)),
                              reads=[Ur, ("USTh", g)], writes=[("UST", i4)])
                            c3 = cvt[:, 0:128].rearrange("p (s t) -> p s t", t=8)
                            A("dve", lambda e, U3=U3, c3=c3, chn=chn: e.tensor_scalar(out=c3, in0=U3[:, :, 0:8], scalar1=ffn_k[:, chn, 0:1], scalar2=ffn_b[:, chn:chn + 1],
                                                                                   op0=ALU.mult, op1=ALU.add), reads=[("UST", i4), "ffn_k", "ffn_b"], writes=[cvr])
                            for tp in (1, 2):
                                A("dve", lambda e, U3=U3, c3=c3, chn=chn, tp=tp: e.scalar_tensor_tensor(out=c3, in0=U3[:, :, tp:tp + 8], scalar=ffn_k[:, chn, tp:tp + 1], in1=c3,
                                                                                                      op0=ALU.mult, op1=ALU.add), reads=[("UST", i4), "ffn_k", cvr], writes=[cvr])
                            A("act", lambda e, Ut=Ut, i4=i4: e.copy(out=UO[:, i4, 2:34].rearrange("p (s t) -> p s t", t=2),
                                                                   in_=Ut[:, 2:130].rearrange("p (s t) -> p s t", t=8)[:, :, 6:8]), reads=[Ur], writes=[("UO", i4)])
                        else:
                            if ti == 0:
                                A("dve", lambda e, Ut=Ut: e.tensor_scalar(out=Ut[:, 2:4], in0=Ut[:, 2:4], scalar1=kmask[:, 0:1], scalar2=None, op0=ALU.mult),
                                  reads=[Ur, "kmask"], writes=[Ur])
                                A("dve", lambda e, Ut=Ut: e.memset(Ut[:, 0:2], 0.0), writes=[Ur])
                            else:
                                A("dve", lambda e, Ut=Ut, chn=chn: e.tensor_copy(out=Ut[:, 0:2], in_=uhist[:, chn, :]), reads=[("uhist", chn)], writes=[Ur])
                            A("dve", lambda e, Ut=Ut, chn=chn: e.tensor_copy(out=uhist[:, chn, :], in_=Ut[:, W:W + 2]), reads=[Ur], writes=[("uhist", chn)])
                            if ti == LASTP:
                                i4 = i // 2 * 2 + t2
                                A("act", lambda e, Ut=Ut, i4=i4: e.copy(out=UO[:, i4, 0:2], in_=Ut[:, W:W + 2]), reads=[Ur], writes=[("UO", i4)])
                            A("dve", lambda e, Ut=Ut, cvt=cvt, chn=chn: e.tensor_scalar(out=cvt[:, 0:W], in0=Ut[:, 0:W], scalar1=ffn_k[:, chn, 0:1], scalar2=ffn_b[:, chn:chn + 1],
                                                                                     op0=ALU.mult, op1=ALU.add), reads=[Ur, "ffn_k", "ffn_b"], writes=[cvr])
                            for tp in (1, 2):
                                A("dve", lambda e, Ut=Ut, cvt=cvt, chn=chn, tp=tp: e.scalar_tensor_tensor(out=cvt[:, 0:W], in0=Ut[:, tp:tp + W], scalar=ffn_k[:, chn, tp:tp + 1], in1=cvt[:, 0:W],
                                                                                                        op0=ALU.mult, op1=ALU.add), reads=[Ur, "ffn_k", cvr], writes=[cvr])
                        res.append((cvt, cvr))
                    (cgt, cgr), (cvt, cvr) = res
                    gi = j % 2
                    A("act", lambda e: e.activation(out=geb[gi][:, 0:W], in_=cgt[:, 0:W], func=AF.Gelu_apprx_tanh), reads=[cgr], writes=[("geb", gi)])
                    A("dve", lambda e: e.tensor_tensor(out=actT[:, j, 0:W], in0=geb[gi][:, 0:W], in1=cvt[:, 0:W], op=ALU.mult),
                      reads=[("geb", gi), cvr], writes=["actT"])

                if is_s:
                    sj = 0
                    spq(lambda e, sj=sj, g=g: e.dma_start(out=sfb[sj][:, :], in_=sf_d.rearrange("s r c -> (s r) c")[:, g * 512:(g + 1) * 512]), writes=[("sfb", sj)])
                    for i4 in range(4):
                        bt = nb()
                        A("pe", lambda e, bt=bt, i4=i4, sj=sj: e.transpose(out=banks[bt][:, 0:32], in_=sfb[sj][0:32, i4 * 128:(i4 + 1) * 128], identity=ident_f[0:32, 0:32]),
                          reads=[("sfb", sj), "ident_f"], writes=[("bk", bt)])
                        A("act", lambda e, bt=bt, i4=i4: e.copy(out=UST[:, i4, :, 0:2], in_=banks[bt][:, 0:32].rearrange("p (s r) -> p s r", r=2)),
                          reads=[("bk", bt)], writes=[("UST", i4), ("USTh", g)])
                fm_group(w_up_v, g * 512, [0, 1, 2, 3], lambda kk: hnT[:, kk, 0:W], W, ["hnT"], ev_up)
                if ti == LASTP or is_s:
                    c_lo, c_n = (0, 2) if ti == LASTP else (2, 32)
                    oi = 0
                    for i4 in range(4):
                        bt = nb()
                        A("pe", lambda e, bt=bt, i4=i4: e.transpose(out=banks[bt][0:c_n, 0:128], in_=UO[:, i4, c_lo:c_lo + c_n], identity=ident_f[:, :]),
                          reads=[("UO", i4), "ident_f"], writes=[("bk", bt)])
                        A("act", lambda e, bt=bt, i4=i4, oi=oi: e.copy(out=uost[oi][0:c_n, i4 * 128:(i4 + 1) * 128], in_=banks[bt][0:c_n, 0:128]),
                          reads=[("bk", bt)], writes=[("uost", oi)])
                    if ti == LASTP:
                        out_dma(lambda e, oi=oi, g=g: e.dma_start(out=ffnp_d[:, g * 512:(g + 1) * 512], in_=uost[oi][0:2, :]), [("uost", oi)], "o_ffnp%d" % oi)
                    else:
                        out_dma(lambda e, oi=oi, g=g: e.dma_start(out=ffns_d[:, g * 512:(g + 1) * 512], in_=uost[oi][0:32, :]), [("uost", oi)], "o_ffns%d" % oi)
            gload(GB0, "GB0", g_fpost_d)
            oblocks = [(bi, r0_, n) for bi, (r0_, n) in enumerate(blocks) if n == 128]
            for cg in range(4):
                bkm = {bi: nb() for (bi, _, _) in oblocks}
                for kq in range(8):
                    s = wload(w_dn_v[:, kq * 8:(kq + 1) * 8, cg * 512:(cg + 1) * 512])
                    col = 0
                    for bi, (r0_, n) in enumerate(blocks):
                        if n == 128:
                            for k in range(8):
                                kk = kq * 8 + k
                                A("pe", lambda e, s=s, k=k, kk=kk, bi=bi, col=col, bkm=bkm: e.matmul(
                                    banks[bkm[bi]][:, :], lhsT=actT[:, kk, col:col + 128], rhs=ring[s][:, k, :], start=(kk == 0), stop=(kk == 63)),
                                  reads=[("w", s), "actT"], writes=[("bk", bkm[bi])])
                        col += n
                for (bi, r0_, n) in oblocks:
                    ms_i = 0
                    A("act", lambda e, bi=bi, ms_i=ms_i, cg=cg, bkm=bkm: e.copy(out=MS[ms_i][:, cg * 512:(cg + 1) * 512], in_=banks[bkm[bi]][:, :]),
                      reads=[("bk", bkm[bi])], writes=[("MS", ms_i)])
            for (bi, r0_, n) in oblocks:
                ms_i = 0
                xb, xr = xbufs[bi]
                rmsnorm_stats(MS[ms_i][:, :], 128, 3, [("MS", ms_i)])
                A("dve", lambda e, ms_i=ms_i: e.scalar_tensor_tensor(out=MS[ms_i][:, :], in0=MS[ms_i][:, :], scalar=rstd_t[:, 3:4], in1=GB0[:, :],
                                                                   op0=ALU.mult, op1=ALU.mult), reads=[("MS", ms_i), ("rstd", 3), "GB0"], writes=[("MS", ms_i)])
                A("dve", lambda e, ms_i=ms_i, xb=xb: e.tensor_tensor(out=MS[ms_i][:, :], in0=MS[ms_i][:, :], in1=xb[:, :], op=ALU.add),
                  reads=[("MS", ms_i), xr], writes=[("MS", ms_i)])
                yrow = r0_ - 130
                out_dma(lambda e, ms_i=ms_i, yrow=yrow: e.dma_start(out=y_d[yrow:yrow + 128, :], in_=MS[ms_i][:, :]), [("MS", ms_i)], "o_y%d" % ms_i)

        for ti_, (kind_, blocks_) in enumerate(TILES):
            process_tile(ti_, kind_, blocks_)

        S.finalize()
        sems = {}
        for eng in S.ENGS:
            sems[("eng", eng)] = es.enter_context(nc.semaphore("s_" + eng))
        for gname in S.dma_final:
            sems[("dma", gname)] = es.enter_context(nc.semaphore("d_" + gname))
        with nc.Block() as block:
            @block.tensor
            def _(e): S.emit("pe", e, sems)

            @block.scalar
            def _(e): S.emit("act", e, sems)

            @block.vector
            def _(e): S.emit("dve", e, sems)

            @block.gpsimd
            def _(e): S.emit("pool", e, sems)

            @block.sync
            def _(e):
                S.emit("sp", e, sems)
                for gname in out_dma_groups:
                    e.wait_ge(sems[("dma", gname)], S.dma_final[gname])
    return nc


_NC_CACHE = {}


def kernel(x_prompt, x_sample, cache_k, cache_v, state_conv, state_ffn_conv,
           norm_mix_pre, w_in, b_in, conv_dw_k, conv_dw_b, conv_ln_g, conv_ln_b, w_conv_proj,
           attn_sinks, rel_bias, w_attn_proj, w_out, norm_mix_post,
           norm_ffn_pre, w_up, ffn_dw_k, ffn_dw_b, w_down, norm_ffn_post):
    f = np.float32
    x_prompt = np.asarray(x_prompt, f); x_sample = np.asarray(x_sample, f)
    wp = win_perm()
    w_in_p = np.ascontiguousarray(np.asarray(w_in, f)[0][:, wp])
    b_in_p = np.asarray(b_in, f)[0][wp]
    b_in_T = fmT(b_in_p, 84)
    b_kv = np.ascontiguousarray(np.asarray(b_in, f)[0][6144:6656])
    arp = attn_row_perm()
    w_ap = np.ascontiguousarray(np.asarray(w_attn_proj, f)[0][arp, :])
    up = wup_perm()
    w_up_p = np.ascontiguousarray(np.asarray(w_up, f)[0][:, up])
    ffn_k_T = np.ascontiguousarray(np.asarray(ffn_dw_k, f)[0][:, up].reshape(3, 128, 128).transpose(2, 1, 0))
    ffn_b_T = fmT(np.asarray(ffn_dw_b, f)[0][up], 128)
    conv_k_T = np.ascontiguousarray(np.asarray(conv_dw_k, f)[0].reshape(31, 16, 128).transpose(2, 1, 0))
    sinks = np.asarray(attn_sinks, f)[0]
    sinkT = np.zeros((128, 16), f)
    rbp = np.zeros((32, 32), f)
    rb = np.asarray(rel_bias, f)
    for cp in range(16):
        for half in range(2):
            h = head_of(half, cp)
            sinkT[half * 64:half * 64 + 64, cp] = sinks[h]
            rbp[:, half * 16 + cp] = rb[:, h]
    bucket = rel_bucket_np(np.arange(128))
    ohb = np.zeros((32, 128), f); ohb[bucket, np.arange(128)] = 1.0
    jm = np.zeros((128, 384), f)
    for dlt in range(128):
        jm[dlt, 255 - dlt] = 1.0
    mneg = np.zeros((128, 2, 129), f)
    s_idx = np.arange(128)[:, None]; r_idx = np.arange(129)[None, :]
    mneg[:, 0, :] = np.where(s_idx >= r_idx, 0.0, NEG)
    mneg[:, 1, :] = np.where(s_idx <= r_idx - 1, 0.0, NEG)

    shared = {
        "w_in": w_in_p, "b_in_T": b_in_T, "b_kv": b_kv,
        "conv_k_T": conv_k_T, "conv_b_T": fmT(np.asarray(conv_dw_b, f)[0], 16),
        "ln_g_T": fmT(np.asarray(conv_ln_g, f)[0], 16), "ln_b_T": fmT(np.asarray(conv_ln_b, f)[0], 16),
        "w_cp": np.ascontiguousarray(np.asarray(w_conv_proj, f)[0]), "w_ap": w_ap, "w_o": np.ascontiguousarray(np.asarray(w_out, f)[0]),
        "g_pre": np.ascontiguousarray(np.asarray(norm_mix_pre, f)[0]), "g_post": np.ascontiguousarray(np.asarray(norm_mix_post, f)[0]),
        "g_fpre": np.ascontiguousarray(np.asarray(norm_ffn_pre, f)[0]), "g_fpost": np.ascontiguousarray(np.asarray(norm_ffn_post, f)[0]),
        "w_up": w_up_p, "ffn_k_T": ffn_k_T, "ffn_b_T": ffn_b_T, "w_dn": np.ascontiguousarray(np.asarray(w_down, f)[0]),
        "sinkT": sinkT, "rbp": rbp, "ohb": ohb, "jm": jm, "mneg": mneg,
    }
    in_maps = []
    for c in range(NCORES):
        b, qd = c // 4, c % 4
        T0 = 1024 * qd
        xin = np.zeros((NROWS, D), f)
        if qd > 0:
            xin[0:130] = x_prompt[b, T0 - 130:T0]
        xin[130:1154] = x_prompt[b, T0:T0 + 1024]
        xin[1154:1282] = x_sample[16 * c:16 * c + 16].reshape(128, D)
        km = np.ones((128, 2), f)
        if qd == 0:
            km[:, 0] = 0.0
            km[0:2, 1] = 0.0
        m = dict(shared)
        m["xin"] = xin; m["kmask"] = km
        m["ck"] = np.ascontiguousarray(np.asarray(cache_k, f)[0, 16 * c:16 * c + 16].reshape(16, 128, 256))
        m["cv"] = np.ascontiguousarray(np.asarray(cache_v, f)[0, 16 * c:16 * c + 16].reshape(16, 128, 256))
        m["sc"] = np.ascontiguousarray(np.asarray(state_conv, f)[0, 16 * c:16 * c + 16])
        m["sf"] = np.ascontiguousarray(np.asarray(state_ffn_conv, f)[0, 16 * c:16 * c + 16][:, :, up])
        in_maps.append(m)

    if "nc" not in _NC_CACHE:
        _NC_CACHE["nc"] = build()
    nc = _NC_CACHE["nc"]
    res = run_bass_kernel_spmd(nc, in_maps, core_ids=list(range(NCORES)))
    R = res.results
    inv_up = np.argsort(up)
    y_p = np.zeros((2, 4096, D), f); y_s = np.zeros((128, 8, D), f)
    k_p = np.zeros((1, 2, 128, 4, 64), f); v_p = np.zeros((1, 2, 128, 4, 64), f)
    conv_p = np.zeros((1, 2, 30, D), f); ffn_p = np.zeros((1, 2, 2, DFF2), f)
    k_s = np.zeros((1, 128, 128, 4, 64), f); v_s = np.zeros((1, 128, 128, 4, 64), f)
    conv_s = np.zeros((1, 128, 30, D), f); ffn_s = np.zeros((1, 128, 2, DFF2), f)
    for c in range(NCORES):
        b, qd = c // 4, c % 4
        r = R[c]
        y_p[b, 1024 * qd:1024 * qd + 1024] = r["y"][0:1024]
        y_s[16 * c:16 * c + 16] = r["y"][1024:1152].reshape(16, 8, D)
        if qd == 3:
            k_p[0, b] = r["kvp"][:, 0:256].reshape(128, 4, 64)
            v_p[0, b] = r["kvp"][:, 256:512].reshape(128, 4, 64)
            conv_p[0, b] = r["convp"]
            ffn_p[0, b] = r["ffnp"][:, inv_up]
        k_s[0, 16 * c:16 * c + 16] = r["ks"].reshape(16, 128, 4, 64)
        v_s[0, 16 * c:16 * c + 16] = r["vs"].reshape(16, 128, 4, 64)
        conv_s[0, 16 * c:16 * c + 16] = r["convs"]
        ffn_s[0, 16 * c:16 * c + 16] = r["ffns"].reshape(16, 2, DFF2)[:, :, inv_up]
    return (y_p, y_s, k_p, v_p, conv_p, ffn_p, k_s, v_s, conv_s, ffn_s)
```

```python
import contextlib
import math
import numpy as np
import concourse.bass as bass
import concourse.mybir as mybir
from concourse.bass_utils import run_bass_kernel_spmd

F32 = mybir.dt.float32
BF16 = mybir.dt.bfloat16
AF = mybir.ActivationFunctionType
ALU = mybir.AluOpType

D = 2048
NH = 32
DIN = 10752
DFF2 = 16384
EPS = 1e-6
NCORES = 8
NROWS = 1282
NEG = -30000.0


class Op:
    __slots__ = ("eng", "fn", "deps", "signal", "sigval", "dma", "idx")

    def __init__(self, eng, fn, dma=None):
        self.eng = eng; self.fn = fn; self.deps = []; self.signal = False
        self.sigval = 0; self.dma = dma; self.idx = 0


class Sched:
    ENGS = ("pe", "act", "dve", "pool", "sp")

    def __init__(self):
        self.q = {e: [] for e in self.ENGS}
        self.lastw = {}
        self.readers = {}
        self.nops = 0

    def add(self, eng, fn, reads=(), writes=(), dma=None):
        op = Op(eng, fn, dma)
        op.idx = self.nops; self.nops += 1
        deps = {}
        for r in reads:
            w = self.lastw.get(r)
            if w is not None: deps[id(w)] = w
        for w in writes:
            lw = self.lastw.get(w)
            if lw is not None: deps[id(lw)] = lw
            for rd in self.readers.get(w, ()):
                deps[id(rd)] = rd
        for d in deps.values():
            if d is op: continue
            if d.dma is None and op.dma is None and d.eng == eng and eng == "pe":
                continue
            op.deps.append(d)
            d.signal = True
        for w in writes:
            self.lastw[w] = op
            self.readers[w] = []
        for r in reads:
            if r in writes: continue
            self.readers.setdefault(r, []).append(op)
        self.q[eng].append(op)
        return op

    def finalize(self):
        for e in self.ENGS:
            cnt = 0
            for op in self.q[e]:
                if op.dma is None and op.signal:
                    cnt += 1; op.sigval = cnt
        gc = {}
        allops = sorted([op for e in self.ENGS for op in self.q[e] if op.dma is not None], key=lambda o: o.idx)
        for op in allops:
            gc[op.dma] = gc.get(op.dma, 0) + 16
            op.sigval = gc[op.dma]
        self.dma_final = gc

    def emit(self, eng, e, sems):
        waited = {}
        for op in self.q[eng]:
            need = {}
            for d in op.deps:
                key = ("dma", d.dma) if d.dma is not None else ("eng", d.eng)
                if need.get(key, 0) < d.sigval: need[key] = d.sigval
            for key, val in need.items():
                if waited.get(key, 0) < val:
                    e.wait_ge(sems[key], val)
                    waited[key] = val
            ins = op.fn(e)
            if op.dma is not None:
                ins.then_inc(sems[("dma", op.dma)], 16)
            elif op.signal:
                ins.then_inc(sems[("eng", eng)], 1)


def head_of(half, cp):
    if cp < 8:
        return cp if half == 0 else 8 + cp
    return 16 + (cp - 8) if half == 0 else 24 + (cp - 8)


def win_perm():
    idx = []
    for c in range(16):
        idx += list(range(c * 128, c * 128 + 128))
        idx += list(range(2048 + c * 128, 2048 + c * 128 + 128))
    for cp in range(16):
        for half in range(2):
            h = head_of(half, cp)
            idx += list(range(4096 + h * 64, 4096 + h * 64 + 64))
    idx += list(range(6144, 6144 + 512))
    idx += list(range(6656, 6656 + 4096))
    return np.array(idx, dtype=np.int64)


def attn_row_perm():
    idx = []
    for cp in range(16):
        for half in range(2):
            h = head_of(half, cp)
            idx += list(range(h * 64, h * 64 + 64))
    return np.array(idx, dtype=np.int64)


def wup_perm():
    idx = []
    for j in range(64):
        idx += list(range(j * 128, j * 128 + 128))
        idx += list(range(8192 + j * 128, 8192 + j * 128 + 128))
    return np.array(idx, dtype=np.int64)


def rel_bucket_np(dist):
    max_exact = 16
    d = np.maximum(dist, 1).astype(np.float32)
    large = max_exact + (np.log(d / max_exact) / math.log(128 / max_exact) * (32 - max_exact)).astype(np.int32)
    large = np.minimum(large, 31)
    return np.where(dist < max_exact, dist, large)


def fmT(v, nch):
    return np.ascontiguousarray(v.reshape(nch, 128).T)


TILES = [("p", [(128, 2), (130, 128)])] + [("p", [(258 + 128 * j, 128)]) for j in range(7)] + [("s", [(1154, 128)])]
LASTP = 7
WMAX = 130
EXT = 128


def build():
    nc = bass.Bass("TRN2", target_bir_lowering=False)

    def din(name, shape, dt=F32):
        return nc.dram_tensor(name, list(shape), dt, kind="ExternalInput").ap()

    def dout(name, shape, dt=F32):
        return nc.dram_tensor(name, list(shape), dt, kind="ExternalOutput").ap()

    xin = din("xin", [NROWS, D])
    kmask_d = din("kmask", [128, 2])
    ck_d = din("ck", [16, 128, 256]); cv_d = din("cv", [16, 128, 256])
    sc_d = din("sc", [16, 30, D]); sf_d = din("sf", [16, 2, DFF2])
    w_in_d = din("w_in", [D, DIN]); b_in_T_d = din("b_in_T", [128, 84]); b_kv_d = din("b_kv", [512])
    conv_k_d = din("conv_k_T", [128, 16, 31]); conv_b_d = din("conv_b_T", [128, 16])
    ln_g_d = din("ln_g_T", [128, 16]); ln_b_d = din("ln_b_T", [128, 16])
    w_cp_d = din("w_cp", [D, D]); w_ap_d = din("w_ap", [D, D]); w_o_d = din("w_o", [D, D])
    g_pre_d = din("g_pre", [D]); g_post_d = din("g_post", [D]); g_fpre_d = din("g_fpre", [D]); g_fpost_d = din("g_fpost", [D])
    w_up_d = din("w_up", [D, DFF2]); ffn_k_d = din("ffn_k_T", [128, 128, 3]); ffn_b_d = din("ffn_b_T", [128, 128])
    w_dn_d = din("w_dn", [8192, D])
    sink_d = din("sinkT", [128, 16]); rbp_d = din("rbp", [32, 32])
    ohb_d = din("ohb", [32, 128]); jm_d = din("jm", [128, 384]); mneg_d = din("mneg", [128, 2, 129])

    y_d = dout("y", [1152, D])
    kvp_d = dout("kvp", [128, 512])
    convp_d = dout("convp", [30, D])
    ffnp_d = dout("ffnp", [2, DFF2])
    ks_d = dout("ks", [16, 128, 256]); vs_d = dout("vs", [16, 128, 256])
    convs_d = dout("convs", [16, 30, D])
    ffns_d = dout("ffns", [32, DFF2])

    S = Sched()
    out_dma_groups = []

    with contextlib.ExitStack() as es:
        def sb(name, shape, dt=F32):
            return es.enter_context(nc.sbuf_tensor(name, list(shape), dt))

        def ps(name, shape, dt=F32):
            return es.enter_context(nc.psum_tensor(name, list(shape), dt))

        A = S.add
        NBK = 6
        banks = [ps("bk%d" % i, [128, 512], F32) for i in range(NBK)]
        tbanks = [ps("tb%d" % i, [128, 1024], BF16) for i in range(2)]
        st = {"bk": 0, "tb": 0, "ws": 0, "dq": 0}

        def nb():
            i = st["bk"]; st["bk"] = (i + 1) % NBK
            return i

        def ntb():
            i = st["tb"]; st["tb"] = (i + 1) % 2
            return i

        NSLOT = 3
        ring = [sb("ring%d" % i, [128, 8, 512], BF16) for i in range(NSLOT)]

        wscr = nc.dram_tensor("wscr", [162, 128, 4096], BF16).ap()
        st["wi"] = 0
        st["tile"] = 0

        def wload(src):
            s = st["ws"]; st["ws"] = (s + 1) % NSLOT
            idx = st["wi"]; st["wi"] += 1
            if st["tile"] == 0:
                A("pool", lambda e, s=s, src=src: e.dma_start(out=ring[s][:], in_=src), writes=[("w", s)], dma="w%d" % s)
                A("sp", lambda e, s=s, idx=idx: e.dma_start(out=wscr[idx], in_=ring[s][:].rearrange("p k c -> p (k c)")),
                  reads=[("w", s)], writes=[("wscr", idx)], dma="wb%d" % s)
            else:
                A("pool", lambda e, s=s, idx=idx: e.dma_start(out=ring[s][:].rearrange("p k c -> p (k c)"), in_=wscr[idx]),
                  reads=[("wscr", idx)], writes=[("w", s)], dma="w%d" % s)
            return s

        w_in_v = w_in_d.rearrange("(k p) c -> p k c", p=128)
        w_cp_v = w_cp_d.rearrange("(k p) c -> p k c", p=128)
        w_ap_v = w_ap_d.rearrange("(k p) c -> p k c", p=128)
        w_o_v = w_o_d.rearrange("(k p) c -> p k c", p=128)
        w_up_v = w_up_d.rearrange("(k p) c -> p k c", p=128)
        w_dn_v = w_dn_d.rearrange("(k p) c -> p k c", p=128)

        def spq(fn, reads=(), writes=(), grp=None):
            if grp is None:
                grp = "q%d" % st["dq"]; st["dq"] = (st["dq"] + 1) % 6
            return A("sp", fn, reads=reads, writes=writes, dma=grp)

        ident_f = sb("ident_f", [128, 128]); ident_b = sb("ident_b", [128, 128], BF16)
        ones_f = sb("ones_f", [128, 128])
        epsT = sb("epsT", [128, 1])
        kmask = sb("kmaskS", [128, 2])
        b_in_T = sb("b_in_TS", [128, 84]); bq8 = sb("bq8", [128, 16])
        bkvB = sb("bkvB", [128, 512])
        conv_k = sb("conv_kS", [128, 16, 31]); conv_b = sb("conv_bS", [128, 16])
        ln_g = sb("ln_gS", [128, 16]); ln_b = sb("ln_bS", [128, 16])
        ffn_k = sb("ffn_kS", [128, 128, 3]); ffn_b = sb("ffn_bS", [128, 128])
        esink = sb("esink", [128, 16])
        GB0 = sb("GB0", [128, D])
        TBL = sb("TBL", [128, 2, 32, 129], BF16)
        OP1 = sb("OP1", [128, 2, 128], BF16); OPF = sb("OPF", [128, 2, 128], BF16); OPB = sb("OPB", [128, 2, 128], BF16)

        A("dve", lambda e: e.memset(ones_f[:], 1.0), writes=["ones_f"])
        A("dve", lambda e: e.memset(epsT[:], EPS), writes=["epsT"])
        A("pool", lambda e: e.memset(ident_f[:], 0.0), writes=["ident_f"])
        A("pool", lambda e: e.affine_select(out=ident_f[:], in_=ident_f[:], pattern=[[-1, 128]], compare_op=ALU.not_equal,
                                            fill=1.0, base=0, channel_multiplier=1), reads=["ident_f"], writes=["ident_f"])
        A("dve", lambda e: e.tensor_copy(out=ident_b[:], in_=ident_f[:]), reads=["ident_f"], writes=["ident_b"])
        for (dst, src, nm) in ((kmask, kmask_d, "kmask"), (b_in_T, b_in_T_d, "b_in_T"), (conv_k, conv_k_d, "conv_k"),
                               (conv_b, conv_b_d, "conv_b"), (ln_g, ln_g_d, "ln_g"), (ln_b, ln_b_d, "ln_b"),
                               (ffn_k, ffn_k_d, "ffn_k"), (ffn_b, ffn_b_d, "ffn_b"), (esink, sink_d, "esink")):
            spq(lambda e, dst=dst, src=src: e.dma_start(out=dst[:], in_=src), writes=[nm])
        spq(lambda e: e.dma_start(out=bkvB[:], in_=b_kv_d.partition_broadcast(128)), writes=["bkvB"])
        A("act", lambda e: e.activation(out=esink[:], in_=esink[:], func=AF.Exp), reads=["esink"], writes=["esink"])
        A("dve", lambda e: e.tensor_scalar(out=bq8[:], in0=b_in_T[:, 32:48], scalar1=0.125, scalar2=None, op0=ALU.mult),
          reads=["b_in_T"], writes=["bq8"])
        for (T, nm, col) in ((OP1, "OP1", None), (OPF, "OPF", 0), (OPB, "OPB", 1)):
            A("dve", lambda e, T=T: e.memset(T[:], 0.0), writes=[nm])
            for half in range(2):
                if col is None:
                    A("dve", lambda e, T=T, half=half: e.memset(T[:, half, half * 64:half * 64 + 64], 1.0), reads=[], writes=[nm])
                else:
                    A("dve", lambda e, T=T, half=half, col=col: e.tensor_scalar(
                        out=T[:, half, half * 64:half * 64 + 64], in0=ones_f[:, 0:64], scalar1=kmask[:, col:col + 1], scalar2=None,
                        op0=ALU.mult), reads=["ones_f", "kmask"], writes=[nm])

        rbp = sb("rbpS", [32, 32]); ohb = sb("ohbS", [32, 128]); jm = sb("jmS", [128, 384]); mneg = sb("mnegS", [128, 2, 129])
        vec = sb("vecS", [128, 32])
        for (dst, src, nm) in ((rbp, rbp_d, "rbp"), (ohb, ohb_d, "ohb"), (jm, jm_d, "jm"), (mneg, mneg_d, "mneg")):
            spq(lambda e, dst=dst, src=src: e.dma_start(out=dst[:], in_=src), writes=[nm])
        b0 = nb()
        A("pe", lambda e, b0=b0: e.matmul(banks[b0][:, 0:32], lhsT=ohb[:, :], rhs=rbp[:, :], start=True, stop=True),
          reads=["ohb", "rbp"], writes=[("bk", b0)])
        A("dve", lambda e, b0=b0: e.tensor_copy(out=vec[:], in_=banks[b0][:, 0:32]), reads=[("bk", b0)], writes=["vec"])
        for kc in range(2):
            for r0 in range(0, 129, 16):
                nr = min(16, 129 - r0)
                bk = nb()
                for rr in range(nr):
                    r = r0 + rr
                    start_col = (128 - r) if kc == 0 else (256 - r)
                    A("pe", lambda e, bk=bk, rr=rr, sc_=start_col: e.matmul(
                        banks[bk][:, rr * 32:(rr + 1) * 32], lhsT=jm[:, sc_:sc_ + 128], rhs=vec[:, :], start=True, stop=True),
                      reads=["jm", "vec"], writes=[("bk", bk)])
                A("dve", lambda e, bk=bk, kc=kc, r0=r0, nr=nr: e.tensor_tensor(
                    out=TBL[:, kc, :, r0:r0 + nr].rearrange("p h r -> p r h"),
                    in0=banks[bk][:, 0:nr * 32].rearrange("p (r h) -> p r h", h=32),
                    in1=mneg[:, kc, r0:r0 + nr].unsqueeze(2).to_broadcast([128, nr, 32]), op=ALU.add),
                  reads=[("bk", bk), "mneg"], writes=["TBL"])

        KT = sb("KT", [128, 2, NROWS], BF16)
        xh = [sb("xh%d" % i, [128, D]) for i in range(2)]
        XS = sb("XS", [128, D], BF16)
        ssq = sb("ssq", [128, 8]); rstd_t = sb("rstd_t", [128, 8])
        xnT = sb("xnT", [128, 16, EXT + WMAX], BF16)
        hnT = sb("hnT", [128, 16, WMAX], BF16)
        arena = sb("arena", [128, 64, WMAX], BF16)
        actT = arena
        cT = arena[:, 0:32, :].rearrange("p a b -> p (a b)").bitcast(F32).rearrange("p (k w) -> p k w", w=WMAX)
        qT = arena[:, 32:48, :]
        oT = arena[:, 48:64, :]
        sT = sb("sT", [128, 16, WMAX], BF16)
        mixT = sb("mixT", [128, 16, WMAX], BF16)
        MS = [sb("MS%d" % i, [128, D]) for i in range(2)]
        Gb = [sb("Gb%d" % i, [128, 32 + WMAX]) for i in range(2)]
        ghist = sb("ghist", [128, 16, 32])
        uhist = sb("uhist", [128, 128, 2])
        tmpA = [sb("tmpA%d" % i, [128, 512]) for i in range(3)]
        VP = [sb("VP%d" % i, [128, 4, 128], BF16) for i in range(6)]
        Vf = sb("Vf", [128, 512])
        Eb = [sb("Eb%d" % i, [128, 2, 512], BF16) for i in range(2)]
        rden = [sb("rden%d" % i, [128, 512]) for i in range(1)]
        lnm = sb("lnm", [128, WMAX]); lnr = sb("lnr", [128, WMAX]); lnt = sb("lnt", [128, WMAX]); lnz = [sb("lnz%d" % i, [128, WMAX]) for i in range(2)]
        CO = sb("CO", [128, 4, WMAX]); T1 = sb("T1", [128, 4, WMAX]); sgt = [sb("sgt%d" % i, [128, WMAX]) for i in range(2)]
        Ub = [sb("Ub%d" % i, [128, 2 + WMAX]) for i in range(4)]
        cvb = [sb("cvb%d" % i, [128, WMAX]) for i in range(4)]
        geb = [sb("geb%d" % i, [128, WMAX]) for i in range(2)]
        ckb = sb("ckb", [128, 2, 256], BF16); KcT = sb("KcT", [128, 2, 2, 128], BF16)
        cvbuf = sb("cvbuf", [128, 2, 256], BF16); VPc = [sb("VPc%d" % i, [128, 4, 128], BF16) for i in range(2)]
        VsP = [sb("VsP%d" % i, [8, 4, 128], BF16) for i in range(2)]
        GS = sb("GS", [128, 16, 38]); scb = [sb("scb%d" % i, [120, 4, 128]) for i in range(1)]
        sfb = [sb("sfb%d" % i, [32, 512]) for i in range(1)]
        UST = sb("UST", [128, 4, 16, 10])
        UO = sb("UO", [128, 4, 34]); uost = [sb("uost%d" % i, [34, 512]) for i in range(1)]
        GO = sb("GO", [128, 16, 30])
        cnt = {"g": 0, "e": 0, "u": 0, "o": 0}

        def rmsnorm_stats(src_ap, n, col, rd):
            A("dve", lambda e: e.memset(ssq[0:n, col:col + 1], 0.0), writes=[("ssq", col)])
            A("act", lambda e: e.activation(out=XS[0:n, :], in_=src_ap, func=AF.Square, accum_out=ssq[0:n, col:col + 1]),
              reads=rd + [("ssq", col)], writes=["XS", ("ssq", col)])
            A("act", lambda e: e.activation(out=rstd_t[0:n, col:col + 1], in_=ssq[0:n, col:col + 1], func=AF.Sqrt,
                                            scale=1.0 / D, bias=epsT[0:n, 0:1]), reads=[("ssq", col), "epsT"], writes=[("rstd", col)])
            A("dve", lambda e: e.reciprocal(out=rstd_t[0:n, col:col + 1], in_=rstd_t[0:n, col:col + 1]),
              reads=[("rstd", col)], writes=[("rstd", col)])

        def transpose_to(dstT, dst_res, col0, n, src_res):
            for half in range(2):
                tb = ntb()
                for k in range(8):
                    kk = half * 8 + k
                    A("pe", lambda e, tb=tb, k=k, kk=kk: e.transpose(out=tbanks[tb][:, k * 128:k * 128 + n],
                                                                  in_=XS[0:n, kk * 128:(kk + 1) * 128], identity=ident_b[0:n, 0:n]),
                      reads=[src_res, "ident_b"], writes=[("tb", tb)])
                A("act", lambda e, tb=tb, half=half: e.copy(
                    out=dstT[:, half * 8:half * 8 + 8, col0:col0 + n],
                    in_=tbanks[tb][:, :].rearrange("p (k c) -> p k c", c=128)[:, :, 0:n]),
                  reads=[("tb", tb)], writes=[dst_res])

        def gload(GBt, nm, src):
            spq(lambda e: e.dma_start(out=GBt[:], in_=src.partition_broadcast(128)), writes=[nm])

        def stage0_block(row0, n, dst_col, xbuf, xres):
            spq(lambda e: e.dma_start(out=xbuf[0:n, :], in_=xin[row0:row0 + n, :]), writes=[xres])
            rmsnorm_stats(xbuf[0:n, :], n, 0, [xres])
            A("dve", lambda e: e.scalar_tensor_tensor(out=XS[0:n, :], in0=xbuf[0:n, :], scalar=rstd_t[0:n, 0:1], in1=GB0[0:n, :],
                                                      op0=ALU.mult, op1=ALU.mult), reads=[xres, ("rstd", 0), "GB0"], writes=["XS"])
            transpose_to(xnT, "xnT", dst_col, n, "XS")

        def fm_group(wv, c0, chunks, rhs_fn, N, rhs_res, evac):
            s0 = wload(wv[:, 0:8, c0:c0 + 512]); s1 = wload(wv[:, 8:16, c0:c0 + 512])
            bks = {i: nb() for i in chunks}
            for h, s in enumerate((s0, s1)):
                for i in chunks:
                    for k in range(8):
                        kk = 8 * h + k
                        A("pe", lambda e, s=s, i=i, k=k, kk=kk: e.matmul(
                            banks[bks[i]][:, 0:N], lhsT=ring[s][:, k, i * 128:(i + 1) * 128], rhs=rhs_fn(kk),
                            start=(kk == 0), stop=(kk == 15)), reads=[("w", s)] + rhs_res, writes=[("bk", bks[i])])
            for i in chunks:
                evac(i, bks[i])
            return s0, s1

        def make_vpad(win_cols, vp_idx, mask, with_k_out=False, s01=None):
            s0, s1 = s01
            bk = nb()
            c_lo = 0 if with_k_out else 256
            for h, s in enumerate((s0, s1)):
                for k in range(8):
                    kk = 8 * h + k
                    A("pe", lambda e, s=s, k=k, kk=kk: e.matmul(
                        banks[bk][:, c_lo:512], lhsT=xnT[:, kk, win_cols:win_cols + 128], rhs=ring[s][:, k, c_lo:512],
                        start=(kk == 0), stop=(kk == 15)), reads=[("w", s), "xnT"], writes=[("bk", bk)])
            A("dve", lambda e: e.tensor_tensor(out=Vf[:, c_lo:512], in0=banks[bk][:, c_lo:512], in1=bkvB[:, c_lo:512], op=ALU.add),
              reads=[("bk", bk), "bkvB"], writes=["Vf"])
            vpad_from(Vf[:, 256:512], "Vf", VP[vp_idx], ("VP", vp_idx), mask)

        def vpad_from(src, src_res, dstT, dst_res, mask, npart=128):
            A("dve", lambda e: e.memset(dstT[0:npart], 0.0), writes=[dst_res])
            for par in range(2):
                sv = src.rearrange("p (g t d) -> p g t d", g=2, t=2)[:, :, par, :]
                dv = dstT[0:npart].rearrange("p (g t) c -> p g t c", t=2)[:, :, par, par * 64:par * 64 + 64]
                if mask is None:
                    A("dve", lambda e, sv=sv, dv=dv: e.tensor_copy(out=dv, in_=sv), reads=[src_res], writes=[dst_res])
                else:
                    A("dve", lambda e, sv=sv, dv=dv: e.tensor_scalar(out=dv, in0=sv, scalar1=kmask[0:npart, mask:mask + 1], scalar2=None,
                                                                 op0=ALU.mult), reads=[src_res, "kmask"], writes=[dst_res])

        def attn_block(qc0, n, r0, k0_fn, k0_res, k1_fn, k1_res, nk1, vp0, vp0_res, op0, vp1, vp1_res, op1):
            CB = 4 if n > 64 else 8
            def do_gc(gp, cb):
                if True:
                    cp0 = gp * 8 + cb * CB
                    ebs = []
                    for half in range(2):
                        eb = Eb[cnt["e"] % 2]; ebr = ("Eb", cnt["e"] % 2); cnt["e"] += 1
                        ebs.append((eb, ebr))
                        hs = slice(half * 64, half * 64 + 64)
                        for kc in range(2):
                            nk = 128 if kc == 0 else nk1
                            kfn, kres = (k0_fn, k0_res) if kc == 0 else (k1_fn, k1_res)
                            bk = nb()
                            A("pe", lambda e, bk=bk, kfn=kfn, hs=hs, nk=nk: e.matmul(
                                banks[bk][0:nk, 0:CB * n].rearrange("p (c q) -> p c q", q=n), lhsT=kfn(gp, hs),
                                rhs=qT[hs, cp0:cp0 + CB, qc0:qc0 + n], start=True, stop=False),
                              reads=[kres, "qT"], writes=[("bk", bk)])
                            A("pe", lambda e, bk=bk, kc=kc, nk=nk, half=half: e.matmul(
                                banks[bk][0:nk, 0:CB * n].rearrange("p (c q) -> p c q", q=n), lhsT=ident_b[0:nk, 0:nk],
                                rhs=TBL[0:nk, kc, half * 16 + cp0:half * 16 + cp0 + CB, r0:r0 + n], start=False, stop=True),
                              reads=["ident_b", "TBL"], writes=[("bk", bk)])
                            A("act", lambda e, bk=bk, kc=kc, nk=nk, eb=eb: e.activation(
                                out=eb[0:nk, kc, 0:CB * n], in_=banks[bk][0:nk, 0:CB * n], func=AF.Exp),
                              reads=[("bk", bk)], writes=[ebr])
                    bo = nb(); bd = nb()
                    first = True
                    for half in range(2):
                        eb, ebr = ebs[half]
                        for kc in range(2):
                            nk = 128 if kc == 0 else nk1
                            vp, vpr, opt = (vp0, vp0_res, op0) if kc == 0 else (vp1, vp1_res, op1)
                            last = (half == 1 and kc == 1)
                            A("pe", lambda e, vp=vp, nk=nk, eb=eb, kc=kc, half=half, first=first, last=last: e.matmul(
                                banks[bo][:, 0:CB * n], lhsT=vp[0:nk, 2 * gp + half, :], rhs=eb[0:nk, kc, 0:CB * n],
                                start=first, stop=last), reads=[vpr, ebr], writes=[("bk", bo)])
                            A("pe", lambda e, opt=opt, nk=nk, eb=eb, kc=kc, half=half, first=first, last=last: e.matmul(
                                banks[bd][:, 0:CB * n], lhsT=opt[0][0:nk, half, :], rhs=eb[0:nk, kc, 0:CB * n],
                                start=first, stop=last), reads=[opt[1], ebr], writes=[("bk", bd)])
                            first = False
                    rd = rden[0]; rdr = ("rden", 0)
                    for i in range(CB):
                        A("dve", lambda e, i=i: e.tensor_scalar(out=rd[:, i * n:(i + 1) * n], in0=banks[bd][:, i * n:(i + 1) * n],
                                                               scalar1=esink[:, cp0 + i:cp0 + i + 1], scalar2=None, op0=ALU.add),
                          reads=[("bk", bd), "esink"], writes=[rdr])
                    A("dve", lambda e: e.reciprocal(out=rd[:, 0:CB * n], in_=rd[:, 0:CB * n]), reads=[rdr], writes=[rdr])
                    A("dve", lambda e: e.tensor_tensor(
                        out=oT[:, cp0:cp0 + CB, qc0:qc0 + n], in0=banks[bo][:, 0:CB * n].rearrange("p (c q) -> p c q", q=n),
                        in1=rd[:, 0:CB * n].rearrange("p (c q) -> p c q", q=n), op=ALU.mult),
                      reads=[("bk", bo), rdr], writes=["oT"])
            for gp_ in range(2):
                for cb_ in range(8 // CB):
                    do_gc(gp_, cb_)

        def out_dma(fn, reads, grp):
            if grp not in out_dma_groups:
                out_dma_groups.append(grp)
            A("sp", fn, reads=reads, dma=grp)

        vp_of_row = {}
        def process_tile(ti, kind, blocks):
            st["tile"] = ti; st["wi"] = 0
            W = sum(n for _, n in blocks)
            row_lo = blocks[0][0]
            mo = EXT if ti == 0 else 0
            is_s = (kind == "s")
            gload(GB0, "GB0", g_pre_d)
            if ti == 0:
                stage0_block(0, 128, 0, xh[1], ("xh", 1))
            col = mo
            xbufs = {}
            for bi, (r0_, n) in enumerate(blocks):
                slot = 1 if n == 2 else 0
                xb, xr = xh[slot], ("xh", slot)
                xbufs[bi] = (xb, xr)
                stage0_block(r0_, n, col, xb, xr)
                col += n
            ge = 32 if ti == 0 else 0
            NG = ge + W
            Wc = W
            def sample_post(ch, Gt, Gr):
                sj = 0
                spq(lambda e: e.dma_start(out=scb[sj][:, :, :], in_=sc_d.rearrange("(a s) r c -> (s r) a c", a=4)[:, :, ch * 128:(ch + 1) * 128]),
                    writes=[("scb", sj)])
                for a4 in range(4):
                    bt = nb()
                    A("pe", lambda e, a4=a4, bt=bt: e.transpose(out=banks[bt][:, 0:120], in_=scb[sj][0:120, a4, :], identity=ident_f[0:120, 0:120]),
                      reads=[("scb", sj), "ident_f"], writes=[("bk", bt)])
                    A("act", lambda e, a4=a4, bt=bt: e.copy(out=GS[:, a4 * 4:a4 * 4 + 4, 0:30],
                                                           in_=banks[bt][:, 0:120].rearrange("p (s r) -> p s r", r=30)),
                      reads=[("bk", bt)], writes=["GS"])
                A("dve", lambda e: e.tensor_copy(out=GS[:, :, 30:38], in_=Gt[:, 32:32 + 128].rearrange("p (s t) -> p s t", t=8)),
                  reads=[Gr], writes=["GS"])
                cv3 = cT[:, ch, 0:128].rearrange("p (s t) -> p s t", t=8)
                A("dve", lambda e: e.tensor_scalar(out=cv3, in0=GS[:, :, 0:8], scalar1=conv_k[:, ch, 0:1], scalar2=conv_b[:, ch:ch + 1],
                                                   op0=ALU.mult, op1=ALU.add), reads=["GS", "conv_k", "conv_b"], writes=[("cT", ch)])
                for j in range(1, 31):
                    A("dve", lambda e, j=j: e.scalar_tensor_tensor(out=cv3, in0=GS[:, :, j:j + 8], scalar=conv_k[:, ch, j:j + 1], in1=cv3,
                                                                  op0=ALU.mult, op1=ALU.add), reads=["GS", "conv_k", ("cT", ch)], writes=[("cT", ch)])
                bt = nb()
                A("pe", lambda e, bt=bt: e.transpose(out=banks[bt][:, 0:128], in_=Gt[:, 32:160], identity=ident_f[:, :]),
                  reads=[Gr, "ident_f"], writes=[("bk", bt)])
                A("act", lambda e, bt=bt: e.copy(out=MS[0][:, ch * 128:(ch + 1) * 128], in_=banks[bt][:, 0:128]),
                  reads=[("bk", bt)], writes=[("MS", 0)])

            for g in range(8):
                s_pair = {}
                posts = []

                def evac_ab(i, bk, g=g):
                    s_pair[i] = bk
                    if i % 2 == 0:
                        return
                    ch = g * 2 + i // 2
                    ba, bb = s_pair[i - 1], bk
                    gi = cnt["g"] % 2; cnt["g"] += 1
                    Gt, Gr = Gb[gi], ("Gb", gi)
                    sg = tmpA[gi]; sgr = ("tmpA", gi)
                    A("act", lambda e: e.activation(out=sg[:, 0:NG], in_=banks[bb][:, 0:NG], func=AF.Sigmoid,
                                                    bias=b_in_T[:, 2 * ch + 1:2 * ch + 2]), reads=[("bk", bb), "b_in_T"], writes=[sgr])
                    g0 = 32 - ge
                    A("dve", lambda e: e.scalar_tensor_tensor(out=Gt[:, g0:g0 + NG], in0=banks[ba][:, 0:NG],
                                                              scalar=b_in_T[:, 2 * ch:2 * ch + 1], in1=sg[:, 0:NG], op0=ALU.add, op1=ALU.mult),
                      reads=[("bk", ba), "b_in_T", sgr], writes=[Gr])
                    if is_s:
                        posts.append((ch, Gt, Gr))
                        return
                    if ti == 0:
                        A("dve", lambda e: e.tensor_scalar(out=Gt[:, 0:34], in0=Gt[:, 0:34], scalar1=kmask[:, 0:1], scalar2=None, op0=ALU.mult),
                          reads=[Gr, "kmask"], writes=[Gr])
                    else:
                        A("dve", lambda e: e.tensor_copy(out=Gt[:, 0:32], in_=ghist[:, ch, :]), reads=[("ghist", ch)], writes=[Gr])
                    A("dve", lambda e: e.tensor_copy(out=ghist[:, ch, :], in_=Gt[:, W:W + 32]), reads=[Gr], writes=[("ghist", ch)])
                    if ti == LASTP:
                        A("act", lambda e: e.copy(out=GO[:, ch, :], in_=Gt[:, 32 + W - 30:32 + W]), reads=[Gr], writes=["GO"])
                    cv = cT[:, ch, 0:Wc]
                    A("dve", lambda e: e.tensor_scalar(out=cv, in0=Gt[:, 2:2 + Wc], scalar1=conv_k[:, ch, 0:1], scalar2=conv_b[:, ch:ch + 1],
                                                       op0=ALU.mult, op1=ALU.add), reads=[Gr, "conv_k", "conv_b"], writes=[("cT", ch)])
                    for j in range(1, 31):
                        A("dve", lambda e, j=j: e.scalar_tensor_tensor(out=cv, in0=Gt[:, 2 + j:2 + j + Wc], scalar=conv_k[:, ch, j:j + 1], in1=cv,
                                                                      op0=ALU.mult, op1=ALU.add), reads=[Gr, "conv_k", ("cT", ch)], writes=[("cT", ch)])

                xc0 = mo - ge
                fm_group(w_in_v, g * 512, [0, 1, 2, 3], lambda kk, xc0=xc0, NG=NG: xnT[:, kk, xc0:xc0 + NG], NG, ["xnT"], evac_ab)
                for (ch_, Gt_, Gr_) in posts:
                    sample_post(ch_, Gt_, Gr_)
            if is_s:
                for s in range(16):
                    out_dma(lambda e, s=s: e.dma_start(out=convs_d[s, 22:30, :], in_=MS[0][s * 8:s * 8 + 8, :]), [("MS", 0)], "o_convs")
                out_dma(lambda e: e.dma_start(out=convs_d[:, 0:22, :], in_=sc_d[:, 8:30, :]), [], "o_convs")
            if ti == LASTP:
                for ch in range(16):
                    bt = nb()
                    A("pe", lambda e, bt=bt, ch=ch: e.transpose(out=banks[bt][0:30, 0:128], in_=GO[:, ch, :], identity=ident_f[:, :]),
                      reads=["GO", "ident_f"], writes=[("bk", bt)])
                    A("act", lambda e, bt=bt, ch=ch: e.copy(out=MS[0][0:30, ch * 128:(ch + 1) * 128], in_=banks[bt][0:30, 0:128]),
                      reads=[("bk", bt)], writes=[("MS", 0)])
                out_dma(lambda e: e.dma_start(out=convp_d[:, :], in_=MS[0][0:30, :]), [("MS", 0)], "o_convp")
            for g in range(4):
                def evac_q(i, bk, g=g):
                    cp = g * 4 + i
                    A("dve", lambda e: e.tensor_scalar(out=qT[:, cp, 0:W], in0=banks[bk][:, 0:W], scalar1=0.125, scalar2=bq8[:, cp:cp + 1],
                                                       op0=ALU.mult, op1=ALU.add), reads=[("bk", bk), "bq8"], writes=["qT"])
                fm_group(w_in_v, (8 + g) * 512, [0, 1, 2, 3], lambda kk: xnT[:, kk, mo:mo + W], W, ["xnT"], evac_q)
            def evac_k(i, bk):
                A("act", lambda e: e.activation(out=KT[:, i, row_lo:row_lo + W], in_=banks[bk][:, 0:W], func=AF.Identity,
                                                bias=b_in_T[:, 48 + i:49 + i]), reads=[("bk", bk), "b_in_T"], writes=["KT"])
            s01 = fm_group(w_in_v, 12 * 512, [0, 1], lambda kk: xnT[:, kk, mo:mo + W], W, ["xnT"], evac_k)
            if ti == 0:
                for i in range(2):
                    bk = nb()
                    for h, s in enumerate(s01):
                        for k in range(8):
                            kk = 8 * h + k
                            A("pe", lambda e, s=s, k=k, kk=kk, i=i, bk=bk: e.matmul(
                                banks[bk][:, 0:128], lhsT=ring[s][:, k, i * 128:(i + 1) * 128], rhs=xnT[:, kk, 0:128],
                                start=(kk == 0), stop=(kk == 15)), reads=[("w", s), "xnT"], writes=[("bk", bk)])
                    A("act", lambda e, i=i, bk=bk: e.activation(out=KT[:, i, 0:128], in_=banks[bk][:, 0:128], func=AF.Identity,
                                                              bias=b_in_T[:, 48 + i:49 + i]), reads=[("bk", bk), "b_in_T"], writes=["KT"])
            if not is_s:
                wins = []
                if ti == 0:
                    wins += [(0, 0), (128, 1), (2, 0)]
                for (r0_, n) in blocks:
                    if n == 128:
                        wins.append((r0_, None))
                for (wr, mask) in wins:
                    vi = len(vp_of_row) % 6
                    vp_of_row[wr] = vi
                    last_win = (wr == 1026)
                    make_vpad(wr - row_lo + mo, vi, mask, with_k_out=last_win, s01=s01)
                    if last_win:
                        out_dma(lambda e: e.dma_start(out=kvp_d[:, :], in_=Vf[:, :]), ["Vf"], "o_kvp")
            else:
                make_vpad(0, 5, None, with_k_out=True, s01=s01)
                for s in range(16):
                    out_dma(lambda e, s=s: e.dma_start(out=ks_d[s, 120:128, :], in_=Vf[s * 8:s * 8 + 8, 0:256]), ["Vf"], "o_ks")
                    out_dma(lambda e, s=s: e.dma_start(out=vs_d[s, 120:128, :], in_=Vf[s * 8:s * 8 + 8, 256:512]), ["Vf"], "o_vs")
                out_dma(lambda e: e.dma_start(out=ks_d[:, 0:120, :], in_=ck_d[:, 8:128, :]), [], "o_ks")
                out_dma(lambda e: e.dma_start(out=vs_d[:, 0:120, :], in_=cv_d[:, 8:128, :]), [], "o_vs")
            if not is_s:
                col = 0
                for (r0_, n) in blocks:
                    if n == 2:
                        w0, w1 = 0, 128
                        o0 = (OPF, "OPF"); o1 = (OPB, "OPB"); rr0 = 1
                    else:
                        w0, w1 = r0_ - 128, r0_
                        o0 = (OPF, "OPF") if w0 == 2 else (OP1, "OP1"); o1 = (OP1, "OP1"); rr0 = 1
                    v0, v1 = vp_of_row[w0], vp_of_row[w1]
                    attn_block(col, n, rr0,
                               lambda gp, hs, w0=w0: KT[hs, gp, w0:w0 + 128], "KT",
                               lambda gp, hs, w1=w1: KT[hs, gp, w1:w1 + 128], "KT", 128,
                               VP[v0], ("VP", v0), o0, VP[v1], ("VP", v1), o1)
                    col += n
            else:
                for s in range(16):
                    j2 = s % 2
                    if j2 == 0:
                        A("pool", lambda e, s=s: e.dma_start(out=ckb[:, :, :], in_=ck_d[s:s + 2].rearrange("s k d -> k s d")), writes=["ckb"], dma="ckb")
                        A("pool", lambda e, s=s: e.dma_start(out=cvbuf[:, :, :], in_=cv_d[s:s + 2].rearrange("s k d -> k s d")), writes=["cvbuf"], dma="cvb")
                        for s2 in range(2):
                            tb = ntb()
                            for gp in range(2):
                                A("pe", lambda e, tb=tb, gp=gp, s2=s2: e.transpose(out=tbanks[tb][:, gp * 128:(gp + 1) * 128],
                                                                              in_=ckb[:, s2, gp * 128:(gp + 1) * 128], identity=ident_b[:, :]),
                                  reads=["ckb", "ident_b"], writes=[("tb", tb)])
                            A("act", lambda e, tb=tb, s2=s2: e.copy(out=KcT[:, s2, :, :], in_=tbanks[tb][:, 0:256].rearrange("p (g k) -> p g k", g=2)),
                              reads=[("tb", tb)], writes=[("KcT", s2)])
                            vpad_from(cvbuf[:, s2, :], "cvbuf", VPc[s2], ("VPc", s2), None)
                    bk = nb()
                    for h, sl in enumerate(s01):
                        for k in range(8):
                            kk = 8 * h + k
                            A("pe", lambda e, sl=sl, k=k, kk=kk, bk=bk, s=s: e.matmul(
                                banks[bk][0:8, 0:256], lhsT=xnT[:, kk, s * 8:s * 8 + 8], rhs=ring[sl][:, k, 256:512],
                                start=(kk == 0), stop=(kk == 15)), reads=[("w", sl), "xnT"], writes=[("bk", bk)])
                    A("dve", lambda e, bk=bk: e.tensor_tensor(out=tmpA[2][0:8, 0:256], in0=banks[bk][0:8, 0:256], in1=bkvB[0:8, 256:512], op=ALU.add),
                      reads=[("bk", bk), "bkvB"], writes=[("tmpA", 2)])
                    vpad_from(tmpA[2][0:8, 0:256], ("tmpA", 2), VsP[j2], ("VsP", j2), None, npart=8)
                    rs = 1154 + s * 8
                    attn_block(s * 8, 8, 1,
                               lambda gp, hs, j2=j2: KcT[hs, j2, gp, :], ("KcT", j2),
                               lambda gp, hs, rs=rs: KT[hs, gp, rs:rs + 8], "KT", 8,
                               VPc[j2], ("VPc", j2), (OP1, "OP1"), VsP[j2], ("VsP", j2), (OP1, "OP1"))
            bsum = nb(); bsq = nb()
            for k in range(16):
                A("pe", lambda e, k=k: e.matmul(banks[bsum][:, 0:W], lhsT=ones_f[:, :], rhs=cT[:, k, 0:W], start=(k == 0), stop=(k == 15)),
                  reads=["ones_f", ("cT", k)], writes=[("bk", bsum)])
                zi = k % 2
                A("act", lambda e, k=k, zi=zi: e.activation(out=lnz[zi][:, 0:W], in_=cT[:, k, 0:W], func=AF.Square), reads=[("cT", k)], writes=[("lnz", zi)])
                A("pe", lambda e, k=k, zi=zi: e.matmul(banks[bsq][:, 0:W], lhsT=ones_f[:, :], rhs=lnz[zi][:, 0:W], start=(k == 0), stop=(k == 15)),
                  reads=["ones_f", ("lnz", zi)], writes=[("bk", bsq)])
            A("dve", lambda e: e.tensor_scalar(out=lnm[:, 0:W], in0=banks[bsum][:, 0:W], scalar1=1.0 / D, scalar2=None, op0=ALU.mult),
              reads=[("bk", bsum)], writes=["lnm"])
            A("dve", lambda e: e.tensor_tensor(out=lnt[:, 0:W], in0=lnm[:, 0:W], in1=lnm[:, 0:W], op=ALU.mult), reads=["lnm"], writes=["lnt"])
            A("dve", lambda e: e.scalar_tensor_tensor(out=lnr[:, 0:W], in0=banks[bsq][:, 0:W], scalar=1.0 / D, in1=lnt[:, 0:W],
                                                      op0=ALU.mult, op1=ALU.subtract), reads=[("bk", bsq), "lnt"], writes=["lnr"])
            A("act", lambda e: e.activation(out=lnr[:, 0:W], in_=lnr[:, 0:W], func=AF.Sqrt, bias=epsT[:, 0:1]), reads=["lnr", "epsT"], writes=["lnr"])
            A("dve", lambda e: e.reciprocal(out=lnr[:, 0:W], in_=lnr[:, 0:W]), reads=["lnr"], writes=["lnr"])
            for k in range(16):
                zi = k % 2
                A("dve", lambda e, k=k, zi=zi: e.tensor_tensor(out=lnz[zi][:, 0:W], in0=cT[:, k, 0:W], in1=lnm[:, 0:W], op=ALU.subtract),
                  reads=[("cT", k), "lnm"], writes=[("lnz", zi)])
                A("dve", lambda e, k=k, zi=zi: e.tensor_tensor(out=lnz[zi][:, 0:W], in0=lnz[zi][:, 0:W], in1=lnr[:, 0:W], op=ALU.mult),
                  reads=[("lnz", zi), "lnr"], writes=[("lnz", zi)])
                A("act", lambda e, k=k, zi=zi: e.activation(out=sT[:, k, 0:W], in_=lnz[zi][:, 0:W], func=AF.Silu, scale=ln_g[:, k:k + 1], bias=ln_b[:, k:k + 1]),
                  reads=[("lnz", zi), "ln_g", "ln_b"], writes=["sT"])
            for cg in range(4):
                def ev_co(i, bk):
                    A("act", lambda e: e.copy(out=CO[:, i, 0:W], in_=banks[bk][:, 0:W]), reads=[("bk", bk)], writes=["CO"])
                fm_group(w_cp_v, cg * 512, [0, 1, 2, 3], lambda kk: sT[:, kk, 0:W], W, ["sT"], ev_co)

                def ev_gc(i, bk, cg=cg):
                    ch = 52 + cg * 4 + i
                    zi = i % 2
                    A("act", lambda e: e.activation(out=sgt[zi][:, 0:W], in_=banks[bk][:, 0:W], func=AF.Sigmoid, bias=b_in_T[:, ch:ch + 1]),
                      reads=[("bk", bk), "b_in_T"], writes=[("sgt", zi)])
                    A("dve", lambda e: e.tensor_tensor(out=T1[:, i, 0:W], in0=sgt[zi][:, 0:W], in1=CO[:, i, 0:W], op=ALU.mult),
                      reads=[("sgt", zi), "CO"], writes=["T1"])
                fm_group(w_in_v, (13 + cg) * 512, [0, 1, 2, 3], lambda kk: xnT[:, kk, mo:mo + W], W, ["xnT"], ev_gc)
                fm_group(w_ap_v, cg * 512, [0, 1, 2, 3], lambda kk: oT[:, kk, 0:W], W, ["oT"], ev_co)

                def ev_ga(i, bk, cg=cg):
                    ch = 68 + cg * 4 + i
                    zi = i % 2
                    A("act", lambda e: e.activation(out=sgt[zi][:, 0:W], in_=banks[bk][:, 0:W], func=AF.Sigmoid, bias=b_in_T[:, ch:ch + 1]),
                      reads=[("bk", bk), "b_in_T"], writes=[("sgt", zi)])
                    A("dve", lambda e: e.tensor_tensor(out=sgt[zi][:, 0:W], in0=sgt[zi][:, 0:W], in1=CO[:, i, 0:W], op=ALU.mult),
                      reads=[("sgt", zi), "CO"], writes=[("sgt", zi)])
                    A("dve", lambda e: e.tensor_tensor(out=mixT[:, cg * 4 + i, 0:W], in0=sgt[zi][:, 0:W], in1=T1[:, i, 0:W], op=ALU.add),
                      reads=[("sgt", zi), "T1"], writes=["mixT"])
                fm_group(w_in_v, (17 + cg) * 512, [0, 1, 2, 3], lambda kk: xnT[:, kk, mo:mo + W], W, ["xnT"], ev_ga)
            gload(GB0, "GB0", g_post_d)
            for cg in range(4):
                s0 = wload(w_o_v[:, 0:8, cg * 512:(cg + 1) * 512]); s1 = wload(w_o_v[:, 8:16, cg * 512:(cg + 1) * 512])
                col = 0
                for bi, (r0_, n) in enumerate(blocks):
                    bk = nb()
                    for h, s in enumerate((s0, s1)):
                        for k in range(8):
                            kk = 8 * h + k
                            A("pe", lambda e, s=s, k=k, kk=kk, bk=bk, col=col, n=n: e.matmul(
                                banks[bk][0:n, :], lhsT=mixT[:, kk, col:col + n], rhs=ring[s][:, k, :], start=(kk == 0), stop=(kk == 15)),
                              reads=[("w", s), "mixT"], writes=[("bk", bk)])
                    ms_i = 1 if n == 2 else 0
                    A("act", lambda e, bk=bk, cg=cg, ms_i=ms_i, n=n: e.copy(out=MS[ms_i][0:n, cg * 512:(cg + 1) * 512], in_=banks[bk][0:n, :]),
                      reads=[("bk", bk)], writes=[("MS", ms_i)])
                    col += n
            col = 0
            for bi, (r0_, n) in enumerate(blocks):
                ms_i = 1 if n == 2 else 0
                xb, xr = xbufs[bi]
                rmsnorm_stats(MS[ms_i][0:n, :], n, 1, [("MS", ms_i)])
                A("dve", lambda e, ms_i=ms_i, n=n: e.scalar_tensor_tensor(out=MS[ms_i][0:n, :], in0=MS[ms_i][0:n, :], scalar=rstd_t[0:n, 1:2], in1=GB0[0:n, :],
                                                                        op0=ALU.mult, op1=ALU.mult), reads=[("MS", ms_i), ("rstd", 1), "GB0"], writes=[("MS", ms_i)])
                A("dve", lambda e, ms_i=ms_i, n=n, xb=xb: e.tensor_tensor(out=xb[0:n, :], in0=xb[0:n, :], in1=MS[ms_i][0:n, :], op=ALU.add),
                  reads=[xr, ("MS", ms_i)], writes=[xr])
                col += n
            gload(GB0, "GB0", g_fpre_d)
            col = 0
            for bi, (r0_, n) in enumerate(blocks):
                xb, xr = xbufs[bi]
                rmsnorm_stats(xb[0:n, :], n, 2, [xr])
                A("dve", lambda e, n=n, xb=xb: e.scalar_tensor_tensor(out=XS[0:n, :], in0=xb[0:n, :], scalar=rstd_t[0:n, 2:3], in1=GB0[0:n, :],
                                                                    op0=ALU.mult, op1=ALU.mult), reads=[xr, ("rstd", 2), "GB0"], writes=["XS"])
                transpose_to(hnT, "hnT", col, n, "XS")
                col += n
            for g in range(32):
                u_pair = {}

                def ev_up(i, bk, g=g):
                    u_pair[i] = bk
                    if i % 2 == 0:
                        return
                    j = g * 2 + i // 2
                    res = []
                    for t2, bkx in enumerate((u_pair[i - 1], bk)):
                        chn = 2 * j + t2
                        ui = cnt["u"] % 4; cnt["u"] += 1
                        Ut, Ur = Ub[ui], ("Ub", ui)
                        A("act", lambda e, Ut=Ut, bkx=bkx: e.copy(out=Ut[:, 2:2 + W], in_=banks[bkx][:, 0:W]), reads=[("bk", bkx)], writes=[Ur])
                        cvt, cvr = cvb[ui], ("cvb", ui)
                        if is_s:
                            i4 = i // 2 * 2 + t2
                            U3 = UST[:, i4, :, :]
                            A("dve", lambda e, Ut=Ut, U3=U3: e.tensor_copy(out=U3[:, :, 2:10], in_=Ut[:, 2:130].rearrange("p (s t) -> p s t", t=8)),
                              reads=[Ur, ("USTh", g)], writes=[("UST", i4)])
                            c3 = cvt[:, 0:128].rearrange("p (s t) -> p s t", t=8)
                            A("dve", lambda e, U3=U3, c3=c3, chn=chn: e.tensor_scalar(out=c3, in0=U3[:, :, 0:8], scalar1=ffn_k[:, chn, 0:1], scalar2=ffn_b[:, chn:chn + 1],
                                                                                   op0=ALU.mult, op1=ALU.add), reads=[("UST", i4), "ffn_k", "ffn_b"], writes=[cvr])
                            for tp in (1, 2):
                                A("dve", lambda e, U3=U3, c3=c3, chn=chn, tp=tp: e.scalar_tensor_tensor(out=c3, in0=U3[:, :, tp:tp + 8], scalar=ffn_k[:, chn, tp:tp + 1], in1=c3,
                                                                                                      op0=ALU.mult, op1=ALU.add), reads=[("UST", i4), "ffn_k", cvr], writes=[cvr])
                            A("act", lambda e, Ut=Ut, i4=i4: e.copy(out=UO[:, i4, 2:34].rearrange("p (s t) -> p s t", t=2),
                                                                   in_=Ut[:, 2:130].rearrange("p (s t) -> p s t", t=8)[:, :, 6:8]), reads=[Ur], writes=[("UO", i4)])
                        else:
                            if ti == 0:
                                A("dve", lambda e, Ut=Ut: e.tensor_scalar(out=Ut[:, 2:4], in0=Ut[:, 2:4], scalar1=kmask[:, 0:1], scalar2=None, op0=ALU.mult),
                                  reads=[Ur, "kmask"], writes=[Ur])
                                A("dve", lambda e, Ut=Ut: e.memset(Ut[:, 0:2], 0.0), writes=[Ur])
                            else:
                                A("dve", lambda e, Ut=Ut, chn=chn: e.tensor_copy(out=Ut[:, 0:2], in_=uhist[:, chn, :]), reads=[("uhist", chn)], writes=[Ur])
                            A("dve", lambda e, Ut=Ut, chn=chn: e.tensor_copy(out=uhist[:, chn, :], in_=Ut[:, W:W + 2]), reads=[Ur], writes=[("uhist", chn)])
                            if ti == LASTP:
                                i4 = i // 2 * 2 + t2
                                A("act", lambda e, Ut=Ut, i4=i4: e.copy(out=UO[:, i4, 0:2], in_=Ut[:, W:W + 2]), reads=[Ur], writes=[("UO", i4)])
                            A("dve", lambda e, Ut=Ut, cvt=cvt, chn=chn: e.tensor_scalar(out=cvt[:, 0:W], in0=Ut[:, 0:W], scalar1=ffn_k[:, chn, 0:1], scalar2=ffn_b[:, chn:chn + 1],
                                                                                     op0=ALU.mult, op1=ALU.add), reads=[Ur, "ffn_k", "ffn_b"], writes=[cvr])
                            for tp in (1, 2):
                                A("dve", lambda e, Ut=Ut, cvt=cvt, chn=chn, tp=tp: e.scalar_tensor_tensor(out=cvt[:, 0:W], in0=Ut[:, tp:tp + W], scalar=ffn_k[:, chn, tp:tp + 1], in1=cvt[:, 0:W],
                                                                                                        op0=ALU.mult, op1=ALU.add), reads=[Ur, "ffn_k", cvr], writes=[cvr])
                        res.append((cvt, cvr))
                    (cgt, cgr), (cvt, cvr) = res
                    gi = j % 2
                    A("act", lambda e: e.activation(out=geb[gi][:, 0:W], in_=cgt[:, 0:W], func=AF.Gelu_apprx_tanh), reads=[cgr], writes=[("geb", gi)])
                    A("dve", lambda e: e.tensor_tensor(out=actT[:, j, 0:W], in0=geb[gi][:, 0:W], in1=cvt[:, 0:W], op=ALU.mult),
                      reads=[("geb", gi), cvr], writes=["actT"])

                if is_s:
                    sj = 0
                    spq(lambda e, sj=sj, g=g: e.dma_start(out=sfb[sj][:, :], in_=sf_d.rearrange("s r c -> (s r) c")[:, g * 512:(g + 1) * 512]), writes=[("sfb", sj)])
                    for i4 in range(4):
                        bt = nb()
                        A("pe", lambda e, bt=bt, i4=i4, sj=sj: e.transpose(out=banks[bt][:, 0:32], in_=sfb[sj][0:32, i4 * 128:(i4 + 1) * 128], identity=ident_f[0:32, 0:32]),
                          reads=[("sfb", sj), "ident_f"], writes=[("bk", bt)])
                        A("act", lambda e, bt=bt, i4=i4: e.copy(out=UST[:, i4, :, 0:2], in_=banks[bt][:, 0:32].rearrange("p (s r) -> p s r", r=2)),
                          reads=[("bk", bt)], writes=[("UST", i4), ("USTh", g)])
                fm_group(w_up_v, g * 512, [0, 1, 2, 3], lambda kk: hnT[:, kk, 0:W], W, ["hnT"], ev_up)
                if ti == LASTP or is_s:
                    c_lo, c_n = (0, 2) if ti == LASTP else (2, 32)
                    oi = 0
                    for i4 in range(4):
                        bt = nb()
                        A("pe", lambda e, bt=bt, i4=i4: e.transpose(out=banks[bt][0:c_n, 0:128], in_=UO[:, i4, c_lo:c_lo + c_n], identity=ident_f[:, :]),
                          reads=[("UO", i4), "ident_f"], writes=[("bk", bt)])
                        A("act", lambda e, bt=bt, i4=i4, oi=oi: e.copy(out=uost[oi][0:c_n, i4 * 128:(i4 + 1) * 128], in_=banks[bt][0:c_n, 0:128]),
                          reads=[("bk", bt)], writes=[("uost", oi)])
                    if ti == LASTP:
                        out_dma(lambda e, oi=oi, g=g: e.dma_start(out=ffnp_d[:, g * 512:(g + 1) * 512], in_=uost[oi][0:2, :]), [("uost", oi)], "o_ffnp%d" % oi)
                    else:
                        out_dma(lambda e, oi=oi, g=g: e.dma_start(out=ffns_d[:, g * 512:(g + 1) * 512], in_=uost[oi][0:32, :]), [("uost", oi)], "o_ffns%d" % oi)
            gload(GB0, "GB0", g_fpost_d)
            oblocks = [(bi, r0_, n) for bi, (r0_, n) in enumerate(blocks) if n == 128]
            for cg in range(4):
                bkm = {bi: nb() for (bi, _, _) in oblocks}
                for kq in range(8):
                    s = wload(w_dn_v[:, kq * 8:(kq + 1) * 8, cg * 512:(cg + 1) * 512])
                    col = 0
                    for bi, (r0_, n) in enumerate(blocks):
                        if n == 128:
                            for k in range(8):
                                kk = kq * 8 + k
                                A("pe", lambda e, s=s, k=k, kk=kk, bi=bi, col=col, bkm=bkm: e.matmul(
                                    banks[bkm[bi]][:, :], lhsT=actT[:, kk, col:col + 128], rhs=ring[s][:, k, :], start=(kk == 0), stop=(kk == 63)),
                                  reads=[("w", s), "actT"], writes=[("bk", bkm[bi])])
                        col += n
                for (bi, r0_, n) in oblocks:
                    ms_i = 0
                    A("act", lambda e, bi=bi, ms_i=ms_i, cg=cg, bkm=bkm: e.copy(out=MS[ms_i][:, cg * 512:(cg + 1) * 512], in_=banks[bkm[bi]][:, :]),
                      reads=[("bk", bkm[bi])], writes=[("MS", ms_i)])
            for (bi, r0_, n) in oblocks:
                ms_i = 0
                xb, xr = xbufs[bi]
                rmsnorm_stats(MS[ms_i][:, :], 128, 3, [("MS", ms_i)])
                A("dve", lambda e, ms_i=ms_i: e.scalar_tensor_tensor(out=MS[ms_i][:, :], in0=MS[ms_i][:, :], scalar=rstd_t[:, 3:4], in1=GB0[:, :],
                                                                   op0=ALU.mult, op1=ALU.mult), reads=[("MS", ms_i), ("rstd", 3), "GB0"], writes=[("MS", ms_i)])
                A("dve", lambda e, ms_i=ms_i, xb=xb: e.tensor_tensor(out=MS[ms_i][:, :], in0=MS[ms_i][:, :], in1=xb[:, :], op=ALU.add),
                  reads=[("MS", ms_i), xr], writes=[("MS", ms_i)])
                yrow = r0_ - 130
                out_dma(lambda e, ms_i=ms_i, yrow=yrow: e.dma_start(out=y_d[yrow:yrow + 128, :], in_=MS[ms_i][:, :]), [("MS", ms_i)], "o_y%d" % ms_i)

        for ti_, (kind_, blocks_) in enumerate(TILES):
            process_tile(ti_, kind_, blocks_)

        S.finalize()
        sems = {}
        for eng in S.ENGS:
            sems[("eng", eng)] = es.enter_context(nc.semaphore("s_" + eng))
        for gname in S.dma_final:
            sems[("dma", gname)] = es.enter_context(nc.semaphore("d_" + gname))
        with nc.Block() as block:
            @block.tensor
            def _(e): S.emit("pe", e, sems)

            @block.scalar
            def _(e): S.emit("act", e, sems)

            @block.vector
            def _(e): S.emit("dve", e, sems)

            @block.gpsimd
            def _(e): S.emit("pool", e, sems)

            @block.sync
            def _(e):
                S.emit("sp", e, sems)
                for gname in out_dma_groups:
                    e.wait_ge(sems[("dma", gname)], S.dma_final[gname])
    return nc


_NC_CACHE = {}


def kernel(x_prompt, x_sample, cache_k, cache_v, state_conv, state_ffn_conv,
           norm_mix_pre, w_in, b_in, conv_dw_k, conv_dw_b, conv_ln_g, conv_ln_b, w_conv_proj,
           attn_sinks, rel_bias, w_attn_proj, w_out, norm_mix_post,
           norm_ffn_pre, w_up, ffn_dw_k, ffn_dw_b, w_down, norm_ffn_post):
    f = np.float32
    x_prompt = np.asarray(x_prompt, f); x_sample = np.asarray(x_sample, f)
    wp = win_perm()
    w_in_p = np.ascontiguousarray(np.asarray(w_in, f)[0][:, wp])
    b_in_p = np.asarray(b_in, f)[0][wp]
    b_in_T = fmT(b_in_p, 84)
    b_kv = np.ascontiguousarray(np.asarray(b_in, f)[0][6144:6656])
    arp = attn_row_perm()
    w_ap = np.ascontiguousarray(np.asarray(w_attn_proj, f)[0][arp, :])
    up = wup_perm()
    w_up_p = np.ascontiguousarray(np.asarray(w_up, f)[0][:, up])
    ffn_k_T = np.ascontiguousarray(np.asarray(ffn_dw_k, f)[0][:, up].reshape(3, 128, 128).transpose(2, 1, 0))
    ffn_b_T = fmT(np.asarray(ffn_dw_b, f)[0][up], 128)
    conv_k_T = np.ascontiguousarray(np.asarray(conv_dw_k, f)[0].reshape(31, 16, 128).transpose(2, 1, 0))
    sinks = np.asarray(attn_sinks, f)[0]
    sinkT = np.zeros((128, 16), f)
    rbp = np.zeros((32, 32), f)
    rb = np.asarray(rel_bias, f)
    for cp in range(16):
        for half in range(2):
            h = head_of(half, cp)
            sinkT[half * 64:half * 64 + 64, cp] = sinks[h]
            rbp[:, half * 16 + cp] = rb[:, h]
    bucket = rel_bucket_np(np.arange(128))
    ohb = np.zeros((32, 128), f); ohb[bucket, np.arange(128)] = 1.0
    jm = np.zeros((128, 384), f)
    for dlt in range(128):
        jm[dlt, 255 - dlt] = 1.0
    mneg = np.zeros((128, 2, 129), f)
    s_idx = np.arange(128)[:, None]; r_idx = np.arange(129)[None, :]
    mneg[:, 0, :] = np.where(s_idx >= r_idx, 0.0, NEG)
    mneg[:, 1, :] = np.where(s_idx <= r_idx - 1, 0.0, NEG)

    shared = {
        "w_in": w_in_p, "b_in_T": b_in_T, "b_kv": b_kv,
        "conv_k_T": conv_k_T, "conv_b_T": fmT(np.asarray(conv_dw_b, f)[0], 16),
        "ln_g_T": fmT(np.asarray(conv_ln_g, f)[0], 16), "ln_b_T": fmT(np.asarray(conv_ln_b, f)[0], 16),
        "w_cp": np.ascontiguousarray(np.asarray(w_conv_proj, f)[0]), "w_ap": w_ap, "w_o": np.ascontiguousarray(np.asarray(w_out, f)[0]),
        "g_pre": np.ascontiguousarray(np.asarray(norm_mix_pre, f)[0]), "g_post": np.ascontiguousarray(np.asarray(norm_mix_post, f)[0]),
        "g_fpre": np.ascontiguousarray(np.asarray(norm_ffn_pre, f)[0]), "g_fpost": np.ascontiguousarray(np.asarray(norm_ffn_post, f)[0]),
        "w_up": w_up_p, "ffn_k_T": ffn_k_T, "ffn_b_T": ffn_b_T, "w_dn": np.ascontiguousarray(np.asarray(w_down, f)[0]),
        "sinkT": sinkT, "rbp": rbp, "ohb": ohb, "jm": jm, "mneg": mneg,
    }
    in_maps = []
    for c in range(NCORES):
        b, qd = c // 4, c % 4
        T0 = 1024 * qd
        xin = np.zeros((NROWS, D), f)
        if qd > 0:
            xin[0:130] = x_prompt[b, T0 - 130:T0]
        xin[130:1154] = x_prompt[b, T0:T0 + 1024]
        xin[1154:1282] = x_sample[16 * c:16 * c + 16].reshape(128, D)
        km = np.ones((128, 2), f)
        if qd == 0:
            km[:, 0] = 0.0
            km[0:2, 1] = 0.0
        m = dict(shared)
        m["xin"] = xin; m["kmask"] = km
        m["ck"] = np.ascontiguousarray(np.asarray(cache_k, f)[0, 16 * c:16 * c + 16].reshape(16, 128, 256))
        m["cv"] = np.ascontiguousarray(np.asarray(cache_v, f)[0, 16 * c:16 * c + 16].reshape(16, 128, 256))
        m["sc"] = np.ascontiguousarray(np.asarray(state_conv, f)[0, 16 * c:16 * c + 16])
        m["sf"] = np.ascontiguousarray(np.asarray(state_ffn_conv, f)[0, 16 * c:16 * c + 16][:, :, up])
        in_maps.append(m)

    if "nc" not in _NC_CACHE:
        _NC_CACHE["nc"] = build()
    nc = _NC_CACHE["nc"]
    res = run_bass_kernel_spmd(nc, in_maps, core_ids=list(range(NCORES)))
    R = res.results
    inv_up = np.argsort(up)
    y_p = np.zeros((2, 4096, D), f); y_s = np.zeros((128, 8, D), f)
    k_p = np.zeros((1, 2, 128, 4, 64), f); v_p = np.zeros((1, 2, 128, 4, 64), f)
    conv_p = np.zeros((1, 2, 30, D), f); ffn_p = np.zeros((1, 2, 2, DFF2), f)
    k_s = np.zeros((1, 128, 128, 4, 64), f); v_s = np.zeros((1, 128, 128, 4, 64), f)
    conv_s = np.zeros((1, 128, 30, D), f); ffn_s = np.zeros((1, 128, 2, DFF2), f)
    for c in range(NCORES):
        b, qd = c // 4, c % 4
        r = R[c]
        y_p[b, 1024 * qd:1024 * qd + 1024] = r["y"][0:1024]
        y_s[16 * c:16 * c + 16] = r["y"][1024:1152].reshape(16, 8, D)
        if qd == 3:
            k_p[0, b] = r["kvp"][:, 0:256].reshape(128, 4, 64)
            v_p[0, b] = r["kvp"][:, 256:512].reshape(128, 4, 64)
            conv_p[0, b] = r["convp"]
            ffn_p[0, b] = r["ffnp"][:, inv_up]
        k_s[0, 16 * c:16 * c + 16] = r["ks"].reshape(16, 128, 4, 64)
        v_s[0, 16 * c:16 * c + 16] = r["vs"].reshape(16, 128, 4, 64)
        conv_s[0, 16 * c:16 * c + 16] = r["convs"]
        ffn_s[0, 16 * c:16 * c + 16] = r["ffns"].reshape(16, 2, DFF2)[:, :, inv_up]
    return (y_p, y_s, k_p, v_p, conv_p, ffn_p, k_s, v_s, conv_s, ffn_s)
```

```python
import contextlib
import math
import numpy as np
import concourse.bass as bass
import concourse.mybir as mybir
from concourse.bass_utils import run_bass_kernel_spmd

F32 = mybir.dt.float32
BF16 = mybir.dt.bfloat16
AF = mybir.ActivationFunctionType
ALU = mybir.AluOpType

D = 2048
NH = 32
DIN = 10752
DFF2 = 16384
EPS = 1e-6
NCORES = 8
NROWS = 1282
NEG = -30000.0


class Op:
    __slots__ = ("eng", "fn", "deps", "signal", "sigval", "dma", "idx")

    def __init__(self, eng, fn, dma=None):
        self.eng = eng; self.fn = fn; self.deps = []; self.signal = False
        self.sigval = 0; self.dma = dma; self.idx = 0


class Sched:
    ENGS = ("pe", "act", "dve", "pool", "sp")

    def __init__(self):
        self.q = {e: [] for e in self.ENGS}
        self.lastw = {}
        self.readers = {}
        self.nops = 0
        self.alias = {}

    def add(self, eng, fn, reads=(), writes=(), dma=None):
        op = Op(eng, fn, dma)
        op.idx = self.nops; self.nops += 1
        writes = list(writes)
        for w in list(writes):
            for a in self.alias.get(w, ()):
                if a not in writes: writes.append(a)
        deps = {}
        for r in reads:
            w = self.lastw.get(r)
            if w is not None: deps[id(w)] = w
        for w in writes:
            lw = self.lastw.get(w)
            if lw is not None: deps[id(lw)] = lw
            for rd in self.readers.get(w, ()):
                deps[id(rd)] = rd
        for d in deps.values():
            if d is op: continue
            if d.dma is None and op.dma is None and d.eng == eng and eng == "pe":
                continue
            op.deps.append(d)
            d.signal = True
        for w in writes:
            self.lastw[w] = op
            self.readers[w] = []
        for r in reads:
            if r in writes: continue
            self.readers.setdefault(r, []).append(op)
        self.q[eng].append(op)
        return op

    def finalize(self):
        for e in self.ENGS:
            cnt = 0
            for op in self.q[e]:
                if op.dma is None and op.signal:
                    cnt += 1; op.sigval = cnt
        gc = {}
        allops = sorted([op for e in self.ENGS for op in self.q[e] if op.dma is not None], key=lambda o: o.idx)
        for op in allops:
            gc[op.dma] = gc.get(op.dma, 0) + 16
            op.sigval = gc[op.dma]
        self.dma_final = gc

    def emit(self, eng, e, sems):
        waited = {}
        for op in self.q[eng]:
            need = {}
            for d in op.deps:
                key = ("dma", d.dma) if d.dma is not None else ("eng", d.eng)
                if need.get(key, 0) < d.sigval: need[key] = d.sigval
            for key, val in need.items():
                if waited.get(key, 0) < val:
                    e.wait_ge(sems[key], val)
                    waited[key] = val
            ins = op.fn(e)
            if op.dma is not None:
                ins.then_inc(sems[("dma", op.dma)], 16)
            elif op.signal:
                ins.then_inc(sems[("eng", eng)], 1)


def head_of(half, cp):
    if cp < 8:
        return cp if half == 0 else 8 + cp
    return 16 + (cp - 8) if half == 0 else 24 + (cp - 8)


def win_perm():
    idx = []
    for c in range(16):
        idx += list(range(c * 128, c * 128 + 128))
        idx += list(range(2048 + c * 128, 2048 + c * 128 + 128))
    for cp in range(16):
        for half in range(2):
            h = head_of(half, cp)
            idx += list(range(4096 + h * 64, 4096 + h * 64 + 64))
    idx += list(range(6144, 6144 + 512))
    idx += list(range(6656, 6656 + 4096))
    return np.array(idx, dtype=np.int64)


def attn_row_perm():
    idx = []
    for cp in range(16):
        for half in range(2):
            h = head_of(half, cp)
            idx += list(range(h * 64, h * 64 + 64))
    return np.array(idx, dtype=np.int64)


def wup_perm():
    idx = []
    for j in range(64):
        idx += list(range(j * 128, j * 128 + 128))
        idx += list(range(8192 + j * 128, 8192 + j * 128 + 128))
    return np.array(idx, dtype=np.int64)


def rel_bucket_np(dist):
    max_exact = 16
    d = np.maximum(dist, 1).astype(np.float32)
    large = max_exact + (np.log(d / max_exact) / math.log(128 / max_exact) * (32 - max_exact)).astype(np.int32)
    large = np.minimum(large, 31)
    return np.where(dist < max_exact, dist, large)


def fmT(v, nch):
    return np.ascontiguousarray(v.reshape(nch, 128).T)


TILES = [("p", [(128, 2), (130, 128)]), ("p", [(258, 128), (386, 128)]), ("p", [(514, 128), (642, 128)]),
         ("p", [(770, 128), (898, 128)]), ("p", [(1026, 128)]), ("s", [(1154, 128)])]
LASTP = 4
WMAX = 258
EXT = 128


def build():
    nc = bass.Bass("TRN2", target_bir_lowering=False)

    def din(name, shape, dt=F32):
        return nc.dram_tensor(name, list(shape), dt, kind="ExternalInput").ap()

    def dout(name, shape, dt=F32):
        return nc.dram_tensor(name, list(shape), dt, kind="ExternalOutput").ap()

    xin = din("xin", [NROWS, D])
    kmask_d = din("kmask", [128, 2])
    ck_d = din("ck", [16, 128, 256]); cv_d = din("cv", [16, 128, 256])
    sc_d = din("sc", [16, 30, D]); sf_d = din("sf", [16, 2, DFF2])
    w_in_d = din("w_in", [D, DIN]); b_in_T_d = din("b_in_T", [128, 84]); b_kv_d = din("b_kv", [512])
    conv_k_d = din("conv_k_T", [128, 16, 31]); conv_b_d = din("conv_b_T", [128, 16])
    ln_g_d = din("ln_g_T", [128, 16]); ln_b_d = din("ln_b_T", [128, 16])
    w_cp_d = din("w_cp", [D, D]); w_ap_d = din("w_ap", [D, D]); w_o_d = din("w_o", [D, D])
    g_pre_d = din("g_pre", [D]); g_post_d = din("g_post", [D]); g_fpre_d = din("g_fpre", [D]); g_fpost_d = din("g_fpost", [D])
    w_up_d = din("w_up", [D, DFF2]); ffn_k_d = din("ffn_k_T", [128, 128, 3]); ffn_b_d = din("ffn_b_T", [128, 128])
    w_dn_d = din("w_dn", [8192, D])
    sink_d = din("sinkT", [128, 16]); rbp_d = din("rbp", [32, 32])
    ohb_d = din("ohb", [32, 128]); jm_d = din("jm", [128, 384]); mneg_d = din("mneg", [128, 2, 129])

    y_d = dout("y", [1152, D])
    kvp_d = dout("kvp", [128, 512])
    convp_d = dout("convp", [30, D])
    ffnp_d = dout("ffnp", [2, DFF2])
    ks_d = dout("ks", [16, 128, 256]); vs_d = dout("vs", [16, 128, 256])
    convs_d = dout("convs", [16, 30, D])
    ffns_d = dout("ffns", [32, DFF2])

    S = Sched()
    out_dma_groups = []

    with contextlib.ExitStack() as es:
        def sb(name, shape, dt=F32):
            return es.enter_context(nc.sbuf_tensor(name, list(shape), dt))

        def ps(name, shape, dt=F32):
            return es.enter_context(nc.psum_tensor(name, list(shape), dt))

        A = S.add
        NBK = 6
        banks = [ps("bk%d" % i, [128, 512], F32) for i in range(NBK)]
        tbanks = [ps("tb%d" % i, [128, 1024], BF16) for i in range(2)]
        st = {"bk": 0, "tb": 0, "ws": 0, "dq": 0}

        def nb():
            i = st["bk"]; st["bk"] = (i + 1) % NBK
            return i

        def ntb():
            i = st["tb"]; st["tb"] = (i + 1) % 2
            return i

        NSLOT = 3
        ring = [sb("ring%d" % i, [128, 8, 512], BF16) for i in range(NSLOT)]

        wscr = nc.dram_tensor("wscr", [162, 128, 4096], BF16).ap()
        st["wi"] = 0
        st["tile"] = 0

        def wload(src):
            s = st["ws"]; st["ws"] = (s + 1) % NSLOT
            idx = st["wi"]; st["wi"] += 1
            if st["tile"] == 0:
                A("pool", lambda e, s=s, src=src: e.dma_start(out=ring[s][:], in_=src), writes=[("w", s)], dma="w%d" % s)
                A("sp", lambda e, s=s, idx=idx: e.dma_start(out=wscr[idx], in_=ring[s][:].rearrange("p k c -> p (k c)")),
                  reads=[("w", s)], writes=[("wscr", idx)], dma="wb%d" % s)
            else:
                A("sp", lambda e, s=s, idx=idx: e.dma_start(out=ring[s][:].rearrange("p k c -> p (k c)"), in_=wscr[idx]),
                  reads=[("wscr", idx)], writes=[("w", s)], dma="w%d" % s)
            return s

        w_in_v = w_in_d.rearrange("(k p) c -> p k c", p=128)
        w_cp_v = w_cp_d.rearrange("(k p) c -> p k c", p=128)
        w_ap_v = w_ap_d.rearrange("(k p) c -> p k c", p=128)
        w_o_v = w_o_d.rearrange("(k p) c -> p k c", p=128)
        w_up_v = w_up_d.rearrange("(k p) c -> p k c", p=128)
        w_dn_v = w_dn_d.rearrange("(k p) c -> p k c", p=128)

        def spq(fn, reads=(), writes=(), grp=None):
            if grp is None:
                grp = "q%d" % st["dq"]; st["dq"] = (st["dq"] + 1) % 6
            return A("sp", fn, reads=reads, writes=writes, dma=grp)

        ident_f = sb("ident_f", [128, 128]); ident_b = sb("ident_b", [128, 128], BF16)
        ones_f = sb("ones_f", [128, 128])
        epsT = sb("epsT", [128, 1])
        kmask = sb("kmaskS", [128, 2])
        b_in_T = sb("b_in_TS", [128, 84]); bq8 = sb("bq8", [128, 16])
        bkvB = sb("bkvB", [128, 512])
        conv_k = sb("conv_kS", [128, 16, 31]); conv_b = sb("conv_bS", [128, 16])
        ln_g = sb("ln_gS", [128, 16]); ln_b = sb("ln_bS", [128, 16])
        ffn_k = sb("ffn_kS", [128, 128, 3]); ffn_b = sb("ffn_bS", [128, 128])
        esink = sb("esink", [128, 16])
        GB0 = sb("GB0", [128, D])
        TBL = sb("TBL", [128, 2, 32, 129], BF16)
        OP1 = sb("OP1", [128, 2, 128], BF16); OPF = sb("OPF", [128, 2, 128], BF16); OPB = sb("OPB", [128, 2, 128], BF16)

        A("dve", lambda e: e.memset(ones_f[:], 1.0), writes=["ones_f"])
        A("dve", lambda e: e.memset(epsT[:], EPS), writes=["epsT"])
        A("pool", lambda e: e.memset(ident_f[:], 0.0), writes=["ident_f"])
        A("pool", lambda e: e.affine_select(out=ident_f[:], in_=ident_f[:], pattern=[[-1, 128]], compare_op=ALU.not_equal,
                                            fill=1.0, base=0, channel_multiplier=1), reads=["ident_f"], writes=["ident_f"])
        A("dve", lambda e: e.tensor_copy(out=ident_b[:], in_=ident_f[:]), reads=["ident_f"], writes=["ident_b"])
        for (dst, src, nm) in ((kmask, kmask_d, "kmask"), (b_in_T, b_in_T_d, "b_in_T"), (conv_k, conv_k_d, "conv_k"),
                               (conv_b, conv_b_d, "conv_b"), (ln_g, ln_g_d, "ln_g"), (ln_b, ln_b_d, "ln_b"),
                               (ffn_k, ffn_k_d, "ffn_k"), (ffn_b, ffn_b_d, "ffn_b"), (esink, sink_d, "esink")):
            spq(lambda e, dst=dst, src=src: e.dma_start(out=dst[:], in_=src), writes=[nm])
        spq(lambda e: e.dma_start(out=bkvB[:], in_=b_kv_d.partition_broadcast(128)), writes=["bkvB"])
        A("act", lambda e: e.activation(out=esink[:], in_=esink[:], func=AF.Exp), reads=["esink"], writes=["esink"])
        A("dve", lambda e: e.tensor_scalar(out=bq8[:], in0=b_in_T[:, 32:48], scalar1=0.125, scalar2=None, op0=ALU.mult),
          reads=["b_in_T"], writes=["bq8"])
        for (T, nm, col) in ((OP1, "OP1", None), (OPF, "OPF", 0), (OPB, "OPB", 1)):
            A("dve", lambda e, T=T: e.memset(T[:], 0.0), writes=[nm])
            for half in range(2):
                if col is None:
                    A("dve", lambda e, T=T, half=half: e.memset(T[:, half, half * 64:half * 64 + 64], 1.0), reads=[], writes=[nm])
                else:
                    A("dve", lambda e, T=T, half=half, col=col: e.tensor_scalar(
                        out=T[:, half, half * 64:half * 64 + 64], in0=ones_f[:, 0:64], scalar1=kmask[:, col:col + 1], scalar2=None,
                        op0=ALU.mult), reads=["ones_f", "kmask"], writes=[nm])

        rbp = sb("rbpS", [32, 32]); ohb = sb("ohbS", [32, 128]); jm = sb("jmS", [128, 384]); mneg = sb("mnegS", [128, 2, 129])
        vec = sb("vecS", [128, 32])
        for (dst, src, nm) in ((rbp, rbp_d, "rbp"), (ohb, ohb_d, "ohb"), (jm, jm_d, "jm"), (mneg, mneg_d, "mneg")):
            spq(lambda e, dst=dst, src=src: e.dma_start(out=dst[:], in_=src), writes=[nm])
        b0 = nb()
        A("pe", lambda e, b0=b0: e.matmul(banks[b0][:, 0:32], lhsT=ohb[:, :], rhs=rbp[:, :], start=True, stop=True),
          reads=["ohb", "rbp"], writes=[("bk", b0)])
        A("dve", lambda e, b0=b0: e.tensor_copy(out=vec[:], in_=banks[b0][:, 0:32]), reads=[("bk", b0)], writes=["vec"])
        for kc in range(2):
            for r0 in range(0, 129, 16):
                nr = min(16, 129 - r0)
                bk = nb()
                for rr in range(nr):
                    r = r0 + rr
                    start_col = (128 - r) if kc == 0 else (256 - r)
                    A("pe", lambda e, bk=bk, rr=rr, sc_=start_col: e.matmul(
                        banks[bk][:, rr * 32:(rr + 1) * 32], lhsT=jm[:, sc_:sc_ + 128], rhs=vec[:, :], start=True, stop=True),
                      reads=["jm", "vec"], writes=[("bk", bk)])
                A("dve", lambda e, bk=bk, kc=kc, r0=r0, nr=nr: e.tensor_tensor(
                    out=TBL[:, kc, :, r0:r0 + nr].rearrange("p h r -> p r h"),
                    in0=banks[bk][:, 0:nr * 32].rearrange("p (r h) -> p r h", h=32),
                    in1=mneg[:, kc, r0:r0 + nr].unsqueeze(2).to_broadcast([128, nr, 32]), op=ALU.add),
                  reads=[("bk", bk), "mneg"], writes=["TBL"])

        KT = sb("KT", [128, 2, NROWS], BF16)
        xh = [sb("xh%d" % i, [128, D]) for i in range(2)]
        XS = sb("XS", [128, D], BF16)
        ssq = sb("ssq", [128, 8]); rstd_t = sb("rstd_t", [128, 8])
        xnT = sb("xnT", [128, 16, EXT + WMAX], BF16)
        hnT = xnT
        arena = sb("arena", [128, 64, WMAX], BF16)
        actT = arena
        cT = arena[:, 0:32, :].rearrange("p a b -> p (a b)").bitcast(F32).rearrange("p (k w) -> p k w", w=WMAX)
        qT = arena[:, 32:48, :]
        qTf = arena[:, 32:48, :].rearrange("p a b -> p (a b)").bitcast(F32)
        oT = arena[:, 48:64, :]
        SM = sb("SM", [128, 32, WMAX], BF16)
        sT = SM[:, 0:16, :]
        mixT = SM[:, 16:32, :]
        SMf = SM[:, :, :].rearrange("p a b -> p (a b)").bitcast(F32)
        MS = [sb("MS%d" % i, [128, D]) for i in range(2)]
        Gb = [sb("Gb%d" % i, [128, 32 + WMAX]) for i in range(6)]
        ghist = sb("ghist", [128, 16, 32])
        uhist = sb("uhist", [128, 128, 2])
        tmpA = [sb("tmpA%d" % i, [128, 512]) for i in range(3)]
        VP = [sb("VP%d" % i, [128, 4, 128], BF16) for i in range(8)]
        Vf = sb("Vf", [128, 512])
        Eb = [sb("Eb%d" % i, [128, 2, 512], BF16) for i in range(2)]
        rden = [sb("rden%d" % i, [128, 512]) for i in range(1)]
        lnm = qTf[:, 0:WMAX]; lnr = qTf[:, WMAX:2 * WMAX]; lnt = qTf[:, 2 * WMAX:3 * WMAX]
        lnz = [qTf[:, (3 + i) * WMAX:(4 + i) * WMAX] for i in range(2)]
        CO = cT[:, 0:4, :]; T1 = cT[:, 4:8, :]; sgt = [cT[:, 8 + i, :] for i in range(2)]
        Ub = [SMf[:, i * (WMAX + 2):(i + 1) * (WMAX + 2)] for i in range(4)]
        _o = 4 * (WMAX + 2)
        cvb = [SMf[:, _o + i * WMAX:_o + (i + 1) * WMAX] for i in range(4)]
        geb = [SMf[:, _o + (4 + i) * WMAX:_o + (5 + i) * WMAX] for i in range(2)]
        S.alias["CO"] = [("cT", k) for k in range(4)]
        S.alias["T1"] = [("cT", k) for k in range(4, 8)]
        for i in range(2):
            S.alias[("sgt", i)] = [("cT", 8 + i)]
            S.alias[("cT", 8 + i)] = [("sgt", i)]
        for k in range(4):
            S.alias[("cT", k)] = ["CO"]
            S.alias[("cT", 4 + k)] = ["T1"]
        lnn = ["lnm", "lnr", "lnt", ("lnz", 0), ("lnz", 1)]
        S.alias["qT"] = list(lnn)
        for nme in lnn:
            S.alias[nme] = ["qT"]
        s5n = [("Ub", i) for i in range(4)] + [("cvb", i) for i in range(4)] + [("geb", i) for i in range(2)]
        S.alias["sT"] = list(s5n); S.alias["mixT"] = list(s5n)
        for nme in s5n:
            S.alias[nme] = ["sT", "mixT"]
        ckb = VP[4][:, :, :].rearrange("p g c -> p (g c)").rearrange("p (s d) -> p s d", s=2)
        KcT = VP[5][:, :, :].rearrange("p (a b) c -> p a b c", a=2)
        cvbuf = VP[6][:, :, :].rearrange("p g c -> p (g c)").rearrange("p (s d) -> p s d", s=2)
        VPc = [VP[0], VP[1]]
        VsP = [VP[2], VP[3]]
        GS = sb("GS", [128, 16, 38]); scb = [tmpA[2][:, :].rearrange("p (a c) -> p a c", a=4)]
        sfb = [tmpA[0]]
        UST = sb("UST", [128, 4, 16, 10])
        UO = sb("UO", [128, 4, 34]); uost = [tmpA[1]]
        GO = sb("GO", [128, 16, 30])
        cnt = {"g": 0, "e": 0, "u": 0, "o": 0}

        print("SBUF bytes remaining after alloc:", nc.sbuf_bytes_remaining)
        def rmsnorm_stats(src_ap, n, col, rd):
            A("dve", lambda e: e.memset(ssq[0:n, col:col + 1], 0.0), writes=[("ssq", col)])
            A("act", lambda e: e.activation(out=XS[0:n, :], in_=src_ap, func=AF.Square, accum_out=ssq[0:n, col:col + 1]),
              reads=rd + [("ssq", col)], writes=["XS", ("ssq", col)])
            A("act", lambda e: e.activation(out=rstd_t[0:n, col:col + 1], in_=ssq[0:n, col:col + 1], func=AF.Sqrt,
                                            scale=1.0 / D, bias=epsT[0:n, 0:1]), reads=[("ssq", col), "epsT"], writes=[("rstd", col)])
            A("dve", lambda e: e.reciprocal(out=rstd_t[0:n, col:col + 1], in_=rstd_t[0:n, col:col + 1]),
              reads=[("rstd", col)], writes=[("rstd", col)])

        def transpose_to(dstT, dst_res, col0, n, src_res):
            for half in range(2):
                tb = ntb()
                for k in range(8):
                    kk = half * 8 + k
                    A("pe", lambda e, tb=tb, k=k, kk=kk: e.transpose(out=tbanks[tb][:, k * 128:k * 128 + n],
                                                                  in_=XS[0:n, kk * 128:(kk + 1) * 128], identity=ident_b[0:n, 0:n]),
                      reads=[src_res, "ident_b"], writes=[("tb", tb)])
                A("act", lambda e, tb=tb, half=half: e.copy(
                    out=dstT[:, half * 8:half * 8 + 8, col0:col0 + n],
                    in_=tbanks[tb][:, :].rearrange("p (k c) -> p k c", c=128)[:, :, 0:n]),
                  reads=[("tb", tb)], writes=[dst_res])

        def gload(GBt, nm, src):
            spq(lambda e: e.dma_start(out=GBt[:], in_=src.partition_broadcast(128)), writes=[nm])

        def stage0_block(row0, n, dst_col, xbuf, xres):
            spq(lambda e: e.dma_start(out=xbuf[0:n, :], in_=xin[row0:row0 + n, :]), writes=[xres])
            rmsnorm_stats(xbuf[0:n, :], n, 0, [xres])
            A("dve", lambda e: e.scalar_tensor_tensor(out=XS[0:n, :], in0=xbuf[0:n, :], scalar=rstd_t[0:n, 0:1], in1=GB0[0:n, :],
                                                      op0=ALU.mult, op1=ALU.mult), reads=[xres, ("rstd", 0), "GB0"], writes=["XS"])
            transpose_to(xnT, "xnT", dst_col, n, "XS")

        def fm_group(wv, c0, chunks, rhs_fn, N, rhs_res, evac):
            s0 = wload(wv[:, 0:8, c0:c0 + 512]); s1 = wload(wv[:, 8:16, c0:c0 + 512])
            bks = {i: nb() for i in chunks}
            for h, s in enumerate((s0, s1)):
                for i in chunks:
                    for k in range(8):
                        kk = 8 * h + k
                        A("pe", lambda e, s=s, i=i, k=k, kk=kk: e.matmul(
                            banks[bks[i]][:, 0:N], lhsT=ring[s][:, k, i * 128:(i + 1) * 128], rhs=rhs_fn(kk),
                            start=(kk == 0), stop=(kk == 15)), reads=[("w", s)] + rhs_res, writes=[("bk", bks[i])])
            for i in chunks:
                evac(i, bks[i])
            return s0, s1

        def make_vpad(win_cols, vp_idx, mask, with_k_out=False, s01=None):
            s0, s1 = s01
            bk = nb()
            c_lo = 0 if with_k_out else 256
            for h, s in enumerate((s0, s1)):
                for k in range(8):
                    kk = 8 * h + k
                    A("pe", lambda e, s=s, k=k, kk=kk: e.matmul(
                        banks[bk][:, c_lo:512], lhsT=xnT[:, kk, win_cols:win_cols + 128], rhs=ring[s][:, k, c_lo:512],
                        start=(kk == 0), stop=(kk == 15)), reads=[("w", s), "xnT"], writes=[("bk", bk)])
            A("dve", lambda e: e.tensor_tensor(out=Vf[:, c_lo:512], in0=banks[bk][:, c_lo:512], in1=bkvB[:, c_lo:512], op=ALU.add),
              reads=[("bk", bk), "bkvB"], writes=["Vf"])
            vpad_from(Vf[:, 256:512], "Vf", VP[vp_idx], ("VP", vp_idx), mask)

        def vpad_from(src, src_res, dstT, dst_res, mask, npart=128):
            A("dve", lambda e: e.memset(dstT[0:npart], 0.0), writes=[dst_res])
            for par in range(2):
                sv = src.rearrange("p (g t d) -> p g t d", g=2, t=2)[:, :, par, :]
                dv = dstT[0:npart].rearrange("p (g t) c -> p g t c", t=2)[:, :, par, par * 64:par * 64 + 64]
                if mask is None:
                    A("dve", lambda e, sv=sv, dv=dv: e.tensor_copy(out=dv, in_=sv), reads=[src_res], writes=[dst_res])
                else:
                    A("dve", lambda e, sv=sv, dv=dv: e.tensor_scalar(out=dv, in0=sv, scalar1=kmask[0:npart, mask:mask + 1], scalar2=None,
                                                                 op0=ALU.mult), reads=[src_res, "kmask"], writes=[dst_res])

        def attn_block(qc0, n, r0, k0_fn, k0_res, k1_fn, k1_res, nk1, vp0, vp0_res, op0, vp1, vp1_res, op1):
            CB = 4 if n > 64 else 8
            def do_gc(gp, cb):
                if True:
                    cp0 = gp * 8 + cb * CB
                    ebs = []
                    for half in range(2):
                        eb = Eb[cnt["e"] % 2]; ebr = ("Eb", cnt["e"] % 2); cnt["e"] += 1
                        ebs.append((eb, ebr))
                        hs = slice(half * 64, half * 64 + 64)
                        for kc in range(2):
                            nk = 128 if kc == 0 else nk1
                            kfn, kres = (k0_fn, k0_res) if kc == 0 else (k1_fn, k1_res)
                            bk = nb()
                            A("pe", lambda e, bk=bk, kfn=kfn, hs=hs, nk=nk: e.matmul(
                                banks[bk][0:nk, 0:CB * n].rearrange("p (c q) -> p c q", q=n), lhsT=kfn(gp, hs),
                                rhs=qT[hs, cp0:cp0 + CB, qc0:qc0 + n], start=True, stop=False),
                              reads=[kres, "qT"], writes=[("bk", bk)])
                            A("pe", lambda e, bk=bk, kc=kc, nk=nk, half=half: e.matmul(
                                banks[bk][0:nk, 0:CB * n].rearrange("p (c q) -> p c q", q=n), lhsT=ident_b[0:nk, 0:nk],
                                rhs=TBL[0:nk, kc, half * 16 + cp0:half * 16 + cp0 + CB, r0:r0 + n], start=False, stop=True),
                              reads=["ident_b", "TBL"], writes=[("bk", bk)])
                            A("act", lambda e, bk=bk, kc=kc, nk=nk, eb=eb: e.activation(
                                out=eb[0:nk, kc, 0:CB * n], in_=banks[bk][0:nk, 0:CB * n], func=AF.Exp),
                              reads=[("bk", bk)], writes=[ebr])
                    bo = nb(); bd = nb()
                    first = True
                    for half in range(2):
                        eb, ebr = ebs[half]
                        for kc in range(2):
                            nk = 128 if kc == 0 else nk1
                            vp, vpr, opt = (vp0, vp0_res, op0) if kc == 0 else (vp1, vp1_res, op1)
                            last = (half == 1 and kc == 1)
                            A("pe", lambda e, vp=vp, nk=nk, eb=eb, kc=kc, half=half, first=first, last=last: e.matmul(
                                banks[bo][:, 0:CB * n], lhsT=vp[0:nk, 2 * gp + half, :], rhs=eb[0:nk, kc, 0:CB * n],
                                start=first, stop=last), reads=[vpr, ebr], writes=[("bk", bo)])
                            A("pe", lambda e, opt=opt, nk=nk, eb=eb, kc=kc, half=half, first=first, last=last: e.matmul(
                                banks[bd][:, 0:CB * n], lhsT=opt[0][0:nk, half, :], rhs=eb[0:nk, kc, 0:CB * n],
                                start=first, stop=last), reads=[opt[1], ebr], writes=[("bk", bd)])
                            first = False
                    rd = rden[0]; rdr = ("rden", 0)
                    for i in range(CB):
                        A("dve", lambda e, i=i: e.tensor_scalar(out=rd[:, i * n:(i + 1) * n], in0=banks[bd][:, i * n:(i + 1) * n],
                                                               scalar1=esink[:, cp0 + i:cp0 + i + 1], scalar2=None, op0=ALU.add),
                          reads=[("bk", bd), "esink"], writes=[rdr])
                    A("dve", lambda e: e.reciprocal(out=rd[:, 0:CB * n], in_=rd[:, 0:CB * n]), reads=[rdr], writes=[rdr])
                    A("dve", lambda e: e.tensor_tensor(
                        out=oT[:, cp0:cp0 + CB, qc0:qc0 + n], in0=banks[bo][:, 0:CB * n].rearrange("p (c q) -> p c q", q=n),
                        in1=rd[:, 0:CB * n].rearrange("p (c q) -> p c q", q=n), op=ALU.mult),
                      reads=[("bk", bo), rdr], writes=["oT"])
            for gp_ in range(2):
                for cb_ in range(8 // CB):
                    do_gc(gp_, cb_)

        def out_dma(fn, reads, grp):
            if grp not in out_dma_groups:
                out_dma_groups.append(grp)
            A("sp", fn, reads=reads, dma=grp)

        vp_of_row = {}
        def process_tile(ti, kind, blocks):
            st["tile"] = ti; st["wi"] = 0
            W = sum(n for _, n in blocks)
            row_lo = blocks[0][0]
            mo = EXT if ti == 0 else 0
            is_s = (kind == "s")
            gload(GB0, "GB0", g_pre_d)
            if ti == 0:
                stage0_block(0, 128, 0, xh[1], ("xh", 1))
            col = mo
            xbufs = {}
            for bi, (r0_, n) in enumerate(blocks):
                slot = 1 if n == 2 else (bi - (1 if ti == 0 else 0))
                xb, xr = xh[slot], ("xh", slot)
                xbufs[bi] = (xb, xr)
                stage0_block(r0_, n, col, xb, xr)
                col += n
            ge = 32 if ti == 0 else 0
            NG = ge + W
            Wc = W
            def sample_post(ch, Gt, Gr):
                sj = 0
                spq(lambda e: e.dma_start(out=scb[sj][0:120, :, :], in_=sc_d.rearrange("(a s) r c -> (s r) a c", a=4)[:, :, ch * 128:(ch + 1) * 128]),
                    writes=[("tmpA", 2)])
                for a4 in range(4):
                    bt = nb()
                    A("pe", lambda e, a4=a4, bt=bt: e.transpose(out=banks[bt][:, 0:120], in_=scb[sj][0:120, a4, :], identity=ident_f[0:120, 0:120]),
                      reads=[("tmpA", 2), "ident_f"], writes=[("bk", bt)])
                    A("act", lambda e, a4=a4, bt=bt: e.copy(out=GS[:, a4 * 4:a4 * 4 + 4, 0:30],
                                                           in_=banks[bt][:, 0:120].rearrange("p (s r) -> p s r", r=30)),
                      reads=[("bk", bt)], writes=["GS"])
                A("dve", lambda e: e.tensor_copy(out=GS[:, :, 30:38], in_=Gt[:, 32:32 + 128].rearrange("p (s t) -> p s t", t=8)),
                  reads=[Gr], writes=["GS"])
                cv3 = cT[:, ch, 0:128].rearrange("p (s t) -> p s t", t=8)
                A("dve", lambda e: e.tensor_scalar(out=cv3, in0=GS[:, :, 0:8], scalar1=conv_k[:, ch, 0:1], scalar2=conv_b[:, ch:ch + 1],
                                                   op0=ALU.mult, op1=ALU.add), reads=["GS", "conv_k", "conv_b"], writes=[("cT", ch)])
                for j in range(1, 31):
                    A("dve", lambda e, j=j: e.scalar_tensor_tensor(out=cv3, in0=GS[:, :, j:j + 8], scalar=conv_k[:, ch, j:j + 1], in1=cv3,
                                                                  op0=ALU.mult, op1=ALU.add), reads=["GS", "conv_k", ("cT", ch)], writes=[("cT", ch)])
                bt = nb()
                A("pe", lambda e, bt=bt: e.transpose(out=banks[bt][:, 0:128], in_=Gt[:, 32:160], identity=ident_f[:, :]),
                  reads=[Gr, "ident_f"], writes=[("bk", bt)])
                A("act", lambda e, bt=bt: e.copy(out=MS[0][:, ch * 128:(ch + 1) * 128], in_=banks[bt][:, 0:128]),
                  reads=[("bk", bt)], writes=[("MS", 0)])

            def prompt_post(plist):
                ce = "dve"
                for (ch, Gt, Gr) in plist:
                    if ti == 0:
                        A("dve", lambda e, Gt=Gt: e.tensor_scalar(out=Gt[:, 0:34], in0=Gt[:, 0:34], scalar1=kmask[:, 0:1], scalar2=None, op0=ALU.mult),
                          reads=[Gr, "kmask"], writes=[Gr])
                    else:
                        A("dve", lambda e, Gt=Gt, ch=ch: e.tensor_copy(out=Gt[:, 0:32], in_=ghist[:, ch, :]), reads=[("ghist", ch)], writes=[Gr])
                    A("dve", lambda e, Gt=Gt, ch=ch: e.tensor_copy(out=ghist[:, ch, :], in_=Gt[:, W:W + 32]), reads=[Gr], writes=[("ghist", ch)])
                    if ti == LASTP:
                        A("act", lambda e, Gt=Gt, ch=ch: e.copy(out=GO[:, ch, :], in_=Gt[:, 32 + W - 30:32 + W]), reads=[Gr], writes=["GO"])
                    A(ce, lambda e, Gt=Gt, ch=ch: e.tensor_scalar(out=cT[:, ch, 0:Wc], in0=Gt[:, 2:2 + Wc], scalar1=conv_k[:, ch, 0:1], scalar2=conv_b[:, ch:ch + 1],
                                                                 op0=ALU.mult, op1=ALU.add), reads=[Gr, "conv_k", "conv_b"], writes=[("cT", ch)])
                for j in range(1, 31):
                    for (ch, Gt, Gr) in plist:
                        A(ce, lambda e, j=j, Gt=Gt, ch=ch: e.scalar_tensor_tensor(out=cT[:, ch, 0:Wc], in0=Gt[:, 2 + j:2 + j + Wc], scalar=conv_k[:, ch, j:j + 1], in1=cT[:, ch, 0:Wc],
                                                                                op0=ALU.mult, op1=ALU.add), reads=[Gr, "conv_k", ("cT", ch)], writes=[("cT", ch)])

            for g in range(8):
                s_pair = {}
                posts = []

                def evac_ab(i, bk, g=g):
                    s_pair[i] = bk
                    if i % 2 == 0:
                        return
                    ch = g * 2 + i // 2
                    ba, bb = s_pair[i - 1], bk
                    gi = cnt["g"] % 6; cnt["g"] += 1
                    Gt, Gr = Gb[gi], ("Gb", gi)
                    sg = tmpA[gi % 2]; sgr = ("tmpA", gi % 2)
                    A("act", lambda e: e.activation(out=sg[:, 0:NG], in_=banks[bb][:, 0:NG], func=AF.Sigmoid,
                                                    bias=b_in_T[:, 2 * ch + 1:2 * ch + 2]), reads=[("bk", bb), "b_in_T"], writes=[sgr])
                    g0 = 32 - ge
                    A("dve", lambda e: e.scalar_tensor_tensor(out=Gt[:, g0:g0 + NG], in0=banks[ba][:, 0:NG],
                                                              scalar=b_in_T[:, 2 * ch:2 * ch + 1], in1=sg[:, 0:NG], op0=ALU.add, op1=ALU.mult),
                      reads=[("bk", ba), "b_in_T", sgr], writes=[Gr])
                    if is_s:
                        posts.append((ch, Gt, Gr))
                        return
                    posts.append((ch, Gt, Gr))

                xc0 = mo - ge
                fm_group(w_in_v, g * 512, [0, 1, 2, 3], lambda kk, xc0=xc0, NG=NG: xnT[:, kk, xc0:xc0 + NG], NG, ["xnT"], evac_ab)
                if is_s:
                    for (ch_, Gt_, Gr_) in posts:
                        sample_post(ch_, Gt_, Gr_)
                else:
                    prompt_post(posts)
            if is_s:
                for s in range(16):
                    out_dma(lambda e, s=s: e.dma_start(out=convs_d[s, 22:30, :], in_=MS[0][s * 8:s * 8 + 8, :]), [("MS", 0)], "o_convs")
                out_dma(lambda e: e.dma_start(out=convs_d[:, 0:22, :], in_=sc_d[:, 8:30, :]), [], "o_convs")
            if ti == LASTP:
                for ch in range(16):
                    bt = nb()
                    A("pe", lambda e, bt=bt, ch=ch: e.transpose(out=banks[bt][0:30, 0:128], in_=GO[:, ch, :], identity=ident_f[:, :]),
                      reads=["GO", "ident_f"], writes=[("bk", bt)])
                    A("act", lambda e, bt=bt, ch=ch: e.copy(out=MS[0][0:30, ch * 128:(ch + 1) * 128], in_=banks[bt][0:30, 0:128]),
                      reads=[("bk", bt)], writes=[("MS", 0)])
                out_dma(lambda e: e.dma_start(out=convp_d[:, :], in_=MS[0][0:30, :]), [("MS", 0)], "o_convp")
            for g in range(4):
                def evac_q(i, bk, g=g):
                    cp = g * 4 + i
                    A("dve", lambda e: e.tensor_scalar(out=qT[:, cp, 0:W], in0=banks[bk][:, 0:W], scalar1=0.125, scalar2=bq8[:, cp:cp + 1],
                                                       op0=ALU.mult, op1=ALU.add), reads=[("bk", bk), "bq8"], writes=["qT"])
                fm_group(w_in_v, (8 + g) * 512, [0, 1, 2, 3], lambda kk: xnT[:, kk, mo:mo + W], W, ["xnT"], evac_q)
            def evac_k(i, bk):
                A("act", lambda e: e.activation(out=KT[:, i, row_lo:row_lo + W], in_=banks[bk][:, 0:W], func=AF.Identity,
                                                bias=b_in_T[:, 48 + i:49 + i]), reads=[("bk", bk), "b_in_T"], writes=["KT"])
            s01 = fm_group(w_in_v, 12 * 512, [0, 1], lambda kk: xnT[:, kk, mo:mo + W], W, ["xnT"], evac_k)
            if ti == 0:
                for i in range(2):
                    bk = nb()
                    for h, s in enumerate(s01):
                        for k in range(8):
                            kk = 8 * h + k
                            A("pe", lambda e, s=s, k=k, kk=kk, i=i, bk=bk: e.matmul(
                                banks[bk][:, 0:128], lhsT=ring[s][:, k, i * 128:(i + 1) * 128], rhs=xnT[:, kk, 0:128],
                                start=(kk == 0), stop=(kk == 15)), reads=[("w", s), "xnT"], writes=[("bk", bk)])
                    A("act", lambda e, i=i, bk=bk: e.activation(out=KT[:, i, 0:128], in_=banks[bk][:, 0:128], func=AF.Identity,
                                                              bias=b_in_T[:, 48 + i:49 + i]), reads=[("bk", bk), "b_in_T"], writes=["KT"])
            if not is_s:
                wins = []
                if ti == 0:
                    wins += [(0, 0), (128, 1), (2, 0)]
                for (r0_, n) in blocks:
                    if n == 128:
                        wins.append((r0_, None))
                for (wr, mask) in wins:
                    vi = len(vp_of_row) % 8
                    vp_of_row[wr] = vi
                    last_win = (wr == 1026)
                    make_vpad(wr - row_lo + mo, vi, mask, with_k_out=last_win, s01=s01)
                    if last_win:
                        out_dma(lambda e: e.dma_start(out=kvp_d[:, :], in_=Vf[:, :]), ["Vf"], "o_kvp")
            else:
                make_vpad(0, 7, None, with_k_out=True, s01=s01)
                for s in range(16):
                    out_dma(lambda e, s=s: e.dma_start(out=ks_d[s, 120:128, :], in_=Vf[s * 8:s * 8 + 8, 0:256]), ["Vf"], "o_ks")
                    out_dma(lambda e, s=s: e.dma_start(out=vs_d[s, 120:128, :], in_=Vf[s * 8:s * 8 + 8, 256:512]), ["Vf"], "o_vs")
                out_dma(lambda e: e.dma_start(out=ks_d[:, 0:120, :], in_=ck_d[:, 8:128, :]), [], "o_ks")
                out_dma(lambda e: e.dma_start(out=vs_d[:, 0:120, :], in_=cv_d[:, 8:128, :]), [], "o_vs")
            if not is_s:
                col = 0
                for (r0_, n) in blocks:
                    if n == 2:
                        w0, w1 = 0, 128
                        o0 = (OPF, "OPF"); o1 = (OPB, "OPB"); rr0 = 1
                    else:
                        w0, w1 = r0_ - 128, r0_
                        o0 = (OPF, "OPF") if w0 == 2 else (OP1, "OP1"); o1 = (OP1, "OP1"); rr0 = 1
                    v0, v1 = vp_of_row[w0], vp_of_row[w1]
                    attn_block(col, n, rr0,
                               lambda gp, hs, w0=w0: KT[hs, gp, w0:w0 + 128], "KT",
                               lambda gp, hs, w1=w1: KT[hs, gp, w1:w1 + 128], "KT", 128,
                               VP[v0], ("VP", v0), o0, VP[v1], ("VP", v1), o1)
                    col += n
            else:
                for s in range(16):
                    j2 = s % 2
                    if j2 == 0:
                        A("pool", lambda e, s=s: e.dma_start(out=ckb[:, :, :], in_=ck_d[s:s + 2].rearrange("s k d -> k s d")), writes=[("VP", 4)], dma="ckb")
                        A("pool", lambda e, s=s: e.dma_start(out=cvbuf[:, :, :], in_=cv_d[s:s + 2].rearrange("s k d -> k s d")), writes=[("VP", 6)], dma="cvb")
                        for s2 in range(2):
                            tb = ntb()
                            for gp in range(2):
                                A("pe", lambda e, tb=tb, gp=gp, s2=s2: e.transpose(out=tbanks[tb][:, gp * 128:(gp + 1) * 128],
                                                                              in_=ckb[:, s2, gp * 128:(gp + 1) * 128], identity=ident_b[:, :]),
                                  reads=[("VP", 4), "ident_b"], writes=[("tb", tb)])
                            A("act", lambda e, tb=tb, s2=s2: e.copy(out=KcT[:, s2, :, :], in_=tbanks[tb][:, 0:256].rearrange("p (g k) -> p g k", g=2)),
                              reads=[("tb", tb)], writes=[("VP", 5)])
                            vpad_from(cvbuf[:, s2, :], ("VP", 6), VPc[s2], ("VP", s2), None)
                    bk = nb()
                    for h, sl in enumerate(s01):
                        for k in range(8):
                            kk = 8 * h + k
                            A("pe", lambda e, sl=sl, k=k, kk=kk, bk=bk, s=s: e.matmul(
                                banks[bk][0:8, 0:256], lhsT=xnT[:, kk, s * 8:s * 8 + 8], rhs=ring[sl][:, k, 256:512],
                                start=(kk == 0), stop=(kk == 15)), reads=[("w", sl), "xnT"], writes=[("bk", bk)])
                    A("dve", lambda e, bk=bk: e.tensor_tensor(out=tmpA[2][0:8, 0:256], in0=banks[bk][0:8, 0:256], in1=bkvB[0:8, 256:512], op=ALU.add),
                      reads=[("bk", bk), "bkvB"], writes=[("tmpA", 2)])
                    vpad_from(tmpA[2][0:8, 0:256], ("tmpA", 2), VsP[j2], ("VP", 2 + j2), None, npart=8)
                    rs = 1154 + s * 8
                    attn_block(s * 8, 8, 1,
                               lambda gp, hs, j2=j2: KcT[hs, j2, gp, :], ("VP", 5),
                               lambda gp, hs, rs=rs: KT[hs, gp, rs:rs + 8], "KT", 8,
                               VPc[j2], ("VP", j2), (OP1, "OP1"), VsP[j2], ("VP", 2 + j2), (OP1, "OP1"))
            bsum = nb(); bsq = nb()
            for k in range(16):
                A("pe", lambda e, k=k: e.matmul(banks[bsum][:, 0:W], lhsT=ones_f[:, :], rhs=cT[:, k, 0:W], start=(k == 0), stop=(k == 15)),
                  reads=["ones_f", ("cT", k)], writes=[("bk", bsum)])
                zi = k % 2
                A("act", lambda e, k=k, zi=zi: e.activation(out=lnz[zi][:, 0:W], in_=cT[:, k, 0:W], func=AF.Square), reads=[("cT", k)], writes=[("lnz", zi)])
                A("pe", lambda e, k=k, zi=zi: e.matmul(banks[bsq][:, 0:W], lhsT=ones_f[:, :], rhs=lnz[zi][:, 0:W], start=(k == 0), stop=(k == 15)),
                  reads=["ones_f", ("lnz", zi)], writes=[("bk", bsq)])
            A("dve", lambda e: e.tensor_scalar(out=lnm[:, 0:W], in0=banks[bsum][:, 0:W], scalar1=1.0 / D, scalar2=None, op0=ALU.mult),
              reads=[("bk", bsum)], writes=["lnm"])
            A("dve", lambda e: e.tensor_tensor(out=lnt[:, 0:W], in0=lnm[:, 0:W], in1=lnm[:, 0:W], op=ALU.mult), reads=["lnm"], writes=["lnt"])
            A("dve", lambda e: e.scalar_tensor_tensor(out=lnr[:, 0:W], in0=banks[bsq][:, 0:W], scalar=1.0 / D, in1=lnt[:, 0:W],
                                                      op0=ALU.mult, op1=ALU.subtract), reads=[("bk", bsq), "lnt"], writes=["lnr"])
            A("act", lambda e: e.activation(out=lnr[:, 0:W], in_=lnr[:, 0:W], func=AF.Sqrt, bias=epsT[:, 0:1]), reads=["lnr", "epsT"], writes=["lnr"])
            A("dve", lambda e: e.reciprocal(out=lnr[:, 0:W], in_=lnr[:, 0:W]), reads=["lnr"], writes=["lnr"])
            for k in range(16):
                zi = k % 2
                A("dve", lambda e, k=k, zi=zi: e.tensor_tensor(out=lnz[zi][:, 0:W], in0=cT[:, k, 0:W], in1=lnm[:, 0:W], op=ALU.subtract),
                  reads=[("cT", k), "lnm"], writes=[("lnz", zi)])
                A("dve", lambda e, k=k, zi=zi: e.tensor_tensor(out=lnz[zi][:, 0:W], in0=lnz[zi][:, 0:W], in1=lnr[:, 0:W], op=ALU.mult),
                  reads=[("lnz", zi), "lnr"], writes=[("lnz", zi)])
                A("act", lambda e, k=k, zi=zi: e.activation(out=sT[:, k, 0:W], in_=lnz[zi][:, 0:W], func=AF.Silu, scale=ln_g[:, k:k + 1], bias=ln_b[:, k:k + 1]),
                  reads=[("lnz", zi), "ln_g", "ln_b"], writes=["sT"])
            for cg in range(4):
                def ev_co(i, bk):
                    A("act", lambda e: e.copy(out=CO[:, i, 0:W], in_=banks[bk][:, 0:W]), reads=[("bk", bk)], writes=["CO"])
                fm_group(w_cp_v, cg * 512, [0, 1, 2, 3], lambda kk: sT[:, kk, 0:W], W, ["sT"], ev_co)

                def ev_gc(i, bk, cg=cg):
                    ch = 52 + cg * 4 + i
                    zi = i % 2
                    A("act", lambda e: e.activation(out=sgt[zi][:, 0:W], in_=banks[bk][:, 0:W], func=AF.Sigmoid, bias=b_in_T[:, ch:ch + 1]),
                      reads=[("bk", bk), "b_in_T"], writes=[("sgt", zi)])
                    A("dve", lambda e: e.tensor_tensor(out=T1[:, i, 0:W], in0=sgt[zi][:, 0:W], in1=CO[:, i, 0:W], op=ALU.mult),
                      reads=[("sgt", zi), "CO"], writes=["T1"])
                fm_group(w_in_v, (13 + cg) * 512, [0, 1, 2, 3], lambda kk: xnT[:, kk, mo:mo + W], W, ["xnT"], ev_gc)
                fm_group(w_ap_v, cg * 512, [0, 1, 2, 3], lambda kk: oT[:, kk, 0:W], W, ["oT"], ev_co)

                def ev_ga(i, bk, cg=cg):
                    ch = 68 + cg * 4 + i
                    zi = i % 2
                    A("act", lambda e: e.activation(out=sgt[zi][:, 0:W], in_=banks[bk][:, 0:W], func=AF.Sigmoid, bias=b_in_T[:, ch:ch + 1]),
                      reads=[("bk", bk), "b_in_T"], writes=[("sgt", zi)])
                    A("dve", lambda e: e.tensor_tensor(out=sgt[zi][:, 0:W], in0=sgt[zi][:, 0:W], in1=CO[:, i, 0:W], op=ALU.mult),
                      reads=[("sgt", zi), "CO"], writes=[("sgt", zi)])
                    A("dve", lambda e: e.tensor_tensor(out=mixT[:, cg * 4 + i, 0:W], in0=sgt[zi][:, 0:W], in1=T1[:, i, 0:W], op=ALU.add),
                      reads=[("sgt", zi), "T1"], writes=["mixT"])
                fm_group(w_in_v, (17 + cg) * 512, [0, 1, 2, 3], lambda kk: xnT[:, kk, mo:mo + W], W, ["xnT"], ev_ga)
            gload(GB0, "GB0", g_post_d)
            for cg in range(4):
                s0 = wload(w_o_v[:, 0:8, cg * 512:(cg + 1) * 512]); s1 = wload(w_o_v[:, 8:16, cg * 512:(cg + 1) * 512])
                col = 0
                for bi, (r0_, n) in enumerate(blocks):
                    bk = nb()
                    for h, s in enumerate((s0, s1)):
                        for k in range(8):
                            kk = 8 * h + k
                            A("pe", lambda e, s=s, k=k, kk=kk, bk=bk, col=col, n=n: e.matmul(
                                banks[bk][0:n, :], lhsT=mixT[:, kk, col:col + n], rhs=ring[s][:, k, :], start=(kk == 0), stop=(kk == 15)),
                              reads=[("w", s), "mixT"], writes=[("bk", bk)])
                    ms_i = 1 if n == 2 else (bi - (1 if ti == 0 else 0))
                    A("act", lambda e, bk=bk, cg=cg, ms_i=ms_i, n=n: e.copy(out=MS[ms_i][0:n, cg * 512:(cg + 1) * 512], in_=banks[bk][0:n, :]),
                      reads=[("bk", bk)], writes=[("MS", ms_i)])
                    col += n
            col = 0
            for bi, (r0_, n) in enumerate(blocks):
                ms_i = 1 if n == 2 else (bi - (1 if ti == 0 else 0))
                xb, xr = xbufs[bi]
                rmsnorm_stats(MS[ms_i][0:n, :], n, 1, [("MS", ms_i)])
                A("dve", lambda e, ms_i=ms_i, n=n: e.scalar_tensor_tensor(out=MS[ms_i][0:n, :], in0=MS[ms_i][0:n, :], scalar=rstd_t[0:n, 1:2], in1=GB0[0:n, :],
                                                                        op0=ALU.mult, op1=ALU.mult), reads=[("MS", ms_i), ("rstd", 1), "GB0"], writes=[("MS", ms_i)])
                A("dve", lambda e, ms_i=ms_i, n=n, xb=xb: e.tensor_tensor(out=xb[0:n, :], in0=xb[0:n, :], in1=MS[ms_i][0:n, :], op=ALU.add),
                  reads=[xr, ("MS", ms_i)], writes=[xr])
                col += n
            gload(GB0, "GB0", g_fpre_d)
            col = 0
            for bi, (r0_, n) in enumerate(blocks):
                xb, xr = xbufs[bi]
                rmsnorm_stats(xb[0:n, :], n, 2, [xr])
                A("dve", lambda e, n=n, xb=xb: e.scalar_tensor_tensor(out=XS[0:n, :], in0=xb[0:n, :], scalar=rstd_t[0:n, 2:3], in1=GB0[0:n, :],
                                                                    op0=ALU.mult, op1=ALU.mult), reads=[xr, ("rstd", 2), "GB0"], writes=["XS"])
                transpose_to(hnT, "xnT", col, n, "XS")
                col += n
            for g in range(32):
                u_pair = {}

                def ev_up(i, bk, g=g):
                    u_pair[i] = bk
                    if i % 2 == 0:
                        return
                    j = g * 2 + i // 2
                    res = []
                    for t2, bkx in enumerate((u_pair[i - 1], bk)):
                        chn = 2 * j + t2
                        ui = cnt["u"] % 4; cnt["u"] += 1
                        Ut, Ur = Ub[ui], ("Ub", ui)
                        A("act", lambda e, Ut=Ut, bkx=bkx: e.copy(out=Ut[:, 2:2 + W], in_=banks[bkx][:, 0:W]), reads=[("bk", bkx)], writes=[Ur])
                        cvt, cvr = cvb[ui], ("cvb", ui)
                        if is_s:
                            i4 = i // 2 * 2 + t2
                            U3 = UST[:, i4, :, :]
                            A("dve", lambda e, Ut=Ut, U3=U3: e.tensor_copy(out=U3[:, :, 2:10], in_=Ut[:, 2:130].rearrange("p (s t) -> p s t", t=8)),
                              reads=[Ur, ("USTh", g)], writes=[("UST", i4)])
                            c3 = cvt[:, 0:128].rearrange("p (s t) -> p s t", t=8)
                            A("dve", lambda e, U3=U3, c3=c3, chn=chn: e.tensor_scalar(out=c3, in0=U3[:, :, 0:8], scalar1=ffn_k[:, chn, 0:1], scalar2=ffn_b[:, chn:chn + 1],
                                                                                   op0=ALU.mult, op1=ALU.add), reads=[("UST", i4), "ffn_k", "ffn_b"], writes=[cvr])
                            for tp in (1, 2):
                                A("dve", lambda e, U3=U3, c3=c3, chn=chn, tp=tp: e.scalar_tensor_tensor(out=c3, in0=U3[:, :, tp:tp + 8], scalar=ffn_k[:, chn, tp:tp + 1], in1=c3,
                                                                                                      op0=ALU.mult, op1=ALU.add), reads=[("UST", i4), "ffn_k", cvr], writes=[cvr])
                            A("act", lambda e, Ut=Ut, i4=i4: e.copy(out=UO[:, i4, 2:34].rearrange("p (s t) -> p s t", t=2),
                                                                   in_=Ut[:, 2:130].rearrange("p (s t) -> p s t", t=8)[:, :, 6:8]), reads=[Ur], writes=[("UO", i4)])
                        else:
                            if ti == 0:
                                A("dve", lambda e, Ut=Ut: e.tensor_scalar(out=Ut[:, 2:4], in0=Ut[:, 2:4], scalar1=kmask[:, 0:1], scalar2=None, op0=ALU.mult),
                                  reads=[Ur, "kmask"], writes=[Ur])
                                A("dve", lambda e, Ut=Ut: e.memset(Ut[:, 0:2], 0.0), writes=[Ur])
                            else:
                                A("dve", lambda e, Ut=Ut, chn=chn: e.tensor_copy(out=Ut[:, 0:2], in_=uhist[:, chn, :]), reads=[("uhist", chn)], writes=[Ur])
                            A("dve", lambda e, Ut=Ut, chn=chn: e.tensor_copy(out=uhist[:, chn, :], in_=Ut[:, W:W + 2]), reads=[Ur], writes=[("uhist", chn)])
                            if ti == LASTP:
                                i4 = i // 2 * 2 + t2
                                A("act", lambda e, Ut=Ut, i4=i4: e.copy(out=UO[:, i4, 0:2], in_=Ut[:, W:W + 2]), reads=[Ur], writes=[("UO", i4)])
                            A("dve", lambda e, Ut=Ut, cvt=cvt, chn=chn: e.tensor_scalar(out=cvt[:, 0:W], in0=Ut[:, 0:W], scalar1=ffn_k[:, chn, 0:1], scalar2=ffn_b[:, chn:chn + 1],
                                                                                     op0=ALU.mult, op1=ALU.add), reads=[Ur, "ffn_k", "ffn_b"], writes=[cvr])
                            for tp in (1, 2):
                                A("dve", lambda e, Ut=Ut, cvt=cvt, chn=chn, tp=tp: e.scalar_tensor_tensor(out=cvt[:, 0:W], in0=Ut[:, tp:tp + W], scalar=ffn_k[:, chn, tp:tp + 1], in1=cvt[:, 0:W],
                                                                                                        op0=ALU.mult, op1=ALU.add), reads=[Ur, "ffn_k", cvr], writes=[cvr])
                        res.append((cvt, cvr))
                    (cgt, cgr), (cvt, cvr) = res
                    gi = j % 2
                    A("act", lambda e: e.activation(out=geb[gi][:, 0:W], in_=cgt[:, 0:W], func=AF.Gelu_apprx_tanh), reads=[cgr], writes=[("geb", gi)])
                    A("dve", lambda e: e.tensor_tensor(out=actT[:, j, 0:W], in0=geb[gi][:, 0:W], in1=cvt[:, 0:W], op=ALU.mult),
                      reads=[("geb", gi), cvr], writes=["actT"])

                if is_s:
                    sj = 0
                    spq(lambda e, sj=sj, g=g: e.dma_start(out=sfb[sj][0:32, :], in_=sf_d.rearrange("s r c -> (s r) c")[:, g * 512:(g + 1) * 512]), writes=[("tmpA", 0)])
                    for i4 in range(4):
                        bt = nb()
                        A("pe", lambda e, bt=bt, i4=i4, sj=sj: e.transpose(out=banks[bt][:, 0:32], in_=sfb[sj][0:32, i4 * 128:(i4 + 1) * 128], identity=ident_f[0:32, 0:32]),
                          reads=[("tmpA", 0), "ident_f"], writes=[("bk", bt)])
                        A("act", lambda e, bt=bt, i4=i4: e.copy(out=UST[:, i4, :, 0:2], in_=banks[bt][:, 0:32].rearrange("p (s r) -> p s r", r=2)),
                          reads=[("bk", bt)], writes=[("UST", i4), ("USTh", g)])
                fm_group(w_up_v, g * 512, [0, 1, 2, 3], lambda kk: hnT[:, kk, 0:W], W, ["xnT"], ev_up)
                if ti == LASTP or is_s:
                    c_lo, c_n = (0, 2) if ti == LASTP else (2, 32)
                    oi = 0
                    for i4 in range(4):
                        bt = nb()
                        A("pe", lambda e, bt=bt, i4=i4: e.transpose(out=banks[bt][0:c_n, 0:128], in_=UO[:, i4, c_lo:c_lo + c_n], identity=ident_f[:, :]),
                          reads=[("UO", i4), "ident_f"], writes=[("bk", bt)])
                        A("act", lambda e, bt=bt, i4=i4, oi=oi: e.copy(out=uost[oi][0:c_n, i4 * 128:(i4 + 1) * 128], in_=banks[bt][0:c_n, 0:128]),
                          reads=[("bk", bt)], writes=[("tmpA", 1)])
                    if ti == LASTP:
                        out_dma(lambda e, oi=oi, g=g: e.dma_start(out=ffnp_d[:, g * 512:(g + 1) * 512], in_=uost[oi][0:2, :]), [("tmpA", 1)], "o_ffnp%d" % oi)
                    else:
                        out_dma(lambda e, oi=oi, g=g: e.dma_start(out=ffns_d[:, g * 512:(g + 1) * 512], in_=uost[oi][0:32, :]), [("tmpA", 1)], "o_ffns%d" % oi)
            gload(GB0, "GB0", g_fpost_d)
            oblocks = [(bi, r0_, n) for bi, (r0_, n) in enumerate(blocks) if n == 128]
            for cg in range(4):
                bkm = {bi: nb() for (bi, _, _) in oblocks}
                for kq in range(8):
                    s = wload(w_dn_v[:, kq * 8:(kq + 1) * 8, cg * 512:(cg + 1) * 512])
                    col = 0
                    for bi, (r0_, n) in enumerate(blocks):
                        if n == 128:
                            for k in range(8):
                                kk = kq * 8 + k
                                A("pe", lambda e, s=s, k=k, kk=kk, bi=bi, col=col, bkm=bkm: e.matmul(
                                    banks[bkm[bi]][:, :], lhsT=actT[:, kk, col:col + 128], rhs=ring[s][:, k, :], start=(kk == 0), stop=(kk == 63)),
                                  reads=[("w", s), "actT"], writes=[("bk", bkm[bi])])
                        col += n
                for (bi, r0_, n) in oblocks:
                    ms_i = bi - (1 if ti == 0 else 0)
                    A("act", lambda e, bi=bi, ms_i=ms_i, cg=cg, bkm=bkm: e.copy(out=MS[ms_i][:, cg * 512:(cg + 1) * 512], in_=banks[bkm[bi]][:, :]),
                      reads=[("bk", bkm[bi])], writes=[("MS", ms_i)])
            for (bi, r0_, n) in oblocks:
                ms_i = bi - (1 if ti == 0 else 0)
                xb, xr = xbufs[bi]
                rmsnorm_stats(MS[ms_i][:, :], 128, 3, [("MS", ms_i)])
                A("dve", lambda e, ms_i=ms_i: e.scalar_tensor_tensor(out=MS[ms_i][:, :], in0=MS[ms_i][:, :], scalar=rstd_t[:, 3:4], in1=GB0[:, :],
                                                                   op0=ALU.mult, op1=ALU.mult), reads=[("MS", ms_i), ("rstd", 3), "GB0"], writes=[("MS", ms_i)])
                A("dve", lambda e, ms_i=ms_i, xb=xb: e.tensor_tensor(out=MS[ms_i][:, :], in0=MS[ms_i][:, :], in1=xb[:, :], op=ALU.add),
                  reads=[("MS", ms_i), xr], writes=[("MS", ms_i)])
                yrow = r0_ - 130
                out_dma(lambda e, ms_i=ms_i, yrow=yrow: e.dma_start(out=y_d[yrow:yrow + 128, :], in_=MS[ms_i][:, :]), [("MS", ms_i)], "o_y%d" % ms_i)

        for ti_, (kind_, blocks_) in enumerate(TILES):
            process_tile(ti_, kind_, blocks_)

        S.finalize()
        sems = {}
        for eng in S.ENGS:
            sems[("eng", eng)] = es.enter_context(nc.semaphore("s_" + eng))
        for gname in S.dma_final:
            sems[("dma", gname)] = es.enter_context(nc.semaphore("d_" + gname))
        with nc.Block() as block:
            @block.tensor
            def _(e): S.emit("pe", e, sems)

            @block.scalar
            def _(e): S.emit("act", e, sems)

            @block.vector
            def _(e): S.emit("dve", e, sems)

            @block.gpsimd
            def _(e): S.emit("pool", e, sems)

            @block.sync
            def _(e):
                S.emit("sp", e, sems)
                for gname in out_dma_groups:
                    e.wait_ge(sems[("dma", gname)], S.dma_final[gname])
    return nc


_NC_CACHE = {}


def kernel(x_prompt, x_sample, cache_k, cache_v, state_conv, state_ffn_conv,
           norm_mix_pre, w_in, b_in, conv_dw_k, conv_dw_b, conv_ln_g, conv_ln_b, w_conv_proj,
           attn_sinks, rel_bias, w_attn_proj, w_out, norm_mix_post,
           norm_ffn_pre, w_up, ffn_dw_k, ffn_dw_b, w_down, norm_ffn_post):
    f = np.float32
    x_prompt = np.asarray(x_prompt, f); x_sample = np.asarray(x_sample, f)
    wp = win_perm()
    w_in_p = np.ascontiguousarray(np.asarray(w_in, f)[0][:, wp])
    b_in_p = np.asarray(b_in, f)[0][wp]
    b_in_T = fmT(b_in_p, 84)
    b_kv = np.ascontiguousarray(np.asarray(b_in, f)[0][6144:6656])
    arp = attn_row_perm()
    w_ap = np.ascontiguousarray(np.asarray(w_attn_proj, f)[0][arp, :])
    up = wup_perm()
    w_up_p = np.ascontiguousarray(np.asarray(w_up, f)[0][:, up])
    ffn_k_T = np.ascontiguousarray(np.asarray(ffn_dw_k, f)[0][:, up].reshape(3, 128, 128).transpose(2, 1, 0))
    ffn_b_T = fmT(np.asarray(ffn_dw_b, f)[0][up], 128)
    conv_k_T = np.ascontiguousarray(np.asarray(conv_dw_k, f)[0].reshape(31, 16, 128).transpose(2, 1, 0))
    sinks = np.asarray(attn_sinks, f)[0]
    sinkT = np.zeros((128, 16), f)
    rbp = np.zeros((32, 32), f)
    rb = np.asarray(rel_bias, f)
    for cp in range(16):
        for half in range(2):
            h = head_of(half, cp)
            sinkT[half * 64:half * 64 + 64, cp] = sinks[h]
            rbp[:, half * 16 + cp] = rb[:, h]
    bucket = rel_bucket_np(np.arange(128))
    ohb = np.zeros((32, 128), f); ohb[bucket, np.arange(128)] = 1.0
    jm = np.zeros((128, 384), f)
    for dlt in range(128):
        jm[dlt, 255 - dlt] = 1.0
    mneg = np.zeros((128, 2, 129), f)
    s_idx = np.arange(128)[:, None]; r_idx = np.arange(129)[None, :]
    mneg[:, 0, :] = np.where(s_idx >= r_idx, 0.0, NEG)
    mneg[:, 1, :] = np.where(s_idx <= r_idx - 1, 0.0, NEG)

    shared = {
        "w_in": w_in_p, "b_in_T": b_in_T, "b_kv": b_kv,
        "conv_k_T": conv_k_T, "conv_b_T": fmT(np.asarray(conv_dw_b, f)[0], 16),
        "ln_g_T": fmT(np.asarray(conv_ln_g, f)[0], 16), "ln_b_T": fmT(np.asarray(conv_ln_b, f)[0], 16),
        "w_cp": np.ascontiguousarray(np.asarray(w_conv_proj, f)[0]), "w_ap": w_ap, "w_o": np.ascontiguousarray(np.asarray(w_out, f)[0]),
        "g_pre": np.ascontiguousarray(np.asarray(norm_mix_pre, f)[0]), "g_post": np.ascontiguousarray(np.asarray(norm_mix_post, f)[0]),
        "g_fpre": np.ascontiguousarray(np.asarray(norm_ffn_pre, f)[0]), "g_fpost": np.ascontiguousarray(np.asarray(norm_ffn_post, f)[0]),
        "w_up": w_up_p, "ffn_k_T": ffn_k_T, "ffn_b_T": ffn_b_T, "w_dn": np.ascontiguousarray(np.asarray(w_down, f)[0]),
        "sinkT": sinkT, "rbp": rbp, "ohb": ohb, "jm": jm, "mneg": mneg,
    }
    in_maps = []
    for c in range(NCORES):
        b, qd = c // 4, c % 4
        T0 = 1024 * qd
        xin = np.zeros((NROWS, D), f)
        if qd > 0:
            xin[0:130] = x_prompt[b, T0 - 130:T0]
        xin[130:1154] = x_prompt[b, T0:T0 + 1024]
        xin[1154:1282] = x_sample[16 * c:16 * c + 16].reshape(128, D)
        km = np.ones((128, 2), f)
        if qd == 0:
            km[:, 0] = 0.0
            km[0:2, 1] = 0.0
        m = dict(shared)
        m["xin"] = xin; m["kmask"] = km
        m["ck"] = np.ascontiguousarray(np.asarray(cache_k, f)[0, 16 * c:16 * c + 16].reshape(16, 128, 256))
        m["cv"] = np.ascontiguousarray(np.asarray(cache_v, f)[0, 16 * c:16 * c + 16].reshape(16, 128, 256))
        m["sc"] = np.ascontiguousarray(np.asarray(state_conv, f)[0, 16 * c:16 * c + 16])
        m["sf"] = np.ascontiguousarray(np.asarray(state_ffn_conv, f)[0, 16 * c:16 * c + 16][:, :, up])
        in_maps.append(m)

    if "nc" not in _NC_CACHE:
        _NC_CACHE["nc"] = build()
    nc = _NC_CACHE["nc"]
    res = run_bass_kernel_spmd(nc, in_maps, core_ids=list(range(NCORES)))
    R = res.results
    inv_up = np.argsort(up)
    y_p = np.zeros((2, 4096, D), f); y_s = np.zeros((128, 8, D), f)
    k_p = np.zeros((1, 2, 128, 4, 64), f); v_p = np.zeros((1, 2, 128, 4, 64), f)
    conv_p = np.zeros((1, 2, 30, D), f); ffn_p = np.zeros((1, 2, 2, DFF2), f)
    k_s = np.zeros((1, 128, 128, 4, 64), f); v_s = np.zeros((1, 128, 128, 4, 64), f)
    conv_s = np.zeros((1, 128, 30, D), f); ffn_s = np.zeros((1, 128, 2, DFF2), f)
    for c in range(NCORES):
        b, qd = c // 4, c % 4
        r = R[c]
        y_p[b, 1024 * qd:1024 * qd + 1024] = r["y"][0:1024]
        y_s[16 * c:16 * c + 16] = r["y"][1024:1152].reshape(16, 8, D)
        if qd == 3:
            k_p[0, b] = r["kvp"][:, 0:256].reshape(128, 4, 64)
            v_p[0, b] = r["kvp"][:, 256:512].reshape(128, 4, 64)
            conv_p[0, b] = r["convp"]
            ffn_p[0, b] = r["ffnp"][:, inv_up]
        k_s[0, 16 * c:16 * c + 16] = r["ks"].reshape(16, 128, 4, 64)
        v_s[0, 16 * c:16 * c + 16] = r["vs"].reshape(16, 128, 4, 64)
        conv_s[0, 16 * c:16 * c + 16] = r["convs"]
        ffn_s[0, 16 * c:16 * c + 16] = r["ffns"].reshape(16, 2, DFF2)[:, :, inv_up]
    return (y_p, y_s, k_p, v_p, conv_p, ffn_p, k_s, v_s, conv_s, ffn_s)
```

```python
import contextlib
import math
import numpy as np
import concourse.bass as bass
import concourse.mybir as mybir
from concourse.bass_utils import run_bass_kernel_spmd

F32 = mybir.dt.float32
BF16 = mybir.dt.bfloat16
AF = mybir.ActivationFunctionType
ALU = mybir.AluOpType

D = 2048
NH = 32
DIN = 10752
DFF2 = 16384
EPS = 1e-6
NCORES = 8
NROWS = 1282
NEG = -30000.0


class Op:
    __slots__ = ("eng", "fn", "deps", "signal", "sigval", "dma", "idx")

    def __init__(self, eng, fn, dma=None):
        self.eng = eng; self.fn = fn; self.deps = []; self.signal = False
        self.sigval = 0; self.dma = dma; self.idx = 0


class Sched:
    ENGS = ("pe", "act", "dve", "pool", "sp")

    def __init__(self):
        self.q = {e: [] for e in self.ENGS}
        self.lastw = {}
        self.readers = {}
        self.nops = 0
        self.alias = {}

    def add(self, eng, fn, reads=(), writes=(), dma=None):
        op = Op(eng, fn, dma)
        op.idx = self.nops; self.nops += 1
        writes = list(writes)
        for w in list(writes):
            for a in self.alias.get(w, ()):
                if a not in writes: writes.append(a)
        deps = {}
        for r in reads:
            w = self.lastw.get(r)
            if w is not None: deps[id(w)] = w
        for w in writes:
            lw = self.lastw.get(w)
            if lw is not None: deps[id(lw)] = lw
            for rd in self.readers.get(w, ()):
                deps[id(rd)] = rd
        for d in deps.values():
            if d is op: continue
            if d.dma is None and op.dma is None and d.eng == eng and eng == "pe":
                continue
            op.deps.append(d)
            d.signal = True
        for w in writes:
            self.lastw[w] = op
            self.readers[w] = []
        for r in reads:
            if r in writes: continue
            self.readers.setdefault(r, []).append(op)
        self.q[eng].append(op)
        return op

    def finalize(self):
        for e in self.ENGS:
            cnt = 0
            for op in self.q[e]:
                if op.dma is None and op.signal:
                    cnt += 1; op.sigval = cnt
        gc = {}
        allops = sorted([op for e in self.ENGS for op in self.q[e] if op.dma is not None], key=lambda o: o.idx)
        for op in allops:
            gc[op.dma] = gc.get(op.dma, 0) + 16
            op.sigval = gc[op.dma]
        self.dma_final = gc

    def emit(self, eng, e, sems):
        waited = {}
        for op in self.q[eng]:
            need = {}
            for d in op.deps:
                key = ("dma", d.dma) if d.dma is not None else ("eng", d.eng)
                if need.get(key, 0) < d.sigval: need[key] = d.sigval
            for key, val in need.items():
                if waited.get(key, 0) < val:
                    e.wait_ge(sems[key], val)
                    waited[key] = val
            ins = op.fn(e)
            if op.dma is not None:
                ins.then_inc(sems[("dma", op.dma)], 16)
            elif op.signal:
                ins.then_inc(sems[("eng", eng)], 1)


def head_of(half, cp):
    if cp < 8:
        return cp if half == 0 else 8 + cp
    return 16 + (cp - 8) if half == 0 else 24 + (cp - 8)


def win_perm():
    idx = []
    for c in range(16):
        idx += list(range(c * 128, c * 128 + 128))
        idx += list(range(2048 + c * 128, 2048 + c * 128 + 128))
    for cp in range(16):
        for half in range(2):
            h = head_of(half, cp)
            idx += list(range(4096 + h * 64, 4096 + h * 64 + 64))
    idx += list(range(6144, 6144 + 512))
    idx += list(range(6656, 6656 + 4096))
    return np.array(idx, dtype=np.int64)


def attn_row_perm():
    idx = []
    for cp in range(16):
        for half in range(2):
            h = head_of(half, cp)
            idx += list(range(h * 64, h * 64 + 64))
    return np.array(idx, dtype=np.int64)


def wup_perm():
    idx = []
    for j in range(64):
        idx += list(range(j * 128, j * 128 + 128))
        idx += list(range(8192 + j * 128, 8192 + j * 128 + 128))
    return np.array(idx, dtype=np.int64)


def rel_bucket_np(dist):
    max_exact = 16
    d = np.maximum(dist, 1).astype(np.float32)
    large = max_exact + (np.log(d / max_exact) / math.log(128 / max_exact) * (32 - max_exact)).astype(np.int32)
    large = np.minimum(large, 31)
    return np.where(dist < max_exact, dist, large)


def fmT(v, nch):
    return np.ascontiguousarray(v.reshape(nch, 128).T)


TILES = [("p", [(128, 2), (130, 128), (258, 128)]), ("p", [(386, 128), (514, 128)]), ("p", [(642, 128), (770, 128)]),
         ("p", [(898, 128), (1026, 128)]), ("s", [(1154, 128)])]
LASTP = 3
WMAX = 258
EXT = 128


def build():
    nc = bass.Bass("TRN2", target_bir_lowering=False)

    def din(name, shape, dt=F32):
        return nc.dram_tensor(name, list(shape), dt, kind="ExternalInput").ap()

    def dout(name, shape, dt=F32):
        return nc.dram_tensor(name, list(shape), dt, kind="ExternalOutput").ap()

    xin = din("xin", [NROWS, D])
    kmask_d = din("kmask", [128, 2])
    ck_d = din("ck", [16, 128, 256]); cv_d = din("cv", [16, 128, 256])
    sc_d = din("sc", [16, 30, D]); sf_d = din("sf", [16, 2, DFF2])
    w_in_d = din("w_in", [D, DIN]); b_in_T_d = din("b_in_T", [128, 84]); b_kv_d = din("b_kv", [512])
    conv_k_d = din("conv_k_T", [128, 16, 31]); conv_b_d = din("conv_b_T", [128, 16])
    ln_g_d = din("ln_g_T", [128, 16]); ln_b_d = din("ln_b_T", [128, 16])
    w_cp_d = din("w_cp", [D, D]); w_ap_d = din("w_ap", [D, D]); w_o_d = din("w_o", [D, D])
    g_pre_d = din("g_pre", [D]); g_post_d = din("g_post", [D]); g_fpre_d = din("g_fpre", [D]); g_fpost_d = din("g_fpost", [D])
    w_up_d = din("w_up", [D, DFF2]); ffn_k_d = din("ffn_k_T", [128, 128, 3]); ffn_b_d = din("ffn_b_T", [128, 128])
    w_dn_d = din("w_dn", [8192, D])
    sink_d = din("sinkT", [128, 16]); rbp_d = din("rbp", [32, 32])
    ohb_d = din("ohb", [32, 128]); jm_d = din("jm", [128, 384]); mneg_d = din("mneg", [128, 2, 129])

    y_d = dout("y", [1152, D])
    kvp_d = dout("kvp", [128, 512])
    convp_d = dout("convp", [30, D])
    ffnp_d = dout("ffnp", [2, DFF2])
    ks_d = dout("ks", [16, 128, 256]); vs_d = dout("vs", [16, 128, 256])
    convs_d = dout("convs", [16, 30, D])
    ffns_d = dout("ffns", [32, DFF2])

    S = Sched()
    out_dma_groups = []

    with contextlib.ExitStack() as es:
        def sb(name, shape, dt=F32):
            return es.enter_context(nc.sbuf_tensor(name, list(shape), dt))

        def ps(name, shape, dt=F32):
            return es.enter_context(nc.psum_tensor(name, list(shape), dt))

        A = S.add
        NBK = 6
        banks = [ps("bk%d" % i, [128, 512], F32) for i in range(NBK)]
        tbanks = [ps("tb%d" % i, [128, 1024], BF16) for i in range(2)]
        st = {"bk": 0, "tb": 0, "ws": 0, "dq": 0}

        def nb():
            i = st["bk"]; st["bk"] = (i + 1) % NBK
            return i

        def ntb():
            i = st["tb"]; st["tb"] = (i + 1) % 2
            return i

        NSLOT = 3
        ring = [sb("ring%d" % i, [128, 8, 512], BF16) for i in range(NSLOT)]

        wscr = nc.dram_tensor("wscr", [162, 128, 4096], BF16).ap()
        st["wi"] = 0
        st["tile"] = 0

        def wload(src):
            s = st["ws"]; st["ws"] = (s + 1) % NSLOT
            idx = st["wi"]; st["wi"] += 1
            if st["tile"] == 0:
                A("pool", lambda e, s=s, src=src: e.dma_start(out=ring[s][:], in_=src), writes=[("w", s)], dma="w%d" % s)
                A("sp", lambda e, s=s, idx=idx: e.dma_start(out=wscr[idx], in_=ring[s][:].rearrange("p k c -> p (k c)")),
                  reads=[("w", s)], writes=[("wscr", idx)], dma="wb%d" % s)
            else:
                A("sp", lambda e, s=s, idx=idx: e.dma_start(out=ring[s][:].rearrange("p k c -> p (k c)"), in_=wscr[idx]),
                  reads=[("wscr", idx)], writes=[("w", s)], dma="w%d" % s)
            return s

        w_in_v = w_in_d.rearrange("(k p) c -> p k c", p=128)
        w_cp_v = w_cp_d.rearrange("(k p) c -> p k c", p=128)
        w_ap_v = w_ap_d.rearrange("(k p) c -> p k c", p=128)
        w_o_v = w_o_d.rearrange("(k p) c -> p k c", p=128)
        w_up_v = w_up_d.rearrange("(k p) c -> p k c", p=128)
        w_dn_v = w_dn_d.rearrange("(k p) c -> p k c", p=128)

        def spq(fn, reads=(), writes=(), grp=None):
            if grp is None:
                grp = "q%d" % st["dq"]; st["dq"] = (st["dq"] + 1) % 6
            return A("sp", fn, reads=reads, writes=writes, dma=grp)

        tmpA = [sb("tmpA%d" % i, [128, 512]) for i in range(3)]
        ident_f = sb("ident_f", [128, 128]); ident_b = sb("ident_b", [128, 128], BF16)
        ones_f = sb("ones_f", [128, 128])
        epsT = sb("epsT", [128, 1])
        kmask = sb("kmaskS", [128, 2])
        b_in_T = sb("b_in_TS", [128, 84]); bq8 = sb("bq8", [128, 16])
        bkvB = sb("bkvB", [128, 512])
        conv_k = sb("conv_kS", [128, 16, 31]); conv_b = sb("conv_bS", [128, 16])
        ln_g = sb("ln_gS", [128, 16]); ln_b = sb("ln_bS", [128, 16])
        ffn_k = sb("ffn_kS", [128, 128, 3]); ffn_b = sb("ffn_bS", [128, 128])
        esink = sb("esink", [128, 16])
        GB0 = sb("GB0", [128, D])
        TBL = sb("TBL", [128, 2, 32, 129], BF16)
        OP1 = sb("OP1", [128, 2, 128], BF16); OPF = sb("OPF", [128, 2, 128], BF16); OPB = sb("OPB", [128, 2, 128], BF16)

        A("dve", lambda e: e.memset(ones_f[:], 1.0), writes=["ones_f"])
        A("dve", lambda e: e.memset(epsT[:], EPS), writes=["epsT"])
        A("pool", lambda e: e.memset(ident_f[:], 0.0), writes=["ident_f"])
        A("pool", lambda e: e.affine_select(out=ident_f[:], in_=ident_f[:], pattern=[[-1, 128]], compare_op=ALU.not_equal,
                                            fill=1.0, base=0, channel_multiplier=1), reads=["ident_f"], writes=["ident_f"])
        A("dve", lambda e: e.tensor_copy(out=ident_b[:], in_=ident_f[:]), reads=["ident_f"], writes=["ident_b"])
        for (dst, src, nm) in ((kmask, kmask_d, "kmask"), (b_in_T, b_in_T_d, "b_in_T"), (conv_k, conv_k_d, "conv_k"),
                               (conv_b, conv_b_d, "conv_b"), (ln_g, ln_g_d, "ln_g"), (ln_b, ln_b_d, "ln_b"),
                               (ffn_k, ffn_k_d, "ffn_k"), (ffn_b, ffn_b_d, "ffn_b"), (esink, sink_d, "esink")):
            spq(lambda e, dst=dst, src=src: e.dma_start(out=dst[:], in_=src), writes=[nm])
        spq(lambda e: e.dma_start(out=bkvB[:], in_=b_kv_d.partition_broadcast(128)), writes=["bkvB"])
        A("act", lambda e: e.activation(out=esink[:], in_=esink[:], func=AF.Exp), reads=["esink"], writes=["esink"])
        A("dve", lambda e: e.tensor_scalar(out=bq8[:], in0=b_in_T[:, 32:48], scalar1=0.125, scalar2=None, op0=ALU.mult),
          reads=["b_in_T"], writes=["bq8"])
        for (T, nm, col) in ((OP1, "OP1", None), (OPF, "OPF", 0), (OPB, "OPB", 1)):
            A("dve", lambda e, T=T: e.memset(T[:], 0.0), writes=[nm])
            for half in range(2):
                if col is None:
                    A("dve", lambda e, T=T, half=half: e.memset(T[:, half, half * 64:half * 64 + 64], 1.0), reads=[], writes=[nm])
                else:
                    A("dve", lambda e, T=T, half=half, col=col: e.tensor_scalar(
                        out=T[:, half, half * 64:half * 64 + 64], in0=ones_f[:, 0:64], scalar1=kmask[:, col:col + 1], scalar2=None,
                        op0=ALU.mult), reads=["ones_f", "kmask"], writes=[nm])

        rbp = sb("rbpS", [32, 32]); ohb = sb("ohbS", [32, 128]); jm = tmpA[0][:, 0:384]; mneg = tmpA[1][:, 0:258].rearrange("p (a b) -> p a b", a=2)
        vec = sb("vecS", [128, 32])
        for (dst, src, nm) in ((rbp[:], rbp_d, "rbp"), (ohb[:], ohb_d, "ohb"), (jm, jm_d, ("tmpA", 0)), (mneg, mneg_d, ("tmpA", 1))):
            spq(lambda e, dst=dst, src=src: e.dma_start(out=dst, in_=src), writes=[nm])
        b0 = nb()
        A("pe", lambda e, b0=b0: e.matmul(banks[b0][:, 0:32], lhsT=ohb[:, :], rhs=rbp[:, :], start=True, stop=True),
          reads=["ohb", "rbp"], writes=[("bk", b0)])
        A("dve", lambda e, b0=b0: e.tensor_copy(out=vec[:], in_=banks[b0][:, 0:32]), reads=[("bk", b0)], writes=["vec"])
        for kc in range(2):
            for r0 in range(0, 129, 16):
                nr = min(16, 129 - r0)
                bk = nb()
                for rr in range(nr):
                    r = r0 + rr
                    start_col = (128 - r) if kc == 0 else (256 - r)
                    A("pe", lambda e, bk=bk, rr=rr, sc_=start_col: e.matmul(
                        banks[bk][:, rr * 32:(rr + 1) * 32], lhsT=jm[:, sc_:sc_ + 128], rhs=vec[:, :], start=True, stop=True),
                      reads=[("tmpA", 0), "vec"], writes=[("bk", bk)])
                A("dve", lambda e, bk=bk, kc=kc, r0=r0, nr=nr: e.tensor_tensor(
                    out=TBL[:, kc, :, r0:r0 + nr].rearrange("p h r -> p r h"),
                    in0=banks[bk][:, 0:nr * 32].rearrange("p (r h) -> p r h", h=32),
                    in1=mneg[:, kc, r0:r0 + nr].unsqueeze(2).to_broadcast([128, nr, 32]), op=ALU.add),
                  reads=[("bk", bk), ("tmpA", 1)], writes=["TBL"])

        KT = sb("KT", [128, 2, NROWS], BF16)
        xh = [sb("xh%d" % i, [128, D]) for i in range(3)]
        XS = sb("XS", [128, D], BF16)
        ssq = sb("ssq", [128, 8]); rstd_t = sb("rstd_t", [128, 8])
        xnT = sb("xnT", [128, 16, EXT + WMAX], BF16)
        hnT = xnT
        arena = sb("arena", [128, 64, WMAX], BF16)
        actT = arena
        cT = arena[:, 0:32, :].rearrange("p a b -> p (a b)").bitcast(F32).rearrange("p (k w) -> p k w", w=WMAX)
        qT = arena[:, 32:48, :]
        qTf = arena[:, 32:48, :].rearrange("p a b -> p (a b)").bitcast(F32)
        oT = arena[:, 48:64, :]
        SM = sb("SM", [128, 32, WMAX], BF16)
        sT = SM[:, 0:16, :]
        mixT = SM[:, 16:32, :]
        SMf = SM[:, :, :].rearrange("p a b -> p (a b)").bitcast(F32)
        MS = [sb("MS%d" % i, [128, D]) for i in range(3)]
        Gb = [sb("Gb%d" % i, [128, 32 + WMAX]) for i in range(4)]
        ghist = sb("ghist", [128, 16, 32])
        uhist = sb("uhist", [128, 128, 2])
        VP = [sb("VP%d" % i, [128, 4, 128], BF16) for i in range(8)]
        Vf = tmpA[1]
        Eb = [sb("Eb%d" % i, [128, 2, 512], BF16) for i in range(2)]
        rden = [tmpA[0]]
        lnm = qTf[:, 0:WMAX]; lnr = qTf[:, WMAX:2 * WMAX]; lnt = qTf[:, 2 * WMAX:3 * WMAX]
        lnz = [qTf[:, (3 + i) * WMAX:(4 + i) * WMAX] for i in range(2)]
        CO = cT[:, 0:4, :]; T1 = cT[:, 4:8, :]; sgt = [cT[:, 8 + i, :] for i in range(2)]
        Ub = [SMf[:, i * (WMAX + 2):(i + 1) * (WMAX + 2)] for i in range(4)]
        _o = 4 * (WMAX + 2)
        cvb = [SMf[:, _o + i * WMAX:_o + (i + 1) * WMAX] for i in range(4)]
        geb = [SMf[:, _o + (4 + i) * WMAX:_o + (5 + i) * WMAX] for i in range(2)]
        S.alias["CO"] = [("cT", k) for k in range(4)]
        S.alias["T1"] = [("cT", k) for k in range(4, 8)]
        for i in range(2):
            S.alias[("sgt", i)] = [("cT", 8 + i)]
            S.alias[("cT", 8 + i)] = [("sgt", i)]
        for k in range(4):
            S.alias[("cT", k)] = ["CO"]
            S.alias[("cT", 4 + k)] = ["T1"]
        lnn = ["lnm", "lnr", "lnt", ("lnz", 0), ("lnz", 1)]
        S.alias["qT"] = list(lnn)
        for nme in lnn:
            S.alias[nme] = ["qT"]
        s5n = [("Ub", i) for i in range(4)] + [("cvb", i) for i in range(4)] + [("geb", i) for i in range(2)]
        S.alias["sT"] = list(s5n); S.alias["mixT"] = list(s5n)
        for nme in s5n:
            S.alias[nme] = ["sT", "mixT"]
        ckb = VP[4][:, :, :].rearrange("p g c -> p (g c)").rearrange("p (s d) -> p s d", s=2)
        KcT = VP[5][:, :, :].rearrange("p (a b) c -> p a b c", a=2)
        cvbuf = VP[6][:, :, :].rearrange("p g c -> p (g c)").rearrange("p (s d) -> p s d", s=2)
        VPc = [VP[0], VP[1]]
        VsP = [VP[2], VP[3]]
        GS = sb("GS", [128, 16, 38]); scb = [tmpA[2][:, :].rearrange("p (a c) -> p a c", a=4)]
        sfb = [tmpA[0]]
        UST = sb("UST", [128, 4, 16, 10])
        UO = sb("UO", [128, 4, 34]); uost = [tmpA[1]]
        GO = GS[:, :, 0:30]
        S.alias["GO"] = ["GS"]; S.alias["GS"] = ["GO"]
        cnt = {"g": 0, "e": 0, "u": 0, "o": 0}

        print("SBUF bytes remaining after alloc:", nc.sbuf_bytes_remaining)
        def rmsnorm_stats(src_ap, n, col, rd):
            A("dve", lambda e: e.memset(ssq[0:n, col:col + 1], 0.0), writes=[("ssq", col)])
            A("act", lambda e: e.activation(out=XS[0:n, :], in_=src_ap, func=AF.Square, accum_out=ssq[0:n, col:col + 1]),
              reads=rd + [("ssq", col)], writes=["XS", ("ssq", col)])
            A("act", lambda e: e.activation(out=rstd_t[0:n, col:col + 1], in_=ssq[0:n, col:col + 1], func=AF.Sqrt,
                                            scale=1.0 / D, bias=epsT[0:n, 0:1]), reads=[("ssq", col), "epsT"], writes=[("rstd", col)])
            A("dve", lambda e: e.reciprocal(out=rstd_t[0:n, col:col + 1], in_=rstd_t[0:n, col:col + 1]),
              reads=[("rstd", col)], writes=[("rstd", col)])

        def transpose_to(dstT, dst_res, col0, n, src_res):
            for half in range(2):
                tb = ntb()
                for k in range(8):
                    kk = half * 8 + k
                    A("pe", lambda e, tb=tb, k=k, kk=kk: e.transpose(out=tbanks[tb][:, k * 128:k * 128 + n],
                                                                  in_=XS[0:n, kk * 128:(kk + 1) * 128], identity=ident_b[0:n, 0:n]),
                      reads=[src_res, "ident_b"], writes=[("tb", tb)])
                A("act", lambda e, tb=tb, half=half: e.copy(
                    out=dstT[:, half * 8:half * 8 + 8, col0:col0 + n],
                    in_=tbanks[tb][:, :].rearrange("p (k c) -> p k c", c=128)[:, :, 0:n]),
                  reads=[("tb", tb)], writes=[dst_res])

        def gload(GBt, nm, src):
            spq(lambda e: e.dma_start(out=GBt[:], in_=src.partition_broadcast(128)), writes=[nm])

        def stage0_block(row0, n, dst_col, xbuf, xres):
            spq(lambda e: e.dma_start(out=xbuf[0:n, :], in_=xin[row0:row0 + n, :]), writes=[xres])
            rmsnorm_stats(xbuf[0:n, :], n, 0, [xres])
            A("dve", lambda e: e.scalar_tensor_tensor(out=XS[0:n, :], in0=xbuf[0:n, :], scalar=rstd_t[0:n, 0:1], in1=GB0[0:n, :],
                                                      op0=ALU.mult, op1=ALU.mult), reads=[xres, ("rstd", 0), "GB0"], writes=["XS"])
            transpose_to(xnT, "xnT", dst_col, n, "XS")

        def fm_group(wv, c0, chunks, rhs_fn, N, rhs_res, evac):
            s0 = wload(wv[:, 0:8, c0:c0 + 512]); s1 = wload(wv[:, 8:16, c0:c0 + 512])
            bks = {i: nb() for i in chunks}
            for h, s in enumerate((s0, s1)):
                for i in chunks:
                    for k in range(8):
                        kk = 8 * h + k
                        A("pe", lambda e, s=s, i=i, k=k, kk=kk: e.matmul(
                            banks[bks[i]][:, 0:N], lhsT=ring[s][:, k, i * 128:(i + 1) * 128], rhs=rhs_fn(kk),
                            start=(kk == 0), stop=(kk == 15)), reads=[("w", s)] + rhs_res, writes=[("bk", bks[i])])
            for i in chunks:
                evac(i, bks[i])
            return s0, s1

        def make_vpad(win_cols, vp_idx, mask, with_k_out=False, s01=None):
            s0, s1 = s01
            bk = nb()
            c_lo = 0 if with_k_out else 256
            for h, s in enumerate((s0, s1)):
                for k in range(8):
                    kk = 8 * h + k
                    A("pe", lambda e, s=s, k=k, kk=kk: e.matmul(
                        banks[bk][:, c_lo:512], lhsT=xnT[:, kk, win_cols:win_cols + 128], rhs=ring[s][:, k, c_lo:512],
                        start=(kk == 0), stop=(kk == 15)), reads=[("w", s), "xnT"], writes=[("bk", bk)])
            A("dve", lambda e: e.tensor_tensor(out=Vf[:, c_lo:512], in0=banks[bk][:, c_lo:512], in1=bkvB[:, c_lo:512], op=ALU.add),
              reads=[("bk", bk), "bkvB"], writes=[("tmpA", 1)])
            vpad_from(Vf[:, 256:512], ("tmpA", 1), VP[vp_idx], ("VP", vp_idx), mask)

        def vpad_from(src, src_res, dstT, dst_res, mask, npart=128):
            A("dve", lambda e: e.memset(dstT[0:npart], 0.0), writes=[dst_res])
            for par in range(2):
                sv = src.rearrange("p (g t d) -> p g t d", g=2, t=2)[:, :, par, :]
                dv = dstT[0:npart].rearrange("p (g t) c -> p g t c", t=2)[:, :, par, par * 64:par * 64 + 64]
                if mask is None:
                    A("dve", lambda e, sv=sv, dv=dv: e.tensor_copy(out=dv, in_=sv), reads=[src_res], writes=[dst_res])
                else:
                    A("dve", lambda e, sv=sv, dv=dv: e.tensor_scalar(out=dv, in0=sv, scalar1=kmask[0:npart, mask:mask + 1], scalar2=None,
                                                                 op0=ALU.mult), reads=[src_res, "kmask"], writes=[dst_res])

        def attn_block(qc0, n, r0, k0_fn, k0_res, k1_fn, k1_res, nk1, vp0, vp0_res, op0, vp1, vp1_res, op1):
            CB = 4 if n > 64 else 8
            def do_gc(gp, cb):
                if True:
                    cp0 = gp * 8 + cb * CB
                    ebs = []
                    for half in range(2):
                        eb = Eb[cnt["e"] % 2]; ebr = ("Eb", cnt["e"] % 2); cnt["e"] += 1
                        ebs.append((eb, ebr))
                        hs = slice(half * 64, half * 64 + 64)
                        for kc in range(2):
                            nk = 128 if kc == 0 else nk1
                            kfn, kres = (k0_fn, k0_res) if kc == 0 else (k1_fn, k1_res)
                            bk = nb()
                            A("pe", lambda e, bk=bk, kfn=kfn, hs=hs, nk=nk: e.matmul(
                                banks[bk][0:nk, 0:CB * n].rearrange("p (c q) -> p c q", q=n), lhsT=kfn(gp, hs),
                                rhs=qT[hs, cp0:cp0 + CB, qc0:qc0 + n], start=True, stop=False),
                              reads=[kres, "qT"], writes=[("bk", bk)])
                            A("pe", lambda e, bk=bk, kc=kc, nk=nk, half=half: e.matmul(
                                banks[bk][0:nk, 0:CB * n].rearrange("p (c q) -> p c q", q=n), lhsT=ident_b[0:nk, 0:nk],
                                rhs=TBL[0:nk, kc, half * 16 + cp0:half * 16 + cp0 + CB, r0:r0 + n], start=False, stop=True),
                              reads=["ident_b", "TBL"], writes=[("bk", bk)])
                            A("act", lambda e, bk=bk, kc=kc, nk=nk, eb=eb: e.activation(
                                out=eb[0:nk, kc, 0:CB * n], in_=banks[bk][0:nk, 0:CB * n], func=AF.Exp),
                              reads=[("bk", bk)], writes=[ebr])
                    bo = nb(); bd = nb()
                    first = True
                    for half in range(2):
                        eb, ebr = ebs[half]
                        for kc in range(2):
                            nk = 128 if kc == 0 else nk1
                            vp, vpr, opt = (vp0, vp0_res, op0) if kc == 0 else (vp1, vp1_res, op1)
                            last = (half == 1 and kc == 1)
                            A("pe", lambda e, vp=vp, nk=nk, eb=eb, kc=kc, half=half, first=first, last=last: e.matmul(
                                banks[bo][:, 0:CB * n], lhsT=vp[0:nk, 2 * gp + half, :], rhs=eb[0:nk, kc, 0:CB * n],
                                start=first, stop=last), reads=[vpr, ebr], writes=[("bk", bo)])
                            A("pe", lambda e, opt=opt, nk=nk, eb=eb, kc=kc, half=half, first=first, last=last: e.matmul(
                                banks[bd][:, 0:CB * n], lhsT=opt[0][0:nk, half, :], rhs=eb[0:nk, kc, 0:CB * n],
                                start=first, stop=last), reads=[opt[1], ebr], writes=[("bk", bd)])
                            first = False
                    rd = rden[0]; rdr = ("tmpA", 0)
                    for i in range(CB):
                        A("dve", lambda e, i=i: e.tensor_scalar(out=rd[:, i * n:(i + 1) * n], in0=banks[bd][:, i * n:(i + 1) * n],
                                                               scalar1=esink[:, cp0 + i:cp0 + i + 1], scalar2=None, op0=ALU.add),
                          reads=[("bk", bd), "esink"], writes=[rdr])
                    A("dve", lambda e: e.reciprocal(out=rd[:, 0:CB * n], in_=rd[:, 0:CB * n]), reads=[rdr], writes=[rdr])
                    A("dve", lambda e: e.tensor_tensor(
                        out=oT[:, cp0:cp0 + CB, qc0:qc0 + n], in0=banks[bo][:, 0:CB * n].rearrange("p (c q) -> p c q", q=n),
                        in1=rd[:, 0:CB * n].rearrange("p (c q) -> p c q", q=n), op=ALU.mult),
                      reads=[("bk", bo), rdr], writes=["oT"])
            for gp_ in range(2):
                for cb_ in range(8 // CB):
                    do_gc(gp_, cb_)

        def out_dma(fn, reads, grp):
            if grp not in out_dma_groups:
                out_dma_groups.append(grp)
            A("sp", fn, reads=reads, dma=grp)

        vp_of_row = {}
        def process_tile(ti, kind, blocks):
            st["tile"] = ti; st["wi"] = 0
            W = sum(n for _, n in blocks)
            row_lo = blocks[0][0]
            mo = EXT if ti == 0 else 0
            is_s = (kind == "s")
            gload(GB0, "GB0", g_pre_d)
            if ti == 0:
                stage0_block(0, 128, 0, xh[2], ("xh", 2))
            col = mo
            xbufs = {}
            for bi, (r0_, n) in enumerate(blocks):
                slot = 2 if n == 2 else (bi - (1 if ti == 0 else 0))
                xb, xr = xh[slot], ("xh", slot)
                xbufs[bi] = (xb, xr)
                stage0_block(r0_, n, col, xb, xr)
                col += n
            ge = 32 if ti == 0 else 0
            NG = ge + W
            Wc = W
            def sample_post(ch, Gt, Gr):
                sj = 0
                spq(lambda e: e.dma_start(out=scb[sj][0:120, :, :], in_=sc_d.rearrange("(a s) r c -> (s r) a c", a=4)[:, :, ch * 128:(ch + 1) * 128]),
                    writes=[("tmpA", 2)])
                for a4 in range(4):
                    bt = nb()
                    A("pe", lambda e, a4=a4, bt=bt: e.transpose(out=banks[bt][:, 0:120], in_=scb[sj][0:120, a4, :], identity=ident_f[0:120, 0:120]),
                      reads=[("tmpA", 2), "ident_f"], writes=[("bk", bt)])
                    A("act", lambda e, a4=a4, bt=bt: e.copy(out=GS[:, a4 * 4:a4 * 4 + 4, 0:30],
                                                           in_=banks[bt][:, 0:120].rearrange("p (s r) -> p s r", r=30)),
                      reads=[("bk", bt)], writes=["GS"])
                A("dve", lambda e: e.tensor_copy(out=GS[:, :, 30:38], in_=Gt[:, 32:32 + 128].rearrange("p (s t) -> p s t", t=8)),
                  reads=[Gr], writes=["GS"])
                cv3 = cT[:, ch, 0:128].rearrange("p (s t) -> p s t", t=8)
                A("dve", lambda e: e.tensor_scalar(out=cv3, in0=GS[:, :, 0:8], scalar1=conv_k[:, ch, 0:1], scalar2=conv_b[:, ch:ch + 1],
                                                   op0=ALU.mult, op1=ALU.add), reads=["GS", "conv_k", "conv_b"], writes=[("cT", ch)])
                for j in range(1, 31):
                    A("dve", lambda e, j=j: e.scalar_tensor_tensor(out=cv3, in0=GS[:, :, j:j + 8], scalar=conv_k[:, ch, j:j + 1], in1=cv3,
                                                                  op0=ALU.mult, op1=ALU.add), reads=["GS", "conv_k", ("cT", ch)], writes=[("cT", ch)])
                bt = nb()
                A("pe", lambda e, bt=bt: e.transpose(out=banks[bt][:, 0:128], in_=Gt[:, 32:160], identity=ident_f[:, :]),
                  reads=[Gr, "ident_f"], writes=[("bk", bt)])
                A("act", lambda e, bt=bt: e.copy(out=MS[0][:, ch * 128:(ch + 1) * 128], in_=banks[bt][:, 0:128]),
                  reads=[("bk", bt)], writes=[("MS", 0)])

            def prompt_post(plist):
                ce = "dve"
                for (ch, Gt, Gr) in plist:
                    if ti == 0:
                        A("dve", lambda e, Gt=Gt: e.tensor_scalar(out=Gt[:, 0:34], in0=Gt[:, 0:34], scalar1=kmask[:, 0:1], scalar2=None, op0=ALU.mult),
                          reads=[Gr, "kmask"], writes=[Gr])
                    else:
                        A("dve", lambda e, Gt=Gt, ch=ch: e.tensor_copy(out=Gt[:, 0:32], in_=ghist[:, ch, :]), reads=[("ghist", ch)], writes=[Gr])
                    A("dve", lambda e, Gt=Gt, ch=ch: e.tensor_copy(out=ghist[:, ch, :], in_=Gt[:, W:W + 32]), reads=[Gr], writes=[("ghist", ch)])
                    if ti == LASTP:
                        A("act", lambda e, Gt=Gt, ch=ch: e.copy(out=GO[:, ch, :], in_=Gt[:, 32 + W - 30:32 + W]), reads=[Gr], writes=["GO"])
                    A(ce, lambda e, Gt=Gt, ch=ch: e.tensor_scalar(out=cT[:, ch, 0:Wc], in0=Gt[:, 2:2 + Wc], scalar1=conv_k[:, ch, 0:1], scalar2=conv_b[:, ch:ch + 1],
                                                                 op0=ALU.mult, op1=ALU.add), reads=[Gr, "conv_k", "conv_b"], writes=[("cT", ch)])
                for j in range(1, 31):
                    for (ch, Gt, Gr) in plist:
                        A(ce, lambda e, j=j, Gt=Gt, ch=ch: e.scalar_tensor_tensor(out=cT[:, ch, 0:Wc], in0=Gt[:, 2 + j:2 + j + Wc], scalar=conv_k[:, ch, j:j + 1], in1=cT[:, ch, 0:Wc],
                                                                                op0=ALU.mult, op1=ALU.add), reads=[Gr, "conv_k", ("cT", ch)], writes=[("cT", ch)])

            for g in range(8):
                s_pair = {}
                posts = []

                def evac_ab(i, bk, g=g):
                    s_pair[i] = bk
                    if i % 2 == 0:
                        return
                    ch = g * 2 + i // 2
                    ba, bb = s_pair[i - 1], bk
                    gi = cnt["g"] % 4; cnt["g"] += 1
                    Gt, Gr = Gb[gi], ("Gb", gi)
                    sg = tmpA[gi % 2]; sgr = ("tmpA", gi % 2)
                    A("act", lambda e: e.activation(out=sg[:, 0:NG], in_=banks[bb][:, 0:NG], func=AF.Sigmoid,
                                                    bias=b_in_T[:, 2 * ch + 1:2 * ch + 2]), reads=[("bk", bb), "b_in_T"], writes=[sgr])
                    g0 = 32 - ge
                    A("dve", lambda e: e.scalar_tensor_tensor(out=Gt[:, g0:g0 + NG], in0=banks[ba][:, 0:NG],
                                                              scalar=b_in_T[:, 2 * ch:2 * ch + 1], in1=sg[:, 0:NG], op0=ALU.add, op1=ALU.mult),
                      reads=[("bk", ba), "b_in_T", sgr], writes=[Gr])
                    if is_s:
                        posts.append((ch, Gt, Gr))
                        return
                    posts.append((ch, Gt, Gr))

                xc0 = mo - ge
                fm_group(w_in_v, g * 512, [0, 1, 2, 3], lambda kk, xc0=xc0, NG=NG: xnT[:, kk, xc0:xc0 + NG], NG, ["xnT"], evac_ab)
                if is_s:
                    for (ch_, Gt_, Gr_) in posts:
                        sample_post(ch_, Gt_, Gr_)
                else:
                    prompt_post(posts)
            if is_s:
                for s in range(16):
                    out_dma(lambda e, s=s: e.dma_start(out=convs_d[s, 22:30, :], in_=MS[0][s * 8:s * 8 + 8, :]), [("MS", 0)], "o_convs")
                out_dma(lambda e: e.dma_start(out=convs_d[:, 0:22, :], in_=sc_d[:, 8:30, :]), [], "o_convs")
            if ti == LASTP:
                for ch in range(16):
                    bt = nb()
                    A("pe", lambda e, bt=bt, ch=ch: e.transpose(out=banks[bt][0:30, 0:128], in_=GO[:, ch, :], identity=ident_f[:, :]),
                      reads=["GO", "ident_f"], writes=[("bk", bt)])
                    A("act", lambda e, bt=bt, ch=ch: e.copy(out=MS[0][0:30, ch * 128:(ch + 1) * 128], in_=banks[bt][0:30, 0:128]),
                      reads=[("bk", bt)], writes=[("MS", 0)])
                out_dma(lambda e: e.dma_start(out=convp_d[:, :], in_=MS[0][0:30, :]), [("MS", 0)], "o_convp")
            for g in range(4):
                def evac_q(i, bk, g=g):
                    cp = g * 4 + i
                    A("dve", lambda e: e.tensor_scalar(out=qT[:, cp, 0:W], in0=banks[bk][:, 0:W], scalar1=0.125, scalar2=bq8[:, cp:cp + 1],
                                                       op0=ALU.mult, op1=ALU.add), reads=[("bk", bk), "bq8"], writes=["qT"])
                fm_group(w_in_v, (8 + g) * 512, [0, 1, 2, 3], lambda kk: xnT[:, kk, mo:mo + W], W, ["xnT"], evac_q)
            def evac_k(i, bk):
                A("act", lambda e: e.activation(out=KT[:, i, row_lo:row_lo + W], in_=banks[bk][:, 0:W], func=AF.Identity,
                                                bias=b_in_T[:, 48 + i:49 + i]), reads=[("bk", bk), "b_in_T"], writes=["KT"])
            s01 = fm_group(w_in_v, 12 * 512, [0, 1], lambda kk: xnT[:, kk, mo:mo + W], W, ["xnT"], evac_k)
            if ti == 0:
                for i in range(2):
                    bk = nb()
                    for h, s in enumerate(s01):
                        for k in range(8):
                            kk = 8 * h + k
                            A("pe", lambda e, s=s, k=k, kk=kk, i=i, bk=bk: e.matmul(
                                banks[bk][:, 0:128], lhsT=ring[s][:, k, i * 128:(i + 1) * 128], rhs=xnT[:, kk, 0:128],
                                start=(kk == 0), stop=(kk == 15)), reads=[("w", s), "xnT"], writes=[("bk", bk)])
                    A("act", lambda e, i=i, bk=bk: e.activation(out=KT[:, i, 0:128], in_=banks[bk][:, 0:128], func=AF.Identity,
                                                              bias=b_in_T[:, 48 + i:49 + i]), reads=[("bk", bk), "b_in_T"], writes=["KT"])
            if not is_s:
                wins = []
                if ti == 0:
                    wins += [(0, 0), (128, 1), (2, 0)]
                for (r0_, n) in blocks:
                    if n == 128:
                        wins.append((r0_, None))
                for (wr, mask) in wins:
                    vi = len(vp_of_row) % 8
                    vp_of_row[wr] = vi
                    last_win = (wr == 1026)
                    make_vpad(wr - row_lo + mo, vi, mask, with_k_out=last_win, s01=s01)
                    if last_win:
                        out_dma(lambda e: e.dma_start(out=kvp_d[:, :], in_=Vf[:, :]), [("tmpA", 1)], "o_kvp")
            else:
                make_vpad(0, 7, None, with_k_out=True, s01=s01)
                for s in range(16):
                    out_dma(lambda e, s=s: e.dma_start(out=ks_d[s, 120:128, :], in_=Vf[s * 8:s * 8 + 8, 0:256]), [("tmpA", 1)], "o_ks")
                    out_dma(lambda e, s=s: e.dma_start(out=vs_d[s, 120:128, :], in_=Vf[s * 8:s * 8 + 8, 256:512]), [("tmpA", 1)], "o_vs")
                out_dma(lambda e: e.dma_start(out=ks_d[:, 0:120, :], in_=ck_d[:, 8:128, :]), [], "o_ks")
                out_dma(lambda e: e.dma_start(out=vs_d[:, 0:120, :], in_=cv_d[:, 8:128, :]), [], "o_vs")
            if not is_s:
                col = 0
                for (r0_, n) in blocks:
                    if n == 2:
                        w0, w1 = 0, 128
                        o0 = (OPF, "OPF"); o1 = (OPB, "OPB"); rr0 = 1
                    else:
                        w0, w1 = r0_ - 128, r0_
                        o0 = (OPF, "OPF") if w0 == 2 else (OP1, "OP1"); o1 = (OP1, "OP1"); rr0 = 1
                    v0, v1 = vp_of_row[w0], vp_of_row[w1]
                    attn_block(col, n, rr0,
                               lambda gp, hs, w0=w0: KT[hs, gp, w0:w0 + 128], "KT",
                               lambda gp, hs, w1=w1: KT[hs, gp, w1:w1 + 128], "KT", 128,
                               VP[v0], ("VP", v0), o0, VP[v1], ("VP", v1), o1)
                    col += n
            else:
                for s in range(16):
                    j2 = s % 2
                    if j2 == 0:
                        A("pool", lambda e, s=s: e.dma_start(out=ckb[:, :, :], in_=ck_d[s:s + 2].rearrange("s k d -> k s d")), writes=[("VP", 4)], dma="ckb")
                        A("pool", lambda e, s=s: e.dma_start(out=cvbuf[:, :, :], in_=cv_d[s:s + 2].rearrange("s k d -> k s d")), writes=[("VP", 6)], dma="cvb")
                        for s2 in range(2):
                            tb = ntb()
                            for gp in range(2):
                                A("pe", lambda e, tb=tb, gp=gp, s2=s2: e.transpose(out=tbanks[tb][:, gp * 128:(gp + 1) * 128],
                                                                              in_=ckb[:, s2, gp * 128:(gp + 1) * 128], identity=ident_b[:, :]),
                                  reads=[("VP", 4), "ident_b"], writes=[("tb", tb)])
                            A("act", lambda e, tb=tb, s2=s2: e.copy(out=KcT[:, s2, :, :], in_=tbanks[tb][:, 0:256].rearrange("p (g k) -> p g k", g=2)),
                              reads=[("tb", tb)], writes=[("VP", 5)])
                            vpad_from(cvbuf[:, s2, :], ("VP", 6), VPc[s2], ("VP", s2), None)
                    bk = nb()
                    for h, sl in enumerate(s01):
                        for k in range(8):
                            kk = 8 * h + k
                            A("pe", lambda e, sl=sl, k=k, kk=kk, bk=bk, s=s: e.matmul(
                                banks[bk][0:8, 0:256], lhsT=xnT[:, kk, s * 8:s * 8 + 8], rhs=ring[sl][:, k, 256:512],
                                start=(kk == 0), stop=(kk == 15)), reads=[("w", sl), "xnT"], writes=[("bk", bk)])
                    A("dve", lambda e, bk=bk: e.tensor_tensor(out=tmpA[2][0:8, 0:256], in0=banks[bk][0:8, 0:256], in1=bkvB[0:8, 256:512], op=ALU.add),
                      reads=[("bk", bk), "bkvB"], writes=[("tmpA", 2)])
                    vpad_from(tmpA[2][0:8, 0:256], ("tmpA", 2), VsP[j2], ("VP", 2 + j2), None, npart=8)
                    rs = 1154 + s * 8
                    attn_block(s * 8, 8, 1,
                               lambda gp, hs, j2=j2: KcT[hs, j2, gp, :], ("VP", 5),
                               lambda gp, hs, rs=rs: KT[hs, gp, rs:rs + 8], "KT", 8,
                               VPc[j2], ("VP", j2), (OP1, "OP1"), VsP[j2], ("VP", 2 + j2), (OP1, "OP1"))
            bsum = nb(); bsq = nb()
            for k in range(16):
                A("pe", lambda e, k=k: e.matmul(banks[bsum][:, 0:W], lhsT=ones_f[:, :], rhs=cT[:, k, 0:W], start=(k == 0), stop=(k == 15)),
                  reads=["ones_f", ("cT", k)], writes=[("bk", bsum)])
                zi = k % 2
                A("act", lambda e, k=k, zi=zi: e.activation(out=lnz[zi][:, 0:W], in_=cT[:, k, 0:W], func=AF.Square), reads=[("cT", k)], writes=[("lnz", zi)])
                A("pe", lambda e, k=k, zi=zi: e.matmul(banks[bsq][:, 0:W], lhsT=ones_f[:, :], rhs=lnz[zi][:, 0:W], start=(k == 0), stop=(k == 15)),
                  reads=["ones_f", ("lnz", zi)], writes=[("bk", bsq)])
            A("dve", lambda e: e.tensor_scalar(out=lnm[:, 0:W], in0=banks[bsum][:, 0:W], scalar1=1.0 / D, scalar2=None, op0=ALU.mult),
              reads=[("bk", bsum)], writes=["lnm"])
            A("dve", lambda e: e.tensor_tensor(out=lnt[:, 0:W], in0=lnm[:, 0:W], in1=lnm[:, 0:W], op=ALU.mult), reads=["lnm"], writes=["lnt"])
            A("dve", lambda e: e.scalar_tensor_tensor(out=lnr[:, 0:W], in0=banks[bsq][:, 0:W], scalar=1.0 / D, in1=lnt[:, 0:W],
                                                      op0=ALU.mult, op1=ALU.subtract), reads=[("bk", bsq), "lnt"], writes=["lnr"])
            A("act", lambda e: e.activation(out=lnr[:, 0:W], in_=lnr[:, 0:W], func=AF.Sqrt, bias=epsT[:, 0:1]), reads=["lnr", "epsT"], writes=["lnr"])
            A("dve", lambda e: e.reciprocal(out=lnr[:, 0:W], in_=lnr[:, 0:W]), reads=["lnr"], writes=["lnr"])
            for k in range(16):
                zi = k % 2
                A("dve", lambda e, k=k, zi=zi: e.tensor_tensor(out=lnz[zi][:, 0:W], in0=cT[:, k, 0:W], in1=lnm[:, 0:W], op=ALU.subtract),
                  reads=[("cT", k), "lnm"], writes=[("lnz", zi)])
                A("dve", lambda e, k=k, zi=zi: e.tensor_tensor(out=lnz[zi][:, 0:W], in0=lnz[zi][:, 0:W], in1=lnr[:, 0:W], op=ALU.mult),
                  reads=[("lnz", zi), "lnr"], writes=[("lnz", zi)])
                A("act", lambda e, k=k, zi=zi: e.activation(out=sT[:, k, 0:W], in_=lnz[zi][:, 0:W], func=AF.Silu, scale=ln_g[:, k:k + 1], bias=ln_b[:, k:k + 1]),
                  reads=[("lnz", zi), "ln_g", "ln_b"], writes=["sT"])
            for cg in range(4):
                def ev_co(i, bk):
                    A("act", lambda e: e.copy(out=CO[:, i, 0:W], in_=banks[bk][:, 0:W]), reads=[("bk", bk)], writes=["CO"])
                fm_group(w_cp_v, cg * 512, [0, 1, 2, 3], lambda kk: sT[:, kk, 0:W], W, ["sT"], ev_co)

                def ev_gc(i, bk, cg=cg):
                    ch = 52 + cg * 4 + i
                    zi = i % 2
                    A("act", lambda e: e.activation(out=sgt[zi][:, 0:W], in_=banks[bk][:, 0:W], func=AF.Sigmoid, bias=b_in_T[:, ch:ch + 1]),
                      reads=[("bk", bk), "b_in_T"], writes=[("sgt", zi)])
                    A("dve", lambda e: e.tensor_tensor(out=T1[:, i, 0:W], in0=sgt[zi][:, 0:W], in1=CO[:, i, 0:W], op=ALU.mult),
                      reads=[("sgt", zi), "CO"], writes=["T1"])
                fm_group(w_in_v, (13 + cg) * 512, [0, 1, 2, 3], lambda kk: xnT[:, kk, mo:mo + W], W, ["xnT"], ev_gc)
                fm_group(w_ap_v, cg * 512, [0, 1, 2, 3], lambda kk: oT[:, kk, 0:W], W, ["oT"], ev_co)

                def ev_ga(i, bk, cg=cg):
                    ch = 68 + cg * 4 + i
                    zi = i % 2
                    A("act", lambda e: e.activation(out=sgt[zi][:, 0:W], in_=banks[bk][:, 0:W], func=AF.Sigmoid, bias=b_in_T[:, ch:ch + 1]),
                      reads=[("bk", bk), "b_in_T"], writes=[("sgt", zi)])
                    A("dve", lambda e: e.tensor_tensor(out=sgt[zi][:, 0:W], in0=sgt[zi][:, 0:W], in1=CO[:, i, 0:W], op=ALU.mult),
                      reads=[("sgt", zi), "CO"], writes=[("sgt", zi)])
                    A("dve", lambda e: e.tensor_tensor(out=mixT[:, cg * 4 + i, 0:W], in0=sgt[zi][:, 0:W], in1=T1[:, i, 0:W], op=ALU.add),
                      reads=[("sgt", zi), "T1"], writes=["mixT"])
                fm_group(w_in_v, (17 + cg) * 512, [0, 1, 2, 3], lambda kk: xnT[:, kk, mo:mo + W], W, ["xnT"], ev_ga)
            gload(GB0, "GB0", g_post_d)
            for cg in range(4):
                s0 = wload(w_o_v[:, 0:8, cg * 512:(cg + 1) * 512]); s1 = wload(w_o_v[:, 8:16, cg * 512:(cg + 1) * 512])
                col = 0
                for bi, (r0_, n) in enumerate(blocks):
                    bk = nb()
                    for h, s in enumerate((s0, s1)):
                        for k in range(8):
                            kk = 8 * h + k
                            A("pe", lambda e, s=s, k=k, kk=kk, bk=bk, col=col, n=n: e.matmul(
                                banks[bk][0:n, :], lhsT=mixT[:, kk, col:col + n], rhs=ring[s][:, k, :], start=(kk == 0), stop=(kk == 15)),
                              reads=[("w", s), "mixT"], writes=[("bk", bk)])
                    ms_i = 2 if n == 2 else (bi - (1 if ti == 0 else 0))
                    A("act", lambda e, bk=bk, cg=cg, ms_i=ms_i, n=n: e.copy(out=MS[ms_i][0:n, cg * 512:(cg + 1) * 512], in_=banks[bk][0:n, :]),
                      reads=[("bk", bk)], writes=[("MS", ms_i)])
                    col += n
            col = 0
            for bi, (r0_, n) in enumerate(blocks):
                ms_i = 2 if n == 2 else (bi - (1 if ti == 0 else 0))
                xb, xr = xbufs[bi]
                rmsnorm_stats(MS[ms_i][0:n, :], n, 1, [("MS", ms_i)])
                A("dve", lambda e, ms_i=ms_i, n=n: e.scalar_tensor_tensor(out=MS[ms_i][0:n, :], in0=MS[ms_i][0:n, :], scalar=rstd_t[0:n, 1:2], in1=GB0[0:n, :],
                                                                        op0=ALU.mult, op1=ALU.mult), reads=[("MS", ms_i), ("rstd", 1), "GB0"], writes=[("MS", ms_i)])
                A("dve", lambda e, ms_i=ms_i, n=n, xb=xb: e.tensor_tensor(out=xb[0:n, :], in0=xb[0:n, :], in1=MS[ms_i][0:n, :], op=ALU.add),
                  reads=[xr, ("MS", ms_i)], writes=[xr])
                col += n
            gload(GB0, "GB0", g_fpre_d)
            col = 0
            for bi, (r0_, n) in enumerate(blocks):
                xb, xr = xbufs[bi]
                rmsnorm_stats(xb[0:n, :], n, 2, [xr])
                A("dve", lambda e, n=n, xb=xb: e.scalar_tensor_tensor(out=XS[0:n, :], in0=xb[0:n, :], scalar=rstd_t[0:n, 2:3], in1=GB0[0:n, :],
                                                                    op0=ALU.mult, op1=ALU.mult), reads=[xr, ("rstd", 2), "GB0"], writes=["XS"])
                transpose_to(hnT, "xnT", col, n, "XS")
                col += n
            for g in range(32):
                u_pair = {}

                def ev_up(i, bk, g=g):
                    u_pair[i] = bk
                    if i % 2 == 0:
                        return
                    j = g * 2 + i // 2
                    res = []
                    for t2, bkx in enumerate((u_pair[i - 1], bk)):
                        chn = 2 * j + t2
                        ui = cnt["u"] % 4; cnt["u"] += 1
                        Ut, Ur = Ub[ui], ("Ub", ui)
                        A("act", lambda e, Ut=Ut, bkx=bkx: e.copy(out=Ut[:, 2:2 + W], in_=banks[bkx][:, 0:W]), reads=[("bk", bkx)], writes=[Ur])
                        cvt, cvr = cvb[ui], ("cvb", ui)
                        if is_s:
                            i4 = i // 2 * 2 + t2
                            U3 = UST[:, i4, :, :]
                            A("dve", lambda e, Ut=Ut, U3=U3: e.tensor_copy(out=U3[:, :, 2:10], in_=Ut[:, 2:130].rearrange("p (s t) -> p s t", t=8)),
                              reads=[Ur, ("USTh", g)], writes=[("UST", i4)])
                            c3 = cvt[:, 0:128].rearrange("p (s t) -> p s t", t=8)
                            A("dve", lambda e, U3=U3, c3=c3, chn=chn: e.tensor_scalar(out=c3, in0=U3[:, :, 0:8], scalar1=ffn_k[:, chn, 0:1], scalar2=ffn_b[:, chn:chn + 1],
                                                                                   op0=ALU.mult, op1=ALU.add), reads=[("UST", i4), "ffn_k", "ffn_b"], writes=[cvr])
                            for tp in (1, 2):
                                A("dve", lambda e, U3=U3, c3=c3, chn=chn, tp=tp: e.scalar_tensor_tensor(out=c3, in0=U3[:, :, tp:tp + 8], scalar=ffn_k[:, chn, tp:tp + 1], in1=c3,
                                                                                                      op0=ALU.mult, op1=ALU.add), reads=[("UST", i4), "ffn_k", cvr], writes=[cvr])
                            A("act", lambda e, Ut=Ut, i4=i4: e.copy(out=UO[:, i4, 2:34].rearrange("p (s t) -> p s t", t=2),
                                                                   in_=Ut[:, 2:130].rearrange("p (s t) -> p s t", t=8)[:, :, 6:8]), reads=[Ur], writes=[("UO", i4)])
                        else:
                            if ti == 0:
                                A("dve", lambda e, Ut=Ut: e.tensor_scalar(out=Ut[:, 2:4], in0=Ut[:, 2:4], scalar1=kmask[:, 0:1], scalar2=None, op0=ALU.mult),
                                  reads=[Ur, "kmask"], writes=[Ur])
                                A("dve", lambda e, Ut=Ut: e.memset(Ut[:, 0:2], 0.0), writes=[Ur])
                            else:
                                A("dve", lambda e, Ut=Ut, chn=chn: e.tensor_copy(out=Ut[:, 0:2], in_=uhist[:, chn, :]), reads=[("uhist", chn)], writes=[Ur])
                            A("dve", lambda e, Ut=Ut, chn=chn: e.tensor_copy(out=uhist[:, chn, :], in_=Ut[:, W:W + 2]), reads=[Ur], writes=[("uhist", chn)])
                            if ti == LASTP:
                                i4 = i // 2 * 2 + t2
                                A("act", lambda e, Ut=Ut, i4=i4: e.copy(out=UO[:, i4, 0:2], in_=Ut[:, W:W + 2]), reads=[Ur], writes=[("UO", i4)])
                            A("dve", lambda e, Ut=Ut, cvt=cvt, chn=chn: e.tensor_scalar(out=cvt[:, 0:W], in0=Ut[:, 0:W], scalar1=ffn_k[:, chn, 0:1], scalar2=ffn_b[:, chn:chn + 1],
                                                                                     op0=ALU.mult, op1=ALU.add), reads=[Ur, "ffn_k", "ffn_b"], writes=[cvr])
                            for tp in (1, 2):
                                A("dve", lambda e, Ut=Ut, cvt=cvt, chn=chn, tp=tp: e.scalar_tensor_tensor(out=cvt[:, 0:W], in0=Ut[:, tp:tp + W], scalar=ffn_k[:, chn, tp:tp + 1], in1=cvt[:, 0:W],
                                                                                                        op0=ALU.mult, op1=ALU.add), reads=[Ur, "ffn_k", cvr], writes=[cvr])
                        res.append((cvt, cvr))
                    (cgt, cgr), (cvt, cvr) = res
                    gi = j % 2
                    A("act", lambda e: e.activation(out=geb[gi][:, 0:W], in_=cgt[:, 0:W], func=AF.Gelu_apprx_tanh), reads=[cgr], writes=[("geb", gi)])
                    A("dve", lambda e: e.tensor_tensor(out=actT[:, j, 0:W], in0=geb[gi][:, 0:W], in1=cvt[:, 0:W], op=ALU.mult),
                      reads=[("geb", gi), cvr], writes=["actT"])

                if is_s:
                    sj = 0
                    spq(lambda e, sj=sj, g=g: e.dma_start(out=sfb[sj][0:32, :], in_=sf_d.rearrange("s r c -> (s r) c")[:, g * 512:(g + 1) * 512]), writes=[("tmpA", 0)])
                    for i4 in range(4):
                        bt = nb()
                        A("pe", lambda e, bt=bt, i4=i4, sj=sj: e.transpose(out=banks[bt][:, 0:32], in_=sfb[sj][0:32, i4 * 128:(i4 + 1) * 128], identity=ident_f[0:32, 0:32]),
                          reads=[("tmpA", 0), "ident_f"], writes=[("bk", bt)])
                        A("act", lambda e, bt=bt, i4=i4: e.copy(out=UST[:, i4, :, 0:2], in_=banks[bt][:, 0:32].rearrange("p (s r) -> p s r", r=2)),
                          reads=[("bk", bt)], writes=[("UST", i4), ("USTh", g)])
                fm_group(w_up_v, g * 512, [0, 1, 2, 3], lambda kk: hnT[:, kk, 0:W], W, ["xnT"], ev_up)
                if ti == LASTP or is_s:
                    c_lo, c_n = (0, 2) if ti == LASTP else (2, 32)
                    oi = 0
                    for i4 in range(4):
                        bt = nb()
                        A("pe", lambda e, bt=bt, i4=i4: e.transpose(out=banks[bt][0:c_n, 0:128], in_=UO[:, i4, c_lo:c_lo + c_n], identity=ident_f[:, :]),
                          reads=[("UO", i4), "ident_f"], writes=[("bk", bt)])
                        A("act", lambda e, bt=bt, i4=i4, oi=oi: e.copy(out=uost[oi][0:c_n, i4 * 128:(i4 + 1) * 128], in_=banks[bt][0:c_n, 0:128]),
                          reads=[("bk", bt)], writes=[("tmpA", 1)])
                    if ti == LASTP:
                        out_dma(lambda e, oi=oi, g=g: e.dma_start(out=ffnp_d[:, g * 512:(g + 1) * 512], in_=uost[oi][0:2, :]), [("tmpA", 1)], "o_ffnp%d" % oi)
                    else:
                        out_dma(lambda e, oi=oi, g=g: e.dma_start(out=ffns_d[:, g * 512:(g + 1) * 512], in_=uost[oi][0:32, :]), [("tmpA", 1)], "o_ffns%d" % oi)
            gload(GB0, "GB0", g_fpost_d)
            oblocks = [(bi, r0_, n) for bi, (r0_, n) in enumerate(blocks) if n == 128]
            for cg in range(4):
                bkm = {bi: nb() for (bi, _, _) in oblocks}
                for kq in range(8):
                    s = wload(w_dn_v[:, kq * 8:(kq + 1) * 8, cg * 512:(cg + 1) * 512])
                    col = 0
                    for bi, (r0_, n) in enumerate(blocks):
                        if n == 128:
                            for k in range(8):
                                kk = kq * 8 + k
                                A("pe", lambda e, s=s, k=k, kk=kk, bi=bi, col=col, bkm=bkm: e.matmul(
                                    banks[bkm[bi]][:, :], lhsT=actT[:, kk, col:col + 128], rhs=ring[s][:, k, :], start=(kk == 0), stop=(kk == 63)),
                                  reads=[("w", s), "actT"], writes=[("bk", bkm[bi])])
                        col += n
                for (bi, r0_, n) in oblocks:
                    ms_i = bi - (1 if ti == 0 else 0)
                    A("act", lambda e, bi=bi, ms_i=ms_i, cg=cg, bkm=bkm: e.copy(out=MS[ms_i][:, cg * 512:(cg + 1) * 512], in_=banks[bkm[bi]][:, :]),
                      reads=[("bk", bkm[bi])], writes=[("MS", ms_i)])
            for (bi, r0_, n) in oblocks:
                ms_i = bi - (1 if ti == 0 else 0)
                xb, xr = xbufs[bi]
                rmsnorm_stats(MS[ms_i][:, :], 128, 3, [("MS", ms_i)])
                A("dve", lambda e, ms_i=ms_i: e.scalar_tensor_tensor(out=MS[ms_i][:, :], in0=MS[ms_i][:, :], scalar=rstd_t[:, 3:4], in1=GB0[:, :],
                                                                   op0=ALU.mult, op1=ALU.mult), reads=[("MS", ms_i), ("rstd", 3), "GB0"], writes=[("MS", ms_i)])
                A("dve", lambda e, ms_i=ms_i, xb=xb: e.tensor_tensor(out=MS[ms_i][:, :], in0=MS[ms_i][:, :], in1=xb[:, :], op=ALU.add),
                  reads=[("MS", ms_i), xr], writes=[("MS", ms_i)])
                yrow = r0_ - 130
                out_dma(lambda e, ms_i=ms_i, yrow=yrow: e.dma_start(out=y_d[yrow:yrow + 128, :], in_=MS[ms_i][:, :]), [("MS", ms_i)], "o_y%d" % ms_i)

        for ti_, (kind_, blocks_) in enumerate(TILES):
            process_tile(ti_, kind_, blocks_)

        S.finalize()
        sems = {}
        for eng in S.ENGS:
            sems[("eng", eng)] = es.enter_context(nc.semaphore("s_" + eng))
        for gname in S.dma_final:
            sems[("dma", gname)] = es.enter_context(nc.semaphore("d_" + gname))
        with nc.Block() as block:
            @block.tensor
            def _(e): S.emit("pe", e, sems)

            @block.scalar
            def _(e): S.emit("act", e, sems)

            @block.vector
            def _(e): S.emit("dve", e, sems)

            @block.gpsimd
            def _(e): S.emit("pool", e, sems)

            @block.sync
            def _(e):
                S.emit("sp", e, sems)
                for gname in out_dma_groups:
                    e.wait_ge(sems[("dma", gname)], S.dma_final[gname])
    return nc


_NC_CACHE = {}


def kernel(x_prompt, x_sample, cache_k, cache_v, state_conv, state_ffn_conv,
           norm_mix_pre, w_in, b_in, conv_dw_k, conv_dw_b, conv_ln_g, conv_ln_b, w_conv_proj,
           attn_sinks, rel_bias, w_attn_proj, w_out, norm_mix_post,
           norm_ffn_pre, w_up, ffn_dw_k, ffn_dw_b, w_down, norm_ffn_post):
    f = np.float32
    x_prompt = np.asarray(x_prompt, f); x_sample = np.asarray(x_sample, f)
    wp = win_perm()
    w_in_p = np.ascontiguousarray(np.asarray(w_in, f)[0][:, wp])
    b_in_p = np.asarray(b_in, f)[0][wp]
    b_in_T = fmT(b_in_p, 84)
    b_kv = np.ascontiguousarray(np.asarray(b_in, f)[0][6144:6656])
    arp = attn_row_perm()
    w_ap = np.ascontiguousarray(np.asarray(w_attn_proj, f)[0][arp, :])
    up = wup_perm()
    w_up_p = np.ascontiguousarray(np.asarray(w_up, f)[0][:, up])
    ffn_k_T = np.ascontiguousarray(np.asarray(ffn_dw_k, f)[0][:, up].reshape(3, 128, 128).transpose(2, 1, 0))
    ffn_b_T = fmT(np.asarray(ffn_dw_b, f)[0][up], 128)
    conv_k_T = np.ascontiguousarray(np.asarray(conv_dw_k, f)[0].reshape(31, 16, 128).transpose(2, 1, 0))
    sinks = np.asarray(attn_sinks, f)[0]
    sinkT = np.zeros((128, 16), f)
    rbp = np.zeros((32, 32), f)
    rb = np.asarray(rel_bias, f)
    for cp in range(16):
        for half in range(2):
            h = head_of(half, cp)
            sinkT[half * 64:half * 64 + 64, cp] = sinks[h]
            rbp[:, half * 16 + cp] = rb[:, h]
    bucket = rel_bucket_np(np.arange(128))
    ohb = np.zeros((32, 128), f); ohb[bucket, np.arange(128)] = 1.0
    jm = np.zeros((128, 384), f)
    for dlt in range(128):
        jm[dlt, 255 - dlt] = 1.0
    mneg = np.zeros((128, 2, 129), f)
    s_idx = np.arange(128)[:, None]; r_idx = np.arange(129)[None, :]
    mneg[:, 0, :] = np.where(s_idx >= r_idx, 0.0, NEG)
    mneg[:, 1, :] = np.where(s_idx <= r_idx - 1, 0.0, NEG)

    shared = {
        "w_in": w_in_p, "b_in_T": b_in_T, "b_kv": b_kv,
        "conv_k_T": conv_k_T, "conv_b_T": fmT(np.asarray(conv_dw_b, f)[0], 16),
        "ln_g_T": fmT(np.asarray(conv_ln_g, f)[0], 16), "ln_b_T": fmT(np.asarray(conv_ln_b, f)[0], 16),
        "w_cp": np.ascontiguousarray(np.asarray(w_conv_proj, f)[0]), "w_ap": w_ap, "w_o": np.ascontiguousarray(np.asarray(w_out, f)[0]),
        "g_pre": np.ascontiguousarray(np.asarray(norm_mix_pre, f)[0]), "g_post": np.ascontiguousarray(np.asarray(norm_mix_post, f)[0]),
        "g_fpre": np.ascontiguousarray(np.asarray(norm_ffn_pre, f)[0]), "g_fpost": np.ascontiguousarray(np.asarray(norm_ffn_post, f)[0]),
        "w_up": w_up_p, "ffn_k_T": ffn_k_T, "ffn_b_T": ffn_b_T, "w_dn": np.ascontiguousarray(np.asarray(w_down, f)[0]),
        "sinkT": sinkT, "rbp": rbp, "ohb": ohb, "jm": jm, "mneg": mneg,
    }
    in_maps = []
    for c in range(NCORES):
        b, qd = c // 4, c % 4
        T0 = 1024 * qd
        xin = np.zeros((NROWS, D), f)
        if qd > 0:
            xin[0:130] = x_prompt[b, T0 - 130:T0]
        xin[130:1154] = x_prompt[b, T0:T0 + 1024]
        xin[1154:1282] = x_sample[16 * c:16 * c + 16].reshape(128, D)
        km = np.ones((128, 2), f)
        if qd == 0:
            km[:, 0] = 0.0
            km[0:2, 1] = 0.0
        m = dict(shared)
        m["xin"] = xin; m["kmask"] = km
        m["ck"] = np.ascontiguousarray(np.asarray(cache_k, f)[0, 16 * c:16 * c + 16].reshape(16, 128, 256))
        m["cv"] = np.ascontiguousarray(np.asarray(cache_v, f)[0, 16 * c:16 * c + 16].reshape(16, 128, 256))
        m["sc"] = np.ascontiguousarray(np.asarray(state_conv, f)[0, 16 * c:16 * c + 16])
        m["sf"] = np.ascontiguousarray(np.asarray(state_ffn_conv, f)[0, 16 * c:16 * c + 16][:, :, up])
        in_maps.append(m)

    if "nc" not in _NC_CACHE:
        _NC_CACHE["nc"] = build()
    nc = _NC_CACHE["nc"]
    res = run_bass_kernel_spmd(nc, in_maps, core_ids=list(range(NCORES)))
    R = res.results
    inv_up = np.argsort(up)
    y_p = np.zeros((2, 4096, D), f); y_s = np.zeros((128, 8, D), f)
    k_p = np.zeros((1, 2, 128, 4, 64), f); v_p = np.zeros((1, 2, 128, 4, 64), f)
    conv_p = np.zeros((1, 2, 30, D), f); ffn_p = np.zeros((1, 2, 2, DFF2), f)
    k_s = np.zeros((1, 128, 128, 4, 64), f); v_s = np.zeros((1, 128, 128, 4, 64), f)
    conv_s = np.zeros((1, 128, 30, D), f); ffn_s = np.zeros((1, 128, 2, DFF2), f)
    for c in range(NCORES):
        b, qd = c // 4, c % 4
        r = R[c]
        y_p[b, 1024 * qd:1024 * qd + 1024] = r["y"][0:1024]
        y_s[16 * c:16 * c + 16] = r["y"][1024:1152].reshape(16, 8, D)
        if qd == 3:
            k_p[0, b] = r["kvp"][:, 0:256].reshape(128, 4, 64)
            v_p[0, b] = r["kvp"][:, 256:512].reshape(128, 4, 64)
            conv_p[0, b] = r["convp"]
            ffn_p[0, b] = r["ffnp"][:, inv_up]
        k_s[0, 16 * c:16 * c + 16] = r["ks"].reshape(16, 128, 4, 64)
        v_s[0, 16 * c:16 * c + 16] = r["vs"].reshape(16, 128, 4, 64)
        conv_s[0, 16 * c:16 * c + 16] = r["convs"]
        ffn_s[0, 16 * c:16 * c + 16] = r["ffns"].reshape(16, 2, DFF2)[:, :, inv_up]
    return (y_p, y_s, k_p, v_p, conv_p, ffn_p, k_s, v_s, conv_s, ffn_s)
```

```python
import contextlib
import math
import numpy as np
import concourse.bass as bass
import concourse.mybir as mybir
from concourse.bass_utils import run_bass_kernel_spmd

F32 = mybir.dt.float32
BF16 = mybir.dt.bfloat16
AF = mybir.ActivationFunctionType
ALU = mybir.AluOpType

D = 2048
NH = 32
DIN = 10752
DFF2 = 16384
EPS = 1e-6
NCORES = 8
NROWS = 1282
NEG = -30000.0


class Op:
    __slots__ = ("eng", "fn", "deps", "signal", "sigval", "dma", "idx")

    def __init__(self, eng, fn, dma=None):
        self.eng = eng; self.fn = fn; self.deps = []; self.signal = False
        self.sigval = 0; self.dma = dma; self.idx = 0


class Sched:
    ENGS = ("pe", "act", "dve", "pool", "sp")

    def __init__(self):
        self.q = {e: [] for e in self.ENGS}
        self.lastw = {}
        self.readers = {}
        self.nops = 0
        self.alias = {}

    def add(self, eng, fn, reads=(), writes=(), dma=None):
        op = Op(eng, fn, dma)
        op.idx = self.nops; self.nops += 1
        writes = list(writes)
        for w in list(writes):
            for a in self.alias.get(w, ()):
                if a not in writes: writes.append(a)
        deps = {}
        for r in reads:
            w = self.lastw.get(r)
            if w is not None: deps[id(w)] = w
        for w in writes:
            lw = self.lastw.get(w)
            if lw is not None: deps[id(lw)] = lw
            for rd in self.readers.get(w, ()):
                deps[id(rd)] = rd
        for d in deps.values():
            if d is op: continue
            if d.dma is None and op.dma is None and d.eng == eng and eng == "pe":
                continue
            op.deps.append(d)
            d.signal = True
        for w in writes:
            self.lastw[w] = op
            self.readers[w] = []
        for r in reads:
            if r in writes: continue
            self.readers.setdefault(r, []).append(op)
        self.q[eng].append(op)
        return op

    def finalize(self):
        for e in self.ENGS:
            cnt = 0
            for op in self.q[e]:
                if op.dma is None and op.signal:
                    cnt += 1; op.sigval = cnt
        gc = {}
        allops = sorted([op for e in self.ENGS for op in self.q[e] if op.dma is not None], key=lambda o: o.idx)
        for op in allops:
            gc[op.dma] = gc.get(op.dma, 0) + 16
            op.sigval = gc[op.dma]
        self.dma_final = gc

    def emit(self, eng, e, sems):
        waited = {}
        for op in self.q[eng]:
            need = {}
            for d in op.deps:
                key = ("dma", d.dma) if d.dma is not None else ("eng", d.eng)
                if need.get(key, 0) < d.sigval: need[key] = d.sigval
            for key, val in need.items():
                if waited.get(key, 0) < val:
                    e.wait_ge(sems[key], val)
                    waited[key] = val
            ins = op.fn(e)
            if op.dma is not None:
                ins.then_inc(sems[("dma", op.dma)], 16)
            elif op.signal:
                ins.then_inc(sems[("eng", eng)], 1)


def head_of(half, cp):
    if cp < 8:
        return cp if half == 0 else 8 + cp
    return 16 + (cp - 8) if half == 0 else 24 + (cp - 8)


def win_perm():
    idx = []
    for c in range(16):
        idx += list(range(c * 128, c * 128 + 128))
        idx += list(range(2048 + c * 128, 2048 + c * 128 + 128))
    for cp in range(16):
        for half in range(2):
            h = head_of(half, cp)
            idx += list(range(4096 + h * 64, 4096 + h * 64 + 64))
    idx += list(range(6144, 6144 + 512))
    idx += list(range(6656, 6656 + 4096))
    return np.array(idx, dtype=np.int64)


def attn_row_perm():
    idx = []
    for cp in range(16):
        for half in range(2):
            h = head_of(half, cp)
            idx += list(range(h * 64, h * 64 + 64))
    return np.array(idx, dtype=np.int64)


def wup_perm():
    idx = []
    for j in range(64):
        idx += list(range(j * 128, j * 128 + 128))
        idx += list(range(8192 + j * 128, 8192 + j * 128 + 128))
    return np.array(idx, dtype=np.int64)


def rel_bucket_np(dist):
    max_exact = 16
    d = np.maximum(dist, 1).astype(np.float32)
    large = max_exact + (np.log(d / max_exact) / math.log(128 / max_exact) * (32 - max_exact)).astype(np.int32)
    large = np.minimum(large, 31)
    return np.where(dist < max_exact, dist, large)


def fmT(v, nch):
    return np.ascontiguousarray(v.reshape(nch, 128).T)


TILES = [("s", [(1154, 128)]),
         ("p", [(128, 2), (130, 128), (258, 128)]), ("p", [(386, 128), (514, 128)]), ("p", [(642, 128), (770, 128)]),
         ("p", [(898, 128), (1026, 128)])]
WMAX = 258
EXT = 128


def build():
    nc = bass.Bass("TRN2", target_bir_lowering=False)

    def din(name, shape, dt=F32):
        return nc.dram_tensor(name, list(shape), dt, kind="ExternalInput").ap()

    def dout(name, shape, dt=F32):
        return nc.dram_tensor(name, list(shape), dt, kind="ExternalOutput").ap()

    xin = din("xin", [NROWS, D])
    kmask_d = din("kmask", [128, 2])
    ck_d = din("ck", [16, 128, 256]); cv_d = din("cv", [16, 128, 256])
    sc_d = din("sc", [16, 30, D]); sf_d = din("sf", [16, 2, DFF2])
    w_in_d = din("w_in", [D, DIN]); b_in_T_d = din("b_in_T", [128, 84]); b_kv_d = din("b_kv", [512])
    conv_k_d = din("conv_k_T", [128, 16, 31]); conv_b_d = din("conv_b_T", [128, 16])
    ln_g_d = din("ln_g_T", [128, 16]); ln_b_d = din("ln_b_T", [128, 16])
    w_cp_d = din("w_cp", [D, D]); w_ap_d = din("w_ap", [D, D]); w_o_d = din("w_o", [D, D])
    g_pre_d = din("g_pre", [D]); g_post_d = din("g_post", [D]); g_fpre_d = din("g_fpre", [D]); g_fpost_d = din("g_fpost", [D])
    w_up_d = din("w_up", [D, DFF2]); ffn_k_d = din("ffn_k_T", [128, 128, 3]); ffn_b_d = din("ffn_b_T", [128, 128])
    w_dn_d = din("w_dn", [8192, D])
    sink_d = din("sinkT", [128, 16]); rbp_d = din("rbp", [32, 32])
    ohb_d = din("ohb", [32, 128]); jm_d = din("jm", [128, 384]); mneg_d = din("mneg", [128, 2, 129])

    y_d = dout("y", [1152, D])
    kvp_d = dout("kvp", [128, 512])
    convp_d = dout("convp", [30, D])
    ffnp_d = dout("ffnp", [2, DFF2])
    ks_d = dout("ks", [16, 128, 256]); vs_d = dout("vs", [16, 128, 256])
    convs_d = dout("convs", [16, 30, D])
    ffns_d = dout("ffns", [32, DFF2])

    S = Sched()
    out_dma_groups = []

    with contextlib.ExitStack() as es:
        def sb(name, shape, dt=F32):
            return es.enter_context(nc.sbuf_tensor(name, list(shape), dt))

        def ps(name, shape, dt=F32):
            return es.enter_context(nc.psum_tensor(name, list(shape), dt))

        A = S.add
        NBK = 6
        banks = [ps("bk%d" % i, [128, 512], F32) for i in range(NBK)]
        tbanks = [ps("tb%d" % i, [128, 1024], BF16) for i in range(2)]
        st = {"bk": 0, "tb": 0, "ws": 0, "dq": 0}

        def nb():
            i = st["bk"]; st["bk"] = (i + 1) % NBK
            return i

        def ntb():
            i = st["tb"]; st["tb"] = (i + 1) % 2
            return i

        NSLOT = 3
        ring = [sb("ring%d" % i, [128, 8, 512], BF16) for i in range(NSLOT)]

        wscr = nc.dram_tensor("wscr", [162, 128, 4096], BF16).ap()
        st["wi"] = 0
        st["tile"] = 0

        def wload(src):
            s = st["ws"]; st["ws"] = (s + 1) % NSLOT
            idx = st["wi"]; st["wi"] += 1
            if st["tile"] == 0:
                A("pool", lambda e, s=s, src=src: e.dma_start(out=ring[s][:], in_=src), writes=[("w", s)], dma="w%d" % s)
                A("sp", lambda e, s=s, idx=idx: e.dma_start(out=wscr[idx], in_=ring[s][:].rearrange("p k c -> p (k c)")),
                  reads=[("w", s)], writes=[("wscr", idx)], dma="wb%d" % s)
            else:
                A("pool", lambda e, s=s, idx=idx: e.dma_start(out=ring[s][:].rearrange("p k c -> p (k c)"), in_=wscr[idx]),
                  reads=[("wscr", idx)], writes=[("w", s)], dma="w%d" % s)
            return s

        w_in_v = w_in_d.rearrange("(k p) c -> p k c", p=128)
        w_cp_v = w_cp_d.rearrange("(k p) c -> p k c", p=128)
        w_ap_v = w_ap_d.rearrange("(k p) c -> p k c", p=128)
        w_o_v = w_o_d.rearrange("(k p) c -> p k c", p=128)
        w_up_v = w_up_d.rearrange("(k p) c -> p k c", p=128)
        w_dn_v = w_dn_d.rearrange("(k p) c -> p k c", p=128)

        def spq(fn, reads=(), writes=(), grp=None):
            if grp is None:
                grp = "q%d" % st["dq"]; st["dq"] = (st["dq"] + 1) % 6
            return A("sp", fn, reads=reads, writes=writes, dma=grp)

        tmpA = [sb("tmpA%d" % i, [128, 512]) for i in range(3)]
        ident_f = sb("ident_f", [128, 128]); ident_b = sb("ident_b", [128, 128], BF16)
        ones_f = sb("ones_f", [128, 128])
        epsT = sb("epsT", [128, 1])
        kmask = sb("kmaskS", [128, 2])
        b_in_T = sb("b_in_TS", [128, 84]); bq8 = sb("bq8", [128, 16])
        bkvB = sb("bkvB", [128, 512])
        conv_k = sb("conv_kS", [128, 16, 31]); conv_b = sb("conv_bS", [128, 16])
        ln_g = sb("ln_gS", [128, 16]); ln_b = sb("ln_bS", [128, 16])
        ffn_k = sb("ffn_kS", [128, 128, 3]); ffn_b = sb("ffn_bS", [128, 128])
        esink = sb("esink", [128, 16])
        GB0 = sb("GB0", [128, D])
        TBL = sb("TBL", [128, 2, 32, 129], BF16)
        OP1 = sb("OP1", [128, 2, 128], BF16); OPF = sb("OPF", [128, 2, 128], BF16); OPB = sb("OPB", [128, 2, 128], BF16)

        A("dve", lambda e: e.memset(ones_f[:], 1.0), writes=["ones_f"])
        A("dve", lambda e: e.memset(epsT[:], EPS), writes=["epsT"])
        A("pool", lambda e: e.memset(ident_f[:], 0.0), writes=["ident_f"])
        A("pool", lambda e: e.affine_select(out=ident_f[:], in_=ident_f[:], pattern=[[-1, 128]], compare_op=ALU.not_equal,
                                            fill=1.0, base=0, channel_multiplier=1), reads=["ident_f"], writes=["ident_f"])
        A("dve", lambda e: e.tensor_copy(out=ident_b[:], in_=ident_f[:]), reads=["ident_f"], writes=["ident_b"])
        for (dst, src, nm) in ((kmask, kmask_d, "kmask"), (b_in_T, b_in_T_d, "b_in_T"), (conv_k, conv_k_d, "conv_k"),
                               (conv_b, conv_b_d, "conv_b"), (ln_g, ln_g_d, "ln_g"), (ln_b, ln_b_d, "ln_b"),
                               (ffn_k, ffn_k_d, "ffn_k"), (ffn_b, ffn_b_d, "ffn_b"), (esink, sink_d, "esink")):
            spq(lambda e, dst=dst, src=src: e.dma_start(out=dst[:], in_=src), writes=[nm])
        spq(lambda e: e.dma_start(out=bkvB[:], in_=b_kv_d.partition_broadcast(128)), writes=["bkvB"])
        A("act", lambda e: e.activation(out=esink[:], in_=esink[:], func=AF.Exp), reads=["esink"], writes=["esink"])
        A("dve", lambda e: e.tensor_scalar(out=bq8[:], in0=b_in_T[:, 32:48], scalar1=0.125, scalar2=None, op0=ALU.mult),
          reads=["b_in_T"], writes=["bq8"])
        for (T, nm, col) in ((OP1, "OP1", None), (OPF, "OPF", 0), (OPB, "OPB", 1)):
            A("dve", lambda e, T=T: e.memset(T[:], 0.0), writes=[nm])
            for half in range(2):
                if col is None:
                    A("dve", lambda e, T=T, half=half: e.memset(T[:, half, half * 64:half * 64 + 64], 1.0), reads=[], writes=[nm])
                else:
                    A("dve", lambda e, T=T, half=half, col=col: e.tensor_scalar(
                        out=T[:, half, half * 64:half * 64 + 64], in0=ones_f[:, 0:64], scalar1=kmask[:, col:col + 1], scalar2=None,
                        op0=ALU.mult), reads=["ones_f", "kmask"], writes=[nm])

        rbp = sb("rbpS", [32, 32]); ohb = sb("ohbS", [32, 128]); jm = tmpA[0][:, 0:384]; mneg = tmpA[1][:, 0:258].rearrange("p (a b) -> p a b", a=2)
        vec = sb("vecS", [128, 32])
        for (dst, src, nm) in ((rbp[:], rbp_d, "rbp"), (ohb[:], ohb_d, "ohb"), (jm, jm_d, ("tmpA", 0)), (mneg, mneg_d, ("tmpA", 1))):
            spq(lambda e, dst=dst, src=src: e.dma_start(out=dst, in_=src), writes=[nm])
        b0 = nb()
        A("pe", lambda e, b0=b0: e.matmul(banks[b0][:, 0:32], lhsT=ohb[:, :], rhs=rbp[:, :], start=True, stop=True),
          reads=["ohb", "rbp"], writes=[("bk", b0)])
        A("dve", lambda e, b0=b0: e.tensor_copy(out=vec[:], in_=banks[b0][:, 0:32]), reads=[("bk", b0)], writes=["vec"])
        for kc in range(2):
            for r0 in range(0, 129, 16):
                nr = min(16, 129 - r0)
                bk = nb()
                for rr in range(nr):
                    r = r0 + rr
                    start_col = (128 - r) if kc == 0 else (256 - r)
                    A("pe", lambda e, bk=bk, rr=rr, sc_=start_col: e.matmul(
                        banks[bk][:, rr * 32:(rr + 1) * 32], lhsT=jm[:, sc_:sc_ + 128], rhs=vec[:, :], start=True, stop=True),
                      reads=[("tmpA", 0), "vec"], writes=[("bk", bk)])
                A("dve", lambda e, bk=bk, kc=kc, r0=r0, nr=nr: e.tensor_tensor(
                    out=TBL[:, kc, :, r0:r0 + nr].rearrange("p h r -> p r h"),
                    in0=banks[bk][:, 0:nr * 32].rearrange("p (r h) -> p r h", h=32),
                    in1=mneg[:, kc, r0:r0 + nr].unsqueeze(2).to_broadcast([128, nr, 32]), op=ALU.add),
                  reads=[("bk", bk), ("tmpA", 1)], writes=["TBL"])

        KT = sb("KT", [128, 2, NROWS], BF16)
        xh = [sb("xh%d" % i, [128, D]) for i in range(3)]
        XS = sb("XS", [128, D], BF16)
        ssq = sb("ssq", [128, 8]); rstd_t = sb("rstd_t", [128, 8])
        xnT = sb("xnT", [128, 16, EXT + WMAX], BF16)
        hnT = xnT
        arena = sb("arena", [128, 64, WMAX], BF16)
        actT = arena
        cT = arena[:, 0:32, :].rearrange("p a b -> p (a b)").bitcast(F32).rearrange("p (k w) -> p k w", w=WMAX)
        qT = arena[:, 32:48, :]
        qTf = arena[:, 32:48, :].rearrange("p a b -> p (a b)").bitcast(F32)
        oT = arena[:, 48:64, :]
        SM = sb("SM", [128, 32, WMAX], BF16)
        sT = SM[:, 0:16, :]
        mixT = SM[:, 16:32, :]
        SMf = SM[:, :, :].rearrange("p a b -> p (a b)").bitcast(F32)
        MS = [sb("MS%d" % i, [128, D]) for i in range(3)]
        Gb = [sb("Gb%d" % i, [128, 32 + WMAX]) for i in range(4)]
        ghist = sb("ghist", [128, 16, 32])
        uhist = sb("uhist", [128, 128, 2])
        VP = [sb("VP%d" % i, [128, 4, 128], BF16) for i in range(8)]
        Vf = tmpA[1]
        Eb = [sb("Eb%d" % i, [128, 2, 512], BF16) for i in range(2)]
        rden = [tmpA[0]]
        lnm = qTf[:, 0:WMAX]; lnr = qTf[:, WMAX:2 * WMAX]; lnt = qTf[:, 2 * WMAX:3 * WMAX]
        lnz = [qTf[:, (3 + i) * WMAX:(4 + i) * WMAX] for i in range(2)]
        CO = cT[:, 0:4, :]; T1 = cT[:, 4:8, :]; sgt = [cT[:, 8 + i, :] for i in range(2)]
        Ub = [SMf[:, i * (WMAX + 2):(i + 1) * (WMAX + 2)] for i in range(4)]
        _o = 4 * (WMAX + 2)
        cvb = [SMf[:, _o + i * WMAX:_o + (i + 1) * WMAX] for i in range(4)]
        geb = [SMf[:, _o + (4 + i) * WMAX:_o + (5 + i) * WMAX] for i in range(2)]
        S.alias["CO"] = [("cT", k) for k in range(4)]
        S.alias["T1"] = [("cT", k) for k in range(4, 8)]
        for i in range(2):
            S.alias[("sgt", i)] = [("cT", 8 + i)]
            S.alias[("cT", 8 + i)] = [("sgt", i)]
        for k in range(4):
            S.alias[("cT", k)] = ["CO"]
            S.alias[("cT", 4 + k)] = ["T1"]
        lnn = ["lnm", "lnr", "lnt", ("lnz", 0), ("lnz", 1)]
        S.alias["qT"] = list(lnn)
        for nme in lnn:
            S.alias[nme] = ["qT"]
        s5n = [("Ub", i) for i in range(4)] + [("cvb", i) for i in range(4)] + [("geb", i) for i in range(2)]
        S.alias["sT"] = list(s5n); S.alias["mixT"] = list(s5n)
        for nme in s5n:
            S.alias[nme] = ["sT", "mixT"]
        ckb = VP[4][:, :, :].rearrange("p g c -> p (g c)").rearrange("p (s d) -> p s d", s=2)
        KcT = VP[5][:, :, :].rearrange("p (a b) c -> p a b c", a=2)
        cvbuf = VP[6][:, :, :].rearrange("p g c -> p (g c)").rearrange("p (s d) -> p s d", s=2)
        VPc = [VP[0], VP[1]]
        VsP = [VP[2], VP[3]]
        GS = sb("GS", [128, 16, 38]); scb = [tmpA[2][:, :].rearrange("p (a c) -> p a c", a=4)]
        sfb = [tmpA[0]]
        UST = sb("UST", [128, 4, 16, 10])
        UO = sb("UO", [128, 4, 34]); uost = [tmpA[1]]
        GO = GS[:, :, 0:30]
        S.alias["GO"] = ["GS"]; S.alias["GS"] = ["GO"]
        cnt = {"g": 0, "e": 0, "u": 0, "o": 0}

        print("SBUF bytes remaining after alloc:", nc.sbuf_bytes_remaining)
        def rmsnorm_stats(src_ap, n, col, rd):
            A("dve", lambda e: e.memset(ssq[0:n, col:col + 1], 0.0), writes=[("ssq", col)])
            A("act", lambda e: e.activation(out=XS[0:n, :], in_=src_ap, func=AF.Square, accum_out=ssq[0:n, col:col + 1]),
              reads=rd + [("ssq", col)], writes=["XS", ("ssq", col)])
            A("act", lambda e: e.activation(out=rstd_t[0:n, col:col + 1], in_=ssq[0:n, col:col + 1], func=AF.Sqrt,
                                            scale=1.0 / D, bias=epsT[0:n, 0:1]), reads=[("ssq", col), "epsT"], writes=[("rstd", col)])
            A("dve", lambda e: e.reciprocal(out=rstd_t[0:n, col:col + 1], in_=rstd_t[0:n, col:col + 1]),
              reads=[("rstd", col)], writes=[("rstd", col)])

        def transpose_to(dstT, dst_res, col0, n, src_res):
            for half in range(2):
                tb = ntb()
                for k in range(8):
                    kk = half * 8 + k
                    A("pe", lambda e, tb=tb, k=k, kk=kk: e.transpose(out=tbanks[tb][:, k * 128:k * 128 + n],
                                                                  in_=XS[0:n, kk * 128:(kk + 1) * 128], identity=ident_b[0:n, 0:n]),
                      reads=[src_res, "ident_b"], writes=[("tb", tb)])
                A("act", lambda e, tb=tb, half=half: e.copy(
                    out=dstT[:, half * 8:half * 8 + 8, col0:col0 + n],
                    in_=tbanks[tb][:, :].rearrange("p (k c) -> p k c", c=128)[:, :, 0:n]),
                  reads=[("tb", tb)], writes=[dst_res])

        def gload(GBt, nm, src):
            spq(lambda e: e.dma_start(out=GBt[:], in_=src.partition_broadcast(128)), writes=[nm])

        def stage0_block(row0, n, dst_col, xbuf, xres):
            spq(lambda e: e.dma_start(out=xbuf[0:n, :], in_=xin[row0:row0 + n, :]), writes=[xres])
            rmsnorm_stats(xbuf[0:n, :], n, 0, [xres])
            A("dve", lambda e: e.scalar_tensor_tensor(out=XS[0:n, :], in0=xbuf[0:n, :], scalar=rstd_t[0:n, 0:1], in1=GB0[0:n, :],
                                                      op0=ALU.mult, op1=ALU.mult), reads=[xres, ("rstd", 0), "GB0"], writes=["XS"])
            transpose_to(xnT, "xnT", dst_col, n, "XS")

        def fm_group(wv, c0, chunks, rhs_fn, N, rhs_res, evac):
            s0 = wload(wv[:, 0:8, c0:c0 + 512]); s1 = wload(wv[:, 8:16, c0:c0 + 512])
            bks = {i: nb() for i in chunks}
            for h, s in enumerate((s0, s1)):
                for i in chunks:
                    for k in range(8):
                        kk = 8 * h + k
                        A("pe", lambda e, s=s, i=i, k=k, kk=kk: e.matmul(
                            banks[bks[i]][:, 0:N], lhsT=ring[s][:, k, i * 128:(i + 1) * 128], rhs=rhs_fn(kk),
                            start=(kk == 0), stop=(kk == 15)), reads=[("w", s)] + rhs_res, writes=[("bk", bks[i])])
            for i in chunks:
                evac(i, bks[i])
            return s0, s1

        def make_vpad(win_cols, vp_idx, mask, with_k_out=False, s01=None):
            s0, s1 = s01
            bk = nb()
            c_lo = 0 if with_k_out else 256
            for h, s in enumerate((s0, s1)):
                for k in range(8):
                    kk = 8 * h + k
                    A("pe", lambda e, s=s, k=k, kk=kk: e.matmul(
                        banks[bk][:, c_lo:512], lhsT=xnT[:, kk, win_cols:win_cols + 128], rhs=ring[s][:, k, c_lo:512],
                        start=(kk == 0), stop=(kk == 15)), reads=[("w", s), "xnT"], writes=[("bk", bk)])
            A("dve", lambda e: e.tensor_tensor(out=Vf[:, c_lo:512], in0=banks[bk][:, c_lo:512], in1=bkvB[:, c_lo:512], op=ALU.add),
              reads=[("bk", bk), "bkvB"], writes=[("tmpA", 1)])
            vpad_from(Vf[:, 256:512], ("tmpA", 1), VP[vp_idx], ("VP", vp_idx), mask)

        def vpad_from(src, src_res, dstT, dst_res, mask, npart=128):
            A("dve", lambda e: e.memset(dstT[0:npart], 0.0), writes=[dst_res])
            for par in range(2):
                sv = src.rearrange("p (g t d) -> p g t d", g=2, t=2)[:, :, par, :]
                dv = dstT[0:npart].rearrange("p (g t) c -> p g t c", t=2)[:, :, par, par * 64:par * 64 + 64]
                if mask is None:
                    A("dve", lambda e, sv=sv, dv=dv: e.tensor_copy(out=dv, in_=sv), reads=[src_res], writes=[dst_res])
                else:
                    A("dve", lambda e, sv=sv, dv=dv: e.tensor_scalar(out=dv, in0=sv, scalar1=kmask[0:npart, mask:mask + 1], scalar2=None,
                                                                 op0=ALU.mult), reads=[src_res, "kmask"], writes=[dst_res])

        def attn_block(qc0, n, r0, k0_fn, k0_res, k1_fn, k1_res, nk1, vp0, vp0_res, op0, vp1, vp1_res, op1):
            CB = 4 if n > 64 else 8
            def do_gc(gp, cb):
                if True:
                    cp0 = gp * 8 + cb * CB
                    ebs = []
                    for half in range(2):
                        eb = Eb[cnt["e"] % 2]; ebr = ("Eb", cnt["e"] % 2); cnt["e"] += 1
                        ebs.append((eb, ebr))
                        hs = slice(half * 64, half * 64 + 64)
                        for kc in range(2):
                            nk = 128 if kc == 0 else nk1
                            kfn, kres = (k0_fn, k0_res) if kc == 0 else (k1_fn, k1_res)
                            bk = nb()
                            A("pe", lambda e, bk=bk, kfn=kfn, hs=hs, nk=nk: e.matmul(
                                banks[bk][0:nk, 0:CB * n].rearrange("p (c q) -> p c q", q=n), lhsT=kfn(gp, hs),
                                rhs=qT[hs, cp0:cp0 + CB, qc0:qc0 + n], start=True, stop=False),
                              reads=[kres, "qT"], writes=[("bk", bk)])
                            A("pe", lambda e, bk=bk, kc=kc, nk=nk, half=half: e.matmul(
                                banks[bk][0:nk, 0:CB * n].rearrange("p (c q) -> p c q", q=n), lhsT=ident_b[0:nk, 0:nk],
                                rhs=TBL[0:nk, kc, half * 16 + cp0:half * 16 + cp0 + CB, r0:r0 + n], start=False, stop=True),
                              reads=["ident_b", "TBL"], writes=[("bk", bk)])
                            A("act", lambda e, bk=bk, kc=kc, nk=nk, eb=eb: e.activation(
                                out=eb[0:nk, kc, 0:CB * n], in_=banks[bk][0:nk, 0:CB * n], func=AF.Exp),
                              reads=[("bk", bk)], writes=[ebr])
                    bo = nb(); bd = nb()
                    first = True
                    for half in range(2):
                        eb, ebr = ebs[half]
                        for kc in range(2):
                            nk = 128 if kc == 0 else nk1
                            vp, vpr, opt = (vp0, vp0_res, op0) if kc == 0 else (vp1, vp1_res, op1)
                            last = (half == 1 and kc == 1)
                            A("pe", lambda e, vp=vp, nk=nk, eb=eb, kc=kc, half=half, first=first, last=last: e.matmul(
                                banks[bo][:, 0:CB * n], lhsT=vp[0:nk, 2 * gp + half, :], rhs=eb[0:nk, kc, 0:CB * n],
                                start=first, stop=last), reads=[vpr, ebr], writes=[("bk", bo)])
                            A("pe", lambda e, opt=opt, nk=nk, eb=eb, kc=kc, half=half, first=first, last=last: e.matmul(
                                banks[bd][:, 0:CB * n], lhsT=opt[0][0:nk, half, :], rhs=eb[0:nk, kc, 0:CB * n],
                                start=first, stop=last), reads=[opt[1], ebr], writes=[("bk", bd)])
                            first = False
                    rd = rden[0]; rdr = ("tmpA", 0)
                    for i in range(CB):
                        A("dve", lambda e, i=i: e.tensor_scalar(out=rd[:, i * n:(i + 1) * n], in0=banks[bd][:, i * n:(i + 1) * n],
                                                               scalar1=esink[:, cp0 + i:cp0 + i + 1], scalar2=None, op0=ALU.add),
                          reads=[("bk", bd), "esink"], writes=[rdr])
                    A("dve", lambda e: e.reciprocal(out=rd[:, 0:CB * n], in_=rd[:, 0:CB * n]), reads=[rdr], writes=[rdr])
                    A("dve", lambda e: e.tensor_tensor(
                        out=oT[:, cp0:cp0 + CB, qc0:qc0 + n], in0=banks[bo][:, 0:CB * n].rearrange("p (c q) -> p c q", q=n),
                        in1=rd[:, 0:CB * n].rearrange("p (c q) -> p c q", q=n), op=ALU.mult),
                      reads=[("bk", bo), rdr], writes=["oT"])
            for gp_ in range(2):
                for cb_ in range(8 // CB):
                    do_gc(gp_, cb_)

        def out_dma(fn, reads, grp):
            if grp not in out_dma_groups:
                out_dma_groups.append(grp)
            A("sp", fn, reads=reads, dma=grp)

        vp_of_row = {}
        def process_tile(ti, kind, blocks):
            st["tile"] = ti; st["wi"] = 0
            hp = (blocks[0] == (128, 2))
            lastp = (kind == "p" and blocks[-1][0] == 1026)
            W = sum(n for _, n in blocks)
            row_lo = blocks[0][0]
            mo = EXT if hp else 0
            is_s = (kind == "s")
            gload(GB0, "GB0", g_pre_d)
            if hp:
                stage0_block(0, 128, 0, xh[2], ("xh", 2))
            col = mo
            xbufs = {}
            for bi, (r0_, n) in enumerate(blocks):
                slot = 2 if n == 2 else (bi - (1 if hp else 0))
                xb, xr = xh[slot], ("xh", slot)
                xbufs[bi] = (xb, xr)
                stage0_block(r0_, n, col, xb, xr)
                col += n
            ge = 32 if hp else 0
            NG = ge + W
            Wc = W
            def sample_post(ch, Gt, Gr):
                sj = 0
                spq(lambda e: e.dma_start(out=scb[sj][0:120, :, :], in_=sc_d.rearrange("(a s) r c -> (s r) a c", a=4)[:, :, ch * 128:(ch + 1) * 128]),
                    writes=[("tmpA", 2)])
                for a4 in range(4):
                    bt = nb()
                    A("pe", lambda e, a4=a4, bt=bt: e.transpose(out=banks[bt][:, 0:120], in_=scb[sj][0:120, a4, :], identity=ident_f[0:120, 0:120]),
                      reads=[("tmpA", 2), "ident_f"], writes=[("bk", bt)])
                    A("act", lambda e, a4=a4, bt=bt: e.copy(out=GS[:, a4 * 4:a4 * 4 + 4, 0:30],
                                                           in_=banks[bt][:, 0:120].rearrange("p (s r) -> p s r", r=30)),
                      reads=[("bk", bt)], writes=["GS"])
                A("dve", lambda e: e.tensor_copy(out=GS[:, :, 30:38], in_=Gt[:, 32:32 + 128].rearrange("p (s t) -> p s t", t=8)),
                  reads=[Gr], writes=["GS"])
                cv3 = cT[:, ch, 0:128].rearrange("p (s t) -> p s t", t=8)
                A("dve", lambda e: e.tensor_scalar(out=cv3, in0=GS[:, :, 0:8], scalar1=conv_k[:, ch, 0:1], scalar2=conv_b[:, ch:ch + 1],
                                                   op0=ALU.mult, op1=ALU.add), reads=["GS", "conv_k", "conv_b"], writes=[("cT", ch)])
                for j in range(1, 31):
                    A("dve", lambda e, j=j: e.scalar_tensor_tensor(out=cv3, in0=GS[:, :, j:j + 8], scalar=conv_k[:, ch, j:j + 1], in1=cv3,
                                                                  op0=ALU.mult, op1=ALU.add), reads=["GS", "conv_k", ("cT", ch)], writes=[("cT", ch)])
                bt = nb()
                A("pe", lambda e, bt=bt: e.transpose(out=banks[bt][:, 0:128], in_=Gt[:, 32:160], identity=ident_f[:, :]),
                  reads=[Gr, "ident_f"], writes=[("bk", bt)])
                A("act", lambda e, bt=bt: e.copy(out=MS[0][:, ch * 128:(ch + 1) * 128], in_=banks[bt][:, 0:128]),
                  reads=[("bk", bt)], writes=[("MS", 0)])

            def prompt_post(plist):
                ce = "dve"
                for (ch, Gt, Gr) in plist:
                    if hp:
                        A("dve", lambda e, Gt=Gt: e.tensor_scalar(out=Gt[:, 0:34], in0=Gt[:, 0:34], scalar1=kmask[:, 0:1], scalar2=None, op0=ALU.mult),
                          reads=[Gr, "kmask"], writes=[Gr])
                    else:
                        A("dve", lambda e, Gt=Gt, ch=ch: e.tensor_copy(out=Gt[:, 0:32], in_=ghist[:, ch, :]), reads=[("ghist", ch)], writes=[Gr])
                    A("dve", lambda e, Gt=Gt, ch=ch: e.tensor_copy(out=ghist[:, ch, :], in_=Gt[:, W:W + 32]), reads=[Gr], writes=[("ghist", ch)])
                    if lastp:
                        A("act", lambda e, Gt=Gt, ch=ch: e.copy(out=GO[:, ch, :], in_=Gt[:, 32 + W - 30:32 + W]), reads=[Gr], writes=["GO"])
                    A(ce, lambda e, Gt=Gt, ch=ch: e.tensor_scalar(out=cT[:, ch, 0:Wc], in0=Gt[:, 2:2 + Wc], scalar1=conv_k[:, ch, 0:1], scalar2=conv_b[:, ch:ch + 1],
                                                                 op0=ALU.mult, op1=ALU.add), reads=[Gr, "conv_k", "conv_b"], writes=[("cT", ch)])
                for j in range(1, 31):
                    for (ch, Gt, Gr) in plist:
                        A(ce, lambda e, j=j, Gt=Gt, ch=ch: e.scalar_tensor_tensor(out=cT[:, ch, 0:Wc], in0=Gt[:, 2 + j:2 + j + Wc], scalar=conv_k[:, ch, j:j + 1], in1=cT[:, ch, 0:Wc],
                                                                                op0=ALU.mult, op1=ALU.add), reads=[Gr, "conv_k", ("cT", ch)], writes=[("cT", ch)])

            for g in range(8):
                s_pair = {}
                posts = []

                def evac_ab(i, bk, g=g):
                    s_pair[i] = bk
                    if i % 2 == 0:
                        return
                    ch = g * 2 + i // 2
                    ba, bb = s_pair[i - 1], bk
                    gi = cnt["g"] % 4; cnt["g"] += 1
                    Gt, Gr = Gb[gi], ("Gb", gi)
                    sg = tmpA[gi % 2]; sgr = ("tmpA", gi % 2)
                    A("act", lambda e: e.activation(out=sg[:, 0:NG], in_=banks[bb][:, 0:NG], func=AF.Sigmoid,
                                                    bias=b_in_T[:, 2 * ch + 1:2 * ch + 2]), reads=[("bk", bb), "b_in_T"], writes=[sgr])
                    g0 = 32 - ge
                    A("dve", lambda e: e.scalar_tensor_tensor(out=Gt[:, g0:g0 + NG], in0=banks[ba][:, 0:NG],
                                                              scalar=b_in_T[:, 2 * ch:2 * ch + 1], in1=sg[:, 0:NG], op0=ALU.add, op1=ALU.mult),
                      reads=[("bk", ba), "b_in_T", sgr], writes=[Gr])
                    if is_s:
                        posts.append((ch, Gt, Gr))
                        return
                    posts.append((ch, Gt, Gr))

                xc0 = mo - ge
                fm_group(w_in_v, g * 512, [0, 1, 2, 3], lambda kk, xc0=xc0, NG=NG: xnT[:, kk, xc0:xc0 + NG], NG, ["xnT"], evac_ab)
                if is_s:
                    for (ch_, Gt_, Gr_) in posts:
                        sample_post(ch_, Gt_, Gr_)
                else:
                    prompt_post(posts)
            if is_s:
                for s in range(16):
                    out_dma(lambda e, s=s: e.dma_start(out=convs_d[s, 22:30, :], in_=MS[0][s * 8:s * 8 + 8, :]), [("MS", 0)], "o_convs")
                out_dma(lambda e: e.dma_start(out=convs_d[:, 0:22, :], in_=sc_d[:, 8:30, :]), [], "o_convs")
            if lastp:
                for ch in range(16):
                    bt = nb()
                    A("pe", lambda e, bt=bt, ch=ch: e.transpose(out=banks[bt][0:30, 0:128], in_=GO[:, ch, :], identity=ident_f[:, :]),
                      reads=["GO", "ident_f"], writes=[("bk", bt)])
                    A("act", lambda e, bt=bt, ch=ch: e.copy(out=MS[0][0:30, ch * 128:(ch + 1) * 128], in_=banks[bt][0:30, 0:128]),
                      reads=[("bk", bt)], writes=[("MS", 0)])
                out_dma(lambda e: e.dma_start(out=convp_d[:, :], in_=MS[0][0:30, :]), [("MS", 0)], "o_convp")
            for g in range(4):
                def evac_q(i, bk, g=g):
                    cp = g * 4 + i
                    A("dve", lambda e: e.tensor_scalar(out=qT[:, cp, 0:W], in0=banks[bk][:, 0:W], scalar1=0.125, scalar2=bq8[:, cp:cp + 1],
                                                       op0=ALU.mult, op1=ALU.add), reads=[("bk", bk), "bq8"], writes=["qT"])
                fm_group(w_in_v, (8 + g) * 512, [0, 1, 2, 3], lambda kk: xnT[:, kk, mo:mo + W], W, ["xnT"], evac_q)
            def evac_k(i, bk):
                A("act", lambda e: e.activation(out=KT[:, i, row_lo:row_lo + W], in_=banks[bk][:, 0:W], func=AF.Identity,
                                                bias=b_in_T[:, 48 + i:49 + i]), reads=[("bk", bk), "b_in_T"], writes=["KT"])
            s01 = fm_group(w_in_v, 12 * 512, [0, 1], lambda kk: xnT[:, kk, mo:mo + W], W, ["xnT"], evac_k)
            if hp:
                for i in range(2):
                    bk = nb()
                    for h, s in enumerate(s01):
                        for k in range(8):
                            kk = 8 * h + k
                            A("pe", lambda e, s=s, k=k, kk=kk, i=i, bk=bk: e.matmul(
                                banks[bk][:, 0:128], lhsT=ring[s][:, k, i * 128:(i + 1) * 128], rhs=xnT[:, kk, 0:128],
                                start=(kk == 0), stop=(kk == 15)), reads=[("w", s), "xnT"], writes=[("bk", bk)])
                    A("act", lambda e, i=i, bk=bk: e.activation(out=KT[:, i, 0:128], in_=banks[bk][:, 0:128], func=AF.Identity,
                                                              bias=b_in_T[:, 48 + i:49 + i]), reads=[("bk", bk), "b_in_T"], writes=["KT"])
            if not is_s:
                wins = []
                if hp:
                    wins += [(0, 0), (128, 1), (2, 0)]
                for (r0_, n) in blocks:
                    if n == 128:
                        wins.append((r0_, None))
                for (wr, mask) in wins:
                    vi = len(vp_of_row) % 8
                    vp_of_row[wr] = vi
                    last_win = (wr == 1026)
                    make_vpad(wr - row_lo + mo, vi, mask, with_k_out=last_win, s01=s01)
                    if last_win:
                        out_dma(lambda e: e.dma_start(out=kvp_d[:, :], in_=Vf[:, :]), [("tmpA", 1)], "o_kvp")
            else:
                make_vpad(0, 7, None, with_k_out=True, s01=s01)
                for s in range(16):
                    out_dma(lambda e, s=s: e.dma_start(out=ks_d[s, 120:128, :], in_=Vf[s * 8:s * 8 + 8, 0:256]), [("tmpA", 1)], "o_ks")
                    out_dma(lambda e, s=s: e.dma_start(out=vs_d[s, 120:128, :], in_=Vf[s * 8:s * 8 + 8, 256:512]), [("tmpA", 1)], "o_vs")
                out_dma(lambda e: e.dma_start(out=ks_d[:, 0:120, :], in_=ck_d[:, 8:128, :]), [], "o_ks")
                out_dma(lambda e: e.dma_start(out=vs_d[:, 0:120, :], in_=cv_d[:, 8:128, :]), [], "o_vs")
            if not is_s:
                col = 0
                for (r0_, n) in blocks:
                    if n == 2:
                        w0, w1 = 0, 128
                        o0 = (OPF, "OPF"); o1 = (OPB, "OPB"); rr0 = 1
                    else:
                        w0, w1 = r0_ - 128, r0_
                        o0 = (OPF, "OPF") if w0 == 2 else (OP1, "OP1"); o1 = (OP1, "OP1"); rr0 = 1
                    v0, v1 = vp_of_row[w0], vp_of_row[w1]
                    attn_block(col, n, rr0,
                               lambda gp, hs, w0=w0: KT[hs, gp, w0:w0 + 128], "KT",
                               lambda gp, hs, w1=w1: KT[hs, gp, w1:w1 + 128], "KT", 128,
                               VP[v0], ("VP", v0), o0, VP[v1], ("VP", v1), o1)
                    col += n
            else:
                for s in range(16):
                    j2 = s % 2
                    if j2 == 0:
                        A("pool", lambda e, s=s: e.dma_start(out=ckb[:, :, :], in_=ck_d[s:s + 2].rearrange("s k d -> k s d")), writes=[("VP", 4)], dma="ckb")
                        A("pool", lambda e, s=s: e.dma_start(out=cvbuf[:, :, :], in_=cv_d[s:s + 2].rearrange("s k d -> k s d")), writes=[("VP", 6)], dma="cvb")
                        for s2 in range(2):
                            tb = ntb()
                            for gp in range(2):
                                A("pe", lambda e, tb=tb, gp=gp, s2=s2: e.transpose(out=tbanks[tb][:, gp * 128:(gp + 1) * 128],
                                                                              in_=ckb[:, s2, gp * 128:(gp + 1) * 128], identity=ident_b[:, :]),
                                  reads=[("VP", 4), "ident_b"], writes=[("tb", tb)])
                            A("act", lambda e, tb=tb, s2=s2: e.copy(out=KcT[:, s2, :, :], in_=tbanks[tb][:, 0:256].rearrange("p (g k) -> p g k", g=2)),
                              reads=[("tb", tb)], writes=[("VP", 5)])
                            vpad_from(cvbuf[:, s2, :], ("VP", 6), VPc[s2], ("VP", s2), None)
                    bk = nb()
                    for h, sl in enumerate(s01):
                        for k in range(8):
                            kk = 8 * h + k
                            A("pe", lambda e, sl=sl, k=k, kk=kk, bk=bk, s=s: e.matmul(
                                banks[bk][0:8, 0:256], lhsT=xnT[:, kk, s * 8:s * 8 + 8], rhs=ring[sl][:, k, 256:512],
                                start=(kk == 0), stop=(kk == 15)), reads=[("w", sl), "xnT"], writes=[("bk", bk)])
                    A("dve", lambda e, bk=bk: e.tensor_tensor(out=tmpA[2][0:8, 0:256], in0=banks[bk][0:8, 0:256], in1=bkvB[0:8, 256:512], op=ALU.add),
                      reads=[("bk", bk), "bkvB"], writes=[("tmpA", 2)])
                    vpad_from(tmpA[2][0:8, 0:256], ("tmpA", 2), VsP[j2], ("VP", 2 + j2), None, npart=8)
                    rs = 1154 + s * 8
                    attn_block(s * 8, 8, 1,
                               lambda gp, hs, j2=j2: KcT[hs, j2, gp, :], ("VP", 5),
                               lambda gp, hs, rs=rs: KT[hs, gp, rs:rs + 8], "KT", 8,
                               VPc[j2], ("VP", j2), (OP1, "OP1"), VsP[j2], ("VP", 2 + j2), (OP1, "OP1"))
            bsum = nb(); bsq = nb()
            for k in range(16):
                A("pe", lambda e, k=k: e.matmul(banks[bsum][:, 0:W], lhsT=ones_f[:, :], rhs=cT[:, k, 0:W], start=(k == 0), stop=(k == 15)),
                  reads=["ones_f", ("cT", k)], writes=[("bk", bsum)])
                zi = k % 2
                A("act", lambda e, k=k, zi=zi: e.activation(out=lnz[zi][:, 0:W], in_=cT[:, k, 0:W], func=AF.Square), reads=[("cT", k)], writes=[("lnz", zi)])
                A("pe", lambda e, k=k, zi=zi: e.matmul(banks[bsq][:, 0:W], lhsT=ones_f[:, :], rhs=lnz[zi][:, 0:W], start=(k == 0), stop=(k == 15)),
                  reads=["ones_f", ("lnz", zi)], writes=[("bk", bsq)])
            A("dve", lambda e: e.tensor_scalar(out=lnm[:, 0:W], in0=banks[bsum][:, 0:W], scalar1=1.0 / D, scalar2=None, op0=ALU.mult),
              reads=[("bk", bsum)], writes=["lnm"])
            A("dve", lambda e: e.tensor_tensor(out=lnt[:, 0:W], in0=lnm[:, 0:W], in1=lnm[:, 0:W], op=ALU.mult), reads=["lnm"], writes=["lnt"])
            A("dve", lambda e: e.scalar_tensor_tensor(out=lnr[:, 0:W], in0=banks[bsq][:, 0:W], scalar=1.0 / D, in1=lnt[:, 0:W],
                                                      op0=ALU.mult, op1=ALU.subtract), reads=[("bk", bsq), "lnt"], writes=["lnr"])
            A("act", lambda e: e.activation(out=lnr[:, 0:W], in_=lnr[:, 0:W], func=AF.Sqrt, bias=epsT[:, 0:1]), reads=["lnr", "epsT"], writes=["lnr"])
            A("dve", lambda e: e.reciprocal(out=lnr[:, 0:W], in_=lnr[:, 0:W]), reads=["lnr"], writes=["lnr"])
            for k in range(16):
                zi = k % 2
                A("dve", lambda e, k=k, zi=zi: e.tensor_tensor(out=lnz[zi][:, 0:W], in0=cT[:, k, 0:W], in1=lnm[:, 0:W], op=ALU.subtract),
                  reads=[("cT", k), "lnm"], writes=[("lnz", zi)])
                A("dve", lambda e, k=k, zi=zi: e.tensor_tensor(out=lnz[zi][:, 0:W], in0=lnz[zi][:, 0:W], in1=lnr[:, 0:W], op=ALU.mult),
                  reads=[("lnz", zi), "lnr"], writes=[("lnz", zi)])
                A("act", lambda e, k=k, zi=zi: e.activation(out=sT[:, k, 0:W], in_=lnz[zi][:, 0:W], func=AF.Silu, scale=ln_g[:, k:k + 1], bias=ln_b[:, k:k + 1]),
                  reads=[("lnz", zi), "ln_g", "ln_b"], writes=["sT"])
            for cg in range(4):
                def ev_co(i, bk):
                    A("act", lambda e: e.copy(out=CO[:, i, 0:W], in_=banks[bk][:, 0:W]), reads=[("bk", bk)], writes=["CO"])
                fm_group(w_cp_v, cg * 512, [0, 1, 2, 3], lambda kk: sT[:, kk, 0:W], W, ["sT"], ev_co)

                def ev_gc(i, bk, cg=cg):
                    ch = 52 + cg * 4 + i
                    zi = i % 2
                    A("act", lambda e: e.activation(out=sgt[zi][:, 0:W], in_=banks[bk][:, 0:W], func=AF.Sigmoid, bias=b_in_T[:, ch:ch + 1]),
                      reads=[("bk", bk), "b_in_T"], writes=[("sgt", zi)])
                    A("dve", lambda e: e.tensor_tensor(out=T1[:, i, 0:W], in0=sgt[zi][:, 0:W], in1=CO[:, i, 0:W], op=ALU.mult),
                      reads=[("sgt", zi), "CO"], writes=["T1"])
                fm_group(w_in_v, (13 + cg) * 512, [0, 1, 2, 3], lambda kk: xnT[:, kk, mo:mo + W], W, ["xnT"], ev_gc)
                fm_group(w_ap_v, cg * 512, [0, 1, 2, 3], lambda kk: oT[:, kk, 0:W], W, ["oT"], ev_co)

                def ev_ga(i, bk, cg=cg):
                    ch = 68 + cg * 4 + i
                    zi = i % 2
                    A("act", lambda e: e.activation(out=sgt[zi][:, 0:W], in_=banks[bk][:, 0:W], func=AF.Sigmoid, bias=b_in_T[:, ch:ch + 1]),
                      reads=[("bk", bk), "b_in_T"], writes=[("sgt", zi)])
                    A("dve", lambda e: e.tensor_tensor(out=sgt[zi][:, 0:W], in0=sgt[zi][:, 0:W], in1=CO[:, i, 0:W], op=ALU.mult),
                      reads=[("sgt", zi), "CO"], writes=[("sgt", zi)])
                    A("dve", lambda e: e.tensor_tensor(out=mixT[:, cg * 4 + i, 0:W], in0=sgt[zi][:, 0:W], in1=T1[:, i, 0:W], op=ALU.add),
                      reads=[("sgt", zi), "T1"], writes=["mixT"])
                fm_group(w_in_v, (17 + cg) * 512, [0, 1, 2, 3], lambda kk: xnT[:, kk, mo:mo + W], W, ["xnT"], ev_ga)
            gload(GB0, "GB0", g_post_d)
            for cg in range(4):
                s0 = wload(w_o_v[:, 0:8, cg * 512:(cg + 1) * 512]); s1 = wload(w_o_v[:, 8:16, cg * 512:(cg + 1) * 512])
                col = 0
                for bi, (r0_, n) in enumerate(blocks):
                    bk = nb()
                    for h, s in enumerate((s0, s1)):
                        for k in range(8):
                            kk = 8 * h + k
                            A("pe", lambda e, s=s, k=k, kk=kk, bk=bk, col=col, n=n: e.matmul(
                                banks[bk][0:n, :], lhsT=mixT[:, kk, col:col + n], rhs=ring[s][:, k, :], start=(kk == 0), stop=(kk == 15)),
                              reads=[("w", s), "mixT"], writes=[("bk", bk)])
                    ms_i = 2 if n == 2 else (bi - (1 if hp else 0))
                    A("act", lambda e, bk=bk, cg=cg, ms_i=ms_i, n=n: e.copy(out=MS[ms_i][0:n, cg * 512:(cg + 1) * 512], in_=banks[bk][0:n, :]),
                      reads=[("bk", bk)], writes=[("MS", ms_i)])
                    col += n
            col = 0
            for bi, (r0_, n) in enumerate(blocks):
                ms_i = 2 if n == 2 else (bi - (1 if hp else 0))
                xb, xr = xbufs[bi]
                rmsnorm_stats(MS[ms_i][0:n, :], n, 1, [("MS", ms_i)])
                A("dve", lambda e, ms_i=ms_i, n=n: e.scalar_tensor_tensor(out=MS[ms_i][0:n, :], in0=MS[ms_i][0:n, :], scalar=rstd_t[0:n, 1:2], in1=GB0[0:n, :],
                                                                        op0=ALU.mult, op1=ALU.mult), reads=[("MS", ms_i), ("rstd", 1), "GB0"], writes=[("MS", ms_i)])
                A("dve", lambda e, ms_i=ms_i, n=n, xb=xb: e.tensor_tensor(out=xb[0:n, :], in0=xb[0:n, :], in1=MS[ms_i][0:n, :], op=ALU.add),
                  reads=[xr, ("MS", ms_i)], writes=[xr])
                col += n
            gload(GB0, "GB0", g_fpre_d)
            col = 0
            for bi, (r0_, n) in enumerate(blocks):
                xb, xr = xbufs[bi]
                rmsnorm_stats(xb[0:n, :], n, 2, [xr])
                A("dve", lambda e, n=n, xb=xb: e.scalar_tensor_tensor(out=XS[0:n, :], in0=xb[0:n, :], scalar=rstd_t[0:n, 2:3], in1=GB0[0:n, :],
                                                                    op0=ALU.mult, op1=ALU.mult), reads=[xr, ("rstd", 2), "GB0"], writes=["XS"])
                transpose_to(hnT, "xnT", col, n, "XS")
                col += n
            for g in range(32):
                u_pair = {}

                def ev_up(i, bk, g=g):
                    u_pair[i] = bk
                    if i % 2 == 0:
                        return
                    j = g * 2 + i // 2
                    res = []
                    for t2, bkx in enumerate((u_pair[i - 1], bk)):
                        chn = 2 * j + t2
                        ui = cnt["u"] % 4; cnt["u"] += 1
                        Ut, Ur = Ub[ui], ("Ub", ui)
                        A("act", lambda e, Ut=Ut, bkx=bkx: e.copy(out=Ut[:, 2:2 + W], in_=banks[bkx][:, 0:W]), reads=[("bk", bkx)], writes=[Ur])
                        cvt, cvr = cvb[ui], ("cvb", ui)
                        if is_s:
                            i4 = i // 2 * 2 + t2
                            U3 = UST[:, i4, :, :]
                            A("dve", lambda e, Ut=Ut, U3=U3: e.tensor_copy(out=U3[:, :, 2:10], in_=Ut[:, 2:130].rearrange("p (s t) -> p s t", t=8)),
                              reads=[Ur, ("USTh", g)], writes=[("UST", i4)])
                            c3 = cvt[:, 0:128].rearrange("p (s t) -> p s t", t=8)
                            A("dve", lambda e, U3=U3, c3=c3, chn=chn: e.tensor_scalar(out=c3, in0=U3[:, :, 0:8], scalar1=ffn_k[:, chn, 0:1], scalar2=ffn_b[:, chn:chn + 1],
                                                                                   op0=ALU.mult, op1=ALU.add), reads=[("UST", i4), "ffn_k", "ffn_b"], writes=[cvr])
                            for tp in (1, 2):
                                A("dve", lambda e, U3=U3, c3=c3, chn=chn, tp=tp: e.scalar_tensor_tensor(out=c3, in0=U3[:, :, tp:tp + 8], scalar=ffn_k[:, chn, tp:tp + 1], in1=c3,
                                                                                                      op0=ALU.mult, op1=ALU.add), reads=[("UST", i4), "ffn_k", cvr], writes=[cvr])
                            A("act", lambda e, Ut=Ut, i4=i4: e.copy(out=UO[:, i4, 2:34].rearrange("p (s t) -> p s t", t=2),
                                                                   in_=Ut[:, 2:130].rearrange("p (s t) -> p s t", t=8)[:, :, 6:8]), reads=[Ur], writes=[("UO", i4)])
                        else:
                            if hp:
                                A("dve", lambda e, Ut=Ut: e.tensor_scalar(out=Ut[:, 2:4], in0=Ut[:, 2:4], scalar1=kmask[:, 0:1], scalar2=None, op0=ALU.mult),
                                  reads=[Ur, "kmask"], writes=[Ur])
                                A("dve", lambda e, Ut=Ut: e.memset(Ut[:, 0:2], 0.0), writes=[Ur])
                            else:
                                A("dve", lambda e, Ut=Ut, chn=chn: e.tensor_copy(out=Ut[:, 0:2], in_=uhist[:, chn, :]), reads=[("uhist", chn)], writes=[Ur])
                            A("dve", lambda e, Ut=Ut, chn=chn: e.tensor_copy(out=uhist[:, chn, :], in_=Ut[:, W:W + 2]), reads=[Ur], writes=[("uhist", chn)])
                            if lastp:
                                i4 = i // 2 * 2 + t2
                                A("act", lambda e, Ut=Ut, i4=i4: e.copy(out=UO[:, i4, 0:2], in_=Ut[:, W:W + 2]), reads=[Ur], writes=[("UO", i4)])
                            A("dve", lambda e, Ut=Ut, cvt=cvt, chn=chn: e.tensor_scalar(out=cvt[:, 0:W], in0=Ut[:, 0:W], scalar1=ffn_k[:, chn, 0:1], scalar2=ffn_b[:, chn:chn + 1],
                                                                                     op0=ALU.mult, op1=ALU.add), reads=[Ur, "ffn_k", "ffn_b"], writes=[cvr])
                            for tp in (1, 2):
                                A("dve", lambda e, Ut=Ut, cvt=cvt, chn=chn, tp=tp: e.scalar_tensor_tensor(out=cvt[:, 0:W], in0=Ut[:, tp:tp + W], scalar=ffn_k[:, chn, tp:tp + 1], in1=cvt[:, 0:W],
                                                                                                        op0=ALU.mult, op1=ALU.add), reads=[Ur, "ffn_k", cvr], writes=[cvr])
                        res.append((cvt, cvr))
                    (cgt, cgr), (cvt, cvr) = res
                    gi = j % 2
                    A("act", lambda e: e.activation(out=geb[gi][:, 0:W], in_=cgt[:, 0:W], func=AF.Gelu_apprx_tanh), reads=[cgr], writes=[("geb", gi)])
                    A("dve", lambda e: e.tensor_tensor(out=actT[:, j, 0:W], in0=geb[gi][:, 0:W], in1=cvt[:, 0:W], op=ALU.mult),
                      reads=[("geb", gi), cvr], writes=["actT"])

                if is_s:
                    sj = 0
                    spq(lambda e, sj=sj, g=g: e.dma_start(out=sfb[sj][0:32, :], in_=sf_d.rearrange("s r c -> (s r) c")[:, g * 512:(g + 1) * 512]), writes=[("tmpA", 0)])
                    for i4 in range(4):
                        bt = nb()
                        A("pe", lambda e, bt=bt, i4=i4, sj=sj: e.transpose(out=banks[bt][:, 0:32], in_=sfb[sj][0:32, i4 * 128:(i4 + 1) * 128], identity=ident_f[0:32, 0:32]),
                          reads=[("tmpA", 0), "ident_f"], writes=[("bk", bt)])
                        A("act", lambda e, bt=bt, i4=i4: e.copy(out=UST[:, i4, :, 0:2], in_=banks[bt][:, 0:32].rearrange("p (s r) -> p s r", r=2)),
                          reads=[("bk", bt)], writes=[("UST", i4), ("USTh", g)])
                fm_group(w_up_v, g * 512, [0, 1, 2, 3], lambda kk: hnT[:, kk, 0:W], W, ["xnT"], ev_up)
                if lastp or is_s:
                    c_lo, c_n = (0, 2) if lastp else (2, 32)
                    oi = 0
                    for i4 in range(4):
                        bt = nb()
                        A("pe", lambda e, bt=bt, i4=i4: e.transpose(out=banks[bt][0:c_n, 0:128], in_=UO[:, i4, c_lo:c_lo + c_n], identity=ident_f[:, :]),
                          reads=[("UO", i4), "ident_f"], writes=[("bk", bt)])
                        A("act", lambda e, bt=bt, i4=i4, oi=oi: e.copy(out=uost[oi][0:c_n, i4 * 128:(i4 + 1) * 128], in_=banks[bt][0:c_n, 0:128]),
                          reads=[("bk", bt)], writes=[("tmpA", 1)])
                    if lastp:
                        out_dma(lambda e, oi=oi, g=g: e.dma_start(out=ffnp_d[:, g * 512:(g + 1) * 512], in_=uost[oi][0:2, :]), [("tmpA", 1)], "o_ffnp%d" % oi)
                    else:
                        out_dma(lambda e, oi=oi, g=g: e.dma_start(out=ffns_d[:, g * 512:(g + 1) * 512], in_=uost[oi][0:32, :]), [("tmpA", 1)], "o_ffns%d" % oi)
            gload(GB0, "GB0", g_fpost_d)
            oblocks = [(bi, r0_, n) for bi, (r0_, n) in enumerate(blocks) if n == 128]
            for cg in range(4):
                bkm = {bi: nb() for (bi, _, _) in oblocks}
                for kq in range(8):
                    s = wload(w_dn_v[:, kq * 8:(kq + 1) * 8, cg * 512:(cg + 1) * 512])
                    col = 0
                    for bi, (r0_, n) in enumerate(blocks):
                        if n == 128:
                            for k in range(8):
                                kk = kq * 8 + k
                                A("pe", lambda e, s=s, k=k, kk=kk, bi=bi, col=col, bkm=bkm: e.matmul(
                                    banks[bkm[bi]][:, :], lhsT=actT[:, kk, col:col + 128], rhs=ring[s][:, k, :], start=(kk == 0), stop=(kk == 63)),
                                  reads=[("w", s), "actT"], writes=[("bk", bkm[bi])])
                        col += n
                for (bi, r0_, n) in oblocks:
                    ms_i = bi - (1 if hp else 0)
                    A("act", lambda e, bi=bi, ms_i=ms_i, cg=cg, bkm=bkm: e.copy(out=MS[ms_i][:, cg * 512:(cg + 1) * 512], in_=banks[bkm[bi]][:, :]),
                      reads=[("bk", bkm[bi])], writes=[("MS", ms_i)])
            for (bi, r0_, n) in oblocks:
                ms_i = bi - (1 if hp else 0)
                xb, xr = xbufs[bi]
                rmsnorm_stats(MS[ms_i][:, :], 128, 3, [("MS", ms_i)])
                A("dve", lambda e, ms_i=ms_i: e.scalar_tensor_tensor(out=MS[ms_i][:, :], in0=MS[ms_i][:, :], scalar=rstd_t[:, 3:4], in1=GB0[:, :],
                                                                   op0=ALU.mult, op1=ALU.mult), reads=[("MS", ms_i), ("rstd", 3), "GB0"], writes=[("MS", ms_i)])
                A("dve", lambda e, ms_i=ms_i, xb=xb: e.tensor_tensor(out=MS[ms_i][:, :], in0=MS[ms_i][:, :], in1=xb[:, :], op=ALU.add),
                  reads=[("MS", ms_i), xr], writes=[("MS", ms_i)])
                yrow = r0_ - 130
                out_dma(lambda e, ms_i=ms_i, yrow=yrow: e.dma_start(out=y_d[yrow:yrow + 128, :], in_=MS[ms_i][:, :]), [("MS", ms_i)], "o_y%d" % ms_i)

        for ti_, (kind_, blocks_) in enumerate(TILES):
            process_tile(ti_, kind_, blocks_)

        S.finalize()
        sems = {}
        for eng in S.ENGS:
            sems[("eng", eng)] = es.enter_context(nc.semaphore("s_" + eng))
        for gname in S.dma_final:
            sems[("dma", gname)] = es.enter_context(nc.semaphore("d_" + gname))
        with nc.Block() as block:
            @block.tensor
            def _(e): S.emit("pe", e, sems)

            @block.scalar
            def _(e): S.emit("act", e, sems)

            @block.vector
            def _(e): S.emit("dve", e, sems)

            @block.gpsimd
            def _(e): S.emit("pool", e, sems)

            @block.sync
            def _(e):
                S.emit("sp", e, sems)
                for gname in out_dma_groups:
                    e.wait_ge(sems[("dma", gname)], S.dma_final[gname])
    return nc


_NC_CACHE = {}


def kernel(x_prompt, x_sample, cache_k, cache_v, state_conv, state_ffn_conv,
           norm_mix_pre, w_in, b_in, conv_dw_k, conv_dw_b, conv_ln_g, conv_ln_b, w_conv_proj,
           attn_sinks, rel_bias, w_attn_proj, w_out, norm_mix_post,
           norm_ffn_pre, w_up, ffn_dw_k, ffn_dw_b, w_down, norm_ffn_post):
    f = np.float32
    x_prompt = np.asarray(x_prompt, f); x_sample = np.asarray(x_sample, f)
    wp = win_perm()
    w_in_p = np.ascontiguousarray(np.asarray(w_in, f)[0][:, wp])
    b_in_p = np.asarray(b_in, f)[0][wp]
    b_in_T = fmT(b_in_p, 84)
    b_kv = np.ascontiguousarray(np.asarray(b_in, f)[0][6144:6656])
    arp = attn_row_perm()
    w_ap = np.ascontiguousarray(np.asarray(w_attn_proj, f)[0][arp, :])
    up = wup_perm()
    w_up_p = np.ascontiguousarray(np.asarray(w_up, f)[0][:, up])
    ffn_k_T = np.ascontiguousarray(np.asarray(ffn_dw_k, f)[0][:, up].reshape(3, 128, 128).transpose(2, 1, 0))
    ffn_b_T = fmT(np.asarray(ffn_dw_b, f)[0][up], 128)
    conv_k_T = np.ascontiguousarray(np.asarray(conv_dw_k, f)[0].reshape(31, 16, 128).transpose(2, 1, 0))
    sinks = np.asarray(attn_sinks, f)[0]
    sinkT = np.zeros((128, 16), f)
    rbp = np.zeros((32, 32), f)
    rb = np.asarray(rel_bias, f)
    for cp in range(16):
        for half in range(2):
            h = head_of(half, cp)
            sinkT[half * 64:half * 64 + 64, cp] = sinks[h]
            rbp[:, half * 16 + cp] = rb[:, h]
    bucket = rel_bucket_np(np.arange(128))
    ohb = np.zeros((32, 128), f); ohb[bucket, np.arange(128)] = 1.0
    jm = np.zeros((128, 384), f)
    for dlt in range(128):
        jm[dlt, 255 - dlt] = 1.0
    mneg = np.zeros((128, 2, 129), f)
    s_idx = np.arange(128)[:, None]; r_idx = np.arange(129)[None, :]
    mneg[:, 0, :] = np.where(s_idx >= r_idx, 0.0, NEG)
    mneg[:, 1, :] = np.where(s_idx <= r_idx - 1, 0.0, NEG)

    shared = {
        "w_in": w_in_p, "b_in_T": b_in_T, "b_kv": b_kv,
        "conv_k_T": conv_k_T, "conv_b_T": fmT(np.asarray(conv_dw_b, f)[0], 16),
        "ln_g_T": fmT(np.asarray(conv_ln_g, f)[0], 16), "ln_b_T": fmT(np.asarray(conv_ln_b, f)[0], 16),
        "w_cp": np.ascontiguousarray(np.asarray(w_conv_proj, f)[0]), "w_ap": w_ap, "w_o": np.ascontiguousarray(np.asarray(w_out, f)[0]),
        "g_pre": np.ascontiguousarray(np.asarray(norm_mix_pre, f)[0]), "g_post": np.ascontiguousarray(np.asarray(norm_mix_post, f)[0]),
        "g_fpre": np.ascontiguousarray(np.asarray(norm_ffn_pre, f)[0]), "g_fpost": np.ascontiguousarray(np.asarray(norm_ffn_post, f)[0]),
        "w_up": w_up_p, "ffn_k_T": ffn_k_T, "ffn_b_T": ffn_b_T, "w_dn": np.ascontiguousarray(np.asarray(w_down, f)[0]),
        "sinkT": sinkT, "rbp": rbp, "ohb": ohb, "jm": jm, "mneg": mneg,
    }
    in_maps = []
    for c in range(NCORES):
        b, qd = c // 4, c % 4
        T0 = 1024 * qd
        xin = np.zeros((NROWS, D), f)
        if qd > 0:
            xin[0:130] = x_prompt[b, T0 - 130:T0]
        xin[130:1154] = x_prompt[b, T0:T0 + 1024]
        xin[1154:1282] = x_sample[16 * c:16 * c + 16].reshape(128, D)
        km = np.ones((128, 2), f)
        if qd == 0:
            km[:, 0] = 0.0
            km[0:2, 1] = 0.0
        m = dict(shared)
        m["xin"] = xin; m["kmask"] = km
        m["ck"] = np.ascontiguousarray(np.asarray(cache_k, f)[0, 16 * c:16 * c + 16].reshape(16, 128, 256))
        m["cv"] = np.ascontiguousarray(np.asarray(cache_v, f)[0, 16 * c:16 * c + 16].reshape(16, 128, 256))
        m["sc"] = np.ascontiguousarray(np.asarray(state_conv, f)[0, 16 * c:16 * c + 16])
        m["sf"] = np.ascontiguousarray(np.asarray(state_ffn_conv, f)[0, 16 * c:16 * c + 16][:, :, up])
        in_maps.append(m)

    if "nc" not in _NC_CACHE:
        _NC_CACHE["nc"] = build()
    nc = _NC_CACHE["nc"]
    res = run_bass_kernel_spmd(nc, in_maps, core_ids=list(range(NCORES)))
    R = res.results
    inv_up = np.argsort(up)
    y_p = np.zeros((2, 4096, D), f); y_s = np.zeros((128, 8, D), f)
    k_p = np.zeros((1, 2, 128, 4, 64), f); v_p = np.zeros((1, 2, 128, 4, 64), f)
    conv_p = np.zeros((1, 2, 30, D), f); ffn_p = np.zeros((1, 2, 2, DFF2), f)
    k_s = np.zeros((1, 128, 128, 4, 64), f); v_s = np.zeros((1, 128, 128, 4, 64), f)
    conv_s = np.zeros((1, 128, 30, D), f); ffn_s = np.zeros((1, 128, 2, DFF2), f)
    for c in range(NCORES):
        b, qd = c // 4, c % 4
        r = R[c]
        y_p[b, 1024 * qd:1024 * qd + 1024] = r["y"][0:1024]
        y_s[16 * c:16 * c + 16] = r["y"][1024:1152].reshape(16, 8, D)
        if qd == 3:
            k_p[0, b] = r["kvp"][:, 0:256].reshape(128, 4, 64)
            v_p[0, b] = r["kvp"][:, 256:512].reshape(128, 4, 64)
            conv_p[0, b] = r["convp"]
            ffn_p[0, b] = r["ffnp"][:, inv_up]
        k_s[0, 16 * c:16 * c + 16] = r["ks"].reshape(16, 128, 4, 64)
        v_s[0, 16 * c:16 * c + 16] = r["vs"].reshape(16, 128, 4, 64)
        conv_s[0, 16 * c:16 * c + 16] = r["convs"]
        ffn_s[0, 16 * c:16 * c + 16] = r["ffns"].reshape(16, 2, DFF2)[:, :, inv_up]
    return (y_p, y_s, k_p, v_p, conv_p, ffn_p, k_s, v_s, conv_s, ffn_s)
```

```python
import contextlib
import math
import numpy as np
import concourse.bass as bass
import concourse.mybir as mybir
from concourse.bass_utils import run_bass_kernel_spmd

F32 = mybir.dt.float32
BF16 = mybir.dt.bfloat16
AF = mybir.ActivationFunctionType
ALU = mybir.AluOpType

D = 2048
NH = 32
DIN = 10752
DFF2 = 16384
EPS = 1e-6
NCORES = 8
NROWS = 1282
NEG = -30000.0


class Op:
    __slots__ = ("eng", "fn", "deps", "signal", "sigval", "dma", "idx")

    def __init__(self, eng, fn, dma=None):
        self.eng = eng; self.fn = fn; self.deps = []; self.signal = False
        self.sigval = 0; self.dma = dma; self.idx = 0


class Sched:
    ENGS = ("pe", "act", "dve", "pool", "sp")

    def __init__(self):
        self.q = {e: [] for e in self.ENGS}
        self.lastw = {}
        self.readers = {}
        self.nops = 0
        self.alias = {}

    def add(self, eng, fn, reads=(), writes=(), dma=None):
        op = Op(eng, fn, dma)
        op.idx = self.nops; self.nops += 1
        writes = list(writes)
        for w in list(writes):
            for a in self.alias.get(w, ()):
                if a not in writes: writes.append(a)
        deps = {}
        for r in reads:
            w = self.lastw.get(r)
            if w is not None: deps[id(w)] = w
        for w in writes:
            lw = self.lastw.get(w)
            if lw is not None: deps[id(lw)] = lw
            for rd in self.readers.get(w, ()):
                deps[id(rd)] = rd
        for d in deps.values():
            if d is op: continue
            if d.dma is None and op.dma is None and d.eng == eng and eng == "pe":
                continue
            op.deps.append(d)
            d.signal = True
        for w in writes:
            self.lastw[w] = op
            self.readers[w] = []
        for r in reads:
            if r in writes: continue
            self.readers.setdefault(r, []).append(op)
        self.q[eng].append(op)
        return op

    def finalize(self):
        for e in self.ENGS:
            cnt = 0
            for op in self.q[e]:
                if op.dma is None and op.signal:
                    cnt += 1; op.sigval = cnt
        gc = {}
        allops = sorted([op for e in self.ENGS for op in self.q[e] if op.dma is not None], key=lambda o: o.idx)
        for op in allops:
            gc[op.dma] = gc.get(op.dma, 0) + 16
            op.sigval = gc[op.dma]
        self.dma_final = gc

    def emit(self, eng, e, sems):
        waited = {}
        for op in self.q[eng]:
            need = {}
            for d in op.deps:
                key = ("dma", d.dma) if d.dma is not None else ("eng", d.eng)
                if need.get(key, 0) < d.sigval: need[key] = d.sigval
            for key, val in need.items():
                if waited.get(key, 0) < val:
                    e.wait_ge(sems[key], val)
                    waited[key] = val
            ins = op.fn(e)
            if op.dma is not None:
                ins.then_inc(sems[("dma", op.dma)], 16)
            elif op.signal:
                ins.then_inc(sems[("eng", eng)], 1)


def head_of(half, cp):
    if cp < 8:
        return cp if half == 0 else 8 + cp
    return 16 + (cp - 8) if half == 0 else 24 + (cp - 8)


def win_perm():
    idx = []
    for c in range(16):
        idx += list(range(c * 128, c * 128 + 128))
        idx += list(range(2048 + c * 128, 2048 + c * 128 + 128))
    for cp in range(16):
        for half in range(2):
            h = head_of(half, cp)
            idx += list(range(4096 + h * 64, 4096 + h * 64 + 64))
    idx += list(range(6144, 6144 + 512))
    idx += list(range(6656, 6656 + 4096))
    return np.array(idx, dtype=np.int64)


def attn_row_perm():
    idx = []
    for cp in range(16):
        for half in range(2):
            h = head_of(half, cp)
            idx += list(range(h * 64, h * 64 + 64))
    return np.array(idx, dtype=np.int64)


def wup_perm():
    idx = []
    for j in range(64):
        idx += list(range(j * 128, j * 128 + 128))
        idx += list(range(8192 + j * 128, 8192 + j * 128 + 128))
    return np.array(idx, dtype=np.int64)


def rel_bucket_np(dist):
    max_exact = 16
    d = np.maximum(dist, 1).astype(np.float32)
    large = max_exact + (np.log(d / max_exact) / math.log(128 / max_exact) * (32 - max_exact)).astype(np.int32)
    large = np.minimum(large, 31)
    return np.where(dist < max_exact, dist, large)


def fmT(v, nch):
    return np.ascontiguousarray(v.reshape(nch, 128).T)


TILES = [("s", [(1154, 128)]),
         ("p", [(128, 2), (130, 128), (258, 128)]), ("p", [(386, 128), (514, 128)]), ("p", [(642, 128), (770, 128)]),
         ("p", [(898, 128), (1026, 128)])]
WMAX = 258
EXT = 128


def build():
    nc = bass.Bass("TRN2", target_bir_lowering=False)

    def din(name, shape, dt=F32):
        return nc.dram_tensor(name, list(shape), dt, kind="ExternalInput").ap()

    def dout(name, shape, dt=F32):
        return nc.dram_tensor(name, list(shape), dt, kind="ExternalOutput").ap()

    xin = din("xin", [NROWS, D])
    kmask_d = din("kmask", [128, 2])
    ck_d = din("ck", [16, 128, 256]); cv_d = din("cv", [16, 128, 256])
    sc_d = din("sc", [16, 30, D]); sf_d = din("sf", [16, 2, DFF2])
    w_in_d = din("w_in", [D, DIN]); b_in_T_d = din("b_in_T", [128, 84]); b_kv_d = din("b_kv", [512])
    conv_k_d = din("conv_k_T", [128, 16, 31]); conv_b_d = din("conv_b_T", [128, 16])
    ln_g_d = din("ln_g_T", [128, 16]); ln_b_d = din("ln_b_T", [128, 16])
    w_cp_d = din("w_cp", [D, D]); w_ap_d = din("w_ap", [D, D]); w_o_d = din("w_o", [D, D])
    g_pre_d = din("g_pre", [D]); g_post_d = din("g_post", [D]); g_fpre_d = din("g_fpre", [D]); g_fpost_d = din("g_fpost", [D])
    w_up_d = din("w_up", [D, DFF2]); ffn_k_d = din("ffn_k_T", [128, 128, 3]); ffn_b_d = din("ffn_b_T", [128, 128])
    w_dn_d = din("w_dn", [8192, D])
    sink_d = din("sinkT", [128, 16]); rbp_d = din("rbp", [32, 32])
    ohb_d = din("ohb", [32, 128]); jm_d = din("jm", [128, 384]); mneg_d = din("mneg", [128, 2, 129])

    y_d = dout("y", [1152, D])
    kvp_d = dout("kvp", [128, 512])
    convp_d = dout("convp", [30, D])
    ffnp_d = dout("ffnp", [2, DFF2])
    ks_d = dout("ks", [16, 128, 256]); vs_d = dout("vs", [16, 128, 256])
    convs_d = dout("convs", [16, 30, D])
    ffns_d = dout("ffns", [32, DFF2])

    S = Sched()
    out_dma_groups = []

    with contextlib.ExitStack() as es:
        def sb(name, shape, dt=F32):
            return es.enter_context(nc.sbuf_tensor(name, list(shape), dt))

        def ps(name, shape, dt=F32):
            return es.enter_context(nc.psum_tensor(name, list(shape), dt))

        A = S.add
        NBK = 6
        banks = [ps("bk%d" % i, [128, 512], F32) for i in range(NBK)]
        tbanks = [ps("tb%d" % i, [128, 1024], BF16) for i in range(2)]
        st = {"bk": 0, "tb": 0, "ws": 0, "dq": 0}

        def nb():
            i = st["bk"]; st["bk"] = (i + 1) % NBK
            return i

        def ntb():
            i = st["tb"]; st["tb"] = (i + 1) % 2
            return i

        NSLOT = 3
        ring = [sb("ring%d" % i, [128, 8, 512], BF16) for i in range(NSLOT)]

        wscr = nc.dram_tensor("wscr", [162, 128, 4096], BF16).ap()
        st["wi"] = 0
        st["tile"] = 0
        wkeys = {}

        def wload(src):
            s = st["ws"]; st["ws"] = (s + 1) % NSLOT
            key = (src.name, str(src.offset))
            if st["tile"] == 0:
                assert key not in wkeys
                wkeys[key] = len(wkeys)
            idx = wkeys[key]
            if st["tile"] == 0:
                A("pool", lambda e, s=s, src=src: e.dma_start(out=ring[s][:], in_=src), writes=[("w", s)], dma="w%d" % s)
                A("sp", lambda e, s=s, idx=idx: e.dma_start(out=wscr[idx], in_=ring[s][:].rearrange("p k c -> p (k c)")),
                  reads=[("w", s)], writes=[("wscr", idx)], dma="wb%d" % s)
            else:
                A("pool", lambda e, s=s, idx=idx: e.dma_start(out=ring[s][:].rearrange("p k c -> p (k c)"), in_=wscr[idx]),
                  reads=[("wscr", idx)], writes=[("w", s)], dma="w%d" % s)
            return s

        w_in_v = w_in_d.rearrange("(k p) c -> p k c", p=128)
        w_cp_v = w_cp_d.rearrange("(k p) c -> p k c", p=128)
        w_ap_v = w_ap_d.rearrange("(k p) c -> p k c", p=128)
        w_o_v = w_o_d.rearrange("(k p) c -> p k c", p=128)
        w_up_v = w_up_d.rearrange("(k p) c -> p k c", p=128)
        w_dn_v = w_dn_d.rearrange("(k p) c -> p k c", p=128)

        def spq(fn, reads=(), writes=(), grp=None):
            if grp is None:
                grp = "q%d" % st["dq"]; st["dq"] = (st["dq"] + 1) % 6
            return A("sp", fn, reads=reads, writes=writes, dma=grp)

        tmpA = [sb("tmpA%d" % i, [128, 512]) for i in range(3)]
        ident_f = sb("ident_f", [128, 128]); ident_b = sb("ident_b", [128, 128], BF16)
        ones_f = sb("ones_f", [128, 128])
        epsT = sb("epsT", [128, 1])
        kmask = sb("kmaskS", [128, 2])
        b_in_T = sb("b_in_TS", [128, 84]); bq8 = sb("bq8", [128, 16])
        bkvB = sb("bkvB", [128, 512])
        conv_k = sb("conv_kS", [128, 16, 31]); conv_b = sb("conv_bS", [128, 16])
        ln_g = sb("ln_gS", [128, 16]); ln_b = sb("ln_bS", [128, 16])
        ffn_k = sb("ffn_kS", [128, 128, 3]); ffn_b = sb("ffn_bS", [128, 128])
        esink = sb("esink", [128, 16])
        GB0 = sb("GB0", [128, D])
        TBL = sb("TBL", [128, 2, 32, 129], BF16)
        OP1 = sb("OP1", [128, 2, 128], BF16); OPF = sb("OPF", [128, 2, 128], BF16); OPB = sb("OPB", [128, 2, 128], BF16)

        A("dve", lambda e: e.memset(ones_f[:], 1.0), writes=["ones_f"])
        A("dve", lambda e: e.memset(epsT[:], EPS), writes=["epsT"])
        A("pool", lambda e: e.memset(ident_f[:], 0.0), writes=["ident_f"])
        A("pool", lambda e: e.affine_select(out=ident_f[:], in_=ident_f[:], pattern=[[-1, 128]], compare_op=ALU.not_equal,
                                            fill=1.0, base=0, channel_multiplier=1), reads=["ident_f"], writes=["ident_f"])
        A("dve", lambda e: e.tensor_copy(out=ident_b[:], in_=ident_f[:]), reads=["ident_f"], writes=["ident_b"])
        for (dst, src, nm) in ((kmask, kmask_d, "kmask"), (b_in_T, b_in_T_d, "b_in_T"), (conv_k, conv_k_d, "conv_k"),
                               (conv_b, conv_b_d, "conv_b"), (ln_g, ln_g_d, "ln_g"), (ln_b, ln_b_d, "ln_b"),
                               (ffn_k, ffn_k_d, "ffn_k"), (ffn_b, ffn_b_d, "ffn_b"), (esink, sink_d, "esink")):
            spq(lambda e, dst=dst, src=src: e.dma_start(out=dst[:], in_=src), writes=[nm])
        spq(lambda e: e.dma_start(out=bkvB[:], in_=b_kv_d.partition_broadcast(128)), writes=["bkvB"])
        A("act", lambda e: e.activation(out=esink[:], in_=esink[:], func=AF.Exp), reads=["esink"], writes=["esink"])
        A("dve", lambda e: e.tensor_scalar(out=bq8[:], in0=b_in_T[:, 32:48], scalar1=0.125, scalar2=None, op0=ALU.mult),
          reads=["b_in_T"], writes=["bq8"])
        for (T, nm, col) in ((OP1, "OP1", None), (OPF, "OPF", 0), (OPB, "OPB", 1)):
            A("dve", lambda e, T=T: e.memset(T[:], 0.0), writes=[nm])
            for half in range(2):
                if col is None:
                    A("dve", lambda e, T=T, half=half: e.memset(T[:, half, half * 64:half * 64 + 64], 1.0), reads=[], writes=[nm])
                else:
                    A("dve", lambda e, T=T, half=half, col=col: e.tensor_scalar(
                        out=T[:, half, half * 64:half * 64 + 64], in0=ones_f[:, 0:64], scalar1=kmask[:, col:col + 1], scalar2=None,
                        op0=ALU.mult), reads=["ones_f", "kmask"], writes=[nm])

        rbp = sb("rbpS", [32, 32]); ohb = sb("ohbS", [32, 128]); jm = tmpA[0][:, 0:384]; mneg = tmpA[1][:, 0:258].rearrange("p (a b) -> p a b", a=2)
        vec = sb("vecS", [128, 32])
        for (dst, src, nm) in ((rbp[:], rbp_d, "rbp"), (ohb[:], ohb_d, "ohb"), (jm, jm_d, ("tmpA", 0)), (mneg, mneg_d, ("tmpA", 1))):
            spq(lambda e, dst=dst, src=src: e.dma_start(out=dst, in_=src), writes=[nm])
        b0 = nb()
        A("pe", lambda e, b0=b0: e.matmul(banks[b0][:, 0:32], lhsT=ohb[:, :], rhs=rbp[:, :], start=True, stop=True),
          reads=["ohb", "rbp"], writes=[("bk", b0)])
        A("dve", lambda e, b0=b0: e.tensor_copy(out=vec[:], in_=banks[b0][:, 0:32]), reads=[("bk", b0)], writes=["vec"])
        for kc in range(2):
            for r0 in range(0, 129, 16):
                nr = min(16, 129 - r0)
                bk = nb()
                for rr in range(nr):
                    r = r0 + rr
                    start_col = (128 - r) if kc == 0 else (256 - r)
                    A("pe", lambda e, bk=bk, rr=rr, sc_=start_col: e.matmul(
                        banks[bk][:, rr * 32:(rr + 1) * 32], lhsT=jm[:, sc_:sc_ + 128], rhs=vec[:, :], start=True, stop=True),
                      reads=[("tmpA", 0), "vec"], writes=[("bk", bk)])
                A("dve", lambda e, bk=bk, kc=kc, r0=r0, nr=nr: e.tensor_tensor(
                    out=TBL[:, kc, :, r0:r0 + nr].rearrange("p h r -> p r h"),
                    in0=banks[bk][:, 0:nr * 32].rearrange("p (r h) -> p r h", h=32),
                    in1=mneg[:, kc, r0:r0 + nr].unsqueeze(2).to_broadcast([128, nr, 32]), op=ALU.add),
                  reads=[("bk", bk), ("tmpA", 1)], writes=["TBL"])

        KT = sb("KT", [128, 2, NROWS], BF16)
        xh = [sb("xh%d" % i, [128, D]) for i in range(3)]
        XS = sb("XS", [128, D], BF16)
        ssq = sb("ssq", [128, 8]); rstd_t = sb("rstd_t", [128, 8])
        xnT = sb("xnT", [128, 16, EXT + WMAX], BF16)
        hnT = xnT
        arena = sb("arena", [128, 64, WMAX], BF16)
        actT = arena
        cT = arena[:, 0:32, :].rearrange("p a b -> p (a b)").bitcast(F32).rearrange("p (k w) -> p k w", w=WMAX)
        qT = arena[:, 32:48, :]
        qTf = arena[:, 32:48, :].rearrange("p a b -> p (a b)").bitcast(F32)
        oT = arena[:, 48:64, :]
        SM = sb("SM", [128, 32, WMAX], BF16)
        sT = SM[:, 0:16, :]
        mixT = SM[:, 16:32, :]
        SMf = SM[:, :, :].rearrange("p a b -> p (a b)").bitcast(F32)
        MS = [sb("MS%d" % i, [128, D]) for i in range(3)]
        Gb = [sb("Gb%d" % i, [128, 32 + WMAX]) for i in range(4)]
        ghist = sb("ghist", [128, 16, 32])
        uhist = sb("uhist", [128, 128, 2])
        VP = [sb("VP%d" % i, [128, 4, 128], BF16) for i in range(8)]
        Vf = tmpA[1]
        Eb = [sb("Eb%d" % i, [128, 2, 512], BF16) for i in range(2)]
        rden = [tmpA[0]]
        lnm = qTf[:, 0:WMAX]; lnr = qTf[:, WMAX:2 * WMAX]; lnt = qTf[:, 2 * WMAX:3 * WMAX]
        lnz = [qTf[:, (3 + i) * WMAX:(4 + i) * WMAX] for i in range(2)]
        CO = cT[:, 0:4, :]; T1 = cT[:, 4:8, :]; sgt = [cT[:, 8 + i, :] for i in range(2)]
        Ub = [SMf[:, i * (WMAX + 2):(i + 1) * (WMAX + 2)] for i in range(4)]
        _o = 4 * (WMAX + 2)
        cvb = [SMf[:, _o + i * WMAX:_o + (i + 1) * WMAX] for i in range(4)]
        geb = [SMf[:, _o + (4 + i) * WMAX:_o + (5 + i) * WMAX] for i in range(2)]
        S.alias["CO"] = [("cT", k) for k in range(4)]
        S.alias["T1"] = [("cT", k) for k in range(4, 8)]
        for i in range(2):
            S.alias[("sgt", i)] = [("cT", 8 + i)]
            S.alias[("cT", 8 + i)] = [("sgt", i)]
        for k in range(4):
            S.alias[("cT", k)] = ["CO"]
            S.alias[("cT", 4 + k)] = ["T1"]
        lnn = ["lnm", "lnr", "lnt", ("lnz", 0), ("lnz", 1)]
        S.alias["qT"] = list(lnn)
        for nme in lnn:
            S.alias[nme] = ["qT"]
        s5n = [("Ub", i) for i in range(4)] + [("cvb", i) for i in range(4)] + [("geb", i) for i in range(2)]
        S.alias["sT"] = list(s5n); S.alias["mixT"] = list(s5n)
        for nme in s5n:
            S.alias[nme] = ["sT", "mixT"]
        ckb = VP[4][:, :, :].rearrange("p g c -> p (g c)").rearrange("p (s d) -> p s d", s=2)
        KcT = VP[5][:, :, :].rearrange("p (a b) c -> p a b c", a=2)
        cvbuf = VP[6][:, :, :].rearrange("p g c -> p (g c)").rearrange("p (s d) -> p s d", s=2)
        VPc = [VP[0], VP[1]]
        VsP = [VP[2], VP[3]]
        GS = sb("GS", [128, 16, 38]); scb = [tmpA[2][:, :].rearrange("p (a c) -> p a c", a=4)]
        sfb = [tmpA[0]]
        UST = sb("UST", [128, 4, 16, 10])
        UO = sb("UO", [128, 4, 34]); uost = [tmpA[1]]
        GO = GS[:, :, 0:30]
        S.alias["GO"] = ["GS"]; S.alias["GS"] = ["GO"]
        cnt = {"g": 0, "e": 0, "u": 0, "o": 0}

        print("SBUF bytes remaining after alloc:", nc.sbuf_bytes_remaining)
        def rmsnorm_stats(src_ap, n, col, rd):
            A("dve", lambda e: e.memset(ssq[0:n, col:col + 1], 0.0), writes=[("ssq", col)])
            A("act", lambda e: e.activation(out=XS[0:n, :], in_=src_ap, func=AF.Square, accum_out=ssq[0:n, col:col + 1]),
              reads=rd + [("ssq", col)], writes=["XS", ("ssq", col)])
            A("act", lambda e: e.activation(out=rstd_t[0:n, col:col + 1], in_=ssq[0:n, col:col + 1], func=AF.Sqrt,
                                            scale=1.0 / D, bias=epsT[0:n, 0:1]), reads=[("ssq", col), "epsT"], writes=[("rstd", col)])
            A("dve", lambda e: e.reciprocal(out=rstd_t[0:n, col:col + 1], in_=rstd_t[0:n, col:col + 1]),
              reads=[("rstd", col)], writes=[("rstd", col)])

        def transpose_to(dstT, dst_res, col0, n, src_res):
            for half in range(2):
                tb = ntb()
                for k in range(8):
                    kk = half * 8 + k
                    A("pe", lambda e, tb=tb, k=k, kk=kk: e.transpose(out=tbanks[tb][:, k * 128:k * 128 + n],
                                                                  in_=XS[0:n, kk * 128:(kk + 1) * 128], identity=ident_b[0:n, 0:n]),
                      reads=[src_res, "ident_b"], writes=[("tb", tb)])
                A("act", lambda e, tb=tb, half=half: e.copy(
                    out=dstT[:, half * 8:half * 8 + 8, col0:col0 + n],
                    in_=tbanks[tb][:, :].rearrange("p (k c) -> p k c", c=128)[:, :, 0:n]),
                  reads=[("tb", tb)], writes=[dst_res])

        def gload(GBt, nm, src):
            spq(lambda e: e.dma_start(out=GBt[:], in_=src.partition_broadcast(128)), writes=[nm])

        def stage0_block(row0, n, dst_col, xbuf, xres):
            spq(lambda e: e.dma_start(out=xbuf[0:n, :], in_=xin[row0:row0 + n, :]), writes=[xres])
            rmsnorm_stats(xbuf[0:n, :], n, 0, [xres])
            A("dve", lambda e: e.scalar_tensor_tensor(out=XS[0:n, :], in0=xbuf[0:n, :], scalar=rstd_t[0:n, 0:1], in1=GB0[0:n, :],
                                                      op0=ALU.mult, op1=ALU.mult), reads=[xres, ("rstd", 0), "GB0"], writes=["XS"])
            transpose_to(xnT, "xnT", dst_col, n, "XS")

        def fm_group(wv, c0, chunks, rhs_fn, N, rhs_res, evac):
            s0 = wload(wv[:, 0:8, c0:c0 + 512]); s1 = wload(wv[:, 8:16, c0:c0 + 512])
            bks = {i: nb() for i in chunks}
            for h, s in enumerate((s0, s1)):
                for i in chunks:
                    for k in range(8):
                        kk = 8 * h + k
                        A("pe", lambda e, s=s, i=i, k=k, kk=kk: e.matmul(
                            banks[bks[i]][:, 0:N], lhsT=ring[s][:, k, i * 128:(i + 1) * 128], rhs=rhs_fn(kk),
                            start=(kk == 0), stop=(kk == 15)), reads=[("w", s)] + rhs_res, writes=[("bk", bks[i])])
            for i in chunks:
                evac(i, bks[i])
            return s0, s1

        def make_vpad(win_cols, vp_idx, mask, with_k_out=False, s01=None):
            s0, s1 = s01
            bk = nb()
            c_lo = 0 if with_k_out else 256
            for h, s in enumerate((s0, s1)):
                for k in range(8):
                    kk = 8 * h + k
                    A("pe", lambda e, s=s, k=k, kk=kk: e.matmul(
                        banks[bk][:, c_lo:512], lhsT=xnT[:, kk, win_cols:win_cols + 128], rhs=ring[s][:, k, c_lo:512],
                        start=(kk == 0), stop=(kk == 15)), reads=[("w", s), "xnT"], writes=[("bk", bk)])
            A("dve", lambda e: e.tensor_tensor(out=Vf[:, c_lo:512], in0=banks[bk][:, c_lo:512], in1=bkvB[:, c_lo:512], op=ALU.add),
              reads=[("bk", bk), "bkvB"], writes=[("tmpA", 1)])
            vpad_from(Vf[:, 256:512], ("tmpA", 1), VP[vp_idx], ("VP", vp_idx), mask)

        def vpad_from(src, src_res, dstT, dst_res, mask, npart=128):
            A("dve", lambda e: e.memset(dstT[0:npart], 0.0), writes=[dst_res])
            for par in range(2):
                sv = src.rearrange("p (g t d) -> p g t d", g=2, t=2)[:, :, par, :]
                dv = dstT[0:npart].rearrange("p (g t) c -> p g t c", t=2)[:, :, par, par * 64:par * 64 + 64]
                if mask is None:
                    A("dve", lambda e, sv=sv, dv=dv: e.tensor_copy(out=dv, in_=sv), reads=[src_res], writes=[dst_res])
                else:
                    A("dve", lambda e, sv=sv, dv=dv: e.tensor_scalar(out=dv, in0=sv, scalar1=kmask[0:npart, mask:mask + 1], scalar2=None,
                                                                 op0=ALU.mult), reads=[src_res, "kmask"], writes=[dst_res])

        def attn_block(qc0, n, r0, k0_fn, k0_res, k1_fn, k1_res, nk1, vp0, vp0_res, op0, vp1, vp1_res, op1):
            CB = 4 if n > 64 else 8
            def do_gc(gp, cb):
                if True:
                    cp0 = gp * 8 + cb * CB
                    ebs = []
                    for half in range(2):
                        eb = Eb[cnt["e"] % 2]; ebr = ("Eb", cnt["e"] % 2); cnt["e"] += 1
                        ebs.append((eb, ebr))
                        hs = slice(half * 64, half * 64 + 64)
                        for kc in range(2):
                            nk = 128 if kc == 0 else nk1
                            kfn, kres = (k0_fn, k0_res) if kc == 0 else (k1_fn, k1_res)
                            bk = nb()
                            A("pe", lambda e, bk=bk, kfn=kfn, hs=hs, nk=nk: e.matmul(
                                banks[bk][0:nk, 0:CB * n].rearrange("p (c q) -> p c q", q=n), lhsT=kfn(gp, hs),
                                rhs=qT[hs, cp0:cp0 + CB, qc0:qc0 + n], start=True, stop=False),
                              reads=[kres, "qT"], writes=[("bk", bk)])
                            A("pe", lambda e, bk=bk, kc=kc, nk=nk, half=half: e.matmul(
                                banks[bk][0:nk, 0:CB * n].rearrange("p (c q) -> p c q", q=n), lhsT=ident_b[0:nk, 0:nk],
                                rhs=TBL[0:nk, kc, half * 16 + cp0:half * 16 + cp0 + CB, r0:r0 + n], start=False, stop=True),
                              reads=["ident_b", "TBL"], writes=[("bk", bk)])
                            A("act", lambda e, bk=bk, kc=kc, nk=nk, eb=eb: e.activation(
                                out=eb[0:nk, kc, 0:CB * n], in_=banks[bk][0:nk, 0:CB * n], func=AF.Exp),
                              reads=[("bk", bk)], writes=[ebr])
                    bo = nb(); bd = nb()
                    first = True
                    for half in range(2):
                        eb, ebr = ebs[half]
                        for kc in range(2):
                            nk = 128 if kc == 0 else nk1
                            vp, vpr, opt = (vp0, vp0_res, op0) if kc == 0 else (vp1, vp1_res, op1)
                            last = (half == 1 and kc == 1)
                            A("pe", lambda e, vp=vp, nk=nk, eb=eb, kc=kc, half=half, first=first, last=last: e.matmul(
                                banks[bo][:, 0:CB * n], lhsT=vp[0:nk, 2 * gp + half, :], rhs=eb[0:nk, kc, 0:CB * n],
                                start=first, stop=last), reads=[vpr, ebr], writes=[("bk", bo)])
                            A("pe", lambda e, opt=opt, nk=nk, eb=eb, kc=kc, half=half, first=first, last=last: e.matmul(
                                banks[bd][:, 0:CB * n], lhsT=opt[0][0:nk, half, :], rhs=eb[0:nk, kc, 0:CB * n],
                                start=first, stop=last), reads=[opt[1], ebr], writes=[("bk", bd)])
                            first = False
                    rd = rden[0]; rdr = ("tmpA", 0)
                    for i in range(CB):
                        A("dve", lambda e, i=i: e.tensor_scalar(out=rd[:, i * n:(i + 1) * n], in0=banks[bd][:, i * n:(i + 1) * n],
                                                               scalar1=esink[:, cp0 + i:cp0 + i + 1], scalar2=None, op0=ALU.add),
                          reads=[("bk", bd), "esink"], writes=[rdr])
                    A("dve", lambda e: e.reciprocal(out=rd[:, 0:CB * n], in_=rd[:, 0:CB * n]), reads=[rdr], writes=[rdr])
                    A("dve", lambda e: e.tensor_tensor(
                        out=oT[:, cp0:cp0 + CB, qc0:qc0 + n], in0=banks[bo][:, 0:CB * n].rearrange("p (c q) -> p c q", q=n),
                        in1=rd[:, 0:CB * n].rearrange("p (c q) -> p c q", q=n), op=ALU.mult),
                      reads=[("bk", bo), rdr], writes=["oT"])
            for gp_ in range(2):
                for cb_ in range(8 // CB):
                    do_gc(gp_, cb_)

        def out_dma(fn, reads, grp):
            if grp not in out_dma_groups:
                out_dma_groups.append(grp)
            A("sp", fn, reads=reads, dma=grp)

        vp_of_row = {}
        def process_tile(ti, kind, blocks):
            st["tile"] = ti; st["wi"] = 0
            hp = (blocks[0] == (128, 2))
            lastp = (kind == "p" and blocks[-1][0] == 1026)
            W = sum(n for _, n in blocks)
            row_lo = blocks[0][0]
            mo = EXT if hp else 0
            is_s = (kind == "s")
            gload(GB0, "GB0", g_pre_d)
            if hp:
                stage0_block(0, 128, 0, xh[2], ("xh", 2))
            col = mo
            xbufs = {}
            for bi, (r0_, n) in enumerate(blocks):
                slot = 2 if n == 2 else (bi - (1 if hp else 0))
                xb, xr = xh[slot], ("xh", slot)
                xbufs[bi] = (xb, xr)
                stage0_block(r0_, n, col, xb, xr)
                col += n
            ge = 32 if hp else 0
            NG = ge + W
            Wc = W
            def sample_post(ch, Gt, Gr):
                sj = 0
                spq(lambda e: e.dma_start(out=scb[sj][0:120, :, :], in_=sc_d.rearrange("(a s) r c -> (s r) a c", a=4)[:, :, ch * 128:(ch + 1) * 128]),
                    writes=[("tmpA", 2)])
                for a4 in range(4):
                    bt = nb()
                    A("pe", lambda e, a4=a4, bt=bt: e.transpose(out=banks[bt][:, 0:120], in_=scb[sj][0:120, a4, :], identity=ident_f[0:120, 0:120]),
                      reads=[("tmpA", 2), "ident_f"], writes=[("bk", bt)])
                    A("act", lambda e, a4=a4, bt=bt: e.copy(out=GS[:, a4 * 4:a4 * 4 + 4, 0:30],
                                                           in_=banks[bt][:, 0:120].rearrange("p (s r) -> p s r", r=30)),
                      reads=[("bk", bt)], writes=["GS"])
                A("dve", lambda e: e.tensor_copy(out=GS[:, :, 30:38], in_=Gt[:, 32:32 + 128].rearrange("p (s t) -> p s t", t=8)),
                  reads=[Gr], writes=["GS"])
                cv3 = cT[:, ch, 0:128].rearrange("p (s t) -> p s t", t=8)
                A("dve", lambda e: e.tensor_scalar(out=cv3, in0=GS[:, :, 0:8], scalar1=conv_k[:, ch, 0:1], scalar2=conv_b[:, ch:ch + 1],
                                                   op0=ALU.mult, op1=ALU.add), reads=["GS", "conv_k", "conv_b"], writes=[("cT", ch)])
                for j in range(1, 31):
                    A("dve", lambda e, j=j: e.scalar_tensor_tensor(out=cv3, in0=GS[:, :, j:j + 8], scalar=conv_k[:, ch, j:j + 1], in1=cv3,
                                                                  op0=ALU.mult, op1=ALU.add), reads=["GS", "conv_k", ("cT", ch)], writes=[("cT", ch)])
                bt = nb()
                A("pe", lambda e, bt=bt: e.transpose(out=banks[bt][:, 0:128], in_=Gt[:, 32:160], identity=ident_f[:, :]),
                  reads=[Gr, "ident_f"], writes=[("bk", bt)])
                A("act", lambda e, bt=bt: e.copy(out=MS[0][:, ch * 128:(ch + 1) * 128], in_=banks[bt][:, 0:128]),
                  reads=[("bk", bt)], writes=[("MS", 0)])

            def prompt_post(plist):
                ce = "dve"
                for (ch, Gt, Gr) in plist:
                    if hp:
                        A("dve", lambda e, Gt=Gt: e.tensor_scalar(out=Gt[:, 0:34], in0=Gt[:, 0:34], scalar1=kmask[:, 0:1], scalar2=None, op0=ALU.mult),
                          reads=[Gr, "kmask"], writes=[Gr])
                    else:
                        A("dve", lambda e, Gt=Gt, ch=ch: e.tensor_copy(out=Gt[:, 0:32], in_=ghist[:, ch, :]), reads=[("ghist", ch)], writes=[Gr])
                    A("dve", lambda e, Gt=Gt, ch=ch: e.tensor_copy(out=ghist[:, ch, :], in_=Gt[:, W:W + 32]), reads=[Gr], writes=[("ghist", ch)])
                    if lastp:
                        A("act", lambda e, Gt=Gt, ch=ch: e.copy(out=GO[:, ch, :], in_=Gt[:, 32 + W - 30:32 + W]), reads=[Gr], writes=["GO"])
                    A(ce, lambda e, Gt=Gt, ch=ch: e.tensor_scalar(out=cT[:, ch, 0:Wc], in0=Gt[:, 2:2 + Wc], scalar1=conv_k[:, ch, 0:1], scalar2=conv_b[:, ch:ch + 1],
                                                                 op0=ALU.mult, op1=ALU.add), reads=[Gr, "conv_k", "conv_b"], writes=[("cT", ch)])
                for j in range(1, 31):
                    for (ch, Gt, Gr) in plist:
                        A(ce, lambda e, j=j, Gt=Gt, ch=ch: e.scalar_tensor_tensor(out=cT[:, ch, 0:Wc], in0=Gt[:, 2 + j:2 + j + Wc], scalar=conv_k[:, ch, j:j + 1], in1=cT[:, ch, 0:Wc],
                                                                                op0=ALU.mult, op1=ALU.add), reads=[Gr, "conv_k", ("cT", ch)], writes=[("cT", ch)])

            def ab_group(g):
                s_pair = {}
                posts = []

                def evac_ab(i, bk, g=g):
                    s_pair[i] = bk
                    if i % 2 == 0:
                        return
                    ch = g * 2 + i // 2
                    ba, bb = s_pair[i - 1], bk
                    gi = cnt["g"] % 4; cnt["g"] += 1
                    Gt, Gr = Gb[gi], ("Gb", gi)
                    sg = tmpA[gi % 2]; sgr = ("tmpA", gi % 2)
                    A("act", lambda e: e.activation(out=sg[:, 0:NG], in_=banks[bb][:, 0:NG], func=AF.Sigmoid,
                                                    bias=b_in_T[:, 2 * ch + 1:2 * ch + 2]), reads=[("bk", bb), "b_in_T"], writes=[sgr])
                    g0 = 32 - ge
                    A("dve", lambda e: e.scalar_tensor_tensor(out=Gt[:, g0:g0 + NG], in0=banks[ba][:, 0:NG],
                                                              scalar=b_in_T[:, 2 * ch:2 * ch + 1], in1=sg[:, 0:NG], op0=ALU.add, op1=ALU.mult),
                      reads=[("bk", ba), "b_in_T", sgr], writes=[Gr])
                    if is_s:
                        posts.append((ch, Gt, Gr))
                        return
                    posts.append((ch, Gt, Gr))

                xc0 = mo - ge
                fm_group(w_in_v, g * 512, [0, 1, 2, 3], lambda kk, xc0=xc0, NG=NG: xnT[:, kk, xc0:xc0 + NG], NG, ["xnT"], evac_ab)
                return posts

            def ab_post(posts):
                if is_s:
                    for (ch_, Gt_, Gr_) in posts:
                        sample_post(ch_, Gt_, Gr_)
                else:
                    prompt_post(posts)

            def ab_tail():
              if is_s:
                for s in range(16):
                    out_dma(lambda e, s=s: e.dma_start(out=convs_d[s, 22:30, :], in_=MS[0][s * 8:s * 8 + 8, :]), [("MS", 0)], "o_convs")
                out_dma(lambda e: e.dma_start(out=convs_d[:, 0:22, :], in_=sc_d[:, 8:30, :]), [], "o_convs")
              if lastp:
                for ch in range(16):
                    bt = nb()
                    A("pe", lambda e, bt=bt, ch=ch: e.transpose(out=banks[bt][0:30, 0:128], in_=GO[:, ch, :], identity=ident_f[:, :]),
                      reads=["GO", "ident_f"], writes=[("bk", bt)])
                    A("act", lambda e, bt=bt, ch=ch: e.copy(out=MS[0][0:30, ch * 128:(ch + 1) * 128], in_=banks[bt][0:30, 0:128]),
                      reads=[("bk", bt)], writes=[("MS", 0)])
                out_dma(lambda e: e.dma_start(out=convp_d[:, :], in_=MS[0][0:30, :]), [("MS", 0)], "o_convp")
            def q_group(g):
                def evac_q(i, bk, g=g):
                    cp = g * 4 + i
                    A("act", lambda e: e.activation(out=qT[:, cp, 0:W], in_=banks[bk][:, 0:W], func=AF.Identity, scale=0.125,
                                                    bias=bq8[:, cp:cp + 1]), reads=[("bk", bk), "bq8"], writes=["qT"])
                fm_group(w_in_v, (8 + g) * 512, [0, 1, 2, 3], lambda kk: xnT[:, kk, mo:mo + W], W, ["xnT"], evac_q)

            kvst = {}

            def kv_part():
              def evac_k(i, bk):
                  A("act", lambda e: e.activation(out=KT[:, i, row_lo:row_lo + W], in_=banks[bk][:, 0:W], func=AF.Identity,
                                                  bias=b_in_T[:, 48 + i:49 + i]), reads=[("bk", bk), "b_in_T"], writes=["KT"])
              s01 = fm_group(w_in_v, 12 * 512, [0, 1], lambda kk: xnT[:, kk, mo:mo + W], W, ["xnT"], evac_k)
              kvst["s01"] = s01
              if hp:
                  for i in range(2):
                      bk = nb()
                      for h, s in enumerate(s01):
                          for k in range(8):
                              kk = 8 * h + k
                              A("pe", lambda e, s=s, k=k, kk=kk, i=i, bk=bk: e.matmul(
                                  banks[bk][:, 0:128], lhsT=ring[s][:, k, i * 128:(i + 1) * 128], rhs=xnT[:, kk, 0:128],
                                  start=(kk == 0), stop=(kk == 15)), reads=[("w", s), "xnT"], writes=[("bk", bk)])
                      A("act", lambda e, i=i, bk=bk: e.activation(out=KT[:, i, 0:128], in_=banks[bk][:, 0:128], func=AF.Identity,
                                                                bias=b_in_T[:, 48 + i:49 + i]), reads=[("bk", bk), "b_in_T"], writes=["KT"])
              if not is_s:
                  wins = []
                  if hp:
                      wins += [(0, 0), (128, 1), (2, 0)]
                  for (r0_, n) in blocks:
                      if n == 128:
                          wins.append((r0_, None))
                  for (wr, mask) in wins:
                      vi = len(vp_of_row) % 8
                      vp_of_row[wr] = vi
                      last_win = (wr == 1026)
                      make_vpad(wr - row_lo + mo, vi, mask, with_k_out=last_win, s01=s01)
                      if last_win:
                          out_dma(lambda e: e.dma_start(out=kvp_d[:, :], in_=Vf[:, :]), [("tmpA", 1)], "o_kvp")
              else:
                  make_vpad(0, 7, None, with_k_out=True, s01=s01)
                  for s in range(16):
                      out_dma(lambda e, s=s: e.dma_start(out=ks_d[s, 120:128, :], in_=Vf[s * 8:s * 8 + 8, 0:256]), [("tmpA", 1)], "o_ks")
                      out_dma(lambda e, s=s: e.dma_start(out=vs_d[s, 120:128, :], in_=Vf[s * 8:s * 8 + 8, 256:512]), [("tmpA", 1)], "o_vs")
                  out_dma(lambda e: e.dma_start(out=ks_d[:, 0:120, :], in_=ck_d[:, 8:128, :]), [], "o_ks")
                  out_dma(lambda e: e.dma_start(out=vs_d[:, 0:120, :], in_=cv_d[:, 8:128, :]), [], "o_vs")
            def attn_p(col, r0_, n):
                if True:
                    if n == 2:
                        w0, w1 = 0, 128
                        o0 = (OPF, "OPF"); o1 = (OPB, "OPB"); rr0 = 1
                    else:
                        w0, w1 = r0_ - 128, r0_
                        o0 = (OPF, "OPF") if w0 == 2 else (OP1, "OP1"); o1 = (OP1, "OP1"); rr0 = 1
                    v0, v1 = vp_of_row[w0], vp_of_row[w1]
                    attn_block(col, n, rr0,
                               lambda gp, hs, w0=w0: KT[hs, gp, w0:w0 + 128], "KT",
                               lambda gp, hs, w1=w1: KT[hs, gp, w1:w1 + 128], "KT", 128,
                               VP[v0], ("VP", v0), o0, VP[v1], ("VP", v1), o1)

            if is_s:
                for g in range(8):
                    ab_post(ab_group(g))
                ab_tail()
                for g in range(4):
                    q_group(g)
                kv_part()
            else:
                Bq = [(lambda g=g: q_group(g)) for g in range(4)] + [kv_part]
                col_ = 0
                for (r0_, n) in blocks:
                    Bq.append(lambda col_=col_, r0_=r0_, n=n: attn_p(col_, r0_, n))
                    col_ += n
                pending = None
                bidx = 0
                for g in range(8):
                    posts_g = ab_group(g)
                    if pending is not None:
                        ab_post(pending)
                    if g >= 1 and bidx < len(Bq):
                        Bq[bidx](); bidx += 1
                    pending = posts_g
                ab_post(pending)
                ab_tail()
                while bidx < len(Bq):
                    Bq[bidx](); bidx += 1
            s01 = kvst["s01"]
            if is_s:
                for s in range(16):
                    j2 = s % 2
                    if j2 == 0:
                        A("pool", lambda e, s=s: e.dma_start(out=ckb[:, :, :], in_=ck_d[s:s + 2].rearrange("s k d -> k s d")), writes=[("VP", 4)], dma="ckb")
                        A("pool", lambda e, s=s: e.dma_start(out=cvbuf[:, :, :], in_=cv_d[s:s + 2].rearrange("s k d -> k s d")), writes=[("VP", 6)], dma="cvb")
                        for s2 in range(2):
                            tb = ntb()
                            for gp in range(2):
                                A("pe", lambda e, tb=tb, gp=gp, s2=s2: e.transpose(out=tbanks[tb][:, gp * 128:(gp + 1) * 128],
                                                                              in_=ckb[:, s2, gp * 128:(gp + 1) * 128], identity=ident_b[:, :]),
                                  reads=[("VP", 4), "ident_b"], writes=[("tb", tb)])
                            A("act", lambda e, tb=tb, s2=s2: e.copy(out=KcT[:, s2, :, :], in_=tbanks[tb][:, 0:256].rearrange("p (g k) -> p g k", g=2)),
                              reads=[("tb", tb)], writes=[("VP", 5)])
                            vpad_from(cvbuf[:, s2, :], ("VP", 6), VPc[s2], ("VP", s2), None)
                    bk = nb()
                    for h, sl in enumerate(s01):
                        for k in range(8):
                            kk = 8 * h + k
                            A("pe", lambda e, sl=sl, k=k, kk=kk, bk=bk, s=s: e.matmul(
                                banks[bk][0:8, 0:256], lhsT=xnT[:, kk, s * 8:s * 8 + 8], rhs=ring[sl][:, k, 256:512],
                                start=(kk == 0), stop=(kk == 15)), reads=[("w", sl), "xnT"], writes=[("bk", bk)])
                    A("dve", lambda e, bk=bk: e.tensor_tensor(out=tmpA[2][0:8, 0:256], in0=banks[bk][0:8, 0:256], in1=bkvB[0:8, 256:512], op=ALU.add),
                      reads=[("bk", bk), "bkvB"], writes=[("tmpA", 2)])
                    vpad_from(tmpA[2][0:8, 0:256], ("tmpA", 2), VsP[j2], ("VP", 2 + j2), None, npart=8)
                    rs = 1154 + s * 8
                    attn_block(s * 8, 8, 1,
                               lambda gp, hs, j2=j2: KcT[hs, j2, gp, :], ("VP", 5),
                               lambda gp, hs, rs=rs: KT[hs, gp, rs:rs + 8], "KT", 8,
                               VPc[j2], ("VP", j2), (OP1, "OP1"), VsP[j2], ("VP", 2 + j2), (OP1, "OP1"))
            bsum = nb(); bsq = nb()
            for k in range(16):
                A("pe", lambda e, k=k: e.matmul(banks[bsum][:, 0:W], lhsT=ones_f[:, :], rhs=cT[:, k, 0:W], start=(k == 0), stop=(k == 15)),
                  reads=["ones_f", ("cT", k)], writes=[("bk", bsum)])
                zi = k % 2
                A("act", lambda e, k=k, zi=zi: e.activation(out=lnz[zi][:, 0:W], in_=cT[:, k, 0:W], func=AF.Square), reads=[("cT", k)], writes=[("lnz", zi)])
                A("pe", lambda e, k=k, zi=zi: e.matmul(banks[bsq][:, 0:W], lhsT=ones_f[:, :], rhs=lnz[zi][:, 0:W], start=(k == 0), stop=(k == 15)),
                  reads=["ones_f", ("lnz", zi)], writes=[("bk", bsq)])
            A("dve", lambda e: e.tensor_scalar(out=lnm[:, 0:W], in0=banks[bsum][:, 0:W], scalar1=1.0 / D, scalar2=None, op0=ALU.mult),
              reads=[("bk", bsum)], writes=["lnm"])
            A("dve", lambda e: e.tensor_tensor(out=lnt[:, 0:W], in0=lnm[:, 0:W], in1=lnm[:, 0:W], op=ALU.mult), reads=["lnm"], writes=["lnt"])
            A("dve", lambda e: e.scalar_tensor_tensor(out=lnr[:, 0:W], in0=banks[bsq][:, 0:W], scalar=1.0 / D, in1=lnt[:, 0:W],
                                                      op0=ALU.mult, op1=ALU.subtract), reads=[("bk", bsq), "lnt"], writes=["lnr"])
            A("act", lambda e: e.activation(out=lnr[:, 0:W], in_=lnr[:, 0:W], func=AF.Sqrt, bias=epsT[:, 0:1]), reads=["lnr", "epsT"], writes=["lnr"])
            A("dve", lambda e: e.reciprocal(out=lnr[:, 0:W], in_=lnr[:, 0:W]), reads=["lnr"], writes=["lnr"])
            for k in range(16):
                zi = k % 2
                A("dve", lambda e, k=k, zi=zi: e.tensor_tensor(out=lnz[zi][:, 0:W], in0=cT[:, k, 0:W], in1=lnm[:, 0:W], op=ALU.subtract),
                  reads=[("cT", k), "lnm"], writes=[("lnz", zi)])
                A("dve", lambda e, k=k, zi=zi: e.tensor_tensor(out=lnz[zi][:, 0:W], in0=lnz[zi][:, 0:W], in1=lnr[:, 0:W], op=ALU.mult),
                  reads=[("lnz", zi), "lnr"], writes=[("lnz", zi)])
                A("act", lambda e, k=k, zi=zi: e.activation(out=sT[:, k, 0:W], in_=lnz[zi][:, 0:W], func=AF.Silu, scale=ln_g[:, k:k + 1], bias=ln_b[:, k:k + 1]),
                  reads=[("lnz", zi), "ln_g", "ln_b"], writes=["sT"])
            for cg in range(4):
                def ev_co(i, bk):
                    A("act", lambda e: e.copy(out=CO[:, i, 0:W], in_=banks[bk][:, 0:W]), reads=[("bk", bk)], writes=["CO"])
                fm_group(w_cp_v, cg * 512, [0, 1, 2, 3], lambda kk: sT[:, kk, 0:W], W, ["sT"], ev_co)

                def ev_gc(i, bk, cg=cg):
                    ch = 52 + cg * 4 + i
                    zi = i % 2
                    A("act", lambda e: e.activation(out=sgt[zi][:, 0:W], in_=banks[bk][:, 0:W], func=AF.Sigmoid, bias=b_in_T[:, ch:ch + 1]),
                      reads=[("bk", bk), "b_in_T"], writes=[("sgt", zi)])
                    A("dve", lambda e: e.tensor_tensor(out=T1[:, i, 0:W], in0=sgt[zi][:, 0:W], in1=CO[:, i, 0:W], op=ALU.mult),
                      reads=[("sgt", zi), "CO"], writes=["T1"])
                fm_group(w_in_v, (13 + cg) * 512, [0, 1, 2, 3], lambda kk: xnT[:, kk, mo:mo + W], W, ["xnT"], ev_gc)
                fm_group(w_ap_v, cg * 512, [0, 1, 2, 3], lambda kk: oT[:, kk, 0:W], W, ["oT"], ev_co)

                def ev_ga(i, bk, cg=cg):
                    ch = 68 + cg * 4 + i
                    zi = i % 2
                    A("act", lambda e: e.activation(out=sgt[zi][:, 0:W], in_=banks[bk][:, 0:W], func=AF.Sigmoid, bias=b_in_T[:, ch:ch + 1]),
                      reads=[("bk", bk), "b_in_T"], writes=[("sgt", zi)])
                    A("dve", lambda e: e.tensor_tensor(out=sgt[zi][:, 0:W], in0=sgt[zi][:, 0:W], in1=CO[:, i, 0:W], op=ALU.mult),
                      reads=[("sgt", zi), "CO"], writes=[("sgt", zi)])
                    A("dve", lambda e: e.tensor_tensor(out=mixT[:, cg * 4 + i, 0:W], in0=sgt[zi][:, 0:W], in1=T1[:, i, 0:W], op=ALU.add),
                      reads=[("sgt", zi), "T1"], writes=["mixT"])
                fm_group(w_in_v, (17 + cg) * 512, [0, 1, 2, 3], lambda kk: xnT[:, kk, mo:mo + W], W, ["xnT"], ev_ga)
            gload(GB0, "GB0", g_post_d)
            for cg in range(4):
                s0 = wload(w_o_v[:, 0:8, cg * 512:(cg + 1) * 512]); s1 = wload(w_o_v[:, 8:16, cg * 512:(cg + 1) * 512])
                col = 0
                for bi, (r0_, n) in enumerate(blocks):
                    bk = nb()
                    for h, s in enumerate((s0, s1)):
                        for k in range(8):
                            kk = 8 * h + k
                            A("pe", lambda e, s=s, k=k, kk=kk, bk=bk, col=col, n=n: e.matmul(
                                banks[bk][0:n, :], lhsT=mixT[:, kk, col:col + n], rhs=ring[s][:, k, :], start=(kk == 0), stop=(kk == 15)),
                              reads=[("w", s), "mixT"], writes=[("bk", bk)])
                    ms_i = 2 if n == 2 else (bi - (1 if hp else 0))
                    A("act", lambda e, bk=bk, cg=cg, ms_i=ms_i, n=n: e.copy(out=MS[ms_i][0:n, cg * 512:(cg + 1) * 512], in_=banks[bk][0:n, :]),
                      reads=[("bk", bk)], writes=[("MS", ms_i)])
                    col += n
            col = 0
            for bi, (r0_, n) in enumerate(blocks):
                ms_i = 2 if n == 2 else (bi - (1 if hp else 0))
                xb, xr = xbufs[bi]
                rmsnorm_stats(MS[ms_i][0:n, :], n, 1, [("MS", ms_i)])
                A("dve", lambda e, ms_i=ms_i, n=n: e.scalar_tensor_tensor(out=MS[ms_i][0:n, :], in0=MS[ms_i][0:n, :], scalar=rstd_t[0:n, 1:2], in1=GB0[0:n, :],
                                                                        op0=ALU.mult, op1=ALU.mult), reads=[("MS", ms_i), ("rstd", 1), "GB0"], writes=[("MS", ms_i)])
                A("dve", lambda e, ms_i=ms_i, n=n, xb=xb: e.tensor_tensor(out=xb[0:n, :], in0=xb[0:n, :], in1=MS[ms_i][0:n, :], op=ALU.add),
                  reads=[xr, ("MS", ms_i)], writes=[xr])
                col += n
            gload(GB0, "GB0", g_fpre_d)
            col = 0
            for bi, (r0_, n) in enumerate(blocks):
                xb, xr = xbufs[bi]
                rmsnorm_stats(xb[0:n, :], n, 2, [xr])
                A("dve", lambda e, n=n, xb=xb: e.scalar_tensor_tensor(out=XS[0:n, :], in0=xb[0:n, :], scalar=rstd_t[0:n, 2:3], in1=GB0[0:n, :],
                                                                    op0=ALU.mult, op1=ALU.mult), reads=[xr, ("rstd", 2), "GB0"], writes=["XS"])
                transpose_to(hnT, "xnT", col, n, "XS")
                col += n
            for g in range(32):
                u_pair = {}

                def ev_up(i, bk, g=g):
                    u_pair[i] = bk
                    if i % 2 == 0:
                        return
                    j = g * 2 + i // 2
                    res = []
                    for t2, bkx in enumerate((u_pair[i - 1], bk)):
                        chn = 2 * j + t2
                        ui = cnt["u"] % 4; cnt["u"] += 1
                        Ut, Ur = Ub[ui], ("Ub", ui)
                        A("act", lambda e, Ut=Ut, bkx=bkx: e.copy(out=Ut[:, 2:2 + W], in_=banks[bkx][:, 0:W]), reads=[("bk", bkx)], writes=[Ur])
                        cvt, cvr = cvb[ui], ("cvb", ui)
                        if is_s:
                            i4 = i // 2 * 2 + t2
                            U3 = UST[:, i4, :, :]
                            A("dve", lambda e, Ut=Ut, U3=U3: e.tensor_copy(out=U3[:, :, 2:10], in_=Ut[:, 2:130].rearrange("p (s t) -> p s t", t=8)),
                              reads=[Ur, ("USTh", g)], writes=[("UST", i4)])
                            c3 = cvt[:, 0:128].rearrange("p (s t) -> p s t", t=8)
                            A("dve", lambda e, U3=U3, c3=c3, chn=chn: e.tensor_scalar(out=c3, in0=U3[:, :, 0:8], scalar1=ffn_k[:, chn, 0:1], scalar2=ffn_b[:, chn:chn + 1],
                                                                                   op0=ALU.mult, op1=ALU.add), reads=[("UST", i4), "ffn_k", "ffn_b"], writes=[cvr])
                            for tp in (1, 2):
                                A("dve", lambda e, U3=U3, c3=c3, chn=chn, tp=tp: e.scalar_tensor_tensor(out=c3, in0=U3[:, :, tp:tp + 8], scalar=ffn_k[:, chn, tp:tp + 1], in1=c3,
                                                                                                      op0=ALU.mult, op1=ALU.add), reads=[("UST", i4), "ffn_k", cvr], writes=[cvr])
                            A("act", lambda e, Ut=Ut, i4=i4: e.copy(out=UO[:, i4, 2:34].rearrange("p (s t) -> p s t", t=2),
                                                                   in_=Ut[:, 2:130].rearrange("p (s t) -> p s t", t=8)[:, :, 6:8]), reads=[Ur], writes=[("UO", i4)])
                        else:
                            if hp:
                                A("dve", lambda e, Ut=Ut: e.tensor_scalar(out=Ut[:, 2:4], in0=Ut[:, 2:4], scalar1=kmask[:, 0:1], scalar2=None, op0=ALU.mult),
                                  reads=[Ur, "kmask"], writes=[Ur])
                                A("dve", lambda e, Ut=Ut: e.memset(Ut[:, 0:2], 0.0), writes=[Ur])
                            else:
                                A("dve", lambda e, Ut=Ut, chn=chn: e.tensor_copy(out=Ut[:, 0:2], in_=uhist[:, chn, :]), reads=[("uhist", chn)], writes=[Ur])
                            A("dve", lambda e, Ut=Ut, chn=chn: e.tensor_copy(out=uhist[:, chn, :], in_=Ut[:, W:W + 2]), reads=[Ur], writes=[("uhist", chn)])
                            if lastp:
                                i4 = i // 2 * 2 + t2
                                A("act", lambda e, Ut=Ut, i4=i4: e.copy(out=UO[:, i4, 0:2], in_=Ut[:, W:W + 2]), reads=[Ur], writes=[("UO", i4)])
                            A("dve", lambda e, Ut=Ut, cvt=cvt, chn=chn: e.tensor_scalar(out=cvt[:, 0:W], in0=Ut[:, 0:W], scalar1=ffn_k[:, chn, 0:1], scalar2=ffn_b[:, chn:chn + 1],
                                                                                     op0=ALU.mult, op1=ALU.add), reads=[Ur, "ffn_k", "ffn_b"], writes=[cvr])
                            for tp in (1, 2):
                                A("dve", lambda e, Ut=Ut, cvt=cvt, chn=chn, tp=tp: e.scalar_tensor_tensor(out=cvt[:, 0:W], in0=Ut[:, tp:tp + W], scalar=ffn_k[:, chn, tp:tp + 1], in1=cvt[:, 0:W],
                                                                                                        op0=ALU.mult, op1=ALU.add), reads=[Ur, "ffn_k", cvr], writes=[cvr])
                        res.append((cvt, cvr))
                    (cgt, cgr), (cvt, cvr) = res
                    gi = j % 2
                    A("act", lambda e: e.activation(out=geb[gi][:, 0:W], in_=cgt[:, 0:W], func=AF.Gelu_apprx_tanh), reads=[cgr], writes=[("geb", gi)])
                    A("dve", lambda e: e.tensor_tensor(out=actT[:, j, 0:W], in0=geb[gi][:, 0:W], in1=cvt[:, 0:W], op=ALU.mult),
                      reads=[("geb", gi), cvr], writes=["actT"])

                if is_s:
                    sj = 0
                    spq(lambda e, sj=sj, g=g: e.dma_start(out=sfb[sj][0:32, :], in_=sf_d.rearrange("s r c -> (s r) c")[:, g * 512:(g + 1) * 512]), writes=[("tmpA", 0)])
                    for i4 in range(4):
                        bt = nb()
                        A("pe", lambda e, bt=bt, i4=i4, sj=sj: e.transpose(out=banks[bt][:, 0:32], in_=sfb[sj][0:32, i4 * 128:(i4 + 1) * 128], identity=ident_f[0:32, 0:32]),
                          reads=[("tmpA", 0), "ident_f"], writes=[("bk", bt)])
                        A("act", lambda e, bt=bt, i4=i4: e.copy(out=UST[:, i4, :, 0:2], in_=banks[bt][:, 0:32].rearrange("p (s r) -> p s r", r=2)),
                          reads=[("bk", bt)], writes=[("UST", i4), ("USTh", g)])
                fm_group(w_up_v, g * 512, [0, 1, 2, 3], lambda kk: hnT[:, kk, 0:W], W, ["xnT"], ev_up)
                if lastp or is_s:
                    c_lo, c_n = (0, 2) if lastp else (2, 32)
                    oi = 0
                    for i4 in range(4):
                        bt = nb()
                        A("pe", lambda e, bt=bt, i4=i4: e.transpose(out=banks[bt][0:c_n, 0:128], in_=UO[:, i4, c_lo:c_lo + c_n], identity=ident_f[:, :]),
                          reads=[("UO", i4), "ident_f"], writes=[("bk", bt)])
                        A("act", lambda e, bt=bt, i4=i4, oi=oi: e.copy(out=uost[oi][0:c_n, i4 * 128:(i4 + 1) * 128], in_=banks[bt][0:c_n, 0:128]),
                          reads=[("bk", bt)], writes=[("tmpA", 1)])
                    if lastp:
                        out_dma(lambda e, oi=oi, g=g: e.dma_start(out=ffnp_d[:, g * 512:(g + 1) * 512], in_=uost[oi][0:2, :]), [("tmpA", 1)], "o_ffnp%d" % oi)
                    else:
                        out_dma(lambda e, oi=oi, g=g: e.dma_start(out=ffns_d[:, g * 512:(g + 1) * 512], in_=uost[oi][0:32, :]), [("tmpA", 1)], "o_ffns%d" % oi)
            gload(GB0, "GB0", g_fpost_d)
            oblocks = [(bi, r0_, n) for bi, (r0_, n) in enumerate(blocks) if n == 128]
            for cg in range(4):
                bkm = {bi: nb() for (bi, _, _) in oblocks}
                for kq in range(8):
                    s = wload(w_dn_v[:, kq * 8:(kq + 1) * 8, cg * 512:(cg + 1) * 512])
                    col = 0
                    for bi, (r0_, n) in enumerate(blocks):
                        if n == 128:
                            for k in range(8):
                                kk = kq * 8 + k
                                A("pe", lambda e, s=s, k=k, kk=kk, bi=bi, col=col, bkm=bkm: e.matmul(
                                    banks[bkm[bi]][:, :], lhsT=actT[:, kk, col:col + 128], rhs=ring[s][:, k, :], start=(kk == 0), stop=(kk == 63)),
                                  reads=[("w", s), "actT"], writes=[("bk", bkm[bi])])
                        col += n
                for (bi, r0_, n) in oblocks:
                    ms_i = bi - (1 if hp else 0)
                    A("act", lambda e, bi=bi, ms_i=ms_i, cg=cg, bkm=bkm: e.copy(out=MS[ms_i][:, cg * 512:(cg + 1) * 512], in_=banks[bkm[bi]][:, :]),
                      reads=[("bk", bkm[bi])], writes=[("MS", ms_i)])
            for (bi, r0_, n) in oblocks:
                ms_i = bi - (1 if hp else 0)
                xb, xr = xbufs[bi]
                rmsnorm_stats(MS[ms_i][:, :], 128, 3, [("MS", ms_i)])
                A("dve", lambda e, ms_i=ms_i: e.scalar_tensor_tensor(out=MS[ms_i][:, :], in0=MS[ms_i][:, :], scalar=rstd_t[:, 3:4], in1=GB0[:, :],
                                                                   op0=ALU.mult, op1=ALU.mult), reads=[("MS", ms_i), ("rstd", 3), "GB0"], writes=[("MS", ms_i)])
                A("dve", lambda e, ms_i=ms_i, xb=xb: e.tensor_tensor(out=MS[ms_i][:, :], in0=MS[ms_i][:, :], in1=xb[:, :], op=ALU.add),
                  reads=[("MS", ms_i), xr], writes=[("MS", ms_i)])
                yrow = r0_ - 130
                out_dma(lambda e, ms_i=ms_i, yrow=yrow: e.dma_start(out=y_d[yrow:yrow + 128, :], in_=MS[ms_i][:, :]), [("MS", ms_i)], "o_y%d" % ms_i)

        for ti_, (kind_, blocks_) in enumerate(TILES):
            process_tile(ti_, kind_, blocks_)

        S.finalize()
        sems = {}
        for eng in S.ENGS:
            sems[("eng", eng)] = es.enter_context(nc.semaphore("s_" + eng))
        for gname in S.dma_final:
            sems[("dma", gname)] = es.enter_context(nc.semaphore("d_" + gname))
        with nc.Block() as block:
            @block.tensor
            def _(e): S.emit("pe", e, sems)

            @block.scalar
            def _(e): S.emit("act", e, sems)

            @block.vector
            def _(e): S.emit("dve", e, sems)

            @block.gpsimd
            def _(e): S.emit("pool", e, sems)

            @block.sync
            def _(e):
                S.emit("sp", e, sems)
                for gname in out_dma_groups:
                    e.wait_ge(sems[("dma", gname)], S.dma_final[gname])
    return nc


_NC_CACHE = {}


def kernel(x_prompt, x_sample, cache_k, cache_v, state_conv, state_ffn_conv,
           norm_mix_pre, w_in, b_in, conv_dw_k, conv_dw_b, conv_ln_g, conv_ln_b, w_conv_proj,
           attn_sinks, rel_bias, w_attn_proj, w_out, norm_mix_post,
           norm_ffn_pre, w_up, ffn_dw_k, ffn_dw_b, w_down, norm_ffn_post):
    f = np.float32
    x_prompt = np.asarray(x_prompt, f); x_sample = np.asarray(x_sample, f)
    wp = win_perm()
    w_in_p = np.ascontiguousarray(np.asarray(w_in, f)[0][:, wp])
    b_in_p = np.asarray(b_in, f)[0][wp]
    b_in_T = fmT(b_in_p, 84)
    b_kv = np.ascontiguousarray(np.asarray(b_in, f)[0][6144:6656])
    arp = attn_row_perm()
    w_ap = np.ascontiguousarray(np.asarray(w_attn_proj, f)[0][arp, :])
    up = wup_perm()
    w_up_p = np.ascontiguousarray(np.asarray(w_up, f)[0][:, up])
    ffn_k_T = np.ascontiguousarray(np.asarray(ffn_dw_k, f)[0][:, up].reshape(3, 128, 128).transpose(2, 1, 0))
    ffn_b_T = fmT(np.asarray(ffn_dw_b, f)[0][up], 128)
    conv_k_T = np.ascontiguousarray(np.asarray(conv_dw_k, f)[0].reshape(31, 16, 128).transpose(2, 1, 0))
    sinks = np.asarray(attn_sinks, f)[0]
    sinkT = np.zeros((128, 16), f)
    rbp = np.zeros((32, 32), f)
    rb = np.asarray(rel_bias, f)
    for cp in range(16):
        for half in range(2):
            h = head_of(half, cp)
            sinkT[half * 64:half * 64 + 64, cp] = sinks[h]
            rbp[:, half * 16 + cp] = rb[:, h]
    bucket = rel_bucket_np(np.arange(128))
    ohb = np.zeros((32, 128), f); ohb[bucket, np.arange(128)] = 1.0
    jm = np.zeros((128, 384), f)
    for dlt in range(128):
        jm[dlt, 255 - dlt] = 1.0
    mneg = np.zeros((128, 2, 129), f)
    s_idx = np.arange(128)[:, None]; r_idx = np.arange(129)[None, :]
    mneg[:, 0, :] = np.where(s_idx >= r_idx, 0.0, NEG)
    mneg[:, 1, :] = np.where(s_idx <= r_idx - 1, 0.0, NEG)

    shared = {
        "w_in": w_in_p, "b_in_T": b_in_T, "b_kv": b_kv,
        "conv_k_T": conv_k_T, "conv_b_T": fmT(np.asarray(conv_dw_b, f)[0], 16),
        "ln_g_T": fmT(np.asarray(conv_ln_g, f)[0], 16), "ln_b_T": fmT(np.asarray(conv_ln_b, f)[0], 16),
        "w_cp": np.ascontiguousarray(np.asarray(w_conv_proj, f)[0]), "w_ap": w_ap, "w_o": np.ascontiguousarray(np.asarray(w_out, f)[0]),
        "g_pre": np.ascontiguousarray(np.asarray(norm_mix_pre, f)[0]), "g_post": np.ascontiguousarray(np.asarray(norm_mix_post, f)[0]),
        "g_fpre": np.ascontiguousarray(np.asarray(norm_ffn_pre, f)[0]), "g_fpost": np.ascontiguousarray(np.asarray(norm_ffn_post, f)[0]),
        "w_up": w_up_p, "ffn_k_T": ffn_k_T, "ffn_b_T": ffn_b_T, "w_dn": np.ascontiguousarray(np.asarray(w_down, f)[0]),
        "sinkT": sinkT, "rbp": rbp, "ohb": ohb, "jm": jm, "mneg": mneg,
    }
    in_maps = []
    for c in range(NCORES):
        b, qd = c // 4, c % 4
        T0 = 1024 * qd
        xin = np.zeros((NROWS, D), f)
        if qd > 0:
            xin[0:130] = x_prompt[b, T0 - 130:T0]
        xin[130:1154] = x_prompt[b, T0:T0 + 1024]
        xin[1154:1282] = x_sample[16 * c:16 * c + 16].reshape(128, D)
        km = np.ones((128, 2), f)
        if qd == 0:
            km[:, 0] = 0.0
            km[0:2, 1] = 0.0
        m = dict(shared)
        m["xin"] = xin; m["kmask"] = km
        m["ck"] = np.ascontiguousarray(np.asarray(cache_k, f)[0, 16 * c:16 * c + 16].reshape(16, 128, 256))
        m["cv"] = np.ascontiguousarray(np.asarray(cache_v, f)[0, 16 * c:16 * c + 16].reshape(16, 128, 256))
        m["sc"] = np.ascontiguousarray(np.asarray(state_conv, f)[0, 16 * c:16 * c + 16])
        m["sf"] = np.ascontiguousarray(np.asarray(state_ffn_conv, f)[0, 16 * c:16 * c + 16][:, :, up])
        in_maps.append(m)

    if "nc" not in _NC_CACHE:
        _NC_CACHE["nc"] = build()
    nc = _NC_CACHE["nc"]
    res = run_bass_kernel_spmd(nc, in_maps, core_ids=list(range(NCORES)))
    R = res.results
    inv_up = np.argsort(up)
    y_p = np.zeros((2, 4096, D), f); y_s = np.zeros((128, 8, D), f)
    k_p = np.zeros((1, 2, 128, 4, 64), f); v_p = np.zeros((1, 2, 128, 4, 64), f)
    conv_p = np.zeros((1, 2, 30, D), f); ffn_p = np.zeros((1, 2, 2, DFF2), f)
    k_s = np.zeros((1, 128, 128, 4, 64), f); v_s = np.zeros((1, 128, 128, 4, 64), f)
    conv_s = np.zeros((1, 128, 30, D), f); ffn_s = np.zeros((1, 128, 2, DFF2), f)
    for c in range(NCORES):
        b, qd = c // 4, c % 4
        r = R[c]
        y_p[b, 1024 * qd:1024 * qd + 1024] = r["y"][0:1024]
        y_s[16 * c:16 * c + 16] = r["y"][1024:1152].reshape(16, 8, D)
        if qd == 3:
            k_p[0, b] = r["kvp"][:, 0:256].reshape(128, 4, 64)
            v_p[0, b] = r["kvp"][:, 256:512].reshape(128, 4, 64)
            conv_p[0, b] = r["convp"]
            ffn_p[0, b] = r["ffnp"][:, inv_up]
        k_s[0, 16 * c:16 * c + 16] = r["ks"].reshape(16, 128, 4, 64)
        v_s[0, 16 * c:16 * c + 16] = r["vs"].reshape(16, 128, 4, 64)
        conv_s[0, 16 * c:16 * c + 16] = r["convs"]
        ffn_s[0, 16 * c:16 * c + 16] = r["ffns"].reshape(16, 2, DFF2)[:, :, inv_up]
    return (y_p, y_s, k_p, v_p, conv_p, ffn_p, k_s, v_s, conv_s, ffn_s)
```

```python
import contextlib
import math
import numpy as np
import concourse.bass as bass
import concourse.mybir as mybir
from concourse.bass_utils import run_bass_kernel_spmd

F32 = mybir.dt.float32
BF16 = mybir.dt.bfloat16
AF = mybir.ActivationFunctionType
ALU = mybir.AluOpType

D = 2048
NH = 32
DIN = 10752
DFF2 = 16384
EPS = 1e-6
NCORES = 8
NROWS = 1282
NEG = -30000.0


class Op:
    __slots__ = ("eng", "fn", "deps", "signal", "sigval", "dma", "idx")

    def __init__(self, eng, fn, dma=None):
        self.eng = eng; self.fn = fn; self.deps = []; self.signal = False
        self.sigval = 0; self.dma = dma; self.idx = 0


class Sched:
    ENGS = ("pe", "act", "dve", "pool", "sp")

    def __init__(self):
        self.q = {e: [] for e in self.ENGS}
        self.lastw = {}
        self.readers = {}
        self.nops = 0
        self.alias = {}

    def add(self, eng, fn, reads=(), writes=(), dma=None):
        op = Op(eng, fn, dma)
        op.idx = self.nops; self.nops += 1
        writes = list(writes)
        for w in list(writes):
            for a in self.alias.get(w, ()):
                if a not in writes: writes.append(a)
        deps = {}
        for r in reads:
            w = self.lastw.get(r)
            if w is not None: deps[id(w)] = w
        for w in writes:
            lw = self.lastw.get(w)
            if lw is not None: deps[id(lw)] = lw
            for rd in self.readers.get(w, ()):
                deps[id(rd)] = rd
        for d in deps.values():
            if d is op: continue
            if d.dma is None and op.dma is None and d.eng == eng and eng == "pe":
                continue
            op.deps.append(d)
            d.signal = True
        for w in writes:
            self.lastw[w] = op
            self.readers[w] = []
        for r in reads:
            if r in writes: continue
            self.readers.setdefault(r, []).append(op)
        self.q[eng].append(op)
        return op

    def finalize(self):
        for e in self.ENGS:
            cnt = 0
            for op in self.q[e]:
                if op.dma is None and op.signal:
                    cnt += 1; op.sigval = cnt
        gc = {}
        allops = sorted([op for e in self.ENGS for op in self.q[e] if op.dma is not None], key=lambda o: o.idx)
        for op in allops:
            gc[op.dma] = gc.get(op.dma, 0) + 16
            op.sigval = gc[op.dma]
        self.dma_final = gc

    def emit(self, eng, e, sems):
        waited = {}
        for op in self.q[eng]:
            need = {}
            for d in op.deps:
                key = ("dma", d.dma) if d.dma is not None else ("eng", d.eng)
                if need.get(key, 0) < d.sigval: need[key] = d.sigval
            for key, val in need.items():
                if waited.get(key, 0) < val:
                    e.wait_ge(sems[key], val)
                    waited[key] = val
            ins = op.fn(e)
            if op.dma is not None:
                ins.then_inc(sems[("dma", op.dma)], 16)
            elif op.signal:
                ins.then_inc(sems[("eng", eng)], 1)


def head_of(half, cp):
    if cp < 8:
        return cp if half == 0 else 8 + cp
    return 16 + (cp - 8) if half == 0 else 24 + (cp - 8)


def win_perm():
    idx = []
    for c in range(16):
        idx += list(range(c * 128, c * 128 + 128))
        idx += list(range(2048 + c * 128, 2048 + c * 128 + 128))
    for cp in range(16):
        for half in range(2):
            h = head_of(half, cp)
            idx += list(range(4096 + h * 64, 4096 + h * 64 + 64))
    idx += list(range(6144, 6144 + 512))
    idx += list(range(6656, 6656 + 4096))
    return np.array(idx, dtype=np.int64)


def attn_row_perm():
    idx = []
    for cp in range(16):
        for half in range(2):
            h = head_of(half, cp)
            idx += list(range(h * 64, h * 64 + 64))
    return np.array(idx, dtype=np.int64)


def wup_perm():
    idx = []
    for j in range(64):
        idx += list(range(j * 128, j * 128 + 128))
        idx += list(range(8192 + j * 128, 8192 + j * 128 + 128))
    return np.array(idx, dtype=np.int64)


def rel_bucket_np(dist):
    max_exact = 16
    d = np.maximum(dist, 1).astype(np.float32)
    large = max_exact + (np.log(d / max_exact) / math.log(128 / max_exact) * (32 - max_exact)).astype(np.int32)
    large = np.minimum(large, 31)
    return np.where(dist < max_exact, dist, large)


def fmT(v, nch):
    return np.ascontiguousarray(v.reshape(nch, 128).T)


TILES = [("s", [(1154, 128)]),
         ("p", [(128, 2), (130, 128), (258, 128)]), ("p", [(386, 128), (514, 128)]), ("p", [(642, 128), (770, 128)]),
         ("p", [(898, 128), (1026, 128)])]
WMAX = 258
EXT = 128


def build():
    nc = bass.Bass("TRN2", target_bir_lowering=False)

    def din(name, shape, dt=F32):
        return nc.dram_tensor(name, list(shape), dt, kind="ExternalInput").ap()

    def dout(name, shape, dt=F32):
        return nc.dram_tensor(name, list(shape), dt, kind="ExternalOutput").ap()

    xin = din("xin", [NROWS, D])
    kmask_d = din("kmask", [128, 2])
    ck_d = din("ck", [16, 128, 256]); cv_d = din("cv", [16, 128, 256])
    sc_d = din("sc", [16, 30, D]); sf_d = din("sf", [16, 2, DFF2])
    w_in_d = din("w_in", [D, DIN]); b_in_T_d = din("b_in_T", [128, 84]); b_kv_d = din("b_kv", [512])
    conv_k_d = din("conv_k_T", [128, 16, 31]); conv_b_d = din("conv_b_T", [128, 16])
    ln_g_d = din("ln_g_T", [128, 16]); ln_b_d = din("ln_b_T", [128, 16])
    w_cp_d = din("w_cp", [D, D]); w_ap_d = din("w_ap", [D, D]); w_o_d = din("w_o", [D, D])
    g_pre_d = din("g_pre", [D]); g_post_d = din("g_post", [D]); g_fpre_d = din("g_fpre", [D]); g_fpost_d = din("g_fpost", [D])
    w_up_d = din("w_up", [D, DFF2]); ffn_k_d = din("ffn_k_T", [128, 128, 3]); ffn_b_d = din("ffn_b_T", [128, 128])
    w_dn_d = din("w_dn", [8192, D])
    sink_d = din("sinkT", [128, 16]); rbp_d = din("rbp", [32, 32])
    ohb_d = din("ohb", [32, 128]); jm_d = din("jm", [128, 384]); mneg_d = din("mneg", [128, 2, 129])

    y_d = dout("y", [1152, D])
    kvp_d = dout("kvp", [128, 512])
    convp_d = dout("convp", [30, D])
    ffnp_d = dout("ffnp", [2, DFF2])
    ks_d = dout("ks", [16, 128, 256]); vs_d = dout("vs", [16, 128, 256])
    convs_d = dout("convs", [16, 30, D])
    ffns_d = dout("ffns", [32, DFF2])

    S = Sched()
    out_dma_groups = []

    with contextlib.ExitStack() as es:
        def sb(name, shape, dt=F32):
            return es.enter_context(nc.sbuf_tensor(name, list(shape), dt))

        def ps(name, shape, dt=F32):
            return es.enter_context(nc.psum_tensor(name, list(shape), dt))

        A = S.add
        NBK = 8
        banks = [ps("bk%d" % i, [128, 512], F32) for i in range(NBK)]
        tbanks = [b[:, :].bitcast(BF16) for b in banks]
        st = {"bk": 0, "tb": 0, "ws": 0, "dq": 0}

        def nb():
            i = st["bk"]; st["bk"] = (i + 1) % NBK
            return i

        def ntb():
            return nb()

        NSLOT = 3
        ring = [sb("ring%d" % i, [128, 8, 512], BF16) for i in range(NSLOT)]

        wscr = nc.dram_tensor("wscr", [162, 128, 4096], BF16).ap()
        st["wi"] = 0
        st["tile"] = 0
        wkeys = {}

        def wload(src):
            s = st["ws"]; st["ws"] = (s + 1) % NSLOT
            key = (src.name, str(src.offset))
            if st["tile"] == 0:
                assert key not in wkeys
                wkeys[key] = len(wkeys)
            idx = wkeys[key]
            if st["tile"] == 0:
                A("pool", lambda e, s=s, src=src: e.dma_start(out=ring[s][:], in_=src), writes=[("w", s)], dma="w%d" % s)
                A("sp", lambda e, s=s, idx=idx: e.dma_start(out=wscr[idx], in_=ring[s][:].rearrange("p k c -> p (k c)")),
                  reads=[("w", s)], writes=[("wscr", idx)], dma="wb%d" % s)
            else:
                A("pool", lambda e, s=s, idx=idx: e.dma_start(out=ring[s][:].rearrange("p k c -> p (k c)"), in_=wscr[idx]),
                  reads=[("wscr", idx)], writes=[("w", s)], dma="w%d" % s)
            return s

        w_in_v = w_in_d.rearrange("(k p) c -> p k c", p=128)
        w_cp_v = w_cp_d.rearrange("(k p) c -> p k c", p=128)
        w_ap_v = w_ap_d.rearrange("(k p) c -> p k c", p=128)
        w_o_v = w_o_d.rearrange("(k p) c -> p k c", p=128)
        w_up_v = w_up_d.rearrange("(k p) c -> p k c", p=128)
        w_dn_v = w_dn_d.rearrange("(k p) c -> p k c", p=128)

        def spq(fn, reads=(), writes=(), grp=None):
            if grp is None:
                grp = "q%d" % st["dq"]; st["dq"] = (st["dq"] + 1) % 6
            return A("sp", fn, reads=reads, writes=writes, dma=grp)

        tmpA = [sb("tmpA%d" % i, [128, 512]) for i in range(3)]
        ident_f = sb("ident_f", [128, 128]); ident_b = sb("ident_b", [128, 128], BF16)
        ones_f = sb("ones_f", [128, 128])
        epsT = sb("epsT", [128, 1])
        kmask = sb("kmaskS", [128, 2])
        b_in_T = sb("b_in_TS", [128, 84]); bq8 = sb("bq8", [128, 16])
        bkvB = sb("bkvB", [128, 512])
        conv_k = sb("conv_kS", [128, 16, 31]); conv_b = sb("conv_bS", [128, 16])
        ln_g = sb("ln_gS", [128, 16]); ln_b = sb("ln_bS", [128, 16])
        ffn_k = sb("ffn_kS", [128, 128, 3]); ffn_b = sb("ffn_bS", [128, 128])
        esink = sb("esink", [128, 16])
        GB0 = sb("GB0", [128, D])
        TBL = sb("TBL", [128, 2, 32, 129], BF16)
        OP1 = sb("OP1", [128, 2, 128], BF16); OPF = sb("OPF", [128, 2, 128], BF16); OPB = sb("OPB", [128, 2, 128], BF16)

        A("dve", lambda e: e.memset(ones_f[:], 1.0), writes=["ones_f"])
        A("dve", lambda e: e.memset(epsT[:], EPS), writes=["epsT"])
        A("pool", lambda e: e.memset(ident_f[:], 0.0), writes=["ident_f"])
        A("pool", lambda e: e.affine_select(out=ident_f[:], in_=ident_f[:], pattern=[[-1, 128]], compare_op=ALU.not_equal,
                                            fill=1.0, base=0, channel_multiplier=1), reads=["ident_f"], writes=["ident_f"])
        A("dve", lambda e: e.tensor_copy(out=ident_b[:], in_=ident_f[:]), reads=["ident_f"], writes=["ident_b"])
        for (dst, src, nm) in ((kmask, kmask_d, "kmask"), (b_in_T, b_in_T_d, "b_in_T"), (conv_k, conv_k_d, "conv_k"),
                               (conv_b, conv_b_d, "conv_b"), (ln_g, ln_g_d, "ln_g"), (ln_b, ln_b_d, "ln_b"),
                               (ffn_k, ffn_k_d, "ffn_k"), (ffn_b, ffn_b_d, "ffn_b"), (esink, sink_d, "esink")):
            spq(lambda e, dst=dst, src=src: e.dma_start(out=dst[:], in_=src), writes=[nm])
        spq(lambda e: e.dma_start(out=bkvB[:], in_=b_kv_d.partition_broadcast(128)), writes=["bkvB"])
        A("act", lambda e: e.activation(out=esink[:], in_=esink[:], func=AF.Exp), reads=["esink"], writes=["esink"])
        A("dve", lambda e: e.tensor_scalar(out=bq8[:], in0=b_in_T[:, 32:48], scalar1=0.125, scalar2=None, op0=ALU.mult),
          reads=["b_in_T"], writes=["bq8"])
        for (T, nm, col) in ((OP1, "OP1", None), (OPF, "OPF", 0), (OPB, "OPB", 1)):
            A("dve", lambda e, T=T: e.memset(T[:], 0.0), writes=[nm])
            for half in range(2):
                if col is None:
                    A("dve", lambda e, T=T, half=half: e.memset(T[:, half, half * 64:half * 64 + 64], 1.0), reads=[], writes=[nm])
                else:
                    A("dve", lambda e, T=T, half=half, col=col: e.tensor_scalar(
                        out=T[:, half, half * 64:half * 64 + 64], in0=ones_f[:, 0:64], scalar1=kmask[:, col:col + 1], scalar2=None,
                        op0=ALU.mult), reads=["ones_f", "kmask"], writes=[nm])

        rbp = sb("rbpS", [32, 32]); ohb = sb("ohbS", [32, 128]); jm = tmpA[0][:, 0:384]; mneg = tmpA[1][:, 0:258].rearrange("p (a b) -> p a b", a=2)
        vec = sb("vecS", [128, 32])
        for (dst, src, nm) in ((rbp[:], rbp_d, "rbp"), (ohb[:], ohb_d, "ohb"), (jm, jm_d, ("tmpA", 0)), (mneg, mneg_d, ("tmpA", 1))):
            spq(lambda e, dst=dst, src=src: e.dma_start(out=dst, in_=src), writes=[nm])
        b0 = nb()
        A("pe", lambda e, b0=b0: e.matmul(banks[b0][:, 0:32], lhsT=ohb[:, :], rhs=rbp[:, :], start=True, stop=True),
          reads=["ohb", "rbp"], writes=[("bk", b0)])
        A("dve", lambda e, b0=b0: e.tensor_copy(out=vec[:], in_=banks[b0][:, 0:32]), reads=[("bk", b0)], writes=["vec"])
        for kc in range(2):
            for r0 in range(0, 129, 16):
                nr = min(16, 129 - r0)
                bk = nb()
                for rr in range(nr):
                    r = r0 + rr
                    start_col = (128 - r) if kc == 0 else (256 - r)
                    A("pe", lambda e, bk=bk, rr=rr, sc_=start_col: e.matmul(
                        banks[bk][:, rr * 32:(rr + 1) * 32], lhsT=jm[:, sc_:sc_ + 128], rhs=vec[:, :], start=True, stop=True),
                      reads=[("tmpA", 0), "vec"], writes=[("bk", bk)])
                A("dve", lambda e, bk=bk, kc=kc, r0=r0, nr=nr: e.tensor_tensor(
                    out=TBL[:, kc, :, r0:r0 + nr].rearrange("p h r -> p r h"),
                    in0=banks[bk][:, 0:nr * 32].rearrange("p (r h) -> p r h", h=32),
                    in1=mneg[:, kc, r0:r0 + nr].unsqueeze(2).to_broadcast([128, nr, 32]), op=ALU.add),
                  reads=[("bk", bk), ("tmpA", 1)], writes=["TBL"])

        KT = sb("KT", [128, 2, NROWS], BF16)
        xh = [sb("xh%d" % i, [128, D]) for i in range(3)]
        XS = sb("XS", [128, D], BF16)
        ssq = sb("ssq", [128, 8]); rstd_t = sb("rstd_t", [128, 8])
        xnT = sb("xnT", [128, 16, EXT + WMAX], BF16)
        hnT = xnT
        arena = sb("arena", [128, 64, WMAX], BF16)
        actT = arena
        cT = arena[:, 0:32, :].rearrange("p a b -> p (a b)").bitcast(F32).rearrange("p (k w) -> p k w", w=WMAX)
        qT = arena[:, 32:48, :]
        qTf = arena[:, 32:48, :].rearrange("p a b -> p (a b)").bitcast(F32)
        oT = arena[:, 48:64, :]
        SM = sb("SM", [128, 32, WMAX], BF16)
        sT = SM[:, 0:16, :]
        mixT = SM[:, 16:32, :]
        SMf = SM[:, :, :].rearrange("p a b -> p (a b)").bitcast(F32)
        MS = [sb("MS%d" % i, [128, D]) for i in range(3)]
        Gb = [sb("Gb%d" % i, [128, 32 + WMAX]) for i in range(4)]
        ghist = sb("ghist", [128, 16, 32])
        uhist = sb("uhist", [128, 128, 2])
        VP = [sb("VP%d" % i, [128, 4, 128], BF16) for i in range(8)]
        Vf = tmpA[1]
        Eb = [sb("Eb%d" % i, [128, 2, 512], BF16) for i in range(2)]
        rden = [tmpA[0]]
        lnm = qTf[:, 0:WMAX]; lnr = qTf[:, WMAX:2 * WMAX]; lnt = qTf[:, 2 * WMAX:3 * WMAX]
        lnz = [qTf[:, (3 + i) * WMAX:(4 + i) * WMAX] for i in range(2)]
        CO = cT[:, 0:4, :]; T1 = cT[:, 4:8, :]; sgt = [cT[:, 8 + i, :] for i in range(2)]
        Ub = [SMf[:, i * (WMAX + 2):(i + 1) * (WMAX + 2)] for i in range(4)]
        _o = 4 * (WMAX + 2)
        cvb = [SMf[:, _o + i * WMAX:_o + (i + 1) * WMAX] for i in range(4)]
        geb = [SMf[:, _o + (4 + i) * WMAX:_o + (5 + i) * WMAX] for i in range(2)]
        S.alias["CO"] = [("cT", k) for k in range(4)]
        S.alias["T1"] = [("cT", k) for k in range(4, 8)]
        for i in range(2):
            S.alias[("sgt", i)] = [("cT", 8 + i)]
            S.alias[("cT", 8 + i)] = [("sgt", i)]
        for k in range(4):
            S.alias[("cT", k)] = ["CO"]
            S.alias[("cT", 4 + k)] = ["T1"]
        lnn = ["lnm", "lnr", "lnt", ("lnz", 0), ("lnz", 1)]
        S.alias["qT"] = list(lnn)
        for nme in lnn:
            S.alias[nme] = ["qT"]
        s5n = [("Ub", i) for i in range(4)] + [("cvb", i) for i in range(4)] + [("geb", i) for i in range(2)]
        S.alias["sT"] = list(s5n); S.alias["mixT"] = list(s5n)
        for nme in s5n:
            S.alias[nme] = ["sT", "mixT"]
        ckb = VP[4][:, :, :].rearrange("p g c -> p (g c)").rearrange("p (s d) -> p s d", s=2)
        KcT = VP[5][:, :, :].rearrange("p (a b) c -> p a b c", a=2)
        cvbuf = VP[6][:, :, :].rearrange("p g c -> p (g c)").rearrange("p (s d) -> p s d", s=2)
        VPc = [VP[0], VP[1]]
        VsP = [VP[2], VP[3]]
        GS = sb("GS", [128, 16, 38]); scb = [tmpA[2][:, :].rearrange("p (a c) -> p a c", a=4)]
        sfb = [tmpA[0]]
        UST = sb("UST", [128, 4, 16, 10])
        UO = sb("UO", [128, 4, 34]); uost = [tmpA[1]]
        GO = GS[:, :, 0:30]
        S.alias["GO"] = ["GS"]; S.alias["GS"] = ["GO"]
        cnt = {"g": 0, "e": 0, "u": 0, "o": 0}

        print("SBUF bytes remaining after alloc:", nc.sbuf_bytes_remaining)
        def rmsnorm_stats(src_ap, n, col, rd):
            A("dve", lambda e: e.memset(ssq[0:n, col:col + 1], 0.0), writes=[("ssq", col)])
            A("act", lambda e: e.activation(out=XS[0:n, :], in_=src_ap, func=AF.Square, accum_out=ssq[0:n, col:col + 1]),
              reads=rd + [("ssq", col)], writes=["XS", ("ssq", col)])
            A("act", lambda e: e.activation(out=rstd_t[0:n, col:col + 1], in_=ssq[0:n, col:col + 1], func=AF.Sqrt,
                                            scale=1.0 / D, bias=epsT[0:n, 0:1]), reads=[("ssq", col), "epsT"], writes=[("rstd", col)])
            A("dve", lambda e: e.reciprocal(out=rstd_t[0:n, col:col + 1], in_=rstd_t[0:n, col:col + 1]),
              reads=[("rstd", col)], writes=[("rstd", col)])

        def transpose_to(dstT, dst_res, col0, n, src_res):
            for half in range(2):
                tb = ntb()
                for k in range(8):
                    kk = half * 8 + k
                    A("pe", lambda e, tb=tb, k=k, kk=kk: e.transpose(out=tbanks[tb][:, k * 128:k * 128 + n],
                                                                  in_=XS[0:n, kk * 128:(kk + 1) * 128], identity=ident_b[0:n, 0:n]),
                      reads=[src_res, "ident_b"], writes=[("bk", tb)])
                A("act", lambda e, tb=tb, half=half: e.copy(
                    out=dstT[:, half * 8:half * 8 + 8, col0:col0 + n],
                    in_=tbanks[tb][:, :].rearrange("p (k c) -> p k c", c=128)[:, :, 0:n]),
                  reads=[("bk", tb)], writes=[dst_res])

        def gload(GBt, nm, src):
            spq(lambda e: e.dma_start(out=GBt[:], in_=src.partition_broadcast(128)), writes=[nm])

        def stage0_block(row0, n, dst_col, xbuf, xres):
            spq(lambda e: e.dma_start(out=xbuf[0:n, :], in_=xin[row0:row0 + n, :]), writes=[xres])
            rmsnorm_stats(xbuf[0:n, :], n, 0, [xres])
            A("dve", lambda e: e.scalar_tensor_tensor(out=XS[0:n, :], in0=xbuf[0:n, :], scalar=rstd_t[0:n, 0:1], in1=GB0[0:n, :],
                                                      op0=ALU.mult, op1=ALU.mult), reads=[xres, ("rstd", 0), "GB0"], writes=["XS"])
            transpose_to(xnT, "xnT", dst_col, n, "XS")

        def fm_group(wv, c0, chunks, rhs_fn, N, rhs_res, evac):
            s0 = wload(wv[:, 0:8, c0:c0 + 512]); s1 = wload(wv[:, 8:16, c0:c0 + 512])
            bks = {i: nb() for i in chunks}
            for h, s in enumerate((s0, s1)):
                for i in chunks:
                    for k in range(8):
                        kk = 8 * h + k
                        A("pe", lambda e, s=s, i=i, k=k, kk=kk: e.matmul(
                            banks[bks[i]][:, 0:N], lhsT=ring[s][:, k, i * 128:(i + 1) * 128], rhs=rhs_fn(kk),
                            start=(kk == 0), stop=(kk == 15)), reads=[("w", s)] + rhs_res, writes=[("bk", bks[i])])
            for i in chunks:
                evac(i, bks[i])
            return s0, s1

        def make_vpad(win_cols, vp_idx, mask, with_k_out=False, s01=None):
            s0, s1 = s01
            bk = nb()
            c_lo = 0 if with_k_out else 256
            for h, s in enumerate((s0, s1)):
                for k in range(8):
                    kk = 8 * h + k
                    A("pe", lambda e, s=s, k=k, kk=kk: e.matmul(
                        banks[bk][:, c_lo:512], lhsT=xnT[:, kk, win_cols:win_cols + 128], rhs=ring[s][:, k, c_lo:512],
                        start=(kk == 0), stop=(kk == 15)), reads=[("w", s), "xnT"], writes=[("bk", bk)])
            A("dve", lambda e: e.tensor_tensor(out=Vf[:, c_lo:512], in0=banks[bk][:, c_lo:512], in1=bkvB[:, c_lo:512], op=ALU.add),
              reads=[("bk", bk), "bkvB"], writes=[("tmpA", 1)])
            vpad_from(Vf[:, 256:512], ("tmpA", 1), VP[vp_idx], ("VP", vp_idx), mask)

        def vpad_from(src, src_res, dstT, dst_res, mask, npart=128):
            A("dve", lambda e: e.memset(dstT[0:npart], 0.0), writes=[dst_res])
            for par in range(2):
                sv = src.rearrange("p (g t d) -> p g t d", g=2, t=2)[:, :, par, :]
                dv = dstT[0:npart].rearrange("p (g t) c -> p g t c", t=2)[:, :, par, par * 64:par * 64 + 64]
                if mask is None:
                    A("dve", lambda e, sv=sv, dv=dv: e.tensor_copy(out=dv, in_=sv), reads=[src_res], writes=[dst_res])
                else:
                    A("dve", lambda e, sv=sv, dv=dv: e.tensor_scalar(out=dv, in0=sv, scalar1=kmask[0:npart, mask:mask + 1], scalar2=None,
                                                                 op0=ALU.mult), reads=[src_res, "kmask"], writes=[dst_res])

        def attn_block(qc0, n, r0, k0_fn, k0_res, k1_fn, k1_res, nk1, vp0, vp0_res, op0, vp1, vp1_res, op1):
            CB = 4 if n > 64 else 8
            def do_gc(gp, cb):
                if True:
                    cp0 = gp * 8 + cb * CB
                    ebs = []
                    for half in range(2):
                        eb = Eb[cnt["e"] % 2]; ebr = ("Eb", cnt["e"] % 2); cnt["e"] += 1
                        ebs.append((eb, ebr))
                        hs = slice(half * 64, half * 64 + 64)
                        for kc in range(2):
                            nk = 128 if kc == 0 else nk1
                            kfn, kres = (k0_fn, k0_res) if kc == 0 else (k1_fn, k1_res)
                            bk = nb()
                            A("pe", lambda e, bk=bk, kfn=kfn, hs=hs, nk=nk: e.matmul(
                                banks[bk][0:nk, 0:CB * n].rearrange("p (c q) -> p c q", q=n), lhsT=kfn(gp, hs),
                                rhs=qT[hs, cp0:cp0 + CB, qc0:qc0 + n], start=True, stop=False),
                              reads=[kres, "qT"], writes=[("bk", bk)])
                            A("pe", lambda e, bk=bk, kc=kc, nk=nk, half=half: e.matmul(
                                banks[bk][0:nk, 0:CB * n].rearrange("p (c q) -> p c q", q=n), lhsT=ident_b[0:nk, 0:nk],
                                rhs=TBL[0:nk, kc, half * 16 + cp0:half * 16 + cp0 + CB, r0:r0 + n], start=False, stop=True),
                              reads=["ident_b", "TBL"], writes=[("bk", bk)])
                            A("act", lambda e, bk=bk, kc=kc, nk=nk, eb=eb: e.activation(
                                out=eb[0:nk, kc, 0:CB * n], in_=banks[bk][0:nk, 0:CB * n], func=AF.Exp),
                              reads=[("bk", bk)], writes=[ebr])
                    bo = nb(); bd = nb()
                    first = True
                    for half in range(2):
                        eb, ebr = ebs[half]
                        for kc in range(2):
                            nk = 128 if kc == 0 else nk1
                            vp, vpr, opt = (vp0, vp0_res, op0) if kc == 0 else (vp1, vp1_res, op1)
                            last = (half == 1 and kc == 1)
                            A("pe", lambda e, vp=vp, nk=nk, eb=eb, kc=kc, half=half, first=first, last=last: e.matmul(
                                banks[bo][:, 0:CB * n], lhsT=vp[0:nk, 2 * gp + half, :], rhs=eb[0:nk, kc, 0:CB * n],
                                start=first, stop=last), reads=[vpr, ebr], writes=[("bk", bo)])
                            A("pe", lambda e, opt=opt, nk=nk, eb=eb, kc=kc, half=half, first=first, last=last: e.matmul(
                                banks[bd][:, 0:CB * n], lhsT=opt[0][0:nk, half, :], rhs=eb[0:nk, kc, 0:CB * n],
                                start=first, stop=last), reads=[opt[1], ebr], writes=[("bk", bd)])
                            first = False
                    rd = rden[0]; rdr = ("tmpA", 0)
                    for i in range(CB):
                        A("dve", lambda e, i=i: e.tensor_scalar(out=rd[:, i * n:(i + 1) * n], in0=banks[bd][:, i * n:(i + 1) * n],
                                                               scalar1=esink[:, cp0 + i:cp0 + i + 1], scalar2=None, op0=ALU.add),
                          reads=[("bk", bd), "esink"], writes=[rdr])
                    A("dve", lambda e: e.reciprocal(out=rd[:, 0:CB * n], in_=rd[:, 0:CB * n]), reads=[rdr], writes=[rdr])
                    A("dve", lambda e: e.tensor_tensor(
                        out=oT[:, cp0:cp0 + CB, qc0:qc0 + n], in0=banks[bo][:, 0:CB * n].rearrange("p (c q) -> p c q", q=n),
                        in1=rd[:, 0:CB * n].rearrange("p (c q) -> p c q", q=n), op=ALU.mult),
                      reads=[("bk", bo), rdr], writes=["oT"])
            for gp_ in range(2):
                for cb_ in range(8 // CB):
                    do_gc(gp_, cb_)

        def out_dma(fn, reads, grp):
            if grp not in out_dma_groups:
                out_dma_groups.append(grp)
            A("sp", fn, reads=reads, dma=grp)

        vp_of_row = {}
        def process_tile(ti, kind, blocks):
            st["tile"] = ti; st["wi"] = 0
            hp = (blocks[0] == (128, 2))
            lastp = (kind == "p" and blocks[-1][0] == 1026)
            W = sum(n for _, n in blocks)
            row_lo = blocks[0][0]
            mo = EXT if hp else 0
            is_s = (kind == "s")
            gload(GB0, "GB0", g_pre_d)
            if hp:
                stage0_block(0, 128, 0, xh[2], ("xh", 2))
            col = mo
            xbufs = {}
            for bi, (r0_, n) in enumerate(blocks):
                slot = 2 if n == 2 else (bi - (1 if hp else 0))
                xb, xr = xh[slot], ("xh", slot)
                xbufs[bi] = (xb, xr)
                stage0_block(r0_, n, col, xb, xr)
                col += n
            ge = 32 if hp else 0
            NG = ge + W
            Wc = W
            def sample_post(ch, Gt, Gr):
                sj = 0
                spq(lambda e: e.dma_start(out=scb[sj][0:120, :, :], in_=sc_d.rearrange("(a s) r c -> (s r) a c", a=4)[:, :, ch * 128:(ch + 1) * 128]),
                    writes=[("tmpA", 2)])
                for a4 in range(4):
                    bt = nb()
                    A("pe", lambda e, a4=a4, bt=bt: e.transpose(out=banks[bt][:, 0:120], in_=scb[sj][0:120, a4, :], identity=ident_f[0:120, 0:120]),
                      reads=[("tmpA", 2), "ident_f"], writes=[("bk", bt)])
                    A("act", lambda e, a4=a4, bt=bt: e.copy(out=GS[:, a4 * 4:a4 * 4 + 4, 0:30],
                                                           in_=banks[bt][:, 0:120].rearrange("p (s r) -> p s r", r=30)),
                      reads=[("bk", bt)], writes=["GS"])
                A("dve", lambda e: e.tensor_copy(out=GS[:, :, 30:38], in_=Gt[:, 32:32 + 128].rearrange("p (s t) -> p s t", t=8)),
                  reads=[Gr], writes=["GS"])
                cv3 = cT[:, ch, 0:128].rearrange("p (s t) -> p s t", t=8)
                A("dve", lambda e: e.tensor_scalar(out=cv3, in0=GS[:, :, 0:8], scalar1=conv_k[:, ch, 0:1], scalar2=conv_b[:, ch:ch + 1],
                                                   op0=ALU.mult, op1=ALU.add), reads=["GS", "conv_k", "conv_b"], writes=[("cT", ch)])
                for j in range(1, 31):
                    A("dve", lambda e, j=j: e.scalar_tensor_tensor(out=cv3, in0=GS[:, :, j:j + 8], scalar=conv_k[:, ch, j:j + 1], in1=cv3,
                                                                  op0=ALU.mult, op1=ALU.add), reads=["GS", "conv_k", ("cT", ch)], writes=[("cT", ch)])
                bt = nb()
                A("pe", lambda e, bt=bt: e.transpose(out=banks[bt][:, 0:128], in_=Gt[:, 32:160], identity=ident_f[:, :]),
                  reads=[Gr, "ident_f"], writes=[("bk", bt)])
                A("act", lambda e, bt=bt: e.copy(out=MS[0][:, ch * 128:(ch + 1) * 128], in_=banks[bt][:, 0:128]),
                  reads=[("bk", bt)], writes=[("MS", 0)])

            def prompt_post(plist):
                ce = "dve"
                for (ch, Gt, Gr) in plist:
                    if hp:
                        A("dve", lambda e, Gt=Gt: e.tensor_scalar(out=Gt[:, 0:34], in0=Gt[:, 0:34], scalar1=kmask[:, 0:1], scalar2=None, op0=ALU.mult),
                          reads=[Gr, "kmask"], writes=[Gr])
                    else:
                        A("dve", lambda e, Gt=Gt, ch=ch: e.tensor_copy(out=Gt[:, 0:32], in_=ghist[:, ch, :]), reads=[("ghist", ch)], writes=[Gr])
                    A("dve", lambda e, Gt=Gt, ch=ch: e.tensor_copy(out=ghist[:, ch, :], in_=Gt[:, W:W + 32]), reads=[Gr], writes=[("ghist", ch)])
                    if lastp:
                        A("act", lambda e, Gt=Gt, ch=ch: e.copy(out=GO[:, ch, :], in_=Gt[:, 32 + W - 30:32 + W]), reads=[Gr], writes=["GO"])
                    A(ce, lambda e, Gt=Gt, ch=ch: e.tensor_scalar(out=cT[:, ch, 0:Wc], in0=Gt[:, 2:2 + Wc], scalar1=conv_k[:, ch, 0:1], scalar2=conv_b[:, ch:ch + 1],
                                                                 op0=ALU.mult, op1=ALU.add), reads=[Gr, "conv_k", "conv_b"], writes=[("cT", ch)])
                for j in range(1, 31):
                    for (ch, Gt, Gr) in plist:
                        A(ce, lambda e, j=j, Gt=Gt, ch=ch: e.scalar_tensor_tensor(out=cT[:, ch, 0:Wc], in0=Gt[:, 2 + j:2 + j + Wc], scalar=conv_k[:, ch, j:j + 1], in1=cT[:, ch, 0:Wc],
                                                                                op0=ALU.mult, op1=ALU.add), reads=[Gr, "conv_k", ("cT", ch)], writes=[("cT", ch)])

            def ab_group(g):
                s_pair = {}
                posts = []

                def evac_ab(i, bk, g=g):
                    s_pair[i] = bk
                    if i % 2 == 0:
                        return
                    ch = g * 2 + i // 2
                    ba, bb = s_pair[i - 1], bk
                    gi = cnt["g"] % 4; cnt["g"] += 1
                    Gt, Gr = Gb[gi], ("Gb", gi)
                    sg = tmpA[gi % 2]; sgr = ("tmpA", gi % 2)
                    A("act", lambda e: e.activation(out=sg[:, 0:NG], in_=banks[bb][:, 0:NG], func=AF.Sigmoid,
                                                    bias=b_in_T[:, 2 * ch + 1:2 * ch + 2]), reads=[("bk", bb), "b_in_T"], writes=[sgr])
                    g0 = 32 - ge
                    A("dve", lambda e: e.scalar_tensor_tensor(out=Gt[:, g0:g0 + NG], in0=banks[ba][:, 0:NG],
                                                              scalar=b_in_T[:, 2 * ch:2 * ch + 1], in1=sg[:, 0:NG], op0=ALU.add, op1=ALU.mult),
                      reads=[("bk", ba), "b_in_T", sgr], writes=[Gr])
                    if is_s:
                        posts.append((ch, Gt, Gr))
                        return
                    posts.append((ch, Gt, Gr))

                xc0 = mo - ge
                fm_group(w_in_v, g * 512, [0, 1, 2, 3], lambda kk, xc0=xc0, NG=NG: xnT[:, kk, xc0:xc0 + NG], NG, ["xnT"], evac_ab)
                return posts

            def ab_post(posts):
                if is_s:
                    for (ch_, Gt_, Gr_) in posts:
                        sample_post(ch_, Gt_, Gr_)
                else:
                    prompt_post(posts)

            def ab_tail():
              if is_s:
                for s in range(16):
                    out_dma(lambda e, s=s: e.dma_start(out=convs_d[s, 22:30, :], in_=MS[0][s * 8:s * 8 + 8, :]), [("MS", 0)], "o_convs")
                out_dma(lambda e: e.dma_start(out=convs_d[:, 0:22, :], in_=sc_d[:, 8:30, :]), [], "o_convs")
              if lastp:
                for ch in range(16):
                    bt = nb()
                    A("pe", lambda e, bt=bt, ch=ch: e.transpose(out=banks[bt][0:30, 0:128], in_=GO[:, ch, :], identity=ident_f[:, :]),
                      reads=["GO", "ident_f"], writes=[("bk", bt)])
                    A("act", lambda e, bt=bt, ch=ch: e.copy(out=MS[0][0:30, ch * 128:(ch + 1) * 128], in_=banks[bt][0:30, 0:128]),
                      reads=[("bk", bt)], writes=[("MS", 0)])
                out_dma(lambda e: e.dma_start(out=convp_d[:, :], in_=MS[0][0:30, :]), [("MS", 0)], "o_convp")
            def q_group(g):
                def evac_q(i, bk, g=g):
                    cp = g * 4 + i
                    A("act", lambda e: e.activation(out=qT[:, cp, 0:W], in_=banks[bk][:, 0:W], func=AF.Identity, scale=0.125,
                                                    bias=bq8[:, cp:cp + 1]), reads=[("bk", bk), "bq8"], writes=["qT"])
                fm_group(w_in_v, (8 + g) * 512, [0, 1, 2, 3], lambda kk: xnT[:, kk, mo:mo + W], W, ["xnT"], evac_q)

            kvst = {}

            def kv_part():
              def evac_k(i, bk):
                  A("act", lambda e: e.activation(out=KT[:, i, row_lo:row_lo + W], in_=banks[bk][:, 0:W], func=AF.Identity,
                                                  bias=b_in_T[:, 48 + i:49 + i]), reads=[("bk", bk), "b_in_T"], writes=["KT"])
              s01 = fm_group(w_in_v, 12 * 512, [0, 1], lambda kk: xnT[:, kk, mo:mo + W], W, ["xnT"], evac_k)
              kvst["s01"] = s01
              if hp:
                  for i in range(2):
                      bk = nb()
                      for h, s in enumerate(s01):
                          for k in range(8):
                              kk = 8 * h + k
                              A("pe", lambda e, s=s, k=k, kk=kk, i=i, bk=bk: e.matmul(
                                  banks[bk][:, 0:128], lhsT=ring[s][:, k, i * 128:(i + 1) * 128], rhs=xnT[:, kk, 0:128],
                                  start=(kk == 0), stop=(kk == 15)), reads=[("w", s), "xnT"], writes=[("bk", bk)])
                      A("act", lambda e, i=i, bk=bk: e.activation(out=KT[:, i, 0:128], in_=banks[bk][:, 0:128], func=AF.Identity,
                                                                bias=b_in_T[:, 48 + i:49 + i]), reads=[("bk", bk), "b_in_T"], writes=["KT"])
              if not is_s:
                  wins = []
                  if hp:
                      wins += [(0, 0), (128, 1), (2, 0)]
                  for (r0_, n) in blocks:
                      if n == 128:
                          wins.append((r0_, None))
                  for (wr, mask) in wins:
                      vi = len(vp_of_row) % 8
                      vp_of_row[wr] = vi
                      last_win = (wr == 1026)
                      make_vpad(wr - row_lo + mo, vi, mask, with_k_out=last_win, s01=s01)
                      if last_win:
                          out_dma(lambda e: e.dma_start(out=kvp_d[:, :], in_=Vf[:, :]), [("tmpA", 1)], "o_kvp")
              else:
                  make_vpad(0, 7, None, with_k_out=True, s01=s01)
                  for s in range(16):
                      out_dma(lambda e, s=s: e.dma_start(out=ks_d[s, 120:128, :], in_=Vf[s * 8:s * 8 + 8, 0:256]), [("tmpA", 1)], "o_ks")
                      out_dma(lambda e, s=s: e.dma_start(out=vs_d[s, 120:128, :], in_=Vf[s * 8:s * 8 + 8, 256:512]), [("tmpA", 1)], "o_vs")
                  out_dma(lambda e: e.dma_start(out=ks_d[:, 0:120, :], in_=ck_d[:, 8:128, :]), [], "o_ks")
                  out_dma(lambda e: e.dma_start(out=vs_d[:, 0:120, :], in_=cv_d[:, 8:128, :]), [], "o_vs")
            def attn_p(col, r0_, n):
                if True:
                    if n == 2:
                        w0, w1 = 0, 128
                        o0 = (OPF, "OPF"); o1 = (OPB, "OPB"); rr0 = 1
                    else:
                        w0, w1 = r0_ - 128, r0_
                        o0 = (OPF, "OPF") if w0 == 2 else (OP1, "OP1"); o1 = (OP1, "OP1"); rr0 = 1
                    v0, v1 = vp_of_row[w0], vp_of_row[w1]
                    attn_block(col, n, rr0,
                               lambda gp, hs, w0=w0: KT[hs, gp, w0:w0 + 128], "KT",
                               lambda gp, hs, w1=w1: KT[hs, gp, w1:w1 + 128], "KT", 128,
                               VP[v0], ("VP", v0), o0, VP[v1], ("VP", v1), o1)

            if is_s:
                for g in range(8):
                    ab_post(ab_group(g))
                ab_tail()
                for g in range(4):
                    q_group(g)
                kv_part()
            else:
                Bq = [(lambda g=g: q_group(g)) for g in range(4)] + [kv_part]
                col_ = 0
                for (r0_, n) in blocks:
                    Bq.append(lambda col_=col_, r0_=r0_, n=n: attn_p(col_, r0_, n))
                    col_ += n
                pending = None
                bidx = 0
                for g in range(8):
                    posts_g = ab_group(g)
                    if pending is not None:
                        ab_post(pending)
                    if g >= 1 and bidx < len(Bq):
                        Bq[bidx](); bidx += 1
                    pending = posts_g
                ab_post(pending)
                ab_tail()
                while bidx < len(Bq):
                    Bq[bidx](); bidx += 1
            s01 = kvst["s01"]
            if is_s:
                for s in range(16):
                    j2 = s % 2
                    if j2 == 0:
                        A("pool", lambda e, s=s: e.dma_start(out=ckb[:, :, :], in_=ck_d[s:s + 2].rearrange("s k d -> k s d")), writes=[("VP", 4)], dma="ckb")
                        A("pool", lambda e, s=s: e.dma_start(out=cvbuf[:, :, :], in_=cv_d[s:s + 2].rearrange("s k d -> k s d")), writes=[("VP", 6)], dma="cvb")
                        for s2 in range(2):
                            tb = ntb()
                            for gp in range(2):
                                A("pe", lambda e, tb=tb, gp=gp, s2=s2: e.transpose(out=tbanks[tb][:, gp * 128:(gp + 1) * 128],
                                                                              in_=ckb[:, s2, gp * 128:(gp + 1) * 128], identity=ident_b[:, :]),
                                  reads=[("VP", 4), "ident_b"], writes=[("bk", tb)])
                            A("act", lambda e, tb=tb, s2=s2: e.copy(out=KcT[:, s2, :, :], in_=tbanks[tb][:, 0:256].rearrange("p (g k) -> p g k", g=2)),
                              reads=[("bk", tb)], writes=[("VP", 5)])
                            vpad_from(cvbuf[:, s2, :], ("VP", 6), VPc[s2], ("VP", s2), None)
                    bk = nb()
                    for h, sl in enumerate(s01):
                        for k in range(8):
                            kk = 8 * h + k
                            A("pe", lambda e, sl=sl, k=k, kk=kk, bk=bk, s=s: e.matmul(
                                banks[bk][0:8, 0:256], lhsT=xnT[:, kk, s * 8:s * 8 + 8], rhs=ring[sl][:, k, 256:512],
                                start=(kk == 0), stop=(kk == 15)), reads=[("w", sl), "xnT"], writes=[("bk", bk)])
                    A("dve", lambda e, bk=bk: e.tensor_tensor(out=tmpA[2][0:8, 0:256], in0=banks[bk][0:8, 0:256], in1=bkvB[0:8, 256:512], op=ALU.add),
                      reads=[("bk", bk), "bkvB"], writes=[("tmpA", 2)])
                    vpad_from(tmpA[2][0:8, 0:256], ("tmpA", 2), VsP[j2], ("VP", 2 + j2), None, npart=8)
                    rs = 1154 + s * 8
                    attn_block(s * 8, 8, 1,
                               lambda gp, hs, j2=j2: KcT[hs, j2, gp, :], ("VP", 5),
                               lambda gp, hs, rs=rs: KT[hs, gp, rs:rs + 8], "KT", 8,
                               VPc[j2], ("VP", j2), (OP1, "OP1"), VsP[j2], ("VP", 2 + j2), (OP1, "OP1"))
            bsum = nb(); bsq = nb()
            for k in range(16):
                A("pe", lambda e, k=k: e.matmul(banks[bsum][:, 0:W], lhsT=ones_f[:, :], rhs=cT[:, k, 0:W], start=(k == 0), stop=(k == 15)),
                  reads=["ones_f", ("cT", k)], writes=[("bk", bsum)])
                zi = k % 2
                A("act", lambda e, k=k, zi=zi: e.activation(out=lnz[zi][:, 0:W], in_=cT[:, k, 0:W], func=AF.Square), reads=[("cT", k)], writes=[("lnz", zi)])
                A("pe", lambda e, k=k, zi=zi: e.matmul(banks[bsq][:, 0:W], lhsT=ones_f[:, :], rhs=lnz[zi][:, 0:W], start=(k == 0), stop=(k == 15)),
                  reads=["ones_f", ("lnz", zi)], writes=[("bk", bsq)])
            A("dve", lambda e: e.tensor_scalar(out=lnm[:, 0:W], in0=banks[bsum][:, 0:W], scalar1=1.0 / D, scalar2=None, op0=ALU.mult),
              reads=[("bk", bsum)], writes=["lnm"])
            A("dve", lambda e: e.tensor_tensor(out=lnt[:, 0:W], in0=lnm[:, 0:W], in1=lnm[:, 0:W], op=ALU.mult), reads=["lnm"], writes=["lnt"])
            A("dve", lambda e: e.scalar_tensor_tensor(out=lnr[:, 0:W], in0=banks[bsq][:, 0:W], scalar=1.0 / D, in1=lnt[:, 0:W],
                                                      op0=ALU.mult, op1=ALU.subtract), reads=[("bk", bsq), "lnt"], writes=["lnr"])
            A("act", lambda e: e.activation(out=lnr[:, 0:W], in_=lnr[:, 0:W], func=AF.Sqrt, bias=epsT[:, 0:1]), reads=["lnr", "epsT"], writes=["lnr"])
            A("dve", lambda e: e.reciprocal(out=lnr[:, 0:W], in_=lnr[:, 0:W]), reads=["lnr"], writes=["lnr"])
            for k in range(16):
                zi = k % 2
                A("dve", lambda e, k=k, zi=zi: e.tensor_tensor(out=lnz[zi][:, 0:W], in0=cT[:, k, 0:W], in1=lnm[:, 0:W], op=ALU.subtract),
                  reads=[("cT", k), "lnm"], writes=[("lnz", zi)])
                A("dve", lambda e, k=k, zi=zi: e.tensor_tensor(out=lnz[zi][:, 0:W], in0=lnz[zi][:, 0:W], in1=lnr[:, 0:W], op=ALU.mult),
                  reads=[("lnz", zi), "lnr"], writes=[("lnz", zi)])
                A("act", lambda e, k=k, zi=zi: e.activation(out=sT[:, k, 0:W], in_=lnz[zi][:, 0:W], func=AF.Silu, scale=ln_g[:, k:k + 1], bias=ln_b[:, k:k + 1]),
                  reads=[("lnz", zi), "ln_g", "ln_b"], writes=["sT"])
            for cg in range(4):
                def ev_co(i, bk):
                    A("act", lambda e: e.copy(out=CO[:, i, 0:W], in_=banks[bk][:, 0:W]), reads=[("bk", bk)], writes=["CO"])
                fm_group(w_cp_v, cg * 512, [0, 1, 2, 3], lambda kk: sT[:, kk, 0:W], W, ["sT"], ev_co)

                def ev_gc(i, bk, cg=cg):
                    ch = 52 + cg * 4 + i
                    zi = i % 2
                    A("act", lambda e: e.activation(out=sgt[zi][:, 0:W], in_=banks[bk][:, 0:W], func=AF.Sigmoid, bias=b_in_T[:, ch:ch + 1]),
                      reads=[("bk", bk), "b_in_T"], writes=[("sgt", zi)])
                    A("dve", lambda e: e.tensor_tensor(out=T1[:, i, 0:W], in0=sgt[zi][:, 0:W], in1=CO[:, i, 0:W], op=ALU.mult),
                      reads=[("sgt", zi), "CO"], writes=["T1"])
                fm_group(w_in_v, (13 + cg) * 512, [0, 1, 2, 3], lambda kk: xnT[:, kk, mo:mo + W], W, ["xnT"], ev_gc)
                fm_group(w_ap_v, cg * 512, [0, 1, 2, 3], lambda kk: oT[:, kk, 0:W], W, ["oT"], ev_co)

                def ev_ga(i, bk, cg=cg):
                    ch = 68 + cg * 4 + i
                    zi = i % 2
                    A("act", lambda e: e.activation(out=sgt[zi][:, 0:W], in_=banks[bk][:, 0:W], func=AF.Sigmoid, bias=b_in_T[:, ch:ch + 1]),
                      reads=[("bk", bk), "b_in_T"], writes=[("sgt", zi)])
                    A("dve", lambda e: e.tensor_tensor(out=sgt[zi][:, 0:W], in0=sgt[zi][:, 0:W], in1=CO[:, i, 0:W], op=ALU.mult),
                      reads=[("sgt", zi), "CO"], writes=[("sgt", zi)])
                    A("dve", lambda e: e.tensor_tensor(out=mixT[:, cg * 4 + i, 0:W], in0=sgt[zi][:, 0:W], in1=T1[:, i, 0:W], op=ALU.add),
                      reads=[("sgt", zi), "T1"], writes=["mixT"])
                fm_group(w_in_v, (17 + cg) * 512, [0, 1, 2, 3], lambda kk: xnT[:, kk, mo:mo + W], W, ["xnT"], ev_ga)
            gload(GB0, "GB0", g_post_d)
            for cg in range(4):
                s0 = wload(w_o_v[:, 0:8, cg * 512:(cg + 1) * 512]); s1 = wload(w_o_v[:, 8:16, cg * 512:(cg + 1) * 512])
                col = 0
                for bi, (r0_, n) in enumerate(blocks):
                    bk = nb()
                    for h, s in enumerate((s0, s1)):
                        for k in range(8):
                            kk = 8 * h + k
                            A("pe", lambda e, s=s, k=k, kk=kk, bk=bk, col=col, n=n: e.matmul(
                                banks[bk][0:n, :], lhsT=mixT[:, kk, col:col + n], rhs=ring[s][:, k, :], start=(kk == 0), stop=(kk == 15)),
                              reads=[("w", s), "mixT"], writes=[("bk", bk)])
                    ms_i = 2 if n == 2 else (bi - (1 if hp else 0))
                    A("act", lambda e, bk=bk, cg=cg, ms_i=ms_i, n=n: e.copy(out=MS[ms_i][0:n, cg * 512:(cg + 1) * 512], in_=banks[bk][0:n, :]),
                      reads=[("bk", bk)], writes=[("MS", ms_i)])
                    col += n
            col = 0
            for bi, (r0_, n) in enumerate(blocks):
                ms_i = 2 if n == 2 else (bi - (1 if hp else 0))
                xb, xr = xbufs[bi]
                rmsnorm_stats(MS[ms_i][0:n, :], n, 1, [("MS", ms_i)])
                A("dve", lambda e, ms_i=ms_i, n=n: e.scalar_tensor_tensor(out=MS[ms_i][0:n, :], in0=MS[ms_i][0:n, :], scalar=rstd_t[0:n, 1:2], in1=GB0[0:n, :],
                                                                        op0=ALU.mult, op1=ALU.mult), reads=[("MS", ms_i), ("rstd", 1), "GB0"], writes=[("MS", ms_i)])
                A("dve", lambda e, ms_i=ms_i, n=n, xb=xb: e.tensor_tensor(out=xb[0:n, :], in0=xb[0:n, :], in1=MS[ms_i][0:n, :], op=ALU.add),
                  reads=[xr, ("MS", ms_i)], writes=[xr])
                col += n
            gload(GB0, "GB0", g_fpre_d)
            col = 0
            for bi, (r0_, n) in enumerate(blocks):
                xb, xr = xbufs[bi]
                rmsnorm_stats(xb[0:n, :], n, 2, [xr])
                A("dve", lambda e, n=n, xb=xb: e.scalar_tensor_tensor(out=XS[0:n, :], in0=xb[0:n, :], scalar=rstd_t[0:n, 2:3], in1=GB0[0:n, :],
                                                                    op0=ALU.mult, op1=ALU.mult), reads=[xr, ("rstd", 2), "GB0"], writes=["XS"])
                transpose_to(hnT, "xnT", col, n, "XS")
                col += n
            for g in range(32):
                u_pair = {}

                def ev_up(i, bk, g=g):
                    u_pair[i] = bk
                    if i % 2 == 0:
                        return
                    j = g * 2 + i // 2
                    res = []
                    for t2, bkx in enumerate((u_pair[i - 1], bk)):
                        chn = 2 * j + t2
                        ui = cnt["u"] % 4; cnt["u"] += 1
                        Ut, Ur = Ub[ui], ("Ub", ui)
                        A("act", lambda e, Ut=Ut, bkx=bkx: e.copy(out=Ut[:, 2:2 + W], in_=banks[bkx][:, 0:W]), reads=[("bk", bkx)], writes=[Ur])
                        cvt, cvr = cvb[ui], ("cvb", ui)
                        if is_s:
                            i4 = i // 2 * 2 + t2
                            U3 = UST[:, i4, :, :]
                            A("dve", lambda e, Ut=Ut, U3=U3: e.tensor_copy(out=U3[:, :, 2:10], in_=Ut[:, 2:130].rearrange("p (s t) -> p s t", t=8)),
                              reads=[Ur, ("USTh", g)], writes=[("UST", i4)])
                            c3 = cvt[:, 0:128].rearrange("p (s t) -> p s t", t=8)
                            A("dve", lambda e, U3=U3, c3=c3, chn=chn: e.tensor_scalar(out=c3, in0=U3[:, :, 0:8], scalar1=ffn_k[:, chn, 0:1], scalar2=ffn_b[:, chn:chn + 1],
                                                                                   op0=ALU.mult, op1=ALU.add), reads=[("UST", i4), "ffn_k", "ffn_b"], writes=[cvr])
                            for tp in (1, 2):
                                A("dve", lambda e, U3=U3, c3=c3, chn=chn, tp=tp: e.scalar_tensor_tensor(out=c3, in0=U3[:, :, tp:tp + 8], scalar=ffn_k[:, chn, tp:tp + 1], in1=c3,
                                                                                                      op0=ALU.mult, op1=ALU.add), reads=[("UST", i4), "ffn_k", cvr], writes=[cvr])
                            A("act", lambda e, Ut=Ut, i4=i4: e.copy(out=UO[:, i4, 2:34].rearrange("p (s t) -> p s t", t=2),
                                                                   in_=Ut[:, 2:130].rearrange("p (s t) -> p s t", t=8)[:, :, 6:8]), reads=[Ur], writes=[("UO", i4)])
                        else:
                            if hp:
                                A("dve", lambda e, Ut=Ut: e.tensor_scalar(out=Ut[:, 2:4], in0=Ut[:, 2:4], scalar1=kmask[:, 0:1], scalar2=None, op0=ALU.mult),
                                  reads=[Ur, "kmask"], writes=[Ur])
                                A("dve", lambda e, Ut=Ut: e.memset(Ut[:, 0:2], 0.0), writes=[Ur])
                            else:
                                A("dve", lambda e, Ut=Ut, chn=chn: e.tensor_copy(out=Ut[:, 0:2], in_=uhist[:, chn, :]), reads=[("uhist", chn)], writes=[Ur])
                            A("dve", lambda e, Ut=Ut, chn=chn: e.tensor_copy(out=uhist[:, chn, :], in_=Ut[:, W:W + 2]), reads=[Ur], writes=[("uhist", chn)])
                            if lastp:
                                i4 = i // 2 * 2 + t2
                                A("act", lambda e, Ut=Ut, i4=i4: e.copy(out=UO[:, i4, 0:2], in_=Ut[:, W:W + 2]), reads=[Ur], writes=[("UO", i4)])
                            A("dve", lambda e, Ut=Ut, cvt=cvt, chn=chn: e.tensor_scalar(out=cvt[:, 0:W], in0=Ut[:, 0:W], scalar1=ffn_k[:, chn, 0:1], scalar2=ffn_b[:, chn:chn + 1],
                                                                                     op0=ALU.mult, op1=ALU.add), reads=[Ur, "ffn_k", "ffn_b"], writes=[cvr])
                            for tp in (1, 2):
                                A("dve", lambda e, Ut=Ut, cvt=cvt, chn=chn, tp=tp: e.scalar_tensor_tensor(out=cvt[:, 0:W], in0=Ut[:, tp:tp + W], scalar=ffn_k[:, chn, tp:tp + 1], in1=cvt[:, 0:W],
                                                                                                        op0=ALU.mult, op1=ALU.add), reads=[Ur, "ffn_k", cvr], writes=[cvr])
                        res.append((cvt, cvr))
                    (cgt, cgr), (cvt, cvr) = res
                    gi = j % 2
                    A("act", lambda e: e.activation(out=geb[gi][:, 0:W], in_=cgt[:, 0:W], func=AF.Gelu_apprx_tanh), reads=[cgr], writes=[("geb", gi)])
                    A("dve", lambda e: e.tensor_tensor(out=actT[:, j, 0:W], in0=geb[gi][:, 0:W], in1=cvt[:, 0:W], op=ALU.mult),
                      reads=[("geb", gi), cvr], writes=["actT"])

                if is_s:
                    sj = 0
                    spq(lambda e, sj=sj, g=g: e.dma_start(out=sfb[sj][0:32, :], in_=sf_d.rearrange("s r c -> (s r) c")[:, g * 512:(g + 1) * 512]), writes=[("tmpA", 0)])
                    for i4 in range(4):
                        bt = nb()
                        A("pe", lambda e, bt=bt, i4=i4, sj=sj: e.transpose(out=banks[bt][:, 0:32], in_=sfb[sj][0:32, i4 * 128:(i4 + 1) * 128], identity=ident_f[0:32, 0:32]),
                          reads=[("tmpA", 0), "ident_f"], writes=[("bk", bt)])
                        A("act", lambda e, bt=bt, i4=i4: e.copy(out=UST[:, i4, :, 0:2], in_=banks[bt][:, 0:32].rearrange("p (s r) -> p s r", r=2)),
                          reads=[("bk", bt)], writes=[("UST", i4), ("USTh", g)])
                fm_group(w_up_v, g * 512, [0, 1, 2, 3], lambda kk: hnT[:, kk, 0:W], W, ["xnT"], ev_up)
                if lastp or is_s:
                    c_lo, c_n = (0, 2) if lastp else (2, 32)
                    oi = 0
                    for i4 in range(4):
                        bt = nb()
                        A("pe", lambda e, bt=bt, i4=i4: e.transpose(out=banks[bt][0:c_n, 0:128], in_=UO[:, i4, c_lo:c_lo + c_n], identity=ident_f[:, :]),
                          reads=[("UO", i4), "ident_f"], writes=[("bk", bt)])
                        A("act", lambda e, bt=bt, i4=i4, oi=oi: e.copy(out=uost[oi][0:c_n, i4 * 128:(i4 + 1) * 128], in_=banks[bt][0:c_n, 0:128]),
                          reads=[("bk", bt)], writes=[("tmpA", 1)])
                    if lastp:
                        out_dma(lambda e, oi=oi, g=g: e.dma_start(out=ffnp_d[:, g * 512:(g + 1) * 512], in_=uost[oi][0:2, :]), [("tmpA", 1)], "o_ffnp%d" % oi)
                    else:
                        out_dma(lambda e, oi=oi, g=g: e.dma_start(out=ffns_d[:, g * 512:(g + 1) * 512], in_=uost[oi][0:32, :]), [("tmpA", 1)], "o_ffns%d" % oi)
            gload(GB0, "GB0", g_fpost_d)
            oblocks = [(bi, r0_, n) for bi, (r0_, n) in enumerate(blocks) if n == 128]
            for cg in range(4):
                bkm = {bi: nb() for (bi, _, _) in oblocks}
                for kq in range(8):
                    s = wload(w_dn_v[:, kq * 8:(kq + 1) * 8, cg * 512:(cg + 1) * 512])
                    col = 0
                    for bi, (r0_, n) in enumerate(blocks):
                        if n == 128:
                            for k in range(8):
                                kk = kq * 8 + k
                                A("pe", lambda e, s=s, k=k, kk=kk, bi=bi, col=col, bkm=bkm: e.matmul(
                                    banks[bkm[bi]][:, :], lhsT=actT[:, kk, col:col + 128], rhs=ring[s][:, k, :], start=(kk == 0), stop=(kk == 63)),
                                  reads=[("w", s), "actT"], writes=[("bk", bkm[bi])])
                        col += n
                for (bi, r0_, n) in oblocks:
                    ms_i = bi - (1 if hp else 0)
                    A("act", lambda e, bi=bi, ms_i=ms_i, cg=cg, bkm=bkm: e.copy(out=MS[ms_i][:, cg * 512:(cg + 1) * 512], in_=banks[bkm[bi]][:, :]),
                      reads=[("bk", bkm[bi])], writes=[("MS", ms_i)])
            for (bi, r0_, n) in oblocks:
                ms_i = bi - (1 if hp else 0)
                xb, xr = xbufs[bi]
                rmsnorm_stats(MS[ms_i][:, :], 128, 3, [("MS", ms_i)])
                A("dve", lambda e, ms_i=ms_i: e.scalar_tensor_tensor(out=MS[ms_i][:, :], in0=MS[ms_i][:, :], scalar=rstd_t[:, 3:4], in1=GB0[:, :],
                                                                   op0=ALU.mult, op1=ALU.mult), reads=[("MS", ms_i), ("rstd", 3), "GB0"], writes=[("MS", ms_i)])
                A("dve", lambda e, ms_i=ms_i, xb=xb: e.tensor_tensor(out=MS[ms_i][:, :], in0=MS[ms_i][:, :], in1=xb[:, :], op=ALU.add),
                  reads=[("MS", ms_i), xr], writes=[("MS", ms_i)])
                yrow = r0_ - 130
                out_dma(lambda e, ms_i=ms_i, yrow=yrow: e.dma_start(out=y_d[yrow:yrow + 128, :], in_=MS[ms_i][:, :]), [("MS", ms_i)], "o_y%d" % ms_i)

        for ti_, (kind_, blocks_) in enumerate(TILES):
            process_tile(ti_, kind_, blocks_)

        S.finalize()
        sems = {}
        for eng in S.ENGS:
            sems[("eng", eng)] = es.enter_context(nc.semaphore("s_" + eng))
        for gname in S.dma_final:
            sems[("dma", gname)] = es.enter_context(nc.semaphore("d_" + gname))
        with nc.Block() as block:
            @block.tensor
            def _(e): S.emit("pe", e, sems)

            @block.scalar
            def _(e): S.emit("act", e, sems)

            @block.vector
            def _(e): S.emit("dve", e, sems)

            @block.gpsimd
            def _(e): S.emit("pool", e, sems)

            @block.sync
            def _(e):
                S.emit("sp", e, sems)
                for gname in out_dma_groups:
                    e.wait_ge(sems[("dma", gname)], S.dma_final[gname])
    return nc


_NC_CACHE = {}


def kernel(x_prompt, x_sample, cache_k, cache_v, state_conv, state_ffn_conv,
           norm_mix_pre, w_in, b_in, conv_dw_k, conv_dw_b, conv_ln_g, conv_ln_b, w_conv_proj,
           attn_sinks, rel_bias, w_attn_proj, w_out, norm_mix_post,
           norm_ffn_pre, w_up, ffn_dw_k, ffn_dw_b, w_down, norm_ffn_post):
    f = np.float32
    x_prompt = np.asarray(x_prompt, f); x_sample = np.asarray(x_sample, f)
    wp = win_perm()
    w_in_p = np.ascontiguousarray(np.asarray(w_in, f)[0][:, wp])
    b_in_p = np.asarray(b_in, f)[0][wp]
    b_in_T = fmT(b_in_p, 84)
    b_kv = np.ascontiguousarray(np.asarray(b_in, f)[0][6144:6656])
    arp = attn_row_perm()
    w_ap = np.ascontiguousarray(np.asarray(w_attn_proj, f)[0][arp, :])
    up = wup_perm()
    w_up_p = np.ascontiguousarray(np.asarray(w_up, f)[0][:, up])
    ffn_k_T = np.ascontiguousarray(np.asarray(ffn_dw_k, f)[0][:, up].reshape(3, 128, 128).transpose(2, 1, 0))
    ffn_b_T = fmT(np.asarray(ffn_dw_b, f)[0][up], 128)
    conv_k_T = np.ascontiguousarray(np.asarray(conv_dw_k, f)[0].reshape(31, 16, 128).transpose(2, 1, 0))
    sinks = np.asarray(attn_sinks, f)[0]
    sinkT = np.zeros((128, 16), f)
    rbp = np.zeros((32, 32), f)
    rb = np.asarray(rel_bias, f)
    for cp in range(16):
        for half in range(2):
            h = head_of(half, cp)
            sinkT[half * 64:half * 64 + 64, cp] = sinks[h]
            rbp[:, half * 16 + cp] = rb[:, h]
    bucket = rel_bucket_np(np.arange(128))
    ohb = np.zeros((32, 128), f); ohb[bucket, np.arange(128)] = 1.0
    jm = np.zeros((128, 384), f)
    for dlt in range(128):
        jm[dlt, 255 - dlt] = 1.0
    mneg = np.zeros((128, 2, 129), f)
    s_idx = np.arange(128)[:, None]; r_idx = np.arange(129)[None, :]
    mneg[:, 0, :] = np.where(s_idx >= r_idx, 0.0, NEG)
    mneg[:, 1, :] = np.where(s_idx <= r_idx - 1, 0.0, NEG)

    shared = {
        "w_in": w_in_p, "b_in_T": b_in_T, "b_kv": b_kv,
        "conv_k_T": conv_k_T, "conv_b_T": fmT(np.asarray(conv_dw_b, f)[0], 16),
        "ln_g_T": fmT(np.asarray(conv_ln_g, f)[0], 16), "ln_b_T": fmT(np.asarray(conv_ln_b, f)[0], 16),
        "w_cp": np.ascontiguousarray(np.asarray(w_conv_proj, f)[0]), "w_ap": w_ap, "w_o": np.ascontiguousarray(np.asarray(w_out, f)[0]),
        "g_pre": np.ascontiguousarray(np.asarray(norm_mix_pre, f)[0]), "g_post": np.ascontiguousarray(np.asarray(norm_mix_post, f)[0]),
        "g_fpre": np.ascontiguousarray(np.asarray(norm_ffn_pre, f)[0]), "g_fpost": np.ascontiguousarray(np.asarray(norm_ffn_post, f)[0]),
        "w_up": w_up_p, "ffn_k_T": ffn_k_T, "ffn_b_T": ffn_b_T, "w_dn": np.ascontiguousarray(np.asarray(w_down, f)[0]),
        "sinkT": sinkT, "rbp": rbp, "ohb": ohb, "jm": jm, "mneg": mneg,
    }
    in_maps = []
    for c in range(NCORES):
        b, qd = c // 4, c % 4
        T0 = 1024 * qd
        xin = np.zeros((NROWS, D), f)
        if qd > 0:
            xin[0:130] = x_prompt[b, T0 - 130:T0]
        xin[130:1154] = x_prompt[b, T0:T0 + 1024]
        xin[1154:1282] = x_sample[16 * c:16 * c + 16].reshape(128, D)
        km = np.ones((128, 2), f)
        if qd == 0:
            km[:, 0] = 0.0
            km[0:2, 1] = 0.0
        m = dict(shared)
        m["xin"] = xin; m["kmask"] = km
        m["ck"] = np.ascontiguousarray(np.asarray(cache_k, f)[0, 16 * c:16 * c + 16].reshape(16, 128, 256))
        m["cv"] = np.ascontiguousarray(np.asarray(cache_v, f)[0, 16 * c:16 * c + 16].reshape(16, 128, 256))
        m["sc"] = np.ascontiguousarray(np.asarray(state_conv, f)[0, 16 * c:16 * c + 16])
        m["sf"] = np.ascontiguousarray(np.asarray(state_ffn_conv, f)[0, 16 * c:16 * c + 16][:, :, up])
        in_maps.append(m)

    if "nc" not in _NC_CACHE:
        _NC_CACHE["nc"] = build()
    nc = _NC_CACHE["nc"]
    res = run_bass_kernel_spmd(nc, in_maps, core_ids=list(range(NCORES)))
    R = res.results
    inv_up = np.argsort(up)
    y_p = np.zeros((2, 4096, D), f); y_s = np.zeros((128, 8, D), f)
    k_p = np.zeros((1, 2, 128, 4, 64), f); v_p = np.zeros((1, 2, 128, 4, 64), f)
    conv_p = np.zeros((1, 2, 30, D), f); ffn_p = np.zeros((1, 2, 2, DFF2), f)
    k_s = np.zeros((1, 128, 128, 4, 64), f); v_s = np.zeros((1, 128, 128, 4, 64), f)
    conv_s = np.zeros((1, 128, 30, D), f); ffn_s = np.zeros((1, 128, 2, DFF2), f)
    for c in range(NCORES):
        b, qd = c // 4, c % 4
        r = R[c]
        y_p[b, 1024 * qd:1024 * qd + 1024] = r["y"][0:1024]
        y_s[16 * c:16 * c + 16] = r["y"][1024:1152].reshape(16, 8, D)
        if qd == 3:
            k_p[0, b] = r["kvp"][:, 0:256].reshape(128, 4, 64)
            v_p[0, b] = r["kvp"][:, 256:512].reshape(128, 4, 64)
            conv_p[0, b] = r["convp"]
            ffn_p[0, b] = r["ffnp"][:, inv_up]
        k_s[0, 16 * c:16 * c + 16] = r["ks"].reshape(16, 128, 4, 64)
        v_s[0, 16 * c:16 * c + 16] = r["vs"].reshape(16, 128, 4, 64)
        conv_s[0, 16 * c:16 * c + 16] = r["convs"]
        ffn_s[0, 16 * c:16 * c + 16] = r["ffns"].reshape(16, 2, DFF2)[:, :, inv_up]
    return (y_p, y_s, k_p, v_p, conv_p, ffn_p, k_s, v_s, conv_s, ffn_s)
```
